# Optimizing a Trainium2 kernel written in Bass

```python
import math
import jax, jax.numpy as jnp
from jax import lax
import numpy as np

D_MODEL = 1024
BATCH = 4
SEQ = 8192
DEPTH = 1
DEC_BATCH = 16
DEC_SEQ = 16
PAST_LEN = 2048

CHUNK = 64
Q_BLOCK = 128
FOX_HEADS = 8
FOX_HEAD_DIM = 64
FOX_WIDTH = FOX_HEADS * FOX_HEAD_DIM
GDN_HEADS = 8
GDN_DK = 64
GDN_DV = 64
GDN_KW = GDN_HEADS * GDN_DK
GDN_VW = GDN_HEADS * GDN_DV
GDN_CONV_DIM = 2 * GDN_KW + GDN_VW
CONV_WIDTH = 4
D_FF = 4 * D_MODEL
IN_WIDTH = 3 * FOX_WIDTH + FOX_HEADS + GDN_CONV_DIM + 2 * GDN_HEADS + GDN_VW + 2 * D_MODEL
EPS = 1e-6

kernel_name = 'hybrid_fox_gdn_stream_step'

F32 = jnp.float32


def rmsnorm(x, gain):
    xf = x.astype(F32)
    y = xf * lax.rsqrt(jnp.mean(xf * xf, axis=-1, keepdims=True) + EPS)
    return (y * gain.astype(F32)).astype(x.dtype)


def l2norm(x):
    xf = x.astype(F32)
    return xf * lax.rsqrt(jnp.sum(xf * xf, axis=-1, keepdims=True) + EPS)


def short_conv(x_ext, w):
    t = x_ext.shape[1] - (CONV_WIDTH - 1)
    y = x_ext[:, 0:t] * w[0]
    for i in range(1, CONV_WIDTH):
        y = y + x_ext[:, i:i + t] * w[i]
    return jax.nn.silu(y)


def in_project(h, w_in, forget_bias, a_log, dt_bias):
    b, t = h.shape[0], h.shape[1]
    sizes = (FOX_WIDTH, FOX_WIDTH, FOX_WIDTH, FOX_HEADS, GDN_CONV_DIM, GDN_HEADS, GDN_HEADS, GDN_VW, D_MODEL, D_MODEL)
    points = [int(p) for p in np.cumsum(sizes)[:-1]]
    z = jnp.einsum('btd,de->bte', h, w_in)
    fq, fk, fv, ff, gqkv, ga, gb, gg, gate_a, gate_b = jnp.split(z, points, axis=-1)
    fq = fq.reshape(b, t, FOX_HEADS, FOX_HEAD_DIM)
    fk = fk.reshape(b, t, FOX_HEADS, FOX_HEAD_DIM)
    fv = fv.reshape(b, t, FOX_HEADS, FOX_HEAD_DIM)
    logf = jax.nn.log_sigmoid(ff.astype(F32) + forget_bias.astype(F32))
    g = -jnp.exp(a_log.astype(F32)) * jax.nn.softplus(ga.astype(F32) + dt_bias.astype(F32))
    beta = jax.nn.sigmoid(gb.astype(F32))
    return fq, fk, fv, logf, gqkv, g, beta, gg, gate_a, gate_b


def gdn_heads(conv_out):
    b, t = conv_out.shape[0], conv_out.shape[1]
    q = conv_out[..., :GDN_KW].reshape(b, t, GDN_HEADS, GDN_DK)
    k = conv_out[..., GDN_KW:2 * GDN_KW].reshape(b, t, GDN_HEADS, GDN_DK)
    v = conv_out[..., 2 * GDN_KW:].reshape(b, t, GDN_HEADS, GDN_DV)
    return l2norm(q) * (GDN_DK ** -0.5), l2norm(k), v.astype(F32)


def fox_prompt(q, k, v, logf):
    b, s = q.shape[0], q.shape[1]
    nblk = s // Q_BLOCK
    scale = FOX_HEAD_DIM ** -0.5
    ft = jnp.cumsum(logf, axis=1).transpose(0, 2, 1)
    q_blocks = jnp.moveaxis(q.reshape(b, nblk, Q_BLOCK, FOX_HEADS, FOX_HEAD_DIM), 1, 0)
    f_blocks = jnp.moveaxis(ft.reshape(b, FOX_HEADS, nblk, Q_BLOCK), 2, 0)
    kpos = jnp.arange(s)

    def one_block(args):
        blk, q_blk, f_blk = args
        sc = jnp.einsum('bqhd,bkhd->bhqk', q_blk, k).astype(F32) * scale
        sc = sc + f_blk[..., :, None] - ft[..., None, :]
        qpos = blk * Q_BLOCK + jnp.arange(Q_BLOCK)
        sc = jnp.where(kpos[None, :] <= qpos[:, None], sc, -jnp.inf)
        p = jax.nn.softmax(sc, axis=-1)
        return jnp.einsum('bhqk,bkhd->bqhd', p.astype(v.dtype), v)

    out = lax.map(one_block, (jnp.arange(nblk), q_blocks, f_blocks))
    return jnp.moveaxis(out, 0, 1).reshape(b, s, FOX_HEADS, FOX_HEAD_DIM)


def fox_sample(q, k, v, logf, past):
    t = q.shape[1]
    scale = FOX_HEAD_DIM ** -0.5
    ft = jnp.cumsum(logf, axis=1).transpose(0, 2, 1)
    sc = jnp.einsum('bqhd,bkhd->bhqk', q, k).astype(F32) * scale
    sc = sc + ft[..., past:, None] - ft[..., None, :]
    mask = jnp.arange(past + t)[None, :] <= (past + jnp.arange(t))[:, None]
    p = jax.nn.softmax(jnp.where(mask, sc, -jnp.inf), axis=-1)
    return jnp.einsum('bhqk,bkhd->bqhd', p.astype(v.dtype), v)


def gdn_chunk(state, q, k, v, g, beta):
    q = q.transpose(0, 2, 1, 3)
    k = k.transpose(0, 2, 1, 3)
    v = v.transpose(0, 2, 1, 3)
    g = g.transpose(0, 2, 1)
    beta = beta.transpose(0, 2, 1)
    c = q.shape[2]
    gc = jnp.cumsum(g, axis=-1)
    pos = jnp.arange(c)
    incl = pos[:, None] >= pos[None, :]
    strict = pos[:, None] > pos[None, :]
    decay = jnp.exp(jnp.where(incl, gc[..., :, None] - gc[..., None, :], -jnp.inf))
    kb = k * beta[..., None]
    lower = jnp.einsum('bhik,bhjk->bhij', kb, k) * jnp.where(strict, decay, 0.0)
    rhs = jnp.concatenate([v * beta[..., None], kb * jnp.exp(gc)[..., None]], axis=-1)
    sol = lax.linalg.triangular_solve(lower, rhs, left_side=True, lower=True, unit_diagonal=True)
    u, w = sol[..., :GDN_DV], sol[..., GDN_DV:]
    v_new = u - jnp.einsum('bhck,bhkv->bhcv', w, state)
    o = (jnp.einsum('bhck,bhkv->bhcv', q * jnp.exp(gc)[..., None], state)
         + jnp.einsum('bhij,bhjv->bhiv', jnp.einsum('bhik,bhjk->bhij', q, k) * decay, v_new))
    g_last = gc[..., -1:]
    new_state = (state * jnp.exp(g_last)[..., None]
                 + jnp.einsum('bhck,bhcv->bhkv', k * jnp.exp(g_last - gc)[..., None], v_new))
    return new_state, o.transpose(0, 2, 1, 3)


def gdn_prompt(q, k, v, g, beta):
    b, s = q.shape[0], q.shape[1]
    n = s // CHUNK
    to_chunks = lambda a: jnp.moveaxis(a.reshape((b, n, CHUNK) + a.shape[2:]), 1, 0)
    xs = (to_chunks(q), to_chunks(k), to_chunks(v), to_chunks(g), to_chunks(beta))
    s0 = jnp.zeros((b, GDN_HEADS, GDN_DK, GDN_DV), F32)

    def step(st, args):
        return gdn_chunk(st, *args)

    s_fin, o = lax.scan(step, s0, xs)
    return s_fin, jnp.moveaxis(o, 0, 1).reshape(b, s, GDN_HEADS, GDN_DV)


def merge(fox_o, gdn_o, gg, gate_a, gate_b, norm_g, w_pa, w_pb, w_out):
    b, t = fox_o.shape[0], fox_o.shape[1]
    ya = fox_o.reshape(b, t, FOX_WIDTH) @ w_pa
    o = rmsnorm(gdn_o, norm_g) * jax.nn.silu(gg.reshape(b, t, GDN_HEADS, GDN_DV).astype(F32))
    yb = o.reshape(b, t, GDN_VW).astype(ya.dtype) @ w_pb
    m = jax.nn.sigmoid(gate_a) * ya + jax.nn.sigmoid(gate_b) * yb
    return m @ w_out


def mixer_prompt(h, w_in, forget_bias, conv_w, a_log, dt_bias, norm_g, w_pa, w_pb, w_out):
    fq, fk, fv, logf, gqkv, g, beta, gg, gate_a, gate_b = in_project(h, w_in, forget_bias, a_log, dt_bias)
    fox_o = fox_prompt(fq, fk, fv, logf)
    conv_ext = jnp.pad(gqkv, ((0, 0), (CONV_WIDTH - 1, 0), (0, 0)))
    q, k, v = gdn_heads(short_conv(conv_ext, conv_w))
    s_fin, gdn_o = gdn_prompt(q, k, v, g, beta)
    y = merge(fox_o, gdn_o, gg, gate_a, gate_b, norm_g, w_pa, w_pb, w_out)
    return y, (fk, fv, logf, s_fin, conv_ext[:, -(CONV_WIDTH - 1):])


def mixer_sample(h, k_cache, v_cache, logf_cache, s_cache, conv_cache,
                 w_in, forget_bias, conv_w, a_log, dt_bias, norm_g, w_pa, w_pb, w_out):
    fq, fk, fv, logf, gqkv, g, beta, gg, gate_a, gate_b = in_project(h, w_in, forget_bias, a_log, dt_bias)
    past = k_cache.shape[1]
    k_cat = jnp.concatenate([k_cache.astype(fk.dtype), fk], axis=1)
    v_cat = jnp.concatenate([v_cache.astype(fv.dtype), fv], axis=1)
    lf_cat = jnp.concatenate([logf_cache.astype(F32), logf], axis=1)
    fox_o = fox_sample(fq, k_cat, v_cat, lf_cat, past)
    conv_ext = jnp.concatenate([conv_cache.astype(gqkv.dtype), gqkv], axis=1)
    q, k, v = gdn_heads(short_conv(conv_ext, conv_w))
    s_new, gdn_o = gdn_chunk(s_cache.astype(F32), q, k, v, g, beta)
    y = merge(fox_o, gdn_o, gg, gate_a, gate_b, norm_g, w_pa, w_pb, w_out)
    return y, (fk, fv, logf, s_new, conv_ext[:, -(CONV_WIDTH - 1):])


def channel_mixer(x, pre_g, post_g, w_up, w_down):
    h = rmsnorm(x, pre_g)
    u = jnp.square(jax.nn.relu(h @ w_up))
    return x + rmsnorm(u @ w_down, post_g)


def setup_inputs(seed: int = 0) -> dict:
    key = jax.random.key(seed)
    ks = jax.random.split(key, 24)

    def nrm(k, shape, scale):
        return jax.random.normal(k, shape, F32) * scale

    def gain(k, shape):
        return 1.0 + 0.02 * jax.random.normal(k, shape, F32)

    x_prompt = nrm(ks[0], (BATCH, SEQ, D_MODEL), 1.0)
    x_sample = nrm(ks[1], (DEC_BATCH, DEC_SEQ, D_MODEL), 1.0)
    cache_fox_k = nrm(ks[2], (DEPTH, DEC_BATCH, PAST_LEN, FOX_HEADS, FOX_HEAD_DIM), 1.0)
    cache_fox_v = nrm(ks[3], (DEPTH, DEC_BATCH, PAST_LEN, FOX_HEADS, FOX_HEAD_DIM), 1.0)
    cache_fox_logf = jax.nn.log_sigmoid(
        jax.random.uniform(ks[4], (DEPTH, DEC_BATCH, PAST_LEN, FOX_HEADS), F32, 1.0, 5.0)
        + nrm(ks[5], (DEPTH, DEC_BATCH, PAST_LEN, FOX_HEADS), 1.0))
    state_gdn = nrm(ks[6], (DEPTH, DEC_BATCH, GDN_HEADS, GDN_DK, GDN_DV), 0.1)
    state_gdn_conv = nrm(ks[7], (DEPTH, DEC_BATCH, CONV_WIDTH - 1, GDN_CONV_DIM), 1.0)
    w_in = nrm(ks[8], (DEPTH, D_MODEL, IN_WIDTH), D_MODEL ** -0.5)
    fox_forget_bias = jax.random.uniform(ks[9], (DEPTH, FOX_HEADS), F32, 1.0, 5.0)
    gdn_conv_w = nrm(ks[10], (DEPTH, CONV_WIDTH, GDN_CONV_DIM), CONV_WIDTH ** -0.5)
    gdn_a_log = jnp.log(jax.random.uniform(ks[11], (DEPTH, GDN_HEADS), F32, 1.0, 16.0))
    dt = jnp.exp(jax.random.uniform(ks[12], (DEPTH, GDN_HEADS), F32, math.log(1e-3), math.log(1e-1)))
    gdn_dt_bias = dt + jnp.log(-jnp.expm1(-dt))
    gdn_norm_g = gain(ks[13], (DEPTH, GDN_DV))
    w_proj_fox = nrm(ks[14], (DEPTH, FOX_WIDTH, D_MODEL), FOX_WIDTH ** -0.5)
    w_proj_gdn = nrm(ks[15], (DEPTH, GDN_VW, D_MODEL), GDN_VW ** -0.5)
    w_out = nrm(ks[16], (DEPTH, D_MODEL, D_MODEL), D_MODEL ** -0.5)
    norm_mix_pre = gain(ks[17], (DEPTH, D_MODEL))
    norm_mix_post = gain(ks[18], (DEPTH, D_MODEL))
    norm_mlp_pre = gain(ks[19], (DEPTH, D_MODEL))
    norm_mlp_post = gain(ks[20], (DEPTH, D_MODEL))
    w_up = nrm(ks[21], (DEPTH, D_MODEL, D_FF), D_MODEL ** -0.5)
    w_down = nrm(ks[22], (DEPTH, D_FF, D_MODEL), D_FF ** -0.5)
    return {'x_prompt': x_prompt, 'x_sample': x_sample,
            'cache_fox_k': cache_fox_k, 'cache_fox_v': cache_fox_v, 'cache_fox_logf': cache_fox_logf,
            'state_gdn': state_gdn, 'state_gdn_conv': state_gdn_conv,
            'w_in': w_in, 'fox_forget_bias': fox_forget_bias, 'gdn_conv_w': gdn_conv_w,
            'gdn_a_log': gdn_a_log, 'gdn_dt_bias': gdn_dt_bias, 'gdn_norm_g': gdn_norm_g,
            'w_proj_fox': w_proj_fox, 'w_proj_gdn': w_proj_gdn, 'w_out': w_out,
            'norm_mix_pre': norm_mix_pre, 'norm_mix_post': norm_mix_post,
            'norm_mlp_pre': norm_mlp_pre, 'norm_mlp_post': norm_mlp_post,
            'w_up': w_up, 'w_down': w_down}


def reference(x_prompt, x_sample, cache_fox_k, cache_fox_v, cache_fox_logf, state_gdn, state_gdn_conv,
              w_in, fox_forget_bias, gdn_conv_w, gdn_a_log, gdn_dt_bias, gdn_norm_g,
              w_proj_fox, w_proj_gdn, w_out, norm_mix_pre, norm_mix_post, norm_mlp_pre, norm_mlp_post,
              w_up, w_down):
    y_p, y_s = x_prompt, x_sample
    st_p, st_s = [], []
    for l in range(DEPTH):
        mw = (w_in[l], fox_forget_bias[l], gdn_conv_w[l], gdn_a_log[l], gdn_dt_bias[l], gdn_norm_g[l],
              w_proj_fox[l], w_proj_gdn[l], w_out[l])
        mix, sp = mixer_prompt(rmsnorm(y_p, norm_mix_pre[l]), *mw)
        y_p = y_p + rmsnorm(mix, norm_mix_post[l])
        y_p = channel_mixer(y_p, norm_mlp_pre[l], norm_mlp_post[l], w_up[l], w_down[l])
        mix, ss = mixer_sample(rmsnorm(y_s, norm_mix_pre[l]), cache_fox_k[l], cache_fox_v[l], cache_fox_logf[l],
                               state_gdn[l], state_gdn_conv[l], *mw)
        y_s = y_s + rmsnorm(mix, norm_mix_post[l])
        y_s = channel_mixer(y_s, norm_mlp_pre[l], norm_mlp_post[l], w_up[l], w_down[l])
        st_p.append(sp)
        st_s.append(ss)
    fk_p, fv_p, lf_p, sg_p, cv_p = [jnp.stack(a) for a in zip(*st_p)]
    fk_s, fv_s, lf_s, sg_s, cv_s = [jnp.stack(a) for a in zip(*st_s)]
    return (y_p, y_s, fk_p, fv_p, lf_p, sg_p, cv_p, fk_s, fv_s, lf_s, sg_s, cv_s)
```

```python
from contextlib import ExitStack
import numpy as np
import ml_dtypes
from concourse.bass_utils import run_bass_kernel_spmd
import concourse.bass as bass
import concourse.mybir as mybir

F32 = mybir.dt.float32
BF16 = mybir.dt.bfloat16
AF = mybir.ActivationFunctionType
ALU = mybir.AluOpType
AX = mybir.AxisListType

ENGS = ("pe", "act", "dve", "pool", "sp")
EPOCH = 30000
NDMASEM = 30
DMA_ENGS = ("sp", "pool")


class Buf:
    __slots__ = ("name", "w", "r")

    def __init__(self, name):
        self.name = name
        self.w = None
        self.r = {}


class FW:
    def __init__(self, nc, stack):
        self.nc = nc
        self.stack = stack
        self.ops = {e: [] for e in ENGS}
        self.cnt = {e: 0 for e in ENGS}
        self.ccnt = {e: 0 for e in ENGS}
        self.sems = {}
        self.dsem = {}
        self.dcnt = {}
        self.drr = {e: 0 for e in ENGS}
        self.waited = {e: {} for e in ENGS}
        for e in DMA_ENGS:
            for i in range(NDMASEM):
                self.dsem[(e, i)] = stack.enter_context(nc.semaphore(f"d_{e}_{i}"))
                self.dcnt[(e, i)] = 0
        self.nsem_ep = {e: 0 for e in ENGS}
        self.pending = {e: [] for e in ENGS}

    def _sem(self, eng, ep):
        k = (eng, ep)
        if k not in self.sems:
            self.sems[k] = self.stack.enter_context(self.nc.semaphore(f"c_{eng}_{ep}"))
        return self.sems[k]

    def _need(self, eng, tok, waits):
        if tok is None:
            return
        key, val = tok
        if eng == "pe" and key[0] == "c" and key[1] == "pe":
            return
        cur = self.waited[eng].get(key, 0)
        if cur >= val:
            return
        self.waited[eng][key] = val
        waits[key] = max(waits.get(key, 0), val)

    def op(self, eng, fn, reads=(), writes=(), dma=False):
        waits = {}
        if self.pending[eng]:
            for t in self.pending[eng]:
                self._need(eng, t, waits)
            self.pending[eng] = []
        for b in reads:
            self._need(eng, b.w, waits)
        for b in writes:
            self._need(eng, b.w, waits)
            for t in b.r.items():
                self._need(eng, t, waits)
        if dma:
            i = self.drr[eng] % NDMASEM
            self.drr[eng] += 1
            if self.dcnt[(eng, i)] > 0:
                self._need(eng, (("d", eng, i), self.dcnt[(eng, i)]), waits)
            self.dcnt[(eng, i)] += 16
            key = ("d", eng, i)
            tok = (key, self.dcnt[(eng, i)])
            inc = (self.dsem[(eng, i)], 16)
        else:
            n = self.ccnt[eng]
            self.ccnt[eng] += 1
            ep = n // EPOCH
            key = ("c", eng, ep)
            tok = (key, n % EPOCH + 1)
            inc = (self._sem(eng, ep), 1)
        self.cnt[eng] += 1
        for b in writes:
            b.w = tok
            b.r = {}
        for b in reads:
            if b not in writes:
                b.r[tok[0]] = max(b.r.get(tok[0], 0), tok[1])
        self.ops[eng].append((waits, fn, inc))
        return tok

    def I(self, eng, method, reads=(), writes=(), **kw):
        dma = method == "dma_start"
        return self.op(eng, lambda e: getattr(e, method)(**kw), reads=reads, writes=writes, dma=dma)

    def semh(self, key):
        if key[0] == "d":
            return self.dsem[(key[1], key[2])]
        return self._sem(key[1], key[2])

    def barrier(self):
        toks = []
        for e in ENGS:
            n = self.ccnt[e]
            if n > 0:
                ep = (n - 1) // EPOCH
                toks.append((("c", e, ep), (n - 1) % EPOCH + 1))
                for ep2 in range(ep):
                    toks.append((("c", e, ep2), EPOCH))
            for i in range(NDMASEM):
                if e in DMA_ENGS and self.dcnt[(e, i)] > 0:
                    toks.append((("d", e, i), self.dcnt[(e, i)]))
        for e in ENGS:
            if self.ops[e]:
                waits = {}
                for t in toks:
                    self._need(e, t, waits)
                if waits:
                    self.ops[e].append((waits, None, None))
            else:
                self.pending[e] = list(toks)

    def emit(self):
        nc = self.nc
        with nc.Block() as block:
            def mk(eng_name):
                def body(e):
                    for waits, fn, inc in self.ops[eng_name]:
                        for key, val in waits.items():
                            e.wait_ge(self.semh(key), val)
                        if fn is not None:
                            ins = fn(e)
                            ins.then_inc(inc[0], inc[1])
                return body
            regs = {"pe": block.tensor, "act": block.scalar, "dve": block.vector, "pool": block.gpsimd, "sp": block.sync}
            for en in ENGS:
                if self.ops[en]:
                    regs[en](mk(en))
        self.ops = {e: [] for e in ENGS}


NP_ = 4096
NO_ = 4096
NS_ = 256
NT_ = NP_ + NO_ + NS_
NQ_ = NO_ + NS_
EPS = 1e-6
C_FQ, C_FK, C_FV, C_FF, C_GQ, C_GA, C_GB, C_GG, C_A, C_B = 0, 512, 1024, 1536, 1544, 3080, 3088, 3096, 3608, 4632


class Ctx:
    pass


def declare(nc, debug, ext_in=()):
    D = Ctx()
    ei = lambda n, s, dt=F32: nc.dram_tensor(n, s, dt, kind="ExternalInput").ap()
    eo = lambda n, s, dt=F32: nc.dram_tensor(n, s, dt, kind="ExternalOutput").ap()
    sc = lambda n, s, dt=F32: nc.dram_tensor(n, s, dt, kind=("ExternalInput" if n in ext_in else ("ExternalOutput" if debug else "Internal"))).ap()
    D.xall = ei("xall", [NT_, 1024])
    D.kmask = ei("kmask", [128, 32])
    D.vmask = ei("vmask", [128, 1])
    D.w_in = ei("w_in", [1024, 5656])
    D.w_sm = ei("w_sm", [1024, 24])
    D.bias24 = ei("bias24", [1, 24])
    D.sgn24 = ei("sgn24", [1, 24])
    D.a_log = ei("a_log", [1, 8])
    D.convT = ei("convT", [128, 12, 4])
    D.conv_hist = ei("conv_hist", [2, 128, 12, 3])
    D.g_mix_pre = ei("g_mix_pre", [1, 1024])
    D.g_mix_post = ei("g_mix_post", [1, 1024])
    D.g_mlp_pre = ei("g_mlp_pre", [1, 1024])
    D.g_mlp_post = ei("g_mlp_post", [1, 1024])
    D.g_gdn = ei("g_gdn", [1, 64])
    D.w_pa = ei("w_pa", [512, 1024])
    D.w_pb = ei("w_pb", [512, 1024])
    D.w_out = ei("w_out", [1024, 1024])
    D.w_up = ei("w_up", [1024, 4096])
    D.w_down = ei("w_down", [4096, 1024])
    D.ck = ei("ck", [2, 2048, 512])
    D.cv = ei("cv", [2, 2048, 512])
    D.clf = ei("clf", [2, 2048, 8])
    D.sgdn = ei("sgdn", [2, 8, 64, 64])
    D.cmask = ei("cmask", [128, 8, 128], BF16)
    D.cf32 = ei("cf32", [128, 6, 128])
    D.y = eo("y", [NQ_, 1024])
    D.fk = eo("fk", [NQ_, 512])
    D.fv = eo("fv", [NQ_, 512])
    D.lf = eo("lf", [NQ_, 8])
    D.sfin = eo("sfin", [3, 8, 64, 64])
    D.convo = eo("convo", [3, 3, 1536])
    D.qT = sc("qT_s", [512, NQ_], BF16)
    D.kT = sc("kT_s", [512, NT_], BF16)
    D.V = sc("V_s", [NT_, 512], BF16)
    D.lfs = sc("lf_s", [NT_, 8])
    D.g = sc("g_s", [NT_, 8])
    D.beta = sc("beta_s", [NT_, 16])
    D.gqT = sc("gqT_s", [512, NT_], BF16)
    D.gkT = sc("gkT_s", [512, NT_], BF16)
    D.gqkv = sc("gqkv_s", [NT_, 1536], BF16)
    D.gg = sc("gg_s", [NQ_, 512])
    D.sgA = sc("sgA_s", [1024, NQ_], BF16)
    D.sgB = sc("sgB_s", [1024, NQ_], BF16)
    D.foT = sc("foT_s", [512, NQ_], BF16)
    D.go = sc("go_s", [NQ_, 512], BF16)
    D.y1 = sc("y1_s", [NQ_, 1024])
    return D


class Rot:
    def __init__(self, items):
        self.items = items
        self.i = 0

    def next(self):
        it = self.items[self.i % len(self.items)]
        self.i += 1
        return it


def phase_a(nc, fw, D, tiles=None):
    I = fw.I
    with ExitStack() as st:
        def sb(name, shape, dt):
            return st.enter_context(nc.sbuf_tensor("A_" + name, shape, dt)), Buf(name)

        def pst(name, shape, dt):
            return st.enter_context(nc.psum_tensor("A_" + name, shape, dt)), Buf(name)
        Win, bWin = sb("Win", [128, 8, 5656], BF16)
        Wsm, bWsm = sb("Wsm", [128, 8, 24], BF16)
        xts = Rot([sb(f"xt{i}", [128, 4, 1024], F32) for i in range(2)])
        hb, bhb = sb("hb", [128, 4, 1024], BF16)
        hTs = Rot([sb(f"hT{i}", [128, 8, 512], BF16) for i in range(1)])
        junk, bjunk = sb("junk", [128, 1024], BF16)
        ss, bss = sb("ss", [128, 4], F32)
        rs, brs = sb("rs", [128, 4], F32)
        gpre, bgpre = sb("gpre", [128, 1024], F32)
        identb, bidentb = sb("identb", [128, 128], BF16)
        diagw, bdiagw = sb("diagw", [128, 48, 128], BF16)
        cwT, bcwT = sb("cwT", [128, 12, 4], F32)
        xg, bxg = sb("xg", [128, 12, 515], BF16)
        cT, bcT = sb("cT", [128, 12, 512], BF16)
        sbf = Rot([sb(f"sbf{i}", [128, 512], BF16) for i in range(4)])
        sf32 = Rot([sb(f"sf32{i}", [128, 512], F32) for i in range(2)])
        tts = Rot([sb(f"tt{i}", [128, 512], BF16) for i in range(3)])
        sqt, bsqt = sb("sqt", [128, 1024], BF16)
        l2ss, bl2ss = sb("l2ss", [128, 4, 16], F32)
        tball, btball = sb("tball", [128, 4, 1536], BF16)
        sm_t, bsm_t = sb("sm_t", [128, 96], F32)
        sm_e, bsm_e = sb("sm_e", [128, 96], F32)
        sm_l, bsm_l = sb("sm_l", [128, 96], F32)
        sm_o, bsm_o = sb("sm_o", [128, 4, 32], F32)
        b24, bb24 = sb("b24", [128, 24], F32)
        s24, bs24 = sb("s24", [128, 24], F32)
        nea, bnea = sb("nea", [128, 8], F32)
        ptr = Rot([pst(f"ptr{i}", [128, 1024], BF16) for i in range(2)])
        pm = Rot([pst(f"pm{i}", [128, 512], F32) for i in range(4)])
        psm, bpsm = pst("psm", [128, 512], F32)

        I("pool", "dma_start", [], [bWin], out=Win[:], in_=D.w_in.rearrange("(c p) n -> p c n", p=128))
        I("pool", "dma_start", [], [bWsm], out=Wsm[:], in_=D.w_sm.rearrange("(c p) n -> p c n", p=128))
        I("sp", "dma_start", [], [bgpre], out=gpre[:], in_=D.g_mix_pre[0:1, :].broadcast_to([128, 1024]))
        I("sp", "dma_start", [], [bidentb], out=identb[:], in_=D.cmask[:, 0, :])
        I("sp", "dma_start", [], [bcwT], out=cwT[:], in_=D.convT[:, :, :])
        I("sp", "dma_start", [], [bb24], out=b24[:], in_=D.bias24[0:1, :].broadcast_to([128, 24]))
        I("sp", "dma_start", [], [bs24], out=s24[:], in_=D.sgn24[0:1, :].broadcast_to([128, 24]))
        I("sp", "dma_start", [], [bnea], out=nea[:], in_=D.a_log[0:1, :].broadcast_to([128, 8]))
        I("act", "activation", [bnea], [bnea], out=nea[:], in_=nea[:], func=AF.Exp)
        I("dve", "tensor_scalar_mul", [bnea], [bnea], out=nea[:], in0=nea[:], scalar1=-1.0)
        I("dve", "tensor_scalar_mul", [bcwT], [bcwT], out=cwT[:], in0=cwT[:], scalar1=0.5)
        for j in range(12):
            for i in range(4):
                I("dve", "tensor_scalar_mul", [bidentb, bcwT], [bdiagw], out=diagw[:, j * 4 + i, :], in0=identb[:], scalar1=cwT[:, j, i:i + 1])
        I("dve", "memset", [], [bxg], ap=xg[:, :, 0:3], constant=0.0)

        all_tiles = [(i * 512, 4, "p") for i in range(8)] + [(NP_ + i * 512, 4, "o") for i in range(8)] + [(NP_ + NO_, 2, "s")]
        if tiles is not None:
            all_tiles = [all_tiles[i] for i in tiles]
        cpy_rr = [0]

        def evac_copy(out_ap, in_ap, reads, writes, scale=None):
            cpy_rr[0] += 1
            if scale is not None:
                I("act", "activation", reads, writes, out=out_ap, in_=in_ap, func=AF.Copy, scale=scale)
            elif cpy_rr[0] % 2 == 0:
                I("act", "activation", reads, writes, out=out_ap, in_=in_ap, func=AF.Copy)
            else:
                I("dve", "tensor_copy", reads, writes, out=out_ap, in_=in_ap)

        def store(eng, out_ap, in_ap, buf):
            I(eng, "dma_start", [buf], [], out=out_ap, in_=in_ap)

        for (t0, ns, kind) in all_tiles:
            N = ns * 128
            own = kind in ("o", "s")
            tq = t0 - NP_
            xt, bxt = xts.next()
            hT, bhT = hTs.next()
            I("sp", "dma_start", [], [bxt], out=xt[:, 0:ns, :], in_=D.xall[t0:t0 + N, :].rearrange("(s p) m -> p s m", p=128))
            for s in range(ns):
                I("act", "activation", [bxt], [bjunk, bss], out=junk[:], in_=xt[:, s, :], func=AF.Square, accum_out=ss[:, s:s + 1])
            I("act", "activation", [bss], [brs], out=rs[:, 0:ns], in_=ss[:, 0:ns], func=AF.Sqrt, bias=EPS, scale=1.0 / 1024)
            I("dve", "reciprocal", [brs], [brs], out=rs[:, 0:ns], in_=rs[:, 0:ns])
            for s in range(ns):
                I("dve", "scalar_tensor_tensor", [bxt, brs, bgpre], [bhb], out=hb[:, s, :], in0=xt[:, s, :], scalar=rs[:, s:s + 1], in1=gpre[:],
                  op0=ALU.mult, op1=ALU.mult)
            for kcp in range(4):
                pt, bpt = ptr.next()
                for kk in range(2):
                    kc = 2 * kcp + kk
                    for s in range(ns):
                        I("pe", "transpose", [bhb, bidentb], [bpt], out=pt[:, kk * 512 + s * 128: kk * 512 + (s + 1) * 128],
                          in_=hb[:, s, kc * 128:(kc + 1) * 128], identity=identb[:])
                for kk in range(2):
                    evac_copy(hT[:, 2 * kcp + kk, 0:N], pt[:, kk * 512: kk * 512 + N], [bpt], [bhT])

            def fm_mm(c0, w=128):
                p, bp = pm.next()
                for kc in range(8):
                    I("pe", "matmul", [bWin, bhT], [bp], out=p[0:w, 0:N], lhsT=Win[:, kc, c0:c0 + w], rhs=hT[:, kc, 0:N], start=(kc == 0), stop=(kc == 7))
                return p, bp

            def tm_mm(s, c0, w, Wt=None, bW=None, out=None):
                if out is None:
                    p, bp = pm.next()
                    o = p[:, 0:w]
                else:
                    p, bp, o = out
                Wt_ = Win if Wt is None else Wt
                bW_ = bWin if bW is None else bW
                for kc in range(8):
                    I("pe", "matmul", [bW_, bhT], [bp], out=o, lhsT=hT[:, kc, s * 128:(s + 1) * 128], rhs=Wt_[:, kc, c0:c0 + w], start=(kc == 0), stop=(kc == 7))
                return p, bp

            fillers = []

            def do_fq(j):
                p, bp = fm_mm(C_FQ + j * 128)
                s_, bs_ = sbf.next()
                evac_copy(s_[:, 0:N], p[:, 0:N], [bp], [bs_], scale=0.125)
                store("sp", D.qT[j * 128:(j + 1) * 128, tq:tq + N], s_[:, 0:N], bs_)

            def do_fk(j):
                p, bp = fm_mm(C_FK + j * 128)
                s_, bs_ = sbf.next()
                evac_copy(s_[:, 0:N], p[:, 0:N], [bp], [bs_])
                store("sp", D.kT[j * 128:(j + 1) * 128, t0:t0 + N], s_[:, 0:N], bs_)

            def do_gate(j):
                p, bp = fm_mm(C_A + j * 128)
                s_, bs_ = sbf.next()
                tt, btt = tts.next()
                I("act", "activation", [bp], [btt], out=tt[:, 0:N], in_=p[:, 0:N], func=AF.Tanh, scale=0.5)
                I("dve", "tensor_scalar", [btt], [bs_], out=s_[:, 0:N], in0=tt[:, 0:N], scalar1=0.5, scalar2=0.5, op0=ALU.mult, op1=ALU.add)
                dst = D.sgA if j < 8 else D.sgB
                store("sp", dst[(j % 8) * 128:(j % 8 + 1) * 128, tq:tq + N], s_[:, 0:N], bs_)

            def do_tm_k(s):
                r0 = t0 + s * 128
                p, bp = tm_mm(s, C_FK, 512)
                sf, bsf = sf32.next()
                evac_copy(sf[:], p[:], [bp], [bsf])
                store("sp", D.fk[r0 - NP_: r0 - NP_ + 128, :], sf[:], bsf)

            def do_tm_v(s):
                r0 = t0 + s * 128
                p, bp = tm_mm(s, C_FV, 512)
                s_, bs_ = sbf.next()
                if own:
                    sf, bsf = sf32.next()
                    I("dve", "tensor_copy", [bp], [bsf], out=sf[:], in_=p[:])
                    store("sp", D.fv[r0 - NP_: r0 - NP_ + 128, :], sf[:], bsf)
                    I("act", "activation", [bsf], [bs_], out=s_[:], in_=sf[:], func=AF.Copy)
                else:
                    I("dve", "tensor_copy", [bp], [bs_], out=s_[:], in_=p[:])
                store("sp", D.V[r0:r0 + 128, :], s_[:], bs_)

            def do_tm_gg(s):
                r0 = t0 + s * 128
                p, bp = tm_mm(s, C_GG, 512)
                sf, bsf = sf32.next()
                tt, btt = tts.next()
                I("act", "activation", [bp], [btt], out=tt[:], in_=p[:], func=AF.Tanh, scale=0.5)
                I("dve", "scalar_tensor_tensor", [btt, bp], [bsf], out=sf[:], in0=tt[:], scalar=1.0, in1=p[:], op0=ALU.add, op1=ALU.mult)
                store("sp", D.gg[r0 - NP_: r0 - NP_ + 128, :], sf[:], bsf)

            def do_tm_small(s):
                tm_mm(s, 0, 24, Wt=Wsm, bW=bWsm, out=(psm, bpsm, psm[:, s * 24:(s + 1) * 24]))

            if own:
                for j in range(4):
                    fillers.append((do_fq, j))
            for j in range(4):
                fillers.append((do_fk, j))
            for s in range(ns):
                if own:
                    fillers.append((do_tm_k, s))
                fillers.append((do_tm_v, s))
                if own:
                    fillers.append((do_tm_gg, s))
                fillers.append((do_tm_small, s))
            if own:
                for j in range(16):
                    fillers.append((do_gate, j))
            per_step = (len(fillers) + 11) // 12

            def run_fillers(n):
                for _ in range(n):
                    if fillers:
                        f_, a_ = fillers.pop(0)
                        f_(a_)

            for j in range(12):
                p, bp = fm_mm(C_GQ + j * 128)
                evac_copy(xg[:, j, 3:3 + N], p[:, 0:N], [bp], [bxg])
            if kind == "s":
                for q_ in range(2):
                    I("pool", "dma_start", [], [bxg], out=xg[:, :, q_ * 128: q_ * 128 + 3], in_=D.conv_hist[q_])

            for j in range(12):
                p, bp = pm.next()
                for i in range(4):
                    I("pe", "matmul", [bdiagw, bxg], [bp], out=p[:, 0:N], lhsT=diagw[:, j * 4 + i, :], rhs=xg[:, j, i:i + N], start=(i == 0), stop=(i == 3))
                tt, btt = tts.next()
                I("act", "activation", [bp], [btt], out=tt[:, 0:N], in_=p[:, 0:N], func=AF.Tanh)
                I("dve", "scalar_tensor_tensor", [btt, bp], [bcT], out=cT[:, j, 0:N], in0=tt[:, 0:N], scalar=1.0, in1=p[:, 0:N], op0=ALU.add, op1=ALU.mult)
                run_fillers(per_step)
            run_fillers(len(fillers))
            I("dve", "tensor_copy", [bxg], [bxg], out=xg[:, :, 0:3], in_=xg[:, :, N:N + 3])
            for s in range(ns):
                for half in range(2):
                    pt, bpt = ptr.next()
                    nj = 8 if half == 0 else 4
                    for jj in range(nj):
                        j = half * 8 + jj
                        I("pe", "transpose", [bcT, bidentb], [bpt], out=pt[:, jj * 128:(jj + 1) * 128], in_=cT[:, j, s * 128:(s + 1) * 128], identity=identb[:])
                    evac_copy(tball[:, s, half * 1024: half * 1024 + nj * 128], pt[:, 0:nj * 128], [bpt], [btball])
                I("pool", "tensor_tensor", [btball], [bsqt], out=sqt[:], in0=tball[:, s, 0:1024], in1=tball[:, s, 0:1024], op=ALU.mult)
                I("dve", "tensor_reduce", [bsqt], [bl2ss], out=l2ss[:, s, :], in_=sqt[:].rearrange("p (h d) -> p h d", d=64), axis=AX.X, op=ALU.add)
            I("act", "activation", [bl2ss], [bl2ss], out=l2ss[:, 0:ns, :], in_=l2ss[:, 0:ns, :], func=AF.Sqrt, bias=EPS)
            I("dve", "reciprocal", [bl2ss], [bl2ss], out=l2ss[:, 0:ns, :], in_=l2ss[:, 0:ns, :])
            I("dve", "tensor_scalar_mul", [bl2ss], [bl2ss], out=l2ss[:, 0:ns, 0:8], in0=l2ss[:, 0:ns, 0:8], scalar1=0.125)
            for s in range(ns):
                qk = tball[:, s, 0:1024].rearrange("p (h d) -> p h d", d=64)
                I("dve", "tensor_tensor", [btball, bl2ss], [btball], out=qk, in0=qk, in1=l2ss[:, s, :].unsqueeze(2).broadcast_to([128, 16, 64]), op=ALU.mult)
                store("sp", D.gqkv[t0 + s * 128: t0 + (s + 1) * 128, :], tball[:, s, :], btball)
            n24 = ns * 24
            for s in range(ns):
                I("dve", "tensor_tensor", [bpsm, bb24], [bsm_t], out=sm_t[:, s * 24:(s + 1) * 24], in0=psm[:, s * 24:(s + 1) * 24], in1=b24[:], op=ALU.add)
                I("dve", "tensor_tensor", [bsm_t, bs24], [bsm_t], out=sm_t[:, s * 24:(s + 1) * 24], in0=sm_t[:, s * 24:(s + 1) * 24], in1=s24[:], op=ALU.mult)
            I("act", "activation", [bsm_t], [bsm_e], out=sm_e[:, 0:n24], in_=sm_t[:, 0:n24], func=AF.Exp)
            I("act", "activation", [bsm_e], [bsm_l], out=sm_l[:, 0:n24], in_=sm_e[:, 0:n24], func=AF.Ln, bias=1.0)
            for s in range(ns):
                I("dve", "tensor_scalar_mul", [bsm_l], [bsm_o], out=sm_o[:, s, 0:8], in0=sm_l[:, s * 24: s * 24 + 8], scalar1=-1.0)
                I("dve", "tensor_tensor", [bsm_l, bnea], [bsm_o], out=sm_o[:, s, 8:16], in0=sm_l[:, s * 24 + 8: s * 24 + 16], in1=nea[:], op=ALU.mult)
                I("dve", "tensor_scalar_add", [bsm_e], [bsm_o], out=sm_o[:, s, 16:24], in0=sm_e[:, s * 24 + 16: s * 24 + 24], scalar1=1.0)
                I("dve", "reciprocal", [bsm_o], [bsm_o], out=sm_o[:, s, 16:24], in_=sm_o[:, s, 16:24])
                I("dve", "tensor_scalar_mul", [bsm_l], [bsm_o], out=sm_o[:, s, 24:32], in0=sm_l[:, s * 24 + 16: s * 24 + 24], scalar1=-1.0)
            for (dst, c0, cw_) in ((D.lfs, 0, 8), (D.g, 8, 8), (D.beta, 16, 16)):
                I("sp", "dma_start", [bsm_o], [], out=dst[t0:t0 + N, :].rearrange("(s p) m -> p s m", p=128), in_=sm_o[:, 0:ns, c0:c0 + cw_])
            if own:
                I("sp", "dma_start", [bsm_o], [], out=D.lf[tq:tq + N, :].rearrange("(s p) m -> p s m", p=128), in_=sm_o[:, 0:ns, 0:8])
            conv_rows = []
            if kind == "o" and t0 == NP_ + NO_ - 512:
                conv_rows = [(3, 125, 0)]
            if kind == "s":
                conv_rows = [(0, 13, 1), (1, 13, 2)]
            for (s, r, oi) in conv_rows:
                for cg in range(3):
                    p, bp = tm_mm(s, C_GQ + cg * 512, 512)
                    sf, bsf = sf32.next()
                    evac_copy(sf[:], p[:], [bp], [bsf])
                    store("sp", D.convo[oi, :, cg * 512:(cg + 1) * 512], sf[r:r + 3, :], bsf)
        fw.barrier()
        fw.emit()


def phase_b(nc, fw, D, pairs=(0, 1, 2, 3), qtiles=tuple(range(8)), samples=(0, 1)):
    with ExitStack() as st:
        for _ in phase_b_body(nc, fw, D, st, None, pairs, qtiles, samples):
            pass
        fw.barrier()
        fw.emit()


def phase_b_body(nc, fw, D, st, banks, pairs=(0, 1, 2, 3), qtiles=tuple(range(8)), samples=(0, 1)):
    I = fw.I
    if True:
        def sb(name, shape, dt):
            return st.enter_context(nc.sbuf_tensor("B_" + name, shape, dt)), Buf(name)

        def pst(name, shape, dt):
            return st.enter_context(nc.psum_tensor("B_" + name, shape, dt)), Buf(name)
        identb, bidentb = sb("identb", [128, 128], BF16)
        mincl, bmincl = sb("mincl", [128, 128], BF16)
        cf, bcf = sb("cf", [128, 4, 128], F32)
        kmask, bkmask = sb("kmask", [128, 32], F32)
        lf, blf = sb("lf", [128, 64, 8], F32)
        Fin, bFin = sb("Fin", [128, 64, 8], F32)
        tot, btot = sb("tot", [128, 64, 8], F32)
        scA, bscA = sb("scA", [128, 64, 8], F32)
        scB, bscB = sb("scB", [128, 64, 8], F32)
        offs, boffs = sb("offs", [128, 64, 8], F32)
        Fm, bFm = sb("Fm", [128, 64, 8], F32)
        cfac, bcfac = sb("cfac", [128, 64, 8], F32)
        psets = [(sb(f"kTp{i}", [128, 8192], BF16), sb(f"qTp{i}", [128, 4096], BF16), sb(f"Vx{i}", [128, 64, 2, 65], BF16)) for i in range(2)]
        boffs_ = [sb(f"boff{i}", [128, 64], F32) for i in range(2)]
        bdgs_ = [sb(f"bdg{i}", [128, 16], F32) for i in range(2)]
        osbs_ = [sb(f"osb{i}", [65, 512], F32) for i in range(2)]
        PTs = Rot([sb(f"PT{i}", [128, 512], BF16) for i in range(6)])
        rsbs = Rot([sb(f"rsb{i}", [65, 512], F32) for i in range(2)])
        rrecs = Rot([sb(f"rrec{i}", [65, 512], F32) for i in range(2)])
        fos = Rot([sb(f"fo{i}", [64, 512], BF16) for i in range(2)])
        kc_tm, bkc_tm = sb("kc_tm", [128, 16, 512], BF16)
        if banks is None:
            LA = 2
            pS = Rot([pst(f"pS{i}", [128, 512], F32) for i in range(LA + 1)])
            pOos = [pst(f"pOo{i}", [128, 512], F32) for i in range(2)]
            pOds = [pst(f"pOd{i}", [128, 512], F32) for i in range(2)]
            pB_, bpB_ = pst("pB", [128, 512], F32)
            getB = lambda: (pB_, bpB_)
            getT = lambda: (pB_[:].bitcast(BF16), bpB_)
            getJ = lambda: (pB_, bpB_)
        else:
            pS = Rot(list(banks[0:2]))
            pOos = [banks[2], banks[2]]
            pOds = [banks[3], banks[3]]
            LA = 1
            getB = lambda: pS.next()
            getJ = lambda: pS.next()

            def getT():
                t_, b_ = pS.next()
                return t_[:].bitcast(BF16), b_
        jk, bjk = sb("jk", [128, 512], BF16)
        I("dve", "memset", [], [bjk], ap=jk[:], constant=0.0)
        JN = 0
        JB = 12

        I("sp", "dma_start", [], [bidentb], out=identb[:], in_=D.cmask[:, 0, :])
        I("sp", "dma_start", [], [bmincl], out=mincl[:], in_=D.cmask[:, 2, :])
        I("sp", "dma_start", [], [bcf], out=cf[:], in_=D.cf32[:, 0:4, :])
        I("sp", "dma_start", [], [bkmask], out=kmask[:], in_=D.kmask[:, :])
        for (_, _, (Vx_, bVx_)) in psets:
            I("dve", "memset", [], [bVx_], ap=Vx_[:, :, :, 64:65], constant=1.0)

        def flat(t, nb):
            return t[:, 0:nb, :].rearrange("p b h -> p (b h)")

        def build_F(nb, masked):
            n = nb * 8
            pB, bpB = getB()
            I("pe", "matmul", [bcf, blf], [bpB], out=pB[:, 0:n], lhsT=cf[:, 1, :], rhs=flat(lf, nb), start=True, stop=True)
            I("dve", "tensor_copy", [bpB], [bFin], out=flat(Fin, nb), in_=pB[:, 0:n])
            pB, bpB = getB()
            I("pe", "matmul", [bcf, bFin], [bpB], out=pB[:, 0:n], lhsT=cf[:, 3, :], rhs=flat(Fin, nb), start=True, stop=True)
            I("dve", "tensor_copy", [bpB], [btot], out=flat(tot, nb), in_=pB[:, 0:n])
            src, bsrc = tot, btot
            k = 1
            pp = [(scA, bscA), (scB, bscB)]
            ii = 0
            while k < nb:
                dst, bdst = pp[ii % 2]
                ii += 1
                I("dve", "tensor_copy", [bsrc], [bdst], out=dst[:, 0:k, :], in_=src[:, 0:k, :])
                I("dve", "tensor_tensor", [bsrc], [bdst], out=dst[:, k:nb, :], in0=src[:, k:nb, :], in1=src[:, 0:nb - k, :], op=ALU.add)
                src, bsrc = dst, bdst
                k *= 2
            I("dve", "tensor_tensor", [bsrc, btot], [boffs], out=offs[:, 0:nb, :], in0=src[:, 0:nb, :], in1=tot[:, 0:nb, :], op=ALU.subtract)
            I("dve", "tensor_tensor", [bFin, boffs], [bFm], out=Fm[:, 0:nb, :], in0=Fin[:, 0:nb, :], in1=offs[:, 0:nb, :], op=ALU.add)
            if masked:
                for h in range(8):
                    I("dve", "tensor_tensor", [bFm, bkmask], [bFm], out=Fm[:, 0:32, h], in0=Fm[:, 0:32, h], in1=kmask[:, :], op=ALU.subtract)

        def attend(pset, streams):
            (kTp, bkTp), (qTp, bqTp), (Vx, bVx) = pset
            ctx = []
            for si, (hh, h, qcol, W, qb0, nsub, out_col) in enumerate(streams):
                boff, bboff = boffs_[si]
                bdg, bbdg = bdgs_[si]
                rsb, brsb = rsbs.next()
                rrec, brrec = rrecs.next()
                if qb0 > 0:
                    I("dve", "tensor_scalar", [bFm, boffs], [bboff], out=boff[:, 0:qb0], in0=Fm[:, 0:qb0, h], scalar1=offs[:, qb0, h:h + 1], scalar2=-1.0,
                      op0=ALU.subtract, op1=ALU.mult)
                for i in range(nsub):
                    I("dve", "tensor_scalar", [bFm, boffs], [bbdg], out=bdg[:, i * 4: i * 4 + i + 1], in0=Fm[:, qb0: qb0 + i + 1, h], scalar1=offs[:, qb0 + i, h:h + 1],
                      scalar2=-1.0, op0=ALU.subtract, op1=ALU.mult)
                if nsub > 1:
                    I("dve", "tensor_scalar", [boffs], [bcfac], out=cfac[:, qb0:qb0 + nsub, h], in0=offs[:, qb0:qb0 + nsub, h], scalar1=offs[:, qb0, h:h + 1], scalar2=None,
                      op0=ALU.subtract)
                    I("act", "activation", [bcfac], [bcfac], out=cfac[:, qb0:qb0 + nsub, h], in_=cfac[:, qb0:qb0 + nsub, h], func=AF.Exp)
                ctx.append((boff, bboff, bdg, bbdg, rsb, brsb, rrec, brrec))
            for _ in range(JB):
                pJ, bpJ = getJ()
                I("pe", "matmul", [bidentb, bjk], [bpJ], out=pJ[:, 0:512], lhsT=identb[:], rhs=jk[:, 0:512], start=True, stop=True)
            yield
            lists = []
            for si, (hh, h, qcol, W, qb0, nsub, out_col) in enumerate(streams):
                lists.append([(si, "o", kb, 0, 0) for kb in range(qb0)] + [(si, "d", qb0 + j, i, j) for i in range(nsub) for j in range(i + 1)])
            work = []
            for t_ in range(max(len(l_) for l_ in lists)):
                for l_ in lists:
                    if t_ < len(l_):
                        work.append(l_[t_])
            inflight = {}
            for idx in range(len(work) + LA):
                if idx < len(work):
                    si, kind, kb, i, j = work[idx]
                    hh, h, qcol, W, qb0, nsub, out_col = streams[si]
                    ps_ = slice(hh * 64, (hh + 1) * 64)
                    Wd = min(W, 128)
                    p, bp = pS.next()
                    if kind == "o":
                        I("pe", "matmul", [bkTp, bqTp], [bp], out=p[:, 0:W], lhsT=kTp[ps_, kb * 128:(kb + 1) * 128], rhs=qTp[ps_, qcol:qcol + W], start=True, stop=True)
                    else:
                        I("pe", "matmul", [bkTp, bqTp], [bp], out=p[:, 0:Wd], lhsT=kTp[ps_, kb * 128:(kb + 1) * 128], rhs=qTp[ps_, qcol + i * 128: qcol + i * 128 + Wd],
                          start=True, stop=(j != i))
                        if j == i:
                            I("pe", "matmul", [bidentb, bmincl], [bp], out=p[:, 0:Wd], lhsT=identb[:], rhs=mincl[:, 0:Wd], start=False, stop=True)
                    inflight[idx] = (p, bp)
                k2 = idx - LA
                if k2 >= 0:
                    si, kind, kb, i, j = work[k2]
                    hh, h, qcol, W, qb0, nsub, out_col = streams[si]
                    boff, bboff, bdg, bbdg = ctx[si][0:4]
                    pOo, bpOo = pOos[si]
                    pOd, bpOd = pOds[si]
                    Wd = min(W, 128)
                    p, bp = inflight.pop(k2)
                    pt, bpt = PTs.next()
                    if kind == "o":
                        I("act", "activation", [bp, bboff], [bpt], out=pt[:, 0:W], in_=p[:, 0:W], func=AF.Exp, bias=boff[:, kb:kb + 1])
                        I("pe", "matmul", [bVx, bpt], [bpOo], out=pOo[0:65, 0:W], lhsT=Vx[:, kb, hh, :], rhs=pt[:, 0:W], start=(kb == 0), stop=(kb == qb0 - 1))
                    else:
                        I("act", "activation", [bp, bbdg], [bpt], out=pt[:, 0:Wd], in_=p[:, 0:Wd], func=AF.Exp, bias=bdg[:, i * 4 + j: i * 4 + j + 1])
                        I("pe", "matmul", [bVx, bpt], [bpOd], out=pOd[0:65, i * 128: i * 128 + Wd], lhsT=Vx[:, kb, hh, :], rhs=pt[:, 0:Wd], start=(j == 0), stop=(j == i))
                yield
            for si, (hh, h, qcol, W, qb0, nsub, out_col) in enumerate(streams):
                boff, bboff, bdg, bbdg, rsb, brsb, rrec, brrec = ctx[si]
                pOo, bpOo = pOos[si]
                pOd, bpOd = pOds[si]
                osb, bosb = osbs_[si]
                Wd = min(W, 128)
                if qb0 > 0:
                    I("act", "activation", [bpOo], [bosb], out=osb[:, 0:W], in_=pOo[0:65, 0:W], func=AF.Copy)
                    for i in range(nsub):
                        sl = slice(i * 128, i * 128 + Wd)
                        if nsub > 1:
                            I("dve", "scalar_tensor_tensor", [bosb, bcfac, bpOd], [brsb], out=rsb[:, sl], in0=osb[:, sl], scalar=cfac[0:65, qb0 + i, h:h + 1], in1=pOd[0:65, sl],
                              op0=ALU.mult, op1=ALU.add)
                        else:
                            I("dve", "tensor_tensor", [bosb, bpOd], [brsb], out=rsb[:, sl], in0=osb[:, sl], in1=pOd[0:65, sl], op=ALU.add)
                else:
                    I("dve", "tensor_copy", [bpOd], [brsb], out=rsb[:, 0:W], in_=pOd[0:65, 0:W])
                I("dve", "reciprocal", [brsb], [brrec], out=rrec[64:65, 0:W], in_=rsb[64:65, 0:W])
                pB, bpB = getB()
                I("pe", "matmul", [bcf, brrec], [bpB], out=pB[0:64, 0:W], lhsT=cf[64:65, 2, 0:64], rhs=rrec[64:65, 0:W], start=True, stop=True)
                fo, bfo = fos.next()
                I("dve", "tensor_tensor", [brsb, bpB], [bfo], out=fo[:, 0:W], in0=rsb[0:64, 0:W], in1=pB[0:64, 0:W], op=ALU.mult)
                I("pool", "dma_start", [bfo], [], out=D.foT[h * 64:(h + 1) * 64, out_col:out_col + W], in_=fo[:, 0:W])
            yield

        if len(qtiles) > 0:
            I("sp", "dma_start", [], [blf], out=lf[:], in_=D.lfs[0:8192, :].rearrange("(b p) h -> p b h", p=128))
            build_F(64, True)
            def load_pair(pr, pset):
                (kTp, bkTp), (qTp, bqTp), (Vx, bVx) = pset
                I("sp", "dma_start", [], [bkTp], out=kTp[:, :], in_=D.kT[pr * 128:(pr + 1) * 128, 0:8192])
                I("sp", "dma_start", [], [bqTp], out=qTp[:, :], in_=D.qT[pr * 128:(pr + 1) * 128, 0:4096])
                for hh in range(2):
                    I("sp", "dma_start", [], [bVx], out=Vx[:, :, hh, 0:64],
                      in_=D.V[0:8192, (2 * pr + hh) * 64:(2 * pr + hh + 1) * 64].rearrange("(b p) d -> p b d", p=128))
            pl = list(pairs)
            load_pair(pl[0], psets[0])
            for n_, pr in enumerate(pl):
                if n_ + 1 < len(pl):
                    load_pair(pl[n_ + 1], psets[(n_ + 1) % 2])
                for qt in qtiles:
                    yield from attend(psets[n_ % 2], [(hh, 2 * pr + hh, qt * 512, 512, 32 + 4 * qt, 4, qt * 512) for hh in range(2)])
        for q in samples:
            r0 = NP_ + NO_ + q * 128
            I("sp", "dma_start", [], [blf], out=lf[:, 0:16, :], in_=D.clf[q].rearrange("(b p) h -> p b h", p=128))
            I("sp", "dma_start", [], [blf], out=lf[:, 16, :], in_=D.lfs[r0:r0 + 128, :])
            build_F(17, False)
            I("pool", "dma_start", [], [bkc_tm], out=kc_tm[:], in_=D.ck[q].rearrange("(b p) c -> p b c", p=128))
            for n_, pr in enumerate(pairs):
                pset = psets[n_ % 2]
                (kTp, bkTp), (qTp, bqTp), (Vx, bVx) = pset
                for b4 in range(2):
                    pT, bpT = getT()
                    for bb in range(8):
                        blk = b4 * 8 + bb
                        I("pe", "transpose", [bkc_tm, bidentb], [bpT], out=pT[:, bb * 128:(bb + 1) * 128], in_=kc_tm[:, blk, pr * 128:(pr + 1) * 128], identity=identb[:])
                    I("dve", "tensor_copy", [bpT], [bkTp], out=kTp[:, b4 * 1024:(b4 + 1) * 1024], in_=pT[:, :])
                I("sp", "dma_start", [], [bkTp], out=kTp[:, 2048:2176], in_=D.kT[pr * 128:(pr + 1) * 128, r0:r0 + 128])
                I("sp", "dma_start", [], [bqTp], out=qTp[:, 0:128], in_=D.qT[pr * 128:(pr + 1) * 128, NO_ + q * 128: NO_ + (q + 1) * 128])
                for hh in range(2):
                    h = 2 * pr + hh
                    I("pool", "dma_start", [], [bVx], out=Vx[:, 0:16, hh, 0:64], in_=D.cv[q][:, h * 64:(h + 1) * 64].rearrange("(b p) d -> p b d", p=128))
                    I("sp", "dma_start", [], [bVx], out=Vx[:, 16, hh, 0:64], in_=D.V[r0:r0 + 128, h * 64:(h + 1) * 64])
                yield from attend(pset, [(hh, 2 * pr + hh, 0, 128, 16, 1, NO_ + q * 128) for hh in range(2)])


def phase_c(nc, fw, D, blocks=None, samples=(0, 1), o_all=False):
    with ExitStack() as st:
        for _ in phase_c_body(nc, fw, D, st, None, blocks, samples, o_all):
            pass
        fw.barrier()
        fw.emit()


def phase_c_body(nc, fw, D, st, banks, blocks=None, samples=(0, 1), o_all=False):
    I = fw.I
    if True:
        def sb(name, shape, dt):
            return st.enter_context(nc.sbuf_tensor("C_" + name, shape, dt)), Buf(name)

        def pst(name, shape, dt):
            return st.enter_context(nc.psum_tensor("C_" + name, shape, dt)), Buf(name)
        identb, bidentb = sb("identb", [128, 128], BF16)
        identf, bidentf = sb("identf", [128, 128], F32)
        mincl, bmincl = sb("mincl", [128, 128], BF16)
        mstr, bmstr = sb("mstr", [128, 128], BF16)
        cf, bcf = sb("cf", [128, 6, 128], F32)
        ggdn, bggdn = sb("ggdn", [128, 8, 64], F32)
        vmask, bvmask = sb("vmask", [128, 1], F32)
        insets = [(sb(f"qkv{i}", [128, 3, 8, 64], BF16), sb(f"kT{i}", [64, 8, 128], BF16), sb(f"qT{i}", [64, 8, 128], BF16),
                   sb(f"g{i}", [128, 8], F32), sb(f"be{i}", [128, 16], F32)) for i in range(2)]
        gc, bgc = sb("gc", [128, 8], F32)
        ngc, bngc = sb("ngc", [128, 8], F32)
        vec2, bvec2 = sb("vec2", [128, 8], F32)
        eg, beg = sb("eg", [128, 8], F32)
        bee, bbee = sb("bee", [128, 8], F32)
        glb, bglb = sb("glb", [128, 8], F32)
        gl12, bgl12 = sb("gl12", [128, 16], F32)
        egl, begl = sb("egl", [128, 2, 8], F32)
        ekl, bekl = sb("ekl", [128, 8], F32)
        dg1, bdg1 = sb("dg1", [128, 8, 128], F32)
        dg2, bdg2 = sb("dg2", [128, 8, 128], F32)
        Dincl, bDincl = sb("Dincl", [128, 8, 128], BF16)
        AbT, bAbT = sb("AbT", [128, 8, 128], BF16)
        Ns = [sb(f"N{i}", [128, 8, 128], BF16) for i in range(2)]
        Ls = [sb(f"L{i}", [128, 8, 128], BF16) for i in range(2)]
        Ws = [sb(f"W{i}", [128, 8, 128], BF16) for i in range(2)]
        AqkT, bAqkT = sb("AqkT", [128, 8, 128], BF16)
        rhs2, brhs2 = sb("rhs2", [128, 8, 128], BF16)
        khat, bkhat = sb("khat", [128, 8, 64], BF16)
        qtl, bqtl = sb("qtl", [128, 8, 64], BF16)
        qtT, bqtT = sb("qtT", [64, 8, 128], BF16)
        U, bU = sb("U", [128, 8, 64], F32)
        WkT, bWkT = sb("WkT", [64, 8, 128], BF16)
        vnew, bvnew = sb("vnew", [128, 8, 64], BF16)
        S, bS = sb("S", [64, 8, 64], F32)
        Sb, bSb = sb("Sb", [64, 8, 64], BF16)
        Sb2, bSb2 = sb("Sb2", [64, 8, 64], BF16)
        osb, bosb = sb("osb", [128, 8, 64], F32)
        osq, bosq = sb("osq", [128, 8, 64], F32)
        oss, boss = sb("oss", [128, 8], F32)
        ggt, bggt = sb("ggt", [128, 8, 64], F32)
        obs_ = [sb(f"ob{i}", [128, 8, 64], BF16) for i in range(2)]
        if banks is None:
            pA = [pst(f"pA{i}", [128, 512], F32) for i in range(2)]
            pL = [pst(f"pL{i}", [128, 512], F32) for i in range(2)]
            pW = [pst(f"pW{i}", [128, 512], F32) for i in range(2)]
            pX, bpX = pst("pX", [128, 512], F32)
            pTt, bpTt = pst("pTt", [128, 1024], BF16)
        else:
            pA = [banks[0], banks[0]]
            pL = [banks[1], banks[1]]
            pW = [banks[2], banks[2]]
            pX, bpX = banks[3]
            pTt, bpTt = banks[0][0][:].bitcast(BF16), banks[0][1]

        I("sp", "dma_start", [], [bidentb], out=identb[:], in_=D.cmask[:, 0, :])
        I("sp", "dma_start", [], [bmincl], out=mincl[:], in_=D.cmask[:, 6, :])
        I("sp", "dma_start", [], [bmstr], out=mstr[:], in_=D.cmask[:, 7, :])
        I("sp", "dma_start", [], [bcf], out=cf[:], in_=D.cf32[:, :, :])
        I("sp", "dma_start", [], [bidentf], out=identf[:], in_=D.cf32[:, 0, :])
        for h in range(8):
            I("sp", "dma_start", [], [bggdn], out=ggdn[:, h, :], in_=D.g_gdn[0:1, :].broadcast_to([128, 64]))
        I("sp", "dma_start", [], [bvmask], out=vmask[:], in_=D.vmask[:, :])
        jk, bjk = sb("jk", [128, 512], BF16)
        I("dve", "memset", [], [bjk], ap=jk[:], constant=0.0)
        JC = 0
        I("dve", "tensor_scalar_mul", [bggdn], [bggdn], out=ggdn[:].rearrange("p h d -> p (h d)"), in0=ggdn[:].rearrange("p h d -> p (h d)"), scalar1=0.5)

        def load(r0, si):
            (qkv, bqkv), (kT, bkT), (qT, bqT), (g_, bg_), (be, bbe) = insets[si]
            I("sp", "dma_start", [], [bqkv], out=qkv[:].rearrange("p a h d -> p (a h d)"), in_=D.gqkv[r0:r0 + 128, :])
            I("sp", "dma_start", [], [bg_], out=g_[:], in_=D.g[r0:r0 + 128, :])
            I("sp", "dma_start", [], [bbe], out=be[:], in_=D.beta[r0:r0 + 128, :])

        def process(si, qrow, want_o, sample):
            (qkv, bqkv), (kT, bkT), (qT, bqT), (g_, bg_), (be, bbe) = insets[si]
            ob, bob = obs_[si]
            if sample:
                I("dve", "tensor_scalar_mul", [bg_, bvmask], [bg_], out=g_[:], in0=g_[:], scalar1=vmask[:, 0:1])
                I("dve", "tensor_scalar_mul", [bqkv, bvmask], [bqkv], out=qkv[:].rearrange("p a h d -> p (a h d)"), in0=qkv[:].rearrange("p a h d -> p (a h d)"),
                  scalar1=vmask[:, 0:1])
            for h in range(8):
                I("pe", "transpose", [bqkv, bidentb], [bpTt], out=pTt[0:64, h * 128:(h + 1) * 128], in_=qkv[:, 1, h, :], identity=identb[:])
            I("act", "activation", [bpTt], [bkT], out=kT[:].rearrange("p h i -> p (h i)"), in_=pTt[0:64, :], func=AF.Copy)
            if want_o:
                for h in range(8):
                    I("pe", "transpose", [bqkv, bidentb], [bpTt], out=pTt[0:64, h * 128:(h + 1) * 128], in_=qkv[:, 0, h, :], identity=identb[:])
                I("dve", "tensor_copy", [bpTt], [bqT], out=qT[:].rearrange("p h i -> p (h i)"), in_=pTt[0:64, :])
            yield
            I("pe", "matmul", [bcf, bg_], [bpX], out=pX[:, 0:8], lhsT=cf[:, 4, :], rhs=g_[:], start=True, stop=True)
            I("dve", "tensor_copy", [bpX], [bgc], out=gc[:], in_=pX[:, 0:8])
            I("dve", "tensor_scalar_mul", [bgc], [bngc], out=ngc[:], in0=gc[:], scalar1=-1.0)
            I("pe", "matmul", [bcf, bgc], [bpX], out=pX[:, 8:16], lhsT=cf[:, 5, :], rhs=gc[:], start=True, stop=True)
            I("pe", "matmul", [bcf, bgc], [bpX], out=pX[:, 16:24], lhsT=cf[:, 3, :], rhs=gc[:], start=True, stop=True)
            I("dve", "tensor_copy", [bpX], [bgl12], out=gl12[:], in_=pX[:, 8:24])
            I("dve", "tensor_copy", [bgl12], [bglb], out=glb[0:64, :], in_=gl12[0:64, 0:8])
            I("dve", "tensor_copy", [bgl12], [bglb], out=glb[64:128, :], in_=gl12[64:128, 8:16])
            I("dve", "tensor_tensor", [bbe, bgc], [bvec2], out=vec2[:], in0=be[:, 8:16], in1=gc[:], op=ALU.add)
            I("act", "activation", [bgc], [beg], out=eg[:], in_=gc[:], func=AF.Exp)
            I("act", "activation", [bvec2], [bbee], out=bee[:], in_=vec2[:], func=AF.Exp)
            I("act", "activation", [bgl12], [begl], out=egl[:].rearrange("p a h -> p (a h)"), in_=gl12[:], func=AF.Exp)
            I("dve", "tensor_tensor", [bglb, bgc], [bekl], out=ekl[:], in0=glb[:], in1=gc[:], op=ALU.subtract)
            I("act", "activation", [bekl], [bekl], out=ekl[:], in_=ekl[:], func=AF.Exp)
            if sample:
                I("dve", "tensor_scalar_mul", [bbe, bvmask], [bbe], out=be[:, 0:8], in0=be[:, 0:8], scalar1=vmask[:, 0:1])
                I("dve", "tensor_scalar_mul", [bbee, bvmask], [bbee], out=bee[:], in0=bee[:], scalar1=vmask[:, 0:1])
            yield
            for h in range(8):
                if want_o:
                    I("dve", "tensor_scalar_mul", [bidentf, bgc], [bdg1], out=dg1[:, h, :], in0=identf[:], scalar1=gc[:, h:h + 1])
                I("dve", "tensor_scalar_mul", [bidentf, bvec2], [bdg2], out=dg2[:, h, :], in0=identf[:], scalar1=vec2[:, h:h + 1])
            N0, bN0 = Ns[0]
            L0, bL0 = Ls[0]
            yield
            for _ in range(JC):
                I("pe", "matmul", [bidentb, bjk], [bpX], out=pX[:, 0:512], lhsT=identb[:], rhs=jk[:, 0:512], start=True, stop=True)
            for hb in range(2):
                yield
                pK, bpK = pA[hb]
                pQ, bpQ = pL[hb]
                p1, bp1 = pW[hb]
                for hq in range(4):
                    h = hb * 4 + hq
                    sl = slice(hq * 128, (hq + 1) * 128)
                    I("pe", "matmul", [bkT], [bpK], out=pK[:, sl], lhsT=kT[:, h, :], rhs=kT[:, h, :], start=True, stop=True)
                    if want_o:
                        I("pe", "matmul", [bkT, bqT], [bpQ], out=pQ[:, sl], lhsT=kT[:, h, :], rhs=qT[:, h, :], start=True, stop=True)
                        I("pe", "matmul", [bcf, bdg1], [bp1], out=p1[:, sl], lhsT=cf[:, 2, :], rhs=dg1[:, h, :], start=True, stop=False)
                        I("pe", "matmul", [bidentb, bmincl], [bp1], out=p1[:, sl], lhsT=identb[:], rhs=mincl[:], start=False, stop=True)
                        I("act", "activation", [bp1, bngc], [bDincl], out=Dincl[:, h, :], in_=p1[:, sl], func=AF.Exp, bias=ngc[:, h:h + 1])
                for hq in range(4):
                    h = hb * 4 + hq
                    sl = slice(hq * 128, (hq + 1) * 128)
                    I("pe", "matmul", [bcf, bdg2], [bpX], out=pX[:, sl], lhsT=cf[:, 2, :], rhs=dg2[:, h, :], start=True, stop=False)
                    I("pe", "matmul", [bidentb, bmstr], [bpX], out=pX[:, sl], lhsT=identb[:], rhs=mstr[:], start=False, stop=True)
                    I("act", "activation", [bpX, bngc], [bAbT], out=AbT[:, h, :], in_=pX[:, sl], func=AF.Exp, bias=ngc[:, h:h + 1])
                hs = slice(hb * 4, hb * 4 + 4)
                fl = lambda t: t[:, hs, :].rearrange("p h i -> p (h i)")
                I("dve", "scalar_tensor_tensor", [bpK, bAbT], [bN0], out=fl(N0), in0=pK[:, :], scalar=-1.0, in1=fl(AbT), op0=ALU.mult, op1=ALU.mult)
                if want_o:
                    I("dve", "tensor_tensor", [bpQ, bDincl], [bAqkT], out=fl(AqkT), in0=pQ[:, :], in1=fl(Dincl), op=ALU.mult)
            yield
            for h in range(8):
                I("pe", "transpose", [bN0, bidentb], [bpTt], out=pTt[:, h * 128:(h + 1) * 128], in_=N0[:, h, :], identity=identb[:])
            I("act", "activation", [bpTt], [bL0], out=L0[:].rearrange("p h i -> p (h i)"), in_=pTt[:, :], func=AF.Copy)
            W0, bW0 = Ws[0]
            for h in range(8):
                I("pool", "tensor_tensor", [bN0, bidentb], [bW0], out=W0[:, h, :], in0=N0[:, h, :], in1=identb[:], op=ALU.add)
            cur = 0
            for m in range(1, 6):
                yield
                Nc, bNc = Ns[cur]
                Lc, bLc = Ls[cur]
                Wc, bWc = Ws[cur]
                Nn, bNn = Ns[1 - cur]
                Ln, bLn = Ls[1 - cur]
                Wn, bWn = Ws[1 - cur]
                def fl(t, hb):
                    return t[:, hb * 4:hb * 4 + 4, :].rearrange("p h i -> p (h i)")
                for hb in range(2):
                    pl_, bpl_ = pL[hb]
                    for hq in range(4):
                        h = hb * 4 + hq
                        I("pe", "matmul", [bNc, bLc], [bpl_], out=pl_[:, hq * 128:(hq + 1) * 128], lhsT=Nc[:, h, :], rhs=Lc[:, h, :], start=True, stop=True)
                    I("act", "activation", [bpl_], [bLn], out=fl(Ln, hb), in_=pl_[:, :], func=AF.Copy)
                if m < 5:
                    for hb in range(2):
                        pa_, bpa_ = pA[hb]
                        for hq in range(4):
                            h = hb * 4 + hq
                            I("pe", "matmul", [bNc, bLc], [bpa_], out=pa_[:, hq * 128:(hq + 1) * 128], lhsT=Lc[:, h, :], rhs=Nc[:, h, :], start=True, stop=True)
                        I("dve", "tensor_copy", [bpa_], [bNn], out=fl(Nn, hb), in_=pa_[:, :])
                for hb in range(2):
                    pw_, bpw_ = pW[hb]
                    for hq in range(4):
                        h = hb * 4 + hq
                        I("pe", "matmul", [bLn, bWc], [bpw_], out=pw_[:, hq * 128:(hq + 1) * 128], lhsT=Ln[:, h, :], rhs=Wc[:, h, :], start=True, stop=True)
                    I("dve", "tensor_tensor", [bpw_, bWc], [bWn], out=fl(Wn, hb), in0=pw_[:, :], in1=fl(Wc, hb), op=ALU.add)
                cur = 1 - cur
            Wf, bWf = Ws[cur]
            yield
            bc = lambda t: t[:, :].unsqueeze(2).broadcast_to([128, 8, 64])
            I("dve", "tensor_tensor", [bqkv, bbe], [brhs2], out=rhs2[:, :, 0:64], in0=qkv[:, 2, :, :], in1=be[:, 0:8].unsqueeze(2).broadcast_to([128, 8, 64]), op=ALU.mult)
            I("dve", "tensor_tensor", [bqkv, bbee], [brhs2], out=rhs2[:, :, 64:128], in0=qkv[:, 1, :, :], in1=bc(bee), op=ALU.mult)
            I("pool", "tensor_tensor", [bqkv, bekl], [bkhat], out=khat[:], in0=qkv[:, 1, :, :], in1=bc(ekl), op=ALU.mult)
            if want_o:
                I("pool", "tensor_tensor", [bqkv, beg], [bqtl], out=qtl[:], in0=qkv[:, 0, :, :], in1=bc(eg), op=ALU.mult)
            for hb in range(2):
                yield
                pu, bpu = pA[hb]
                for hq in range(4):
                    h = hb * 4 + hq
                    I("pe", "matmul", [bWf, brhs2], [bpu], out=pu[:, hq * 128:(hq + 1) * 128], lhsT=Wf[:, h, :], rhs=rhs2[:, h, :], start=True, stop=True)
                I("dve", "tensor_copy", [bpu], [bU], out=U[:, hb * 4:hb * 4 + 4, :], in_=pu[:, :].rearrange("p (h c) -> p h c", c=128)[:, :, 0:64])
                pk_, bpk_ = pL[hb]
                for hq in range(4):
                    h = hb * 4 + hq
                    I("pe", "matmul", [brhs2, bWf], [bpk_], out=pk_[0:64, hq * 128:(hq + 1) * 128], lhsT=rhs2[:, h, 64:128], rhs=Wf[:, h, :], start=True, stop=True)
                I("act", "activation", [bpk_], [bWkT], out=WkT[:, hb * 4:hb * 4 + 4, :].rearrange("p h i -> p (h i)"), in_=pk_[0:64, :], func=AF.Copy)
            if want_o:
                for h in range(8):
                    I("pe", "transpose", [bqtl, bidentb], [bpTt], out=pTt[0:64, h * 128:(h + 1) * 128], in_=qtl[:, h, :], identity=identb[:])
                I("act", "activation", [bpTt], [bqtT], out=qtT[:].rearrange("p h i -> p (h i)"), in_=pTt[0:64, :], func=AF.Copy)
            fo_ = lambda t: t[:].rearrange("p h d -> p (h d)")
            halves = [(slice(0, 64), Sb, bSb, 0), (slice(64, 128), Sb2, bSb2, 1)]
            for (rs_, Sc, bSc, ci) in halves:
                yield
                for h in range(8):
                    I("pe", "matmul", [bWkT, bSc], [bpX], out=pX[:, h * 64:(h + 1) * 64], lhsT=WkT[:, h, :], rhs=Sc[:, h, :], start=True, stop=True)
                I("dve", "tensor_tensor", [bU, bpX], [bvnew], out=fo_(vnew)[rs_, :], in0=fo_(U)[rs_, :], in1=pX[rs_, :], op=ALU.subtract)
                ps_, bps_ = pW[1]
                for h in range(8):
                    I("pe", "matmul", [bkhat, bvnew], [bps_], out=ps_[0:64, h * 64:(h + 1) * 64], lhsT=khat[rs_, h, :], rhs=vnew[rs_, h, :], start=True, stop=True)
                I("dve", "tensor_tensor", [bS, begl], [bS], out=S[:], in0=S[:], in1=egl[0:64, ci, :].unsqueeze(2).broadcast_to([64, 8, 64]), op=ALU.mult)
                I("dve", "tensor_tensor", [bS, bps_], [bS], out=fo_(S), in0=fo_(S), in1=ps_[0:64, :], op=ALU.add)
                Sn, bSn = (Sb2, bSb2) if ci == 0 else (Sb, bSb)
                if want_o or ci == 0:
                    pass
                I("act", "activation", [bS], [bSn], out=fo_(Sn), in_=fo_(S), func=AF.Copy)
                if want_o:
                    po, bpo = pW[0]
                    for h in range(8):
                        I("pe", "matmul", [bqtT, bSc], [bpo], out=po[:, h * 64:(h + 1) * 64], lhsT=qtT[:, h, :], rhs=Sc[:, h, :], start=True, stop=False)
                        I("pe", "matmul", [bAqkT, bvnew], [bpo], out=po[:, h * 64:(h + 1) * 64], lhsT=AqkT[:, h, :], rhs=vnew[:, h, :], start=False, stop=True)
                    I("act", "activation", [bpo], [bob], out=fo_(ob)[rs_, :], in_=po[rs_, :], func=AF.Copy)
            yield
            if want_o:
                I("pool", "dma_start", [bob], [], out=D.go[qrow:qrow + 128, :], in_=fo_(ob))

        blks = list(range(64)) if blocks is None else list(blocks)
        if blks:
            I("dve", "memset", [], [bvnew], ap=vnew[:].rearrange("p h d -> p (h d)"), constant=0.0)
            I("dve", "memset", [], [bS], ap=S[:].rearrange("p h d -> p (h d)"), constant=0.0)
            I("dve", "memset", [], [bSb], ap=Sb[:].rearrange("p h d -> p (h d)"), constant=0.0)
            load(blks[0] * 128, 0)
            for n_, b in enumerate(blks):
                if n_ + 1 < len(blks):
                    load(blks[n_ + 1] * 128, (n_ + 1) % 2)
                want = o_all or b >= 32
                yield from process(n_ % 2, max(b * 128 - NP_, 0), want, False)
            I("sp", "dma_start", [bS], [], out=D.sfin[0].rearrange("h k v -> k h v"), in_=S[:])
        for q in samples:
            load(NP_ + NO_ + q * 128, q % 2)
            I("sp", "dma_start", [], [bS], out=S[:], in_=D.sgdn[q].rearrange("h k v -> k h v"))
            I("act", "activation", [bS], [bSb], out=Sb[:].rearrange("p h d -> p (h d)"), in_=S[:].rearrange("p h d -> p (h d)"), func=AF.Copy)
            yield from process(q % 2, NO_ + q * 128, True, True)
            I("sp", "dma_start", [bS], [], out=D.sfin[1 + q].rearrange("h k v -> k h v"), in_=S[:])


QTILES = [(i * 512, 4) for i in range(8)] + [(NO_, 2)]


def phase_d1(nc, fw, D, tiles=None):
    I = fw.I
    with ExitStack() as st:
        def sb(name, shape, dt):
            return st.enter_context(nc.sbuf_tensor("D1_" + name, shape, dt)), Buf(name)

        def pst(name, shape, dt):
            return st.enter_context(nc.psum_tensor("D1_" + name, shape, dt)), Buf(name)
        identb, bidentb = sb("identb", [128, 128], BF16)
        wpa, bwpa = sb("wpa", [128, 4, 1024], BF16)
        wpb, bwpb = sb("wpb", [128, 4, 1024], BF16)
        wout, bwout = sb("wout", [128, 8, 1024], BF16)
        gpost, bgpost = sb("gpost", [128, 1024], F32)
        foT, bfoT = sb("foT", [128, 4, 512], BF16)
        gob, bgob = sb("gob", [128, 4, 512], BF16)
        gob2, bgob2 = sb("gob2", [128, 4, 512], BF16)
        ggt, bggt = sb("ggt", [128, 4, 512], F32)
        ggdn, bggdn = sb("ggdn", [128, 8, 64], F32)
        osq, bosq = sb("osq", [128, 512], F32)
        ou, bou = sb("ou", [128, 512], F32)
        oss, boss = sb("oss", [128, 4, 8], F32)
        goT, bgoT = sb("goT", [128, 4, 512], BF16)
        sgA, bsgA = sb("sgA", [128, 8, 512], BF16)
        sgB, bsgB = sb("sgB", [128, 8, 512], BF16)
        t1s = Rot([sb(f"t1{i}", [128, 512], F32) for i in range(2)])
        t2s = Rot([sb(f"t2{i}", [128, 512], F32) for i in range(2)])
        mT, bmT = sb("mT", [128, 8, 512], BF16)
        xt, bxt = sb("xt", [128, 4, 1024], F32)
        mix, bmix = sb("mix", [128, 1024], F32)
        junk, bjunk = sb("junk", [128, 1024], BF16)
        ss, bss = sb("ss", [128, 1], F32)
        y1, by1 = sb("y1", [128, 4, 1024], F32)
        pa = Rot([pst(f"pa{i}", [128, 512], F32) for i in range(2)])
        pb = Rot([pst(f"pb{i}", [128, 512], F32) for i in range(2)])
        pm = Rot([pst(f"pm{i}", [128, 512], F32) for i in range(2)])
        ptr, bptr = pst("ptr", [128, 1024], BF16)

        I("sp", "dma_start", [], [bidentb], out=identb[:], in_=D.cmask[:, 0, :])
        I("pool", "dma_start", [], [bwpa], out=wpa[:], in_=D.w_pa.rearrange("(c p) n -> p c n", p=128))
        I("pool", "dma_start", [], [bwpb], out=wpb[:], in_=D.w_pb.rearrange("(c p) n -> p c n", p=128))
        I("pool", "dma_start", [], [bwout], out=wout[:], in_=D.w_out.rearrange("(c p) n -> p c n", p=128))
        I("sp", "dma_start", [], [bgpost], out=gpost[:], in_=D.g_mix_post[0:1, :].broadcast_to([128, 1024]))
        for h in range(8):
            I("sp", "dma_start", [], [bggdn], out=ggdn[:, h, :], in_=D.g_gdn[0:1, :].broadcast_to([128, 64]))
        I("dve", "tensor_scalar_mul", [bggdn], [bggdn], out=ggdn[:].rearrange("p h d -> p (h d)"), in0=ggdn[:].rearrange("p h d -> p (h d)"), scalar1=0.5)
        tl = QTILES if tiles is None else [QTILES[i] for i in tiles]
        for (tq, ns) in tl:
            N = ns * 128
            I("sp", "dma_start", [], [bfoT], out=foT[:, :, 0:N], in_=D.foT[:, tq:tq + N].rearrange("(c p) n -> p c n", p=128))
            I("sp", "dma_start", [], [bgob], out=gob[:, 0:ns, :], in_=D.go[tq:tq + N, :].rearrange("(s p) n -> p s n", p=128))
            I("sp", "dma_start", [], [bsgA], out=sgA[:, :, 0:N], in_=D.sgA[:, tq:tq + N].rearrange("(c p) n -> p c n", p=128))
            I("sp", "dma_start", [], [bsgB], out=sgB[:, :, 0:N], in_=D.sgB[:, tq:tq + N].rearrange("(c p) n -> p c n", p=128))
            I("sp", "dma_start", [], [bxt], out=xt[:, 0:ns, :], in_=D.xall[NP_ + tq: NP_ + tq + N, :].rearrange("(s p) m -> p s m", p=128))
            I("sp", "dma_start", [], [bggt], out=ggt[:, 0:ns, :], in_=D.gg[tq:tq + N, :].rearrange("(s p) n -> p s n", p=128))
            for s in range(ns):
                I("dve", "tensor_tensor", [bgob], [bosq], out=osq[:], in0=gob[:, s, :], in1=gob[:, s, :], op=ALU.mult)
                I("dve", "tensor_reduce", [bosq], [boss], out=oss[:, s, :], in_=osq[:].rearrange("p (h d) -> p h d", d=64), axis=AX.X, op=ALU.add)
                I("pool", "tensor_tensor", [bggt, bggdn], [bggt], out=ggt[:, s, :], in0=ggt[:, s, :], in1=ggdn[:].rearrange("p h d -> p (h d)"), op=ALU.mult)
            I("act", "activation", [boss], [boss], out=oss[:, 0:ns, :], in_=oss[:, 0:ns, :], func=AF.Sqrt, bias=EPS, scale=1.0 / 64)
            I("dve", "reciprocal", [boss], [boss], out=oss[:, 0:ns, :], in_=oss[:, 0:ns, :])
            for s in range(ns):
                I("dve", "tensor_tensor", [bgob, boss], [bou], out=ou[:].rearrange("p (h d) -> p h d", d=64), in0=gob[:, s, :].rearrange("p (h d) -> p h d", d=64),
                  in1=oss[:, s, :].unsqueeze(2).broadcast_to([128, 8, 64]), op=ALU.mult)
                I("pool", "tensor_tensor", [bou, bggt], [bgob2], out=gob2[:, s, :], in0=ou[:], in1=ggt[:, s, :], op=ALU.mult)
            for half in range(2):
                for cc in range(2):
                    c = half * 2 + cc
                    for s in range(ns):
                        I("pe", "transpose", [bgob2, bidentb], [bptr], out=ptr[:, cc * 512 + s * 128: cc * 512 + (s + 1) * 128], in_=gob2[:, s, c * 128:(c + 1) * 128],
                          identity=identb[:])
                for cc in range(2):
                    I("act", "activation", [bptr], [bgoT], out=goT[:, half * 2 + cc, 0:N], in_=ptr[:, cc * 512: cc * 512 + N], func=AF.Copy)
            for oc in range(8):
                p1, bp1 = pa.next()
                p2, bp2 = pb.next()
                for c in range(4):
                    I("pe", "matmul", [bwpa, bfoT], [bp1], out=p1[:, 0:N], lhsT=wpa[:, c, oc * 128:(oc + 1) * 128], rhs=foT[:, c, 0:N], start=(c == 0), stop=(c == 3))
                for c in range(4):
                    I("pe", "matmul", [bwpb, bgoT], [bp2], out=p2[:, 0:N], lhsT=wpb[:, c, oc * 128:(oc + 1) * 128], rhs=goT[:, c, 0:N], start=(c == 0), stop=(c == 3))
                t1, bt1 = t1s.next()
                t2, bt2 = t2s.next()
                I("dve", "tensor_tensor", [bp1, bsgA], [bt1], out=t1[:, 0:N], in0=p1[:, 0:N], in1=sgA[:, oc, 0:N], op=ALU.mult)
                I("dve", "tensor_tensor", [bp2, bsgB], [bt2], out=t2[:, 0:N], in0=p2[:, 0:N], in1=sgB[:, oc, 0:N], op=ALU.mult)
                I("pool", "tensor_tensor", [bt1, bt2], [bmT], out=mT[:, oc, 0:N], in0=t1[:, 0:N], in1=t2[:, 0:N], op=ALU.add)
            for s in range(ns):
                for cg in range(2):
                    p, bp = pm.next()
                    for kc in range(8):
                        I("pe", "matmul", [bmT, bwout], [bp], out=p[:, :], lhsT=mT[:, kc, s * 128:(s + 1) * 128], rhs=wout[:, kc, cg * 512:(cg + 1) * 512], start=(kc == 0), stop=(kc == 7))
                    if cg == 0:
                        I("act", "activation", [bp], [bmix], out=mix[:, 0:512], in_=p[:, :], func=AF.Copy)
                    else:
                        I("dve", "tensor_copy", [bp], [bmix], out=mix[:, 512:1024], in_=p[:, :])
                I("act", "activation", [bmix], [bjunk, bss], out=junk[:], in_=mix[:], func=AF.Square, accum_out=ss[:, 0:1])
                I("act", "activation", [bss], [bss], out=ss[:], in_=ss[:], func=AF.Sqrt, bias=EPS, scale=1.0 / 1024)
                I("dve", "reciprocal", [bss], [bss], out=ss[:], in_=ss[:])
                I("dve", "scalar_tensor_tensor", [bmix, bss, bgpost], [bmix], out=mix[:], in0=mix[:], scalar=ss[:, 0:1], in1=gpost[:], op0=ALU.mult, op1=ALU.mult)
                I("pool", "tensor_tensor", [bmix, bxt], [by1], out=y1[:, s, :], in0=mix[:], in1=xt[:, s, :], op=ALU.add)
            I("sp", "dma_start", [by1], [], out=D.y1[tq:tq + N, :].rearrange("(s p) m -> p s m", p=128), in_=y1[:, 0:ns, :])
        fw.barrier()
        fw.emit()


def phase_d2(nc, fw, D, tiles=None):
    I = fw.I
    with ExitStack() as st:
        def sb(name, shape, dt):
            return st.enter_context(nc.sbuf_tensor("D2_" + name, shape, dt)), Buf(name)

        def pst(name, shape, dt):
            return st.enter_context(nc.psum_tensor("D2_" + name, shape, dt)), Buf(name)
        identb, bidentb = sb("identb", [128, 128], BF16)
        wup, bwup = sb("wup", [128, 8, 4096], BF16)
        wdn, bwdn = sb("wdn", [128, 32, 1024], BF16)
        gpre, bgpre = sb("gpre", [128, 1024], F32)
        gpost, bgpost = sb("gpost", [128, 1024], F32)
        y1, by1 = sb("y1", [128, 4, 1024], F32)
        hbs = Rot([sb(f"hb{i}", [128, 1024], BF16) for i in range(2)])
        hT, bhT = sb("hT", [128, 8, 512], BF16)
        uT, buT = sb("uT", [128, 32, 512], BF16)
        rl = Rot([sb(f"rl{i}", [128, 512], F32) for i in range(2)])
        dsb, bdsb = sb("dsb", [128, 1024], F32)
        junk, bjunk = sb("junk", [128, 1024], BF16)
        ss, bss = sb("ss", [128, 4], F32)
        pu = Rot([pst(f"pu{i}", [128, 512], F32) for i in range(4)])
        pd = Rot([pst(f"pd{i}", [128, 512], F32) for i in range(2)])
        ptr = Rot([pst(f"ptr{i}", [128, 1024], BF16) for i in range(2)])

        I("sp", "dma_start", [], [bidentb], out=identb[:], in_=D.cmask[:, 0, :])
        for kc in range(8):
            I("pool", "dma_start", [], [bwup], out=wup[:, kc, :], in_=D.w_up[kc * 128:(kc + 1) * 128, :])
        for f4 in range(8):
            I("pool", "dma_start", [], [bwdn], out=wdn[:, f4 * 4:(f4 + 1) * 4, :], in_=D.w_down[f4 * 512:(f4 + 1) * 512, :].rearrange("(c p) n -> p c n", p=128))
        I("sp", "dma_start", [], [bgpre], out=gpre[:], in_=D.g_mlp_pre[0:1, :].broadcast_to([128, 1024]))
        I("sp", "dma_start", [], [bgpost], out=gpost[:], in_=D.g_mlp_post[0:1, :].broadcast_to([128, 1024]))
        tl = QTILES if tiles is None else [QTILES[i] for i in tiles]
        for (tq, ns) in tl:
            N = ns * 128
            I("sp", "dma_start", [], [by1], out=y1[:, 0:ns, :], in_=D.y1[tq:tq + N, :].rearrange("(s p) m -> p s m", p=128))
            for s in range(ns):
                I("act", "activation", [by1], [bjunk, bss], out=junk[:], in_=y1[:, s, :], func=AF.Square, accum_out=ss[:, s:s + 1])
            I("act", "activation", [bss], [bss], out=ss[:, 0:ns], in_=ss[:, 0:ns], func=AF.Sqrt, bias=EPS, scale=1.0 / 1024)
            I("dve", "reciprocal", [bss], [bss], out=ss[:, 0:ns], in_=ss[:, 0:ns])
            for s in range(ns):
                hb, bhb = hbs.next()
                I("dve", "scalar_tensor_tensor", [by1, bss, bgpre], [bhb], out=hb[:], in0=y1[:, s, :], scalar=ss[:, s:s + 1], in1=gpre[:], op0=ALU.mult, op1=ALU.mult)
                pt, bpt = ptr.next()
                for kc in range(8):
                    I("pe", "transpose", [bhb, bidentb], [bpt], out=pt[:, kc * 128:(kc + 1) * 128], in_=hb[:, kc * 128:(kc + 1) * 128], identity=identb[:])
                if s % 2 == 0:
                    I("act", "activation", [bpt], [bhT], out=hT[:, :, s * 128:(s + 1) * 128], in_=pt[:, :].rearrange("p (k t) -> p k t", t=128), func=AF.Copy)
                else:
                    I("dve", "tensor_copy", [bpt], [bhT], out=hT[:, :, s * 128:(s + 1) * 128], in_=pt[:, :].rearrange("p (k t) -> p k t", t=128))
            for fc in range(32):
                p, bp = pu.next()
                for kc in range(8):
                    I("pe", "matmul", [bwup, bhT], [bp], out=p[:, 0:N], lhsT=wup[:, kc, fc * 128:(fc + 1) * 128], rhs=hT[:, kc, 0:N], start=(kc == 0), stop=(kc == 7))
                r, br = rl.next()
                I("act", "activation", [bp], [br], out=r[:, 0:N], in_=p[:, 0:N], func=AF.Relu)
                eng = "pool" if fc % 2 == 0 else "dve"
                I(eng, "tensor_tensor", [br], [buT], out=uT[:, fc, 0:N], in0=r[:, 0:N], in1=r[:, 0:N], op=ALU.mult)
            for s in range(ns):
                for cg in range(2):
                    p, bp = pd.next()
                    for fc in range(32):
                        I("pe", "matmul", [buT, bwdn], [bp], out=p[:, :], lhsT=uT[:, fc, s * 128:(s + 1) * 128], rhs=wdn[:, fc, cg * 512:(cg + 1) * 512], start=(fc == 0), stop=(fc == 31))
                    if cg == 0:
                        I("act", "activation", [bp], [bdsb], out=dsb[:, 0:512], in_=p[:, :], func=AF.Copy)
                    else:
                        I("dve", "tensor_copy", [bp], [bdsb], out=dsb[:, 512:1024], in_=p[:, :])
                I("act", "activation", [bdsb], [bjunk, bss], out=junk[:], in_=dsb[:], func=AF.Square, accum_out=ss[:, s:s + 1])
                I("act", "activation", [bss], [bss], out=ss[:, s:s + 1], in_=ss[:, s:s + 1], func=AF.Sqrt, bias=EPS, scale=1.0 / 1024)
                I("dve", "reciprocal", [bss], [bss], out=ss[:, s:s + 1], in_=ss[:, s:s + 1])
                I("dve", "scalar_tensor_tensor", [bdsb, bss, bgpost], [bdsb], out=dsb[:], in0=dsb[:], scalar=ss[:, s:s + 1], in1=gpost[:], op0=ALU.mult, op1=ALU.mult)
                I("pool", "tensor_tensor", [bdsb, by1], [by1], out=y1[:, s, :], in0=dsb[:], in1=y1[:, s, :], op=ALU.add)
            I("sp", "dma_start", [by1], [], out=D.y[tq:tq + N, :].rearrange("(s p) m -> p s m", p=128), in_=y1[:, 0:ns, :])
        fw.barrier()
        fw.emit()


def phase_bc(nc, fw, D, ratio=2.7):
    with ExitStack() as st:
        banks = []
        for i in range(8):
            t = st.enter_context(nc.psum_tensor(f"BC_ps{i}", [128, 512], F32))
            banks.append((t, Buf(f"BC_ps{i}")))
        gb = phase_b_body(nc, fw, D, st, banks[0:4])
        gc = phase_c_body(nc, fw, D, st, banks[4:8])
        b_alive = c_alive = True
        acc = 0.0
        while b_alive or c_alive:
            if c_alive:
                try:
                    next(gc)
                except StopIteration:
                    c_alive = False
            acc += ratio if c_alive else 1.0
            while b_alive and acc >= 1.0:
                acc -= 1.0
                try:
                    next(gb)
                except StopIteration:
                    b_alive = False
        fw.barrier()
        fw.emit()


BF = ml_dtypes.bfloat16


def const_masks():
    p = np.arange(128)
    cm = np.zeros((128, 8, 128), np.float32)
    cm[:, 0, :] = np.eye(128)
    cm[:, 1, :] = (p[:, None] // 64 == p[None, :] // 64)
    NEG = -30000.0
    cm[:, 2, :] = np.where(p[None, :] >= p[:, None], 0.0, NEG)
    cm[:, 3, :] = np.where(p[None, :] > p[:, None], 0.0, NEG)
    cm[:, 4, :] = np.where(p[:, None] > p[None, :], 0.0, NEG)
    cm[:, 5, :] = 1.0
    same = (p[:, None] // 64 == p[None, :] // 64)
    cm[:, 6, :] = np.where(same & (p[None, :] >= p[:, None]), 0.0, NEG)
    cm[:, 7, :] = np.where(same & (p[None, :] > p[:, None]), 0.0, NEG)
    cf = np.zeros((128, 6, 128), np.float32)
    cf[:, 0, :] = np.eye(128)
    cf[:, 1, :] = (p[:, None] <= p[None, :])
    cf[:, 2, :] = 1.0
    cf[127, 3, :] = 1.0
    cf[:, 4, :] = (p[:, None] <= p[None, :]) & (p[:, None] // 64 == p[None, :] // 64)
    cf[63, 5, :] = 1.0
    return cm.astype(BF), cf


def prep_core(inp, c):
    b, half = c // 2, c % 2
    xp = inp["x_prompt"][b]
    xall = np.zeros((4096 + 4096 + 256, 1024), np.float32)
    kmask = np.zeros((128, 32), np.float32)
    if half == 1:
        xall[:8192] = xp
    else:
        xall[4096:8192] = xp[:4096]
        kmask[:] = -30000.0
    for q in range(2):
        xall[8192 + q * 128: 8192 + q * 128 + 16] = inp["x_sample"][2 * c + q]
    w_in = inp["w_in"][0]
    w_sm = np.concatenate([w_in[:, 1536:1544], w_in[:, 3080:3096]], axis=1)
    bias24 = np.concatenate([inp["fox_forget_bias"][0], inp["gdn_dt_bias"][0], np.zeros(8, np.float32)])[None, :]
    sgn24 = np.concatenate([-np.ones(8), np.ones(8), -np.ones(8)]).astype(np.float32)[None, :]
    convT = np.ascontiguousarray(inp["gdn_conv_w"][0].T.reshape(12, 128, 4).transpose(1, 0, 2))
    ch = inp["state_gdn_conv"][0, 2 * c: 2 * c + 2]
    conv_hist = np.ascontiguousarray(ch.transpose(0, 2, 1).reshape(2, 12, 128, 3).transpose(0, 2, 1, 3))
    cm, cf = const_masks()
    d = {
        "xall": xall, "kmask": kmask, "vmask": (np.arange(128) < 16).astype(np.float32)[:, None], "w_in": w_in, "w_sm": np.ascontiguousarray(w_sm), "bias24": bias24.astype(np.float32),
        "sgn24": sgn24, "a_log": inp["gdn_a_log"], "convT": convT, "conv_hist": conv_hist,
        "g_mix_pre": inp["norm_mix_pre"], "g_mix_post": inp["norm_mix_post"], "g_mlp_pre": inp["norm_mlp_pre"], "g_mlp_post": inp["norm_mlp_post"],
        "g_gdn": inp["gdn_norm_g"], "w_pa": inp["w_proj_fox"][0], "w_pb": inp["w_proj_gdn"][0], "w_out": inp["w_out"][0],
        "w_up": inp["w_up"][0], "w_down": inp["w_down"][0],
        "ck": inp["cache_fox_k"][0, 2 * c:2 * c + 2].reshape(2, 2048, 512), "cv": inp["cache_fox_v"][0, 2 * c:2 * c + 2].reshape(2, 2048, 512),
        "clf": inp["cache_fox_logf"][0, 2 * c:2 * c + 2], "sgdn": inp["state_gdn"][0, 2 * c:2 * c + 2],
        "cmask": cm, "cf32": cf,
    }
    return {k: np.ascontiguousarray(v) for k, v in d.items()}


def build_program():
    nc = bass.Bass("TRN2", target_bir_lowering=False)
    with ExitStack() as st:
        D = declare(nc, False)
        fw = FW(nc, st)
        phase_a(nc, fw, D)
        phase_b(nc, fw, D)
        phase_c(nc, fw, D)
        phase_d1(nc, fw, D)
        phase_d2(nc, fw, D)
    return nc


def kernel(**inp):
    inp = {k: np.asarray(v) for k, v in inp.items()}
    nc = build_program()
    in_maps = [prep_core(inp, c) for c in range(8)]
    res = run_bass_kernel_spmd(nc, in_maps, core_ids=list(range(8)))
    R = res.results
    f = lambda a: np.asarray(a, dtype=np.float32)
    y_p = np.zeros((4, 8192, 1024), np.float32)
    y_s = np.zeros((16, 16, 1024), np.float32)
    fk_p = np.zeros((1, 4, 8192, 8, 64), np.float32)
    fv_p = np.zeros_like(fk_p)
    lf_p = np.zeros((1, 4, 8192, 8), np.float32)
    sg_p = np.zeros((1, 4, 8, 64, 64), np.float32)
    cv_p = np.zeros((1, 4, 3, 1536), np.float32)
    fk_s = np.zeros((1, 16, 16, 8, 64), np.float32)
    fv_s = np.zeros_like(fk_s)
    lf_s = np.zeros((1, 16, 16, 8), np.float32)
    sg_s = np.zeros((1, 16, 8, 64, 64), np.float32)
    cv_s = np.zeros((1, 16, 3, 1536), np.float32)
    for c in range(8):
        b, half = c // 2, c % 2
        r = R[c]
        sl = slice(half * 4096, (half + 1) * 4096)
        y_p[b, sl] = f(r["y"])[:4096]
        fk_p[0, b, sl] = f(r["fk"])[:4096].reshape(4096, 8, 64)
        fv_p[0, b, sl] = f(r["fv"])[:4096].reshape(4096, 8, 64)
        lf_p[0, b, sl] = f(r["lf"])[:4096]
        if half == 1:
            sg_p[0, b] = f(r["sfin"])[0]
            cv_p[0, b] = f(r["convo"])[0]
        for q in range(2):
            s = 2 * c + q
            rows = slice(4096 + q * 128, 4096 + q * 128 + 16)
            y_s[s] = f(r["y"])[rows]
            fk_s[0, s] = f(r["fk"])[rows].reshape(16, 8, 64)
            fv_s[0, s] = f(r["fv"])[rows].reshape(16, 8, 64)
            lf_s[0, s] = f(r["lf"])[rows]
            sg_s[0, s] = f(r["sfin"])[1 + q]
            cv_s[0, s] = f(r["convo"])[1 + q]
    return (y_p, y_s, fk_p, fv_p, lf_p, sg_p, cv_p, fk_s, fv_s, lf_s, sg_s, cv_s)
```

```python
from contextlib import ExitStack
import numpy as np
import ml_dtypes
from concourse.bass_utils import run_bass_kernel_spmd
import concourse.bass as bass
import concourse.mybir as mybir

F32 = mybir.dt.float32
BF16 = mybir.dt.bfloat16
AF = mybir.ActivationFunctionType
ALU = mybir.AluOpType
AX = mybir.AxisListType

ENGS = ("pe", "act", "dve", "pool", "sp")
EPOCH = 30000
NDMASEM = 30
DMA_ENGS = ("sp", "pool")


class Buf:
    __slots__ = ("name", "w", "r")

    def __init__(self, name):
        self.name = name
        self.w = None
        self.r = {}


class FW:
    def __init__(self, nc, stack):
        self.nc = nc
        self.stack = stack
        self.ops = {e: [] for e in ENGS}
        self.cnt = {e: 0 for e in ENGS}
        self.ccnt = {e: 0 for e in ENGS}
        self.sems = {}
        self.dsem = {}
        self.dcnt = {}
        self.drr = {e: 0 for e in ENGS}
        self.waited = {e: {} for e in ENGS}
        for e in DMA_ENGS:
            for i in range(NDMASEM):
                self.dsem[(e, i)] = stack.enter_context(nc.semaphore(f"d_{e}_{i}"))
                self.dcnt[(e, i)] = 0
        self.nsem_ep = {e: 0 for e in ENGS}
        self.pending = {e: [] for e in ENGS}

    def _sem(self, eng, ep):
        k = (eng, ep)
        if k not in self.sems:
            self.sems[k] = self.stack.enter_context(self.nc.semaphore(f"c_{eng}_{ep}"))
        return self.sems[k]

    def _need(self, eng, tok, waits):
        if tok is None:
            return
        key, val = tok
        if eng == "pe" and key[0] == "c" and key[1] == "pe":
            return
        cur = self.waited[eng].get(key, 0)
        if cur >= val:
            return
        self.waited[eng][key] = val
        waits[key] = max(waits.get(key, 0), val)

    def op(self, eng, fn, reads=(), writes=(), dma=False):
        waits = {}
        if self.pending[eng]:
            for t in self.pending[eng]:
                self._need(eng, t, waits)
            self.pending[eng] = []
        for b in reads:
            self._need(eng, b.w, waits)
        for b in writes:
            self._need(eng, b.w, waits)
            for t in b.r.items():
                self._need(eng, t, waits)
        if dma:
            i = self.drr[eng] % NDMASEM
            self.drr[eng] += 1
            if self.dcnt[(eng, i)] > 0:
                self._need(eng, (("d", eng, i), self.dcnt[(eng, i)]), waits)
            self.dcnt[(eng, i)] += 16
            key = ("d", eng, i)
            tok = (key, self.dcnt[(eng, i)])
            inc = (self.dsem[(eng, i)], 16)
        else:
            n = self.ccnt[eng]
            self.ccnt[eng] += 1
            ep = n // EPOCH
            key = ("c", eng, ep)
            tok = (key, n % EPOCH + 1)
            inc = (self._sem(eng, ep), 1)
        self.cnt[eng] += 1
        for b in writes:
            b.w = tok
            b.r = {}
        for b in reads:
            if b not in writes:
                b.r[tok[0]] = max(b.r.get(tok[0], 0), tok[1])
        self.ops[eng].append((waits, fn, inc))
        return tok

    def I(self, eng, method, reads=(), writes=(), **kw):
        dma = method == "dma_start"
        return self.op(eng, lambda e: getattr(e, method)(**kw), reads=reads, writes=writes, dma=dma)

    def semh(self, key):
        if key[0] == "d":
            return self.dsem[(key[1], key[2])]
        return self._sem(key[1], key[2])

    def barrier(self):
        toks = []
        for e in ENGS:
            n = self.ccnt[e]
            if n > 0:
                ep = (n - 1) // EPOCH
                toks.append((("c", e, ep), (n - 1) % EPOCH + 1))
                for ep2 in range(ep):
                    toks.append((("c", e, ep2), EPOCH))
            for i in range(NDMASEM):
                if e in DMA_ENGS and self.dcnt[(e, i)] > 0:
                    toks.append((("d", e, i), self.dcnt[(e, i)]))
        for e in ENGS:
            if self.ops[e]:
                waits = {}
                for t in toks:
                    self._need(e, t, waits)
                if waits:
                    self.ops[e].append((waits, None, None))
            else:
                self.pending[e] = list(toks)

    def emit(self):
        nc = self.nc
        with nc.Block() as block:
            def mk(eng_name):
                def body(e):
                    for waits, fn, inc in self.ops[eng_name]:
                        for key, val in waits.items():
                            e.wait_ge(self.semh(key), val)
                        if fn is not None:
                            ins = fn(e)
                            ins.then_inc(inc[0], inc[1])
                return body
            regs = {"pe": block.tensor, "act": block.scalar, "dve": block.vector, "pool": block.gpsimd, "sp": block.sync}
            for en in ENGS:
                if self.ops[en]:
                    regs[en](mk(en))
        self.ops = {e: [] for e in ENGS}


NP_ = 4096
NO_ = 4096
NS_ = 256
NT_ = NP_ + NO_ + NS_
NQ_ = NO_ + NS_
EPS = 1e-6
C_FQ, C_FK, C_FV, C_FF, C_GQ, C_GA, C_GB, C_GG, C_A, C_B = 0, 512, 1024, 1536, 1544, 3080, 3088, 3096, 3608, 4632


class Ctx:
    pass


def declare(nc, debug, ext_in=()):
    D = Ctx()
    ei = lambda n, s, dt=F32: nc.dram_tensor(n, s, dt, kind="ExternalInput").ap()
    eo = lambda n, s, dt=F32: nc.dram_tensor(n, s, dt, kind="ExternalOutput").ap()
    sc = lambda n, s, dt=F32: nc.dram_tensor(n, s, dt, kind=("ExternalInput" if n in ext_in else ("ExternalOutput" if debug else "Internal"))).ap()
    D.xall = ei("xall", [NT_, 1024])
    D.kmask = ei("kmask", [128, 32])
    D.vmask = ei("vmask", [128, 1])
    D.w_in = ei("w_in", [1024, 5656])
    D.w_sm = ei("w_sm", [1024, 24])
    D.bias24 = ei("bias24", [1, 24])
    D.sgn24 = ei("sgn24", [1, 24])
    D.a_log = ei("a_log", [1, 8])
    D.convT = ei("convT", [128, 12, 4])
    D.conv_hist = ei("conv_hist", [2, 128, 12, 3])
    D.g_mix_pre = ei("g_mix_pre", [1, 1024])
    D.g_mix_post = ei("g_mix_post", [1, 1024])
    D.g_mlp_pre = ei("g_mlp_pre", [1, 1024])
    D.g_mlp_post = ei("g_mlp_post", [1, 1024])
    D.g_gdn = ei("g_gdn", [1, 64])
    D.w_pa = ei("w_pa", [512, 1024])
    D.w_pb = ei("w_pb", [512, 1024])
    D.w_out = ei("w_out", [1024, 1024])
    D.w_up = ei("w_up", [1024, 4096])
    D.w_down = ei("w_down", [4096, 1024])
    D.ck = ei("ck", [2, 2048, 512])
    D.cv = ei("cv", [2, 2048, 512])
    D.clf = ei("clf", [2, 2048, 8])
    D.sgdn = ei("sgdn", [2, 8, 64, 64])
    D.cmask = ei("cmask", [128, 8, 128], BF16)
    D.cf32 = ei("cf32", [128, 6, 128])
    D.y = eo("y", [NQ_, 1024])
    D.fk = eo("fk", [NQ_, 512])
    D.fv = eo("fv", [NQ_, 512])
    D.lf = eo("lf", [NQ_, 8])
    D.sfin = eo("sfin", [3, 8, 64, 64])
    D.convo = eo("convo", [3, 3, 1536])
    D.qT = sc("qT_s", [512, NQ_], BF16)
    D.kT = sc("kT_s", [512, NT_], BF16)
    D.V = sc("V_s", [NT_, 512], BF16)
    D.lfs = sc("lf_s", [NT_, 8])
    D.g = sc("g_s", [NT_, 8])
    D.beta = sc("beta_s", [NT_, 16])
    D.gqT = sc("gqT_s", [512, NT_], BF16)
    D.gkT = sc("gkT_s", [512, NT_], BF16)
    D.gqkv = sc("gqkv_s", [NT_, 1536], BF16)
    D.gg = sc("gg_s", [NQ_, 512])
    D.sgA = sc("sgA_s", [1024, NQ_], BF16)
    D.sgB = sc("sgB_s", [1024, NQ_], BF16)
    D.foT = sc("foT_s", [512, NQ_], BF16)
    D.go = sc("go_s", [NQ_, 512], BF16)
    D.y1 = sc("y1_s", [NQ_, 1024])
    return D


class Rot:
    def __init__(self, items):
        self.items = items
        self.i = 0

    def next(self):
        it = self.items[self.i % len(self.items)]
        self.i += 1
        return it


def phase_a(nc, fw, D, tiles=None):
    I = fw.I
    with ExitStack() as st:
        def sb(name, shape, dt):
            return st.enter_context(nc.sbuf_tensor("A_" + name, shape, dt)), Buf(name)

        def pst(name, shape, dt):
            return st.enter_context(nc.psum_tensor("A_" + name, shape, dt)), Buf(name)
        Win, bWin = sb("Win", [128, 8, 5656], BF16)
        Wsm, bWsm = sb("Wsm", [128, 8, 24], BF16)
        xts = Rot([sb(f"xt{i}", [128, 4, 1024], F32) for i in range(2)])
        hb, bhb = sb("hb", [128, 4, 1024], BF16)
        hTs = Rot([sb(f"hT{i}", [128, 8, 512], BF16) for i in range(1)])
        junk, bjunk = sb("junk", [128, 1024], BF16)
        ss, bss = sb("ss", [128, 4], F32)
        rs, brs = sb("rs", [128, 4], F32)
        gpre, bgpre = sb("gpre", [128, 1024], F32)
        identb, bidentb = sb("identb", [128, 128], BF16)
        diagw, bdiagw = sb("diagw", [128, 48, 128], BF16)
        cwT, bcwT = sb("cwT", [128, 12, 4], F32)
        xg, bxg = sb("xg", [128, 12, 515], BF16)
        cT, bcT = sb("cT", [128, 12, 512], BF16)
        sbf = Rot([sb(f"sbf{i}", [128, 512], BF16) for i in range(4)])
        sf32 = Rot([sb(f"sf32{i}", [128, 512], F32) for i in range(2)])
        tts = Rot([sb(f"tt{i}", [128, 512], BF16) for i in range(3)])
        sqt, bsqt = sb("sqt", [128, 1024], BF16)
        l2ss, bl2ss = sb("l2ss", [128, 4, 16], F32)
        tball, btball = sb("tball", [128, 4, 1536], BF16)
        sm_t, bsm_t = sb("sm_t", [128, 96], F32)
        sm_e, bsm_e = sb("sm_e", [128, 96], F32)
        sm_l, bsm_l = sb("sm_l", [128, 96], F32)
        sm_o, bsm_o = sb("sm_o", [128, 4, 32], F32)
        b24, bb24 = sb("b24", [128, 24], F32)
        s24, bs24 = sb("s24", [128, 24], F32)
        nea, bnea = sb("nea", [128, 8], F32)
        ptr = Rot([pst(f"ptr{i}", [128, 1024], BF16) for i in range(2)])
        pm = Rot([pst(f"pm{i}", [128, 512], F32) for i in range(4)])
        psm, bpsm = pst("psm", [128, 512], F32)

        I("pool", "dma_start", [], [bWin], out=Win[:], in_=D.w_in.rearrange("(c p) n -> p c n", p=128))
        I("pool", "dma_start", [], [bWsm], out=Wsm[:], in_=D.w_sm.rearrange("(c p) n -> p c n", p=128))
        I("sp", "dma_start", [], [bgpre], out=gpre[:], in_=D.g_mix_pre[0:1, :].broadcast_to([128, 1024]))
        I("sp", "dma_start", [], [bidentb], out=identb[:], in_=D.cmask[:, 0, :])
        I("sp", "dma_start", [], [bcwT], out=cwT[:], in_=D.convT[:, :, :])
        I("sp", "dma_start", [], [bb24], out=b24[:], in_=D.bias24[0:1, :].broadcast_to([128, 24]))
        I("sp", "dma_start", [], [bs24], out=s24[:], in_=D.sgn24[0:1, :].broadcast_to([128, 24]))
        I("sp", "dma_start", [], [bnea], out=nea[:], in_=D.a_log[0:1, :].broadcast_to([128, 8]))
        I("act", "activation", [bnea], [bnea], out=nea[:], in_=nea[:], func=AF.Exp)
        I("dve", "tensor_scalar_mul", [bnea], [bnea], out=nea[:], in0=nea[:], scalar1=-1.0)
        I("dve", "tensor_scalar_mul", [bcwT], [bcwT], out=cwT[:], in0=cwT[:], scalar1=0.5)
        for j in range(12):
            for i in range(4):
                I("dve", "tensor_scalar_mul", [bidentb, bcwT], [bdiagw], out=diagw[:, j * 4 + i, :], in0=identb[:], scalar1=cwT[:, j, i:i + 1])
        I("dve", "memset", [], [bxg], ap=xg[:, :, 0:3], constant=0.0)

        all_tiles = [(i * 512, 4, "p") for i in range(8)] + [(NP_ + i * 512, 4, "o") for i in range(8)] + [(NP_ + NO_, 2, "s")]
        if tiles is not None:
            all_tiles = [all_tiles[i] for i in tiles]
        cpy_rr = [0]

        def evac_copy(out_ap, in_ap, reads, writes, scale=None):
            cpy_rr[0] += 1
            if scale is not None:
                I("act", "activation", reads, writes, out=out_ap, in_=in_ap, func=AF.Copy, scale=scale)
            elif cpy_rr[0] % 2 == 0:
                I("act", "activation", reads, writes, out=out_ap, in_=in_ap, func=AF.Copy)
            else:
                I("dve", "tensor_copy", reads, writes, out=out_ap, in_=in_ap)

        def store(eng, out_ap, in_ap, buf):
            I(eng, "dma_start", [buf], [], out=out_ap, in_=in_ap)

        for (t0, ns, kind) in all_tiles:
            N = ns * 128
            own = kind in ("o", "s")
            tq = t0 - NP_
            xt, bxt = xts.next()
            hT, bhT = hTs.next()
            I("sp", "dma_start", [], [bxt], out=xt[:, 0:ns, :], in_=D.xall[t0:t0 + N, :].rearrange("(s p) m -> p s m", p=128))
            for s in range(ns):
                I("act", "activation", [bxt], [bjunk, bss], out=junk[:], in_=xt[:, s, :], func=AF.Square, accum_out=ss[:, s:s + 1])
            I("act", "activation", [bss], [brs], out=rs[:, 0:ns], in_=ss[:, 0:ns], func=AF.Sqrt, bias=EPS, scale=1.0 / 1024)
            I("dve", "reciprocal", [brs], [brs], out=rs[:, 0:ns], in_=rs[:, 0:ns])
            for s in range(ns):
                I("dve", "scalar_tensor_tensor", [bxt, brs, bgpre], [bhb], out=hb[:, s, :], in0=xt[:, s, :], scalar=rs[:, s:s + 1], in1=gpre[:],
                  op0=ALU.mult, op1=ALU.mult)
            for kcp in range(4):
                pt, bpt = ptr.next()
                for kk in range(2):
                    kc = 2 * kcp + kk
                    for s in range(ns):
                        I("pe", "transpose", [bhb, bidentb], [bpt], out=pt[:, kk * 512 + s * 128: kk * 512 + (s + 1) * 128],
                          in_=hb[:, s, kc * 128:(kc + 1) * 128], identity=identb[:])
                for kk in range(2):
                    evac_copy(hT[:, 2 * kcp + kk, 0:N], pt[:, kk * 512: kk * 512 + N], [bpt], [bhT])

            def fm_mm(c0, w=128):
                p, bp = pm.next()
                for kc in range(8):
                    I("pe", "matmul", [bWin, bhT], [bp], out=p[0:w, 0:N], lhsT=Win[:, kc, c0:c0 + w], rhs=hT[:, kc, 0:N], start=(kc == 0), stop=(kc == 7))
                return p, bp

            def tm_mm(s, c0, w, Wt=None, bW=None, out=None):
                if out is None:
                    p, bp = pm.next()
                    o = p[:, 0:w]
                else:
                    p, bp, o = out
                Wt_ = Win if Wt is None else Wt
                bW_ = bWin if bW is None else bW
                for kc in range(8):
                    I("pe", "matmul", [bW_, bhT], [bp], out=o, lhsT=hT[:, kc, s * 128:(s + 1) * 128], rhs=Wt_[:, kc, c0:c0 + w], start=(kc == 0), stop=(kc == 7))
                return p, bp

            fillers = []

            def do_fq(j):
                p, bp = fm_mm(C_FQ + j * 128)
                s_, bs_ = sbf.next()
                evac_copy(s_[:, 0:N], p[:, 0:N], [bp], [bs_], scale=0.125)
                store("sp", D.qT[j * 128:(j + 1) * 128, tq:tq + N], s_[:, 0:N], bs_)

            def do_fk(j):
                p, bp = fm_mm(C_FK + j * 128)
                s_, bs_ = sbf.next()
                evac_copy(s_[:, 0:N], p[:, 0:N], [bp], [bs_])
                store("sp", D.kT[j * 128:(j + 1) * 128, t0:t0 + N], s_[:, 0:N], bs_)

            def do_gate(j):
                p, bp = fm_mm(C_A + j * 128)
                s_, bs_ = sbf.next()
                tt, btt = tts.next()
                I("act", "activation", [bp], [btt], out=tt[:, 0:N], in_=p[:, 0:N], func=AF.Tanh, scale=0.5)
                I("dve", "tensor_scalar", [btt], [bs_], out=s_[:, 0:N], in0=tt[:, 0:N], scalar1=0.5, scalar2=0.5, op0=ALU.mult, op1=ALU.add)
                dst = D.sgA if j < 8 else D.sgB
                store("sp", dst[(j % 8) * 128:(j % 8 + 1) * 128, tq:tq + N], s_[:, 0:N], bs_)

            def do_tm_k(s):
                r0 = t0 + s * 128
                p, bp = tm_mm(s, C_FK, 512)
                sf, bsf = sf32.next()
                evac_copy(sf[:], p[:], [bp], [bsf])
                store("sp", D.fk[r0 - NP_: r0 - NP_ + 128, :], sf[:], bsf)

            def do_tm_v(s):
                r0 = t0 + s * 128
                p, bp = tm_mm(s, C_FV, 512)
                s_, bs_ = sbf.next()
                if own:
                    sf, bsf = sf32.next()
                    I("dve", "tensor_copy", [bp], [bsf], out=sf[:], in_=p[:])
                    store("sp", D.fv[r0 - NP_: r0 - NP_ + 128, :], sf[:], bsf)
                    I("act", "activation", [bsf], [bs_], out=s_[:], in_=sf[:], func=AF.Copy)
                else:
                    I("dve", "tensor_copy", [bp], [bs_], out=s_[:], in_=p[:])
                store("sp", D.V[r0:r0 + 128, :], s_[:], bs_)

            def do_tm_gg(s):
                r0 = t0 + s * 128
                p, bp = tm_mm(s, C_GG, 512)
                sf, bsf = sf32.next()
                tt, btt = tts.next()
                I("act", "activation", [bp], [btt], out=tt[:], in_=p[:], func=AF.Tanh, scale=0.5)
                I("dve", "scalar_tensor_tensor", [btt, bp], [bsf], out=sf[:], in0=tt[:], scalar=1.0, in1=p[:], op0=ALU.add, op1=ALU.mult)
                store("sp", D.gg[r0 - NP_: r0 - NP_ + 128, :], sf[:], bsf)

            def do_tm_small(s):
                tm_mm(s, 0, 24, Wt=Wsm, bW=bWsm, out=(psm, bpsm, psm[:, s * 24:(s + 1) * 24]))

            if own:
                for j in range(4):
                    fillers.append((do_fq, j))
            for j in range(4):
                fillers.append((do_fk, j))
            for s in range(ns):
                if own:
                    fillers.append((do_tm_k, s))
                fillers.append((do_tm_v, s))
                if own:
                    fillers.append((do_tm_gg, s))
                fillers.append((do_tm_small, s))
            if own:
                for j in range(16):
                    fillers.append((do_gate, j))
            per_step = (len(fillers) + 11) // 12

            def run_fillers(n):
                for _ in range(n):
                    if fillers:
                        f_, a_ = fillers.pop(0)
                        f_(a_)

            for j in range(12):
                p, bp = fm_mm(C_GQ + j * 128)
                evac_copy(xg[:, j, 3:3 + N], p[:, 0:N], [bp], [bxg])
            if kind == "s":
                for q_ in range(2):
                    I("pool", "dma_start", [], [bxg], out=xg[:, :, q_ * 128: q_ * 128 + 3], in_=D.conv_hist[q_])

            for j in range(12):
                p, bp = pm.next()
                for i in range(4):
                    I("pe", "matmul", [bdiagw, bxg], [bp], out=p[:, 0:N], lhsT=diagw[:, j * 4 + i, :], rhs=xg[:, j, i:i + N], start=(i == 0), stop=(i == 3))
                tt, btt = tts.next()
                I("act", "activation", [bp], [btt], out=tt[:, 0:N], in_=p[:, 0:N], func=AF.Tanh)
                I("dve", "scalar_tensor_tensor", [btt, bp], [bcT], out=cT[:, j, 0:N], in0=tt[:, 0:N], scalar=1.0, in1=p[:, 0:N], op0=ALU.add, op1=ALU.mult)
                run_fillers(per_step)
            run_fillers(len(fillers))
            I("dve", "tensor_copy", [bxg], [bxg], out=xg[:, :, 0:3], in_=xg[:, :, N:N + 3])
            for s in range(ns):
                for half in range(2):
                    pt, bpt = ptr.next()
                    nj = 8 if half == 0 else 4
                    for jj in range(nj):
                        j = half * 8 + jj
                        I("pe", "transpose", [bcT, bidentb], [bpt], out=pt[:, jj * 128:(jj + 1) * 128], in_=cT[:, j, s * 128:(s + 1) * 128], identity=identb[:])
                    evac_copy(tball[:, s, half * 1024: half * 1024 + nj * 128], pt[:, 0:nj * 128], [bpt], [btball])
                I("pool", "tensor_tensor", [btball], [bsqt], out=sqt[:], in0=tball[:, s, 0:1024], in1=tball[:, s, 0:1024], op=ALU.mult)
                I("dve", "tensor_reduce", [bsqt], [bl2ss], out=l2ss[:, s, :], in_=sqt[:].rearrange("p (h d) -> p h d", d=64), axis=AX.X, op=ALU.add)
            I("act", "activation", [bl2ss], [bl2ss], out=l2ss[:, 0:ns, :], in_=l2ss[:, 0:ns, :], func=AF.Sqrt, bias=EPS)
            I("dve", "reciprocal", [bl2ss], [bl2ss], out=l2ss[:, 0:ns, :], in_=l2ss[:, 0:ns, :])
            I("dve", "tensor_scalar_mul", [bl2ss], [bl2ss], out=l2ss[:, 0:ns, 0:8], in0=l2ss[:, 0:ns, 0:8], scalar1=0.125)
            for s in range(ns):
                qk = tball[:, s, 0:1024].rearrange("p (h d) -> p h d", d=64)
                I("dve", "tensor_tensor", [btball, bl2ss], [btball], out=qk, in0=qk, in1=l2ss[:, s, :].unsqueeze(2).broadcast_to([128, 16, 64]), op=ALU.mult)
                store("sp", D.gqkv[t0 + s * 128: t0 + (s + 1) * 128, :], tball[:, s, :], btball)
            n24 = ns * 24
            for s in range(ns):
                I("dve", "tensor_tensor", [bpsm, bb24], [bsm_t], out=sm_t[:, s * 24:(s + 1) * 24], in0=psm[:, s * 24:(s + 1) * 24], in1=b24[:], op=ALU.add)
                I("dve", "tensor_tensor", [bsm_t, bs24], [bsm_t], out=sm_t[:, s * 24:(s + 1) * 24], in0=sm_t[:, s * 24:(s + 1) * 24], in1=s24[:], op=ALU.mult)
            I("act", "activation", [bsm_t], [bsm_e], out=sm_e[:, 0:n24], in_=sm_t[:, 0:n24], func=AF.Exp)
            I("act", "activation", [bsm_e], [bsm_l], out=sm_l[:, 0:n24], in_=sm_e[:, 0:n24], func=AF.Ln, bias=1.0)
            for s in range(ns):
                I("dve", "tensor_scalar_mul", [bsm_l], [bsm_o], out=sm_o[:, s, 0:8], in0=sm_l[:, s * 24: s * 24 + 8], scalar1=-1.0)
                I("dve", "tensor_tensor", [bsm_l, bnea], [bsm_o], out=sm_o[:, s, 8:16], in0=sm_l[:, s * 24 + 8: s * 24 + 16], in1=nea[:], op=ALU.mult)
                I("dve", "tensor_scalar_add", [bsm_e], [bsm_o], out=sm_o[:, s, 16:24], in0=sm_e[:, s * 24 + 16: s * 24 + 24], scalar1=1.0)
                I("dve", "reciprocal", [bsm_o], [bsm_o], out=sm_o[:, s, 16:24], in_=sm_o[:, s, 16:24])
                I("dve", "tensor_scalar_mul", [bsm_l], [bsm_o], out=sm_o[:, s, 24:32], in0=sm_l[:, s * 24 + 16: s * 24 + 24], scalar1=-1.0)
            for (dst, c0, cw_) in ((D.lfs, 0, 8), (D.g, 8, 8), (D.beta, 16, 16)):
                I("sp", "dma_start", [bsm_o], [], out=dst[t0:t0 + N, :].rearrange("(s p) m -> p s m", p=128), in_=sm_o[:, 0:ns, c0:c0 + cw_])
            if own:
                I("sp", "dma_start", [bsm_o], [], out=D.lf[tq:tq + N, :].rearrange("(s p) m -> p s m", p=128), in_=sm_o[:, 0:ns, 0:8])
            conv_rows = []
            if kind == "o" and t0 == NP_ + NO_ - 512:
                conv_rows = [(3, 125, 0)]
            if kind == "s":
                conv_rows = [(0, 13, 1), (1, 13, 2)]
            for (s, r, oi) in conv_rows:
                for cg in range(3):
                    p, bp = tm_mm(s, C_GQ + cg * 512, 512)
                    sf, bsf = sf32.next()
                    evac_copy(sf[:], p[:], [bp], [bsf])
                    store("sp", D.convo[oi, :, cg * 512:(cg + 1) * 512], sf[r:r + 3, :], bsf)
        fw.barrier()
        fw.emit()


def phase_b(nc, fw, D, pairs=(0, 1, 2, 3), qtiles=tuple(range(8)), samples=(0, 1)):
    with ExitStack() as st:
        for _ in phase_b_body(nc, fw, D, st, None, pairs, qtiles, samples):
            pass
        fw.barrier()
        fw.emit()


def phase_b_body(nc, fw, D, st, banks, pairs=(0, 1, 2, 3), qtiles=tuple(range(8)), samples=(0, 1)):
    I = fw.I
    if True:
        def sb(name, shape, dt):
            return st.enter_context(nc.sbuf_tensor("B_" + name, shape, dt)), Buf(name)

        def pst(name, shape, dt):
            return st.enter_context(nc.psum_tensor("B_" + name, shape, dt)), Buf(name)
        identb, bidentb = sb("identb", [128, 128], BF16)
        mincl, bmincl = sb("mincl", [128, 128], BF16)
        cf, bcf = sb("cf", [128, 4, 128], F32)
        kmask, bkmask = sb("kmask", [128, 32], F32)
        lf, blf = sb("lf", [128, 64, 8], F32)
        Fin, bFin = sb("Fin", [128, 64, 8], F32)
        tot, btot = sb("tot", [128, 64, 8], F32)
        scA, bscA = sb("scA", [128, 64, 8], F32)
        scB, bscB = sb("scB", [128, 64, 8], F32)
        offs, boffs = sb("offs", [128, 64, 8], F32)
        Fm, bFm = sb("Fm", [128, 64, 8], F32)
        cfac, bcfac = sb("cfac", [128, 64, 8], F32)
        psets = [(sb(f"kTp{i}", [128, 8192], BF16), sb(f"qTp{i}", [128, 4096], BF16), sb(f"Vx{i}", [128, 64, 2, 65], BF16)) for i in range(2)]
        boffs_ = [sb(f"boff{i}", [128, 64], F32) for i in range(2)]
        bdgs_ = [sb(f"bdg{i}", [128, 16], F32) for i in range(2)]
        osbs_ = [sb(f"osb{i}", [65, 512], F32) for i in range(2)]
        PTs = Rot([sb(f"PT{i}", [128, 512], BF16) for i in range(6)])
        rsbs = Rot([sb(f"rsb{i}", [65, 512], F32) for i in range(2)])
        rrecs = Rot([sb(f"rrec{i}", [65, 512], F32) for i in range(2)])
        fos = Rot([sb(f"fo{i}", [64, 512], BF16) for i in range(2)])
        kc_tm, bkc_tm = sb("kc_tm", [128, 16, 512], BF16)
        if banks is None:
            LA = 2
            pS = Rot([pst(f"pS{i}", [128, 512], F32) for i in range(LA + 1)])
            pOos = [pst(f"pOo{i}", [128, 512], F32) for i in range(2)]
            pOds = [pst(f"pOd{i}", [128, 512], F32) for i in range(2)]
            pB_, bpB_ = pst("pB", [128, 512], F32)
            getB = lambda: (pB_, bpB_)
            getT = lambda: (pB_[:].bitcast(BF16), bpB_)
            getJ = lambda: (pB_, bpB_)
        else:
            pS = Rot(list(banks[0:2]))
            pOos = [banks[2], banks[2]]
            pOds = [banks[3], banks[3]]
            LA = 1
            getB = lambda: pS.next()
            getJ = lambda: pS.next()

            def getT():
                t_, b_ = pS.next()
                return t_[:].bitcast(BF16), b_
        jk, bjk = sb("jk", [128, 512], BF16)
        I("dve", "memset", [], [bjk], ap=jk[:], constant=0.0)
        JN = 0
        JB = 12

        I("sp", "dma_start", [], [bidentb], out=identb[:], in_=D.cmask[:, 0, :])
        I("sp", "dma_start", [], [bmincl], out=mincl[:], in_=D.cmask[:, 2, :])
        I("sp", "dma_start", [], [bcf], out=cf[:], in_=D.cf32[:, 0:4, :])
        I("sp", "dma_start", [], [bkmask], out=kmask[:], in_=D.kmask[:, :])
        for (_, _, (Vx_, bVx_)) in psets:
            I("dve", "memset", [], [bVx_], ap=Vx_[:, :, :, 64:65], constant=1.0)

        def flat(t, nb):
            return t[:, 0:nb, :].rearrange("p b h -> p (b h)")

        def build_F(nb, masked):
            n = nb * 8
            pB, bpB = getB()
            I("pe", "matmul", [bcf, blf], [bpB], out=pB[:, 0:n], lhsT=cf[:, 1, :], rhs=flat(lf, nb), start=True, stop=True)
            I("dve", "tensor_copy", [bpB], [bFin], out=flat(Fin, nb), in_=pB[:, 0:n])
            pB, bpB = getB()
            I("pe", "matmul", [bcf, bFin], [bpB], out=pB[:, 0:n], lhsT=cf[:, 3, :], rhs=flat(Fin, nb), start=True, stop=True)
            I("dve", "tensor_copy", [bpB], [btot], out=flat(tot, nb), in_=pB[:, 0:n])
            src, bsrc = tot, btot
            k = 1
            pp = [(scA, bscA), (scB, bscB)]
            ii = 0
            while k < nb:
                dst, bdst = pp[ii % 2]
                ii += 1
                I("dve", "tensor_copy", [bsrc], [bdst], out=dst[:, 0:k, :], in_=src[:, 0:k, :])
                I("dve", "tensor_tensor", [bsrc], [bdst], out=dst[:, k:nb, :], in0=src[:, k:nb, :], in1=src[:, 0:nb - k, :], op=ALU.add)
                src, bsrc = dst, bdst
                k *= 2
            I("dve", "tensor_tensor", [bsrc, btot], [boffs], out=offs[:, 0:nb, :], in0=src[:, 0:nb, :], in1=tot[:, 0:nb, :], op=ALU.subtract)
            I("dve", "tensor_tensor", [bFin, boffs], [bFm], out=Fm[:, 0:nb, :], in0=Fin[:, 0:nb, :], in1=offs[:, 0:nb, :], op=ALU.add)
            if masked:
                for h in range(8):
                    I("dve", "tensor_tensor", [bFm, bkmask], [bFm], out=Fm[:, 0:32, h], in0=Fm[:, 0:32, h], in1=kmask[:, :], op=ALU.subtract)

        def attend(pset, streams):
            (kTp, bkTp), (qTp, bqTp), (Vx, bVx) = pset
            ctx = []
            for si, (hh, h, qcol, W, qb0, nsub, out_col) in enumerate(streams):
                boff, bboff = boffs_[si]
                bdg, bbdg = bdgs_[si]
                rsb, brsb = rsbs.next()
                rrec, brrec = rrecs.next()
                if qb0 > 0:
                    I("dve", "tensor_scalar", [bFm, boffs], [bboff], out=boff[:, 0:qb0], in0=Fm[:, 0:qb0, h], scalar1=offs[:, qb0, h:h + 1], scalar2=-1.0,
                      op0=ALU.subtract, op1=ALU.mult)
                for i in range(nsub):
                    I("dve", "tensor_scalar", [bFm, boffs], [bbdg], out=bdg[:, i * 4: i * 4 + i + 1], in0=Fm[:, qb0: qb0 + i + 1, h], scalar1=offs[:, qb0 + i, h:h + 1],
                      scalar2=-1.0, op0=ALU.subtract, op1=ALU.mult)
                if nsub > 1:
                    I("dve", "tensor_scalar", [boffs], [bcfac], out=cfac[:, qb0:qb0 + nsub, h], in0=offs[:, qb0:qb0 + nsub, h], scalar1=offs[:, qb0, h:h + 1], scalar2=None,
                      op0=ALU.subtract)
                    I("act", "activation", [bcfac], [bcfac], out=cfac[:, qb0:qb0 + nsub, h], in_=cfac[:, qb0:qb0 + nsub, h], func=AF.Exp)
                ctx.append((boff, bboff, bdg, bbdg, rsb, brsb, rrec, brrec))
            for _ in range(JB):
                pJ, bpJ = getJ()
                I("pe", "matmul", [bidentb, bjk], [bpJ], out=pJ[:, 0:512], lhsT=identb[:], rhs=jk[:, 0:512], start=True, stop=True)
            yield
            lists = []
            for si, (hh, h, qcol, W, qb0, nsub, out_col) in enumerate(streams):
                lists.append([(si, "o", kb, 0, 0) for kb in range(qb0)] + [(si, "d", qb0 + j, i, j) for i in range(nsub) for j in range(i + 1)])
            work = []
            for t_ in range(max(len(l_) for l_ in lists)):
                for l_ in lists:
                    if t_ < len(l_):
                        work.append(l_[t_])
            inflight = {}
            for idx in range(len(work) + LA):
                if idx < len(work):
                    si, kind, kb, i, j = work[idx]
                    hh, h, qcol, W, qb0, nsub, out_col = streams[si]
                    ps_ = slice(hh * 64, (hh + 1) * 64)
                    Wd = min(W, 128)
                    p, bp = pS.next()
                    if kind == "o":
                        I("pe", "matmul", [bkTp, bqTp], [bp], out=p[:, 0:W], lhsT=kTp[ps_, kb * 128:(kb + 1) * 128], rhs=qTp[ps_, qcol:qcol + W], start=True, stop=True)
                    else:
                        I("pe", "matmul", [bkTp, bqTp], [bp], out=p[:, 0:Wd], lhsT=kTp[ps_, kb * 128:(kb + 1) * 128], rhs=qTp[ps_, qcol + i * 128: qcol + i * 128 + Wd],
                          start=True, stop=(j != i))
                        if j == i:
                            I("pe", "matmul", [bidentb, bmincl], [bp], out=p[:, 0:Wd], lhsT=identb[:], rhs=mincl[:, 0:Wd], start=False, stop=True)
                    inflight[idx] = (p, bp)
                k2 = idx - LA
                if k2 >= 0:
                    si, kind, kb, i, j = work[k2]
                    hh, h, qcol, W, qb0, nsub, out_col = streams[si]
                    boff, bboff, bdg, bbdg = ctx[si][0:4]
                    pOo, bpOo = pOos[si]
                    pOd, bpOd = pOds[si]
                    Wd = min(W, 128)
                    p, bp = inflight.pop(k2)
                    pt, bpt = PTs.next()
                    if kind == "o":
                        I("act", "activation", [bp, bboff], [bpt], out=pt[:, 0:W], in_=p[:, 0:W], func=AF.Exp, bias=boff[:, kb:kb + 1])
                        I("pe", "matmul", [bVx, bpt], [bpOo], out=pOo[0:65, 0:W], lhsT=Vx[:, kb, hh, :], rhs=pt[:, 0:W], start=(kb == 0), stop=(kb == qb0 - 1))
                    else:
                        I("act", "activation", [bp, bbdg], [bpt], out=pt[:, 0:Wd], in_=p[:, 0:Wd], func=AF.Exp, bias=bdg[:, i * 4 + j: i * 4 + j + 1])
                        I("pe", "matmul", [bVx, bpt], [bpOd], out=pOd[0:65, i * 128: i * 128 + Wd], lhsT=Vx[:, kb, hh, :], rhs=pt[:, 0:Wd], start=(j == 0), stop=(j == i))
                yield
            for si, (hh, h, qcol, W, qb0, nsub, out_col) in enumerate(streams):
                boff, bboff, bdg, bbdg, rsb, brsb, rrec, brrec = ctx[si]
                pOo, bpOo = pOos[si]
                pOd, bpOd = pOds[si]
                osb, bosb = osbs_[si]
                Wd = min(W, 128)
                if qb0 > 0:
                    I("act", "activation", [bpOo], [bosb], out=osb[:, 0:W], in_=pOo[0:65, 0:W], func=AF.Copy)
                    for i in range(nsub):
                        sl = slice(i * 128, i * 128 + Wd)
                        if nsub > 1:
                            I("dve", "scalar_tensor_tensor", [bosb, bcfac, bpOd], [brsb], out=rsb[:, sl], in0=osb[:, sl], scalar=cfac[0:65, qb0 + i, h:h + 1], in1=pOd[0:65, sl],
                              op0=ALU.mult, op1=ALU.add)
                        else:
                            I("dve", "tensor_tensor", [bosb, bpOd], [brsb], out=rsb[:, sl], in0=osb[:, sl], in1=pOd[0:65, sl], op=ALU.add)
                else:
                    I("dve", "tensor_copy", [bpOd], [brsb], out=rsb[:, 0:W], in_=pOd[0:65, 0:W])
                I("dve", "reciprocal", [brsb], [brrec], out=rrec[64:65, 0:W], in_=rsb[64:65, 0:W])
                pB, bpB = getB()
                I("pe", "matmul", [bcf, brrec], [bpB], out=pB[0:64, 0:W], lhsT=cf[64:65, 2, 0:64], rhs=rrec[64:65, 0:W], start=True, stop=True)
                fo, bfo = fos.next()
                I("dve", "tensor_tensor", [brsb, bpB], [bfo], out=fo[:, 0:W], in0=rsb[0:64, 0:W], in1=pB[0:64, 0:W], op=ALU.mult)
                I("pool", "dma_start", [bfo], [], out=D.foT[h * 64:(h + 1) * 64, out_col:out_col + W], in_=fo[:, 0:W])
            yield

        if len(qtiles) > 0:
            I("sp", "dma_start", [], [blf], out=lf[:], in_=D.lfs[0:8192, :].rearrange("(b p) h -> p b h", p=128))
            build_F(64, True)
            def load_pair(pr, pset):
                (kTp, bkTp), (qTp, bqTp), (Vx, bVx) = pset
                I("sp", "dma_start", [], [bkTp], out=kTp[:, :], in_=D.kT[pr * 128:(pr + 1) * 128, 0:8192])
                I("sp", "dma_start", [], [bqTp], out=qTp[:, :], in_=D.qT[pr * 128:(pr + 1) * 128, 0:4096])
                for hh in range(2):
                    I("sp", "dma_start", [], [bVx], out=Vx[:, :, hh, 0:64],
                      in_=D.V[0:8192, (2 * pr + hh) * 64:(2 * pr + hh + 1) * 64].rearrange("(b p) d -> p b d", p=128))
            pl = list(pairs)
            load_pair(pl[0], psets[0])
            for n_, pr in enumerate(pl):
                if n_ + 1 < len(pl):
                    load_pair(pl[n_ + 1], psets[(n_ + 1) % 2])
                for qt in qtiles:
                    yield from attend(psets[n_ % 2], [(hh, 2 * pr + hh, qt * 512, 512, 32 + 4 * qt, 4, qt * 512) for hh in range(2)])
        for q in samples:
            r0 = NP_ + NO_ + q * 128
            I("sp", "dma_start", [], [blf], out=lf[:, 0:16, :], in_=D.clf[q].rearrange("(b p) h -> p b h", p=128))
            I("sp", "dma_start", [], [blf], out=lf[:, 16, :], in_=D.lfs[r0:r0 + 128, :])
            build_F(17, False)
            I("pool", "dma_start", [], [bkc_tm], out=kc_tm[:], in_=D.ck[q].rearrange("(b p) c -> p b c", p=128))
            for n_, pr in enumerate(pairs):
                pset = psets[n_ % 2]
                (kTp, bkTp), (qTp, bqTp), (Vx, bVx) = pset
                for b4 in range(2):
                    pT, bpT = getT()
                    for bb in range(8):
                        blk = b4 * 8 + bb
                        I("pe", "transpose", [bkc_tm, bidentb], [bpT], out=pT[:, bb * 128:(bb + 1) * 128], in_=kc_tm[:, blk, pr * 128:(pr + 1) * 128], identity=identb[:])
                    I("dve", "tensor_copy", [bpT], [bkTp], out=kTp[:, b4 * 1024:(b4 + 1) * 1024], in_=pT[:, :])
                I("sp", "dma_start", [], [bkTp], out=kTp[:, 2048:2176], in_=D.kT[pr * 128:(pr + 1) * 128, r0:r0 + 128])
                I("sp", "dma_start", [], [bqTp], out=qTp[:, 0:128], in_=D.qT[pr * 128:(pr + 1) * 128, NO_ + q * 128: NO_ + (q + 1) * 128])
                for hh in range(2):
                    h = 2 * pr + hh
                    I("pool", "dma_start", [], [bVx], out=Vx[:, 0:16, hh, 0:64], in_=D.cv[q][:, h * 64:(h + 1) * 64].rearrange("(b p) d -> p b d", p=128))
                    I("sp", "dma_start", [], [bVx], out=Vx[:, 16, hh, 0:64], in_=D.V[r0:r0 + 128, h * 64:(h + 1) * 64])
                yield from attend(pset, [(hh, 2 * pr + hh, 0, 128, 16, 1, NO_ + q * 128) for hh in range(2)])


def phase_c(nc, fw, D, blocks=None, samples=(0, 1), o_all=False):
    with ExitStack() as st:
        for _ in phase_c_body(nc, fw, D, st, None, blocks, samples, o_all):
            pass
        fw.barrier()
        fw.emit()


def phase_c_body(nc, fw, D, st, banks, blocks=None, samples=(0, 1), o_all=False):
    I = fw.I
    if True:
        def sb(name, shape, dt):
            return st.enter_context(nc.sbuf_tensor("C_" + name, shape, dt)), Buf(name)

        def pst(name, shape, dt):
            return st.enter_context(nc.psum_tensor("C_" + name, shape, dt)), Buf(name)
        identb, bidentb = sb("identb", [128, 128], BF16)
        identf, bidentf = sb("identf", [128, 128], F32)
        mincl, bmincl = sb("mincl", [128, 128], BF16)
        mstr, bmstr = sb("mstr", [128, 128], BF16)
        cf, bcf = sb("cf", [128, 6, 128], F32)
        ggdn, bggdn = sb("ggdn", [128, 8, 64], F32)
        vmask, bvmask = sb("vmask", [128, 1], F32)
        insets = [(sb(f"qkv{i}", [128, 3, 8, 64], BF16), sb(f"kT{i}", [64, 8, 128], BF16), sb(f"qT{i}", [64, 8, 128], BF16),
                   sb(f"g{i}", [128, 8], F32), sb(f"be{i}", [128, 16], F32)) for i in range(2)]
        gc, bgc = sb("gc", [128, 8], F32)
        ngc, bngc = sb("ngc", [128, 8], F32)
        vec2, bvec2 = sb("vec2", [128, 8], F32)
        eg, beg = sb("eg", [128, 8], F32)
        bee, bbee = sb("bee", [128, 8], F32)
        glb, bglb = sb("glb", [128, 8], F32)
        gl12, bgl12 = sb("gl12", [128, 16], F32)
        egl, begl = sb("egl", [128, 2, 8], F32)
        ekl, bekl = sb("ekl", [128, 8], F32)
        dg1, bdg1 = sb("dg1", [128, 8, 128], F32)
        dg2, bdg2 = sb("dg2", [128, 8, 128], F32)
        Dincl, bDincl = sb("Dincl", [128, 8, 128], BF16)
        AbT, bAbT = sb("AbT", [128, 8, 128], BF16)
        Ns = [sb(f"N{i}", [128, 8, 128], BF16) for i in range(2)]
        Ls = [sb(f"L{i}", [128, 8, 128], BF16) for i in range(2)]
        Ws = [sb(f"W{i}", [128, 8, 128], BF16) for i in range(2)]
        AqkT, bAqkT = sb("AqkT", [128, 8, 128], BF16)
        rhs2, brhs2 = sb("rhs2", [128, 8, 128], BF16)
        khat, bkhat = sb("khat", [128, 8, 64], BF16)
        qtl, bqtl = sb("qtl", [128, 8, 64], BF16)
        qtT, bqtT = sb("qtT", [64, 8, 128], BF16)
        U, bU = sb("U", [128, 8, 64], F32)
        WkT, bWkT = sb("WkT", [64, 8, 128], BF16)
        vnew, bvnew = sb("vnew", [128, 8, 64], BF16)
        S, bS = sb("S", [64, 8, 64], F32)
        Sb, bSb = sb("Sb", [64, 8, 64], BF16)
        Sb2, bSb2 = sb("Sb2", [64, 8, 64], BF16)
        osb, bosb = sb("osb", [128, 8, 64], F32)
        osq, bosq = sb("osq", [128, 8, 64], F32)
        oss, boss = sb("oss", [128, 8], F32)
        ggt, bggt = sb("ggt", [128, 8, 64], F32)
        ob, bob = sb("ob", [128, 8, 64], BF16)
        if banks is None:
            pA = [pst(f"pA{i}", [128, 512], F32) for i in range(2)]
            pL = [pst(f"pL{i}", [128, 512], F32) for i in range(2)]
            pW = [pst(f"pW{i}", [128, 512], F32) for i in range(2)]
            pX, bpX = pst("pX", [128, 512], F32)
            pTt, bpTt = pst("pTt", [128, 1024], BF16)
        else:
            pA = [banks[0], banks[0]]
            pL = [banks[1], banks[1]]
            pW = [banks[2], banks[2]]
            pX, bpX = banks[3]
            pTt, bpTt = banks[0][0][:].bitcast(BF16), banks[0][1]

        I("sp", "dma_start", [], [bidentb], out=identb[:], in_=D.cmask[:, 0, :])
        I("sp", "dma_start", [], [bmincl], out=mincl[:], in_=D.cmask[:, 6, :])
        I("sp", "dma_start", [], [bmstr], out=mstr[:], in_=D.cmask[:, 7, :])
        I("sp", "dma_start", [], [bcf], out=cf[:], in_=D.cf32[:, :, :])
        I("sp", "dma_start", [], [bidentf], out=identf[:], in_=D.cf32[:, 0, :])
        for h in range(8):
            I("sp", "dma_start", [], [bggdn], out=ggdn[:, h, :], in_=D.g_gdn[0:1, :].broadcast_to([128, 64]))
        I("sp", "dma_start", [], [bvmask], out=vmask[:], in_=D.vmask[:, :])
        jk, bjk = sb("jk", [128, 512], BF16)
        I("dve", "memset", [], [bjk], ap=jk[:], constant=0.0)
        JC = 0
        I("dve", "tensor_scalar_mul", [bggdn], [bggdn], out=ggdn[:].rearrange("p h d -> p (h d)"), in0=ggdn[:].rearrange("p h d -> p (h d)"), scalar1=0.5)

        def load(r0, si):
            (qkv, bqkv), (kT, bkT), (qT, bqT), (g_, bg_), (be, bbe) = insets[si]
            I("sp", "dma_start", [], [bqkv], out=qkv[:].rearrange("p a h d -> p (a h d)"), in_=D.gqkv[r0:r0 + 128, :])
            I("sp", "dma_start", [], [bg_], out=g_[:], in_=D.g[r0:r0 + 128, :])
            I("sp", "dma_start", [], [bbe], out=be[:], in_=D.beta[r0:r0 + 128, :])

        def process(si, qrow, want_o, sample):
            (qkv, bqkv), (kT, bkT), (qT, bqT), (g_, bg_), (be, bbe) = insets[si]
            if sample:
                I("dve", "tensor_scalar_mul", [bg_, bvmask], [bg_], out=g_[:], in0=g_[:], scalar1=vmask[:, 0:1])
                I("dve", "tensor_scalar_mul", [bqkv, bvmask], [bqkv], out=qkv[:].rearrange("p a h d -> p (a h d)"), in0=qkv[:].rearrange("p a h d -> p (a h d)"),
                  scalar1=vmask[:, 0:1])
            for h in range(8):
                I("pe", "transpose", [bqkv, bidentb], [bpTt], out=pTt[0:64, h * 128:(h + 1) * 128], in_=qkv[:, 1, h, :], identity=identb[:])
            I("act", "activation", [bpTt], [bkT], out=kT[:].rearrange("p h i -> p (h i)"), in_=pTt[0:64, :], func=AF.Copy)
            if want_o:
                for h in range(8):
                    I("pe", "transpose", [bqkv, bidentb], [bpTt], out=pTt[0:64, h * 128:(h + 1) * 128], in_=qkv[:, 0, h, :], identity=identb[:])
                I("dve", "tensor_copy", [bpTt], [bqT], out=qT[:].rearrange("p h i -> p (h i)"), in_=pTt[0:64, :])
            yield
            I("pe", "matmul", [bcf, bg_], [bpX], out=pX[:, 0:8], lhsT=cf[:, 4, :], rhs=g_[:], start=True, stop=True)
            I("dve", "tensor_copy", [bpX], [bgc], out=gc[:], in_=pX[:, 0:8])
            I("dve", "tensor_scalar_mul", [bgc], [bngc], out=ngc[:], in0=gc[:], scalar1=-1.0)
            I("pe", "matmul", [bcf, bgc], [bpX], out=pX[:, 8:16], lhsT=cf[:, 5, :], rhs=gc[:], start=True, stop=True)
            I("pe", "matmul", [bcf, bgc], [bpX], out=pX[:, 16:24], lhsT=cf[:, 3, :], rhs=gc[:], start=True, stop=True)
            I("dve", "tensor_copy", [bpX], [bgl12], out=gl12[:], in_=pX[:, 8:24])
            I("dve", "tensor_copy", [bgl12], [bglb], out=glb[0:64, :], in_=gl12[0:64, 0:8])
            I("dve", "tensor_copy", [bgl12], [bglb], out=glb[64:128, :], in_=gl12[64:128, 8:16])
            I("dve", "tensor_tensor", [bbe, bgc], [bvec2], out=vec2[:], in0=be[:, 8:16], in1=gc[:], op=ALU.add)
            I("act", "activation", [bgc], [beg], out=eg[:], in_=gc[:], func=AF.Exp)
            I("act", "activation", [bvec2], [bbee], out=bee[:], in_=vec2[:], func=AF.Exp)
            I("act", "activation", [bgl12], [begl], out=egl[:].rearrange("p a h -> p (a h)"), in_=gl12[:], func=AF.Exp)
            I("dve", "tensor_tensor", [bglb, bgc], [bekl], out=ekl[:], in0=glb[:], in1=gc[:], op=ALU.subtract)
            I("act", "activation", [bekl], [bekl], out=ekl[:], in_=ekl[:], func=AF.Exp)
            if sample:
                I("dve", "tensor_scalar_mul", [bbe, bvmask], [bbe], out=be[:, 0:8], in0=be[:, 0:8], scalar1=vmask[:, 0:1])
                I("dve", "tensor_scalar_mul", [bbee, bvmask], [bbee], out=bee[:], in0=bee[:], scalar1=vmask[:, 0:1])
            yield
            for h in range(8):
                if want_o:
                    I("dve", "tensor_scalar_mul", [bidentf, bgc], [bdg1], out=dg1[:, h, :], in0=identf[:], scalar1=gc[:, h:h + 1])
                I("dve", "tensor_scalar_mul", [bidentf, bvec2], [bdg2], out=dg2[:, h, :], in0=identf[:], scalar1=vec2[:, h:h + 1])
            N0, bN0 = Ns[0]
            L0, bL0 = Ls[0]
            yield
            for _ in range(JC):
                I("pe", "matmul", [bidentb, bjk], [bpX], out=pX[:, 0:512], lhsT=identb[:], rhs=jk[:, 0:512], start=True, stop=True)
            for hb in range(2):
                yield
                pK, bpK = pA[hb]
                pQ, bpQ = pL[hb]
                p1, bp1 = pW[hb]
                for hq in range(4):
                    h = hb * 4 + hq
                    sl = slice(hq * 128, (hq + 1) * 128)
                    I("pe", "matmul", [bkT], [bpK], out=pK[:, sl], lhsT=kT[:, h, :], rhs=kT[:, h, :], start=True, stop=True)
                    if want_o:
                        I("pe", "matmul", [bkT, bqT], [bpQ], out=pQ[:, sl], lhsT=kT[:, h, :], rhs=qT[:, h, :], start=True, stop=True)
                        I("pe", "matmul", [bcf, bdg1], [bp1], out=p1[:, sl], lhsT=cf[:, 2, :], rhs=dg1[:, h, :], start=True, stop=False)
                        I("pe", "matmul", [bidentb, bmincl], [bp1], out=p1[:, sl], lhsT=identb[:], rhs=mincl[:], start=False, stop=True)
                        I("act", "activation", [bp1, bngc], [bDincl], out=Dincl[:, h, :], in_=p1[:, sl], func=AF.Exp, bias=ngc[:, h:h + 1])
                for hq in range(4):
                    h = hb * 4 + hq
                    sl = slice(hq * 128, (hq + 1) * 128)
                    I("pe", "matmul", [bcf, bdg2], [bpX], out=pX[:, sl], lhsT=cf[:, 2, :], rhs=dg2[:, h, :], start=True, stop=False)
                    I("pe", "matmul", [bidentb, bmstr], [bpX], out=pX[:, sl], lhsT=identb[:], rhs=mstr[:], start=False, stop=True)
                    I("act", "activation", [bpX, bngc], [bAbT], out=AbT[:, h, :], in_=pX[:, sl], func=AF.Exp, bias=ngc[:, h:h + 1])
                hs = slice(hb * 4, hb * 4 + 4)
                fl = lambda t: t[:, hs, :].rearrange("p h i -> p (h i)")
                I("dve", "scalar_tensor_tensor", [bpK, bAbT], [bN0], out=fl(N0), in0=pK[:, :], scalar=-1.0, in1=fl(AbT), op0=ALU.mult, op1=ALU.mult)
                if want_o:
                    I("dve", "tensor_tensor", [bpQ, bDincl], [bAqkT], out=fl(AqkT), in0=pQ[:, :], in1=fl(Dincl), op=ALU.mult)
            yield
            for h in range(8):
                I("pe", "transpose", [bN0, bidentb], [bpTt], out=pTt[:, h * 128:(h + 1) * 128], in_=N0[:, h, :], identity=identb[:])
            I("act", "activation", [bpTt], [bL0], out=L0[:].rearrange("p h i -> p (h i)"), in_=pTt[:, :], func=AF.Copy)
            W0, bW0 = Ws[0]
            for h in range(8):
                I("pool", "tensor_tensor", [bN0, bidentb], [bW0], out=W0[:, h, :], in0=N0[:, h, :], in1=identb[:], op=ALU.add)
            cur = 0
            for m in range(1, 6):
                yield
                Nc, bNc = Ns[cur]
                Lc, bLc = Ls[cur]
                Wc, bWc = Ws[cur]
                Nn, bNn = Ns[1 - cur]
                Ln, bLn = Ls[1 - cur]
                Wn, bWn = Ws[1 - cur]
                def fl(t, hb):
                    return t[:, hb * 4:hb * 4 + 4, :].rearrange("p h i -> p (h i)")
                for hb in range(2):
                    pl_, bpl_ = pL[hb]
                    for hq in range(4):
                        h = hb * 4 + hq
                        I("pe", "matmul", [bNc, bLc], [bpl_], out=pl_[:, hq * 128:(hq + 1) * 128], lhsT=Nc[:, h, :], rhs=Lc[:, h, :], start=True, stop=True)
                    I("act", "activation", [bpl_], [bLn], out=fl(Ln, hb), in_=pl_[:, :], func=AF.Copy)
                if m < 5:
                    for hb in range(2):
                        pa_, bpa_ = pA[hb]
                        for hq in range(4):
                            h = hb * 4 + hq
                            I("pe", "matmul", [bNc, bLc], [bpa_], out=pa_[:, hq * 128:(hq + 1) * 128], lhsT=Lc[:, h, :], rhs=Nc[:, h, :], start=True, stop=True)
                        I("dve", "tensor_copy", [bpa_], [bNn], out=fl(Nn, hb), in_=pa_[:, :])
                for hb in range(2):
                    pw_, bpw_ = pW[hb]
                    for hq in range(4):
                        h = hb * 4 + hq
                        I("pe", "matmul", [bLn, bWc], [bpw_], out=pw_[:, hq * 128:(hq + 1) * 128], lhsT=Ln[:, h, :], rhs=Wc[:, h, :], start=True, stop=True)
                    I("dve", "tensor_tensor", [bpw_, bWc], [bWn], out=fl(Wn, hb), in0=pw_[:, :], in1=fl(Wc, hb), op=ALU.add)
                cur = 1 - cur
            Wf, bWf = Ws[cur]
            yield
            bc = lambda t: t[:, :].unsqueeze(2).broadcast_to([128, 8, 64])
            I("dve", "tensor_tensor", [bqkv, bbe], [brhs2], out=rhs2[:, :, 0:64], in0=qkv[:, 2, :, :], in1=be[:, 0:8].unsqueeze(2).broadcast_to([128, 8, 64]), op=ALU.mult)
            I("dve", "tensor_tensor", [bqkv, bbee], [brhs2], out=rhs2[:, :, 64:128], in0=qkv[:, 1, :, :], in1=bc(bee), op=ALU.mult)
            I("pool", "tensor_tensor", [bqkv, bekl], [bkhat], out=khat[:], in0=qkv[:, 1, :, :], in1=bc(ekl), op=ALU.mult)
            if want_o:
                I("pool", "tensor_tensor", [bqkv, beg], [bqtl], out=qtl[:], in0=qkv[:, 0, :, :], in1=bc(eg), op=ALU.mult)
            for hb in range(2):
                yield
                pu, bpu = pA[hb]
                for hq in range(4):
                    h = hb * 4 + hq
                    I("pe", "matmul", [bWf, brhs2], [bpu], out=pu[:, hq * 128:(hq + 1) * 128], lhsT=Wf[:, h, :], rhs=rhs2[:, h, :], start=True, stop=True)
                I("dve", "tensor_copy", [bpu], [bU], out=U[:, hb * 4:hb * 4 + 4, :], in_=pu[:, :].rearrange("p (h c) -> p h c", c=128)[:, :, 0:64])
                pk_, bpk_ = pL[hb]
                for hq in range(4):
                    h = hb * 4 + hq
                    I("pe", "matmul", [brhs2, bWf], [bpk_], out=pk_[0:64, hq * 128:(hq + 1) * 128], lhsT=rhs2[:, h, 64:128], rhs=Wf[:, h, :], start=True, stop=True)
                I("act", "activation", [bpk_], [bWkT], out=WkT[:, hb * 4:hb * 4 + 4, :].rearrange("p h i -> p (h i)"), in_=pk_[0:64, :], func=AF.Copy)
            if want_o:
                for h in range(8):
                    I("pe", "transpose", [bqtl, bidentb], [bpTt], out=pTt[0:64, h * 128:(h + 1) * 128], in_=qtl[:, h, :], identity=identb[:])
                I("act", "activation", [bpTt], [bqtT], out=qtT[:].rearrange("p h i -> p (h i)"), in_=pTt[0:64, :], func=AF.Copy)
            fo_ = lambda t: t[:].rearrange("p h d -> p (h d)")
            halves = [(slice(0, 64), Sb, bSb, 0), (slice(64, 128), Sb2, bSb2, 1)]
            for (rs_, Sc, bSc, ci) in halves:
                yield
                for h in range(8):
                    I("pe", "matmul", [bWkT, bSc], [bpX], out=pX[:, h * 64:(h + 1) * 64], lhsT=WkT[:, h, :], rhs=Sc[:, h, :], start=True, stop=True)
                I("dve", "tensor_tensor", [bU, bpX], [bvnew], out=fo_(vnew)[rs_, :], in0=fo_(U)[rs_, :], in1=pX[rs_, :], op=ALU.subtract)
                ps_, bps_ = pW[1]
                for h in range(8):
                    I("pe", "matmul", [bkhat, bvnew], [bps_], out=ps_[0:64, h * 64:(h + 1) * 64], lhsT=khat[rs_, h, :], rhs=vnew[rs_, h, :], start=True, stop=True)
                I("dve", "tensor_tensor", [bS, begl], [bS], out=S[:], in0=S[:], in1=egl[0:64, ci, :].unsqueeze(2).broadcast_to([64, 8, 64]), op=ALU.mult)
                I("dve", "tensor_tensor", [bS, bps_], [bS], out=fo_(S), in0=fo_(S), in1=ps_[0:64, :], op=ALU.add)
                Sn, bSn = (Sb2, bSb2) if ci == 0 else (Sb, bSb)
                if want_o or ci == 0:
                    pass
                I("act", "activation", [bS], [bSn], out=fo_(Sn), in_=fo_(S), func=AF.Copy)
                if want_o:
                    po, bpo = pW[0]
                    for h in range(8):
                        I("pe", "matmul", [bqtT, bSc], [bpo], out=po[:, h * 64:(h + 1) * 64], lhsT=qtT[:, h, :], rhs=Sc[:, h, :], start=True, stop=False)
                        I("pe", "matmul", [bAqkT, bvnew], [bpo], out=po[:, h * 64:(h + 1) * 64], lhsT=AqkT[:, h, :], rhs=vnew[:, h, :], start=False, stop=True)
                    I("act", "activation", [bpo], [bosb], out=fo_(osb)[rs_, :], in_=po[rs_, :], func=AF.Copy)
            yield
            if want_o:
                I("pool", "tensor_tensor", [bosb], [bosq], out=fo_(osq), in0=fo_(osb), in1=fo_(osb), op=ALU.mult)
                I("dve", "tensor_reduce", [bosq], [boss], out=oss[:], in_=osq[:], axis=AX.X, op=ALU.add)
                I("act", "activation", [boss], [boss], out=oss[:], in_=oss[:], func=AF.Sqrt, bias=EPS, scale=1.0 / 64)
                I("dve", "reciprocal", [boss], [boss], out=oss[:], in_=oss[:])
                I("sp", "dma_start", [], [bggt], out=fo_(ggt), in_=D.gg[qrow:qrow + 128, :])
                I("dve", "tensor_tensor", [bosb, boss], [bosb], out=osb[:], in0=osb[:], in1=oss[:, :].unsqueeze(2).broadcast_to([128, 8, 64]), op=ALU.mult)
                I("pool", "tensor_tensor", [bggt, bggdn], [bggt], out=fo_(ggt), in0=fo_(ggt), in1=fo_(ggdn), op=ALU.mult)
                I("dve", "tensor_tensor", [bosb, bggt], [bob], out=fo_(ob), in0=fo_(osb), in1=fo_(ggt), op=ALU.mult)
                I("pool", "dma_start", [bob], [], out=D.go[qrow:qrow + 128, :], in_=fo_(ob))

        blks = list(range(64)) if blocks is None else list(blocks)
        if blks:
            I("dve", "memset", [], [bvnew], ap=vnew[:].rearrange("p h d -> p (h d)"), constant=0.0)
            I("dve", "memset", [], [bS], ap=S[:].rearrange("p h d -> p (h d)"), constant=0.0)
            I("dve", "memset", [], [bSb], ap=Sb[:].rearrange("p h d -> p (h d)"), constant=0.0)
            load(blks[0] * 128, 0)
            for n_, b in enumerate(blks):
                if n_ + 1 < len(blks):
                    load(blks[n_ + 1] * 128, (n_ + 1) % 2)
                want = o_all or b >= 32
                yield from process(n_ % 2, max(b * 128 - NP_, 0), want, False)
            I("sp", "dma_start", [bS], [], out=D.sfin[0].rearrange("h k v -> k h v"), in_=S[:])
        for q in samples:
            load(NP_ + NO_ + q * 128, q % 2)
            I("sp", "dma_start", [], [bS], out=S[:], in_=D.sgdn[q].rearrange("h k v -> k h v"))
            I("act", "activation", [bS], [bSb], out=Sb[:].rearrange("p h d -> p (h d)"), in_=S[:].rearrange("p h d -> p (h d)"), func=AF.Copy)
            yield from process(q % 2, NO_ + q * 128, True, True)
            I("sp", "dma_start", [bS], [], out=D.sfin[1 + q].rearrange("h k v -> k h v"), in_=S[:])


QTILES = [(i * 512, 4) for i in range(8)] + [(NO_, 2)]


def phase_d1(nc, fw, D, tiles=None):
    I = fw.I
    with ExitStack() as st:
        def sb(name, shape, dt):
            return st.enter_context(nc.sbuf_tensor("D1_" + name, shape, dt)), Buf(name)

        def pst(name, shape, dt):
            return st.enter_context(nc.psum_tensor("D1_" + name, shape, dt)), Buf(name)
        identb, bidentb = sb("identb", [128, 128], BF16)
        wpa, bwpa = sb("wpa", [128, 4, 1024], BF16)
        wpb, bwpb = sb("wpb", [128, 4, 1024], BF16)
        wout, bwout = sb("wout", [128, 8, 1024], BF16)
        gpost, bgpost = sb("gpost", [128, 1024], F32)
        foT, bfoT = sb("foT", [128, 4, 512], BF16)
        gob, bgob = sb("gob", [128, 4, 512], BF16)
        goT, bgoT = sb("goT", [128, 4, 512], BF16)
        sgA, bsgA = sb("sgA", [128, 8, 512], BF16)
        sgB, bsgB = sb("sgB", [128, 8, 512], BF16)
        t1s = Rot([sb(f"t1{i}", [128, 512], F32) for i in range(2)])
        t2s = Rot([sb(f"t2{i}", [128, 512], F32) for i in range(2)])
        mT, bmT = sb("mT", [128, 8, 512], BF16)
        xt, bxt = sb("xt", [128, 4, 1024], F32)
        mixes = Rot([sb(f"mix{i}", [128, 1024], F32) for i in range(2)])
        junk, bjunk = sb("junk", [128, 1024], BF16)
        sss = [sb(f"ss{i}", [128, 1], F32) for i in range(4)]
        y1, by1 = sb("y1", [128, 4, 1024], F32)
        pa = Rot([pst(f"pa{i}", [128, 512], F32) for i in range(2)])
        pb = Rot([pst(f"pb{i}", [128, 512], F32) for i in range(2)])
        pm = Rot([pst(f"pm{i}", [128, 512], F32) for i in range(2)])
        ptr, bptr = pst("ptr", [128, 1024], BF16)

        I("sp", "dma_start", [], [bidentb], out=identb[:], in_=D.cmask[:, 0, :])
        I("pool", "dma_start", [], [bwpa], out=wpa[:], in_=D.w_pa.rearrange("(c p) n -> p c n", p=128))
        I("pool", "dma_start", [], [bwpb], out=wpb[:], in_=D.w_pb.rearrange("(c p) n -> p c n", p=128))
        I("pool", "dma_start", [], [bwout], out=wout[:], in_=D.w_out.rearrange("(c p) n -> p c n", p=128))
        I("sp", "dma_start", [], [bgpost], out=gpost[:], in_=D.g_mix_post[0:1, :].broadcast_to([128, 1024]))
        tl = QTILES if tiles is None else [QTILES[i] for i in tiles]
        for (tq, ns) in tl:
            N = ns * 128
            I("sp", "dma_start", [], [bfoT], out=foT[:, :, 0:N], in_=D.foT[:, tq:tq + N].rearrange("(c p) n -> p c n", p=128))
            I("sp", "dma_start", [], [bgob], out=gob[:, 0:ns, :], in_=D.go[tq:tq + N, :].rearrange("(s p) n -> p s n", p=128))
            I("sp", "dma_start", [], [bsgA], out=sgA[:, :, 0:N], in_=D.sgA[:, tq:tq + N].rearrange("(c p) n -> p c n", p=128))
            I("sp", "dma_start", [], [bsgB], out=sgB[:, :, 0:N], in_=D.sgB[:, tq:tq + N].rearrange("(c p) n -> p c n", p=128))
            I("sp", "dma_start", [], [bxt], out=xt[:, 0:ns, :], in_=D.xall[NP_ + tq: NP_ + tq + N, :].rearrange("(s p) m -> p s m", p=128))
            for half in range(2):
                for cc in range(2):
                    c = half * 2 + cc
                    for s in range(ns):
                        I("pe", "transpose", [bgob, bidentb], [bptr], out=ptr[:, cc * 512 + s * 128: cc * 512 + (s + 1) * 128], in_=gob[:, s, c * 128:(c + 1) * 128],
                          identity=identb[:])
                for cc in range(2):
                    I("act", "activation", [bptr], [bgoT], out=goT[:, half * 2 + cc, 0:N], in_=ptr[:, cc * 512: cc * 512 + N], func=AF.Copy)
            for oc in range(8):
                p1, bp1 = pa.next()
                p2, bp2 = pb.next()
                for c in range(4):
                    I("pe", "matmul", [bwpa, bfoT], [bp1], out=p1[:, 0:N], lhsT=wpa[:, c, oc * 128:(oc + 1) * 128], rhs=foT[:, c, 0:N], start=(c == 0), stop=(c == 3))
                for c in range(4):
                    I("pe", "matmul", [bwpb, bgoT], [bp2], out=p2[:, 0:N], lhsT=wpb[:, c, oc * 128:(oc + 1) * 128], rhs=goT[:, c, 0:N], start=(c == 0), stop=(c == 3))
                t1, bt1 = t1s.next()
                t2, bt2 = t2s.next()
                I("dve", "tensor_tensor", [bp1, bsgA], [bt1], out=t1[:, 0:N], in0=p1[:, 0:N], in1=sgA[:, oc, 0:N], op=ALU.mult)
                I("dve", "tensor_tensor", [bp2, bsgB], [bt2], out=t2[:, 0:N], in0=p2[:, 0:N], in1=sgB[:, oc, 0:N], op=ALU.mult)
                I("pool", "tensor_tensor", [bt1, bt2], [bmT], out=mT[:, oc, 0:N], in0=t1[:, 0:N], in1=t2[:, 0:N], op=ALU.add)
            for s in range(ns):
                mix, bmix = mixes.next()
                ss, bss = sss[s]
                for cg in range(2):
                    p, bp = pm.next()
                    for kc in range(8):
                        I("pe", "matmul", [bmT, bwout], [bp], out=p[:, :], lhsT=mT[:, kc, s * 128:(s + 1) * 128], rhs=wout[:, kc, cg * 512:(cg + 1) * 512], start=(kc == 0), stop=(kc == 7))
                    if cg == 0:
                        I("act", "activation", [bp], [bmix], out=mix[:, 0:512], in_=p[:, :], func=AF.Copy)
                    else:
                        I("dve", "tensor_copy", [bp], [bmix], out=mix[:, 512:1024], in_=p[:, :])
                I("act", "activation", [bmix], [bjunk, bss], out=junk[:], in_=mix[:], func=AF.Square, accum_out=ss[:, 0:1])
                I("act", "activation", [bss], [bss], out=ss[:], in_=ss[:], func=AF.Sqrt, bias=EPS, scale=1.0 / 1024)
                I("dve", "reciprocal", [bss], [bss], out=ss[:], in_=ss[:])
                I("dve", "scalar_tensor_tensor", [bmix, bss, bgpost], [bmix], out=mix[:], in0=mix[:], scalar=ss[:, 0:1], in1=gpost[:], op0=ALU.mult, op1=ALU.mult)
                I("dve", "tensor_tensor", [bmix, bxt], [by1], out=y1[:, s, :], in0=mix[:], in1=xt[:, s, :], op=ALU.add)
            I("sp", "dma_start", [by1], [], out=D.y1[tq:tq + N, :].rearrange("(s p) m -> p s m", p=128), in_=y1[:, 0:ns, :])
        fw.barrier()
        fw.emit()


def phase_d2(nc, fw, D, tiles=None):
    I = fw.I
    with ExitStack() as st:
        def sb(name, shape, dt):
            return st.enter_context(nc.sbuf_tensor("D2_" + name, shape, dt)), Buf(name)

        def pst(name, shape, dt):
            return st.enter_context(nc.psum_tensor("D2_" + name, shape, dt)), Buf(name)
        identb, bidentb = sb("identb", [128, 128], BF16)
        wup, bwup = sb("wup", [128, 8, 4096], BF16)
        wdn, bwdn = sb("wdn", [128, 32, 1024], BF16)
        gpre, bgpre = sb("gpre", [128, 1024], F32)
        gpost, bgpost = sb("gpost", [128, 1024], F32)
        y1, by1 = sb("y1", [128, 4, 1024], F32)
        hbs = Rot([sb(f"hb{i}", [128, 1024], BF16) for i in range(2)])
        hT, bhT = sb("hT", [128, 8, 512], BF16)
        uT, buT = sb("uT", [128, 32, 512], BF16)
        rl = Rot([sb(f"rl{i}", [128, 512], F32) for i in range(2)])
        dsb, bdsb = sb("dsb", [128, 1024], F32)
        junk, bjunk = sb("junk", [128, 1024], BF16)
        ss, bss = sb("ss", [128, 4], F32)
        pu = Rot([pst(f"pu{i}", [128, 512], F32) for i in range(4)])
        pd = Rot([pst(f"pd{i}", [128, 512], F32) for i in range(2)])
        ptr = Rot([pst(f"ptr{i}", [128, 1024], BF16) for i in range(2)])

        I("sp", "dma_start", [], [bidentb], out=identb[:], in_=D.cmask[:, 0, :])
        for kc in range(8):
            I("pool", "dma_start", [], [bwup], out=wup[:, kc, :], in_=D.w_up[kc * 128:(kc + 1) * 128, :])
        for f4 in range(8):
            I("pool", "dma_start", [], [bwdn], out=wdn[:, f4 * 4:(f4 + 1) * 4, :], in_=D.w_down[f4 * 512:(f4 + 1) * 512, :].rearrange("(c p) n -> p c n", p=128))
        I("sp", "dma_start", [], [bgpre], out=gpre[:], in_=D.g_mlp_pre[0:1, :].broadcast_to([128, 1024]))
        I("sp", "dma_start", [], [bgpost], out=gpost[:], in_=D.g_mlp_post[0:1, :].broadcast_to([128, 1024]))
        tl = QTILES if tiles is None else [QTILES[i] for i in tiles]
        for (tq, ns) in tl:
            N = ns * 128
            I("sp", "dma_start", [], [by1], out=y1[:, 0:ns, :], in_=D.y1[tq:tq + N, :].rearrange("(s p) m -> p s m", p=128))
            for s in range(ns):
                I("act", "activation", [by1], [bjunk, bss], out=junk[:], in_=y1[:, s, :], func=AF.Square, accum_out=ss[:, s:s + 1])
            I("act", "activation", [bss], [bss], out=ss[:, 0:ns], in_=ss[:, 0:ns], func=AF.Sqrt, bias=EPS, scale=1.0 / 1024)
            I("dve", "reciprocal", [bss], [bss], out=ss[:, 0:ns], in_=ss[:, 0:ns])
            for s in range(ns):
                hb, bhb = hbs.next()
                I("dve", "scalar_tensor_tensor", [by1, bss, bgpre], [bhb], out=hb[:], in0=y1[:, s, :], scalar=ss[:, s:s + 1], in1=gpre[:], op0=ALU.mult, op1=ALU.mult)
                pt, bpt = ptr.next()
                for kc in range(8):
                    I("pe", "transpose", [bhb, bidentb], [bpt], out=pt[:, kc * 128:(kc + 1) * 128], in_=hb[:, kc * 128:(kc + 1) * 128], identity=identb[:])
                if s % 2 == 0:
                    I("act", "activation", [bpt], [bhT], out=hT[:, :, s * 128:(s + 1) * 128], in_=pt[:, :].rearrange("p (k t) -> p k t", t=128), func=AF.Copy)
                else:
                    I("dve", "tensor_copy", [bpt], [bhT], out=hT[:, :, s * 128:(s + 1) * 128], in_=pt[:, :].rearrange("p (k t) -> p k t", t=128))
            for fc in range(32):
                p, bp = pu.next()
                for kc in range(8):
                    I("pe", "matmul", [bwup, bhT], [bp], out=p[:, 0:N], lhsT=wup[:, kc, fc * 128:(fc + 1) * 128], rhs=hT[:, kc, 0:N], start=(kc == 0), stop=(kc == 7))
                r, br = rl.next()
                I("act", "activation", [bp], [br], out=r[:, 0:N], in_=p[:, 0:N], func=AF.Relu)
                eng = "pool" if fc % 2 == 0 else "dve"
                I(eng, "tensor_tensor", [br], [buT], out=uT[:, fc, 0:N], in0=r[:, 0:N], in1=r[:, 0:N], op=ALU.mult)
            for s in range(ns):
                for cg in range(2):
                    p, bp = pd.next()
                    for fc in range(32):
                        I("pe", "matmul", [buT, bwdn], [bp], out=p[:, :], lhsT=uT[:, fc, s * 128:(s + 1) * 128], rhs=wdn[:, fc, cg * 512:(cg + 1) * 512], start=(fc == 0), stop=(fc == 31))
                    if cg == 0:
                        I("act", "activation", [bp], [bdsb], out=dsb[:, 0:512], in_=p[:, :], func=AF.Copy)
                    else:
                        I("dve", "tensor_copy", [bp], [bdsb], out=dsb[:, 512:1024], in_=p[:, :])
                I("act", "activation", [bdsb], [bjunk, bss], out=junk[:], in_=dsb[:], func=AF.Square, accum_out=ss[:, s:s + 1])
                I("act", "activation", [bss], [bss], out=ss[:, s:s + 1], in_=ss[:, s:s + 1], func=AF.Sqrt, bias=EPS, scale=1.0 / 1024)
                I("dve", "reciprocal", [bss], [bss], out=ss[:, s:s + 1], in_=ss[:, s:s + 1])
                I("dve", "scalar_tensor_tensor", [bdsb, bss, bgpost], [bdsb], out=dsb[:], in0=dsb[:], scalar=ss[:, s:s + 1], in1=gpost[:], op0=ALU.mult, op1=ALU.mult)
                I("pool", "tensor_tensor", [bdsb, by1], [by1], out=y1[:, s, :], in0=dsb[:], in1=y1[:, s, :], op=ALU.add)
            I("sp", "dma_start", [by1], [], out=D.y[tq:tq + N, :].rearrange("(s p) m -> p s m", p=128), in_=y1[:, 0:ns, :])
        fw.barrier()
        fw.emit()


def phase_bc(nc, fw, D, ratio=2.7):
    with ExitStack() as st:
        banks = []
        for i in range(8):
            t = st.enter_context(nc.psum_tensor(f"BC_ps{i}", [128, 512], F32))
            banks.append((t, Buf(f"BC_ps{i}")))
        gb = phase_b_body(nc, fw, D, st, banks[0:4])
        gc = phase_c_body(nc, fw, D, st, banks[4:8])
        b_alive = c_alive = True
        acc = 0.0
        while b_alive or c_alive:
            if c_alive:
                try:
                    next(gc)
                except StopIteration:
                    c_alive = False
            acc += ratio if c_alive else 1.0
            while b_alive and acc >= 1.0:
                acc -= 1.0
                try:
                    next(gb)
                except StopIteration:
                    b_alive = False
        fw.barrier()
        fw.emit()


BF = ml_dtypes.bfloat16


def const_masks():
    p = np.arange(128)
    cm = np.zeros((128, 8, 128), np.float32)
    cm[:, 0, :] = np.eye(128)
    cm[:, 1, :] = (p[:, None] // 64 == p[None, :] // 64)
    NEG = -30000.0
    cm[:, 2, :] = np.where(p[None, :] >= p[:, None], 0.0, NEG)
    cm[:, 3, :] = np.where(p[None, :] > p[:, None], 0.0, NEG)
    cm[:, 4, :] = np.where(p[:, None] > p[None, :], 0.0, NEG)
    cm[:, 5, :] = 1.0
    same = (p[:, None] // 64 == p[None, :] // 64)
    cm[:, 6, :] = np.where(same & (p[None, :] >= p[:, None]), 0.0, NEG)
    cm[:, 7, :] = np.where(same & (p[None, :] > p[:, None]), 0.0, NEG)
    cf = np.zeros((128, 6, 128), np.float32)
    cf[:, 0, :] = np.eye(128)
    cf[:, 1, :] = (p[:, None] <= p[None, :])
    cf[:, 2, :] = 1.0
    cf[127, 3, :] = 1.0
    cf[:, 4, :] = (p[:, None] <= p[None, :]) & (p[:, None] // 64 == p[None, :] // 64)
    cf[63, 5, :] = 1.0
    return cm.astype(BF), cf


def prep_core(inp, c):
    b, half = c // 2, c % 2
    xp = inp["x_prompt"][b]
    xall = np.zeros((4096 + 4096 + 256, 1024), np.float32)
    kmask = np.zeros((128, 32), np.float32)
    if half == 1:
        xall[:8192] = xp
    else:
        xall[4096:8192] = xp[:4096]
        kmask[:] = -30000.0
    for q in range(2):
        xall[8192 + q * 128: 8192 + q * 128 + 16] = inp["x_sample"][2 * c + q]
    w_in = inp["w_in"][0]
    w_sm = np.concatenate([w_in[:, 1536:1544], w_in[:, 3080:3096]], axis=1)
    bias24 = np.concatenate([inp["fox_forget_bias"][0], inp["gdn_dt_bias"][0], np.zeros(8, np.float32)])[None, :]
    sgn24 = np.concatenate([-np.ones(8), np.ones(8), -np.ones(8)]).astype(np.float32)[None, :]
    convT = np.ascontiguousarray(inp["gdn_conv_w"][0].T.reshape(12, 128, 4).transpose(1, 0, 2))
    ch = inp["state_gdn_conv"][0, 2 * c: 2 * c + 2]
    conv_hist = np.ascontiguousarray(ch.transpose(0, 2, 1).reshape(2, 12, 128, 3).transpose(0, 2, 1, 3))
    cm, cf = const_masks()
    d = {
        "xall": xall, "kmask": kmask, "vmask": (np.arange(128) < 16).astype(np.float32)[:, None], "w_in": w_in, "w_sm": np.ascontiguousarray(w_sm), "bias24": bias24.astype(np.float32),
        "sgn24": sgn24, "a_log": inp["gdn_a_log"], "convT": convT, "conv_hist": conv_hist,
        "g_mix_pre": inp["norm_mix_pre"], "g_mix_post": inp["norm_mix_post"], "g_mlp_pre": inp["norm_mlp_pre"], "g_mlp_post": inp["norm_mlp_post"],
        "g_gdn": inp["gdn_norm_g"], "w_pa": inp["w_proj_fox"][0], "w_pb": inp["w_proj_gdn"][0], "w_out": inp["w_out"][0],
        "w_up": inp["w_up"][0], "w_down": inp["w_down"][0],
        "ck": inp["cache_fox_k"][0, 2 * c:2 * c + 2].reshape(2, 2048, 512), "cv": inp["cache_fox_v"][0, 2 * c:2 * c + 2].reshape(2, 2048, 512),
        "clf": inp["cache_fox_logf"][0, 2 * c:2 * c + 2], "sgdn": inp["state_gdn"][0, 2 * c:2 * c + 2],
        "cmask": cm, "cf32": cf,
    }
    return {k: np.ascontiguousarray(v) for k, v in d.items()}


def build_program():
    nc = bass.Bass("TRN2", target_bir_lowering=False)
    with ExitStack() as st:
        D = declare(nc, False)
        fw = FW(nc, st)
        phase_a(nc, fw, D)
        phase_b(nc, fw, D)
        phase_c(nc, fw, D)
        phase_d1(nc, fw, D)
        phase_d2(nc, fw, D)
    return nc


def kernel(**inp):
    inp = {k: np.asarray(v) for k, v in inp.items()}
    nc = build_program()
    in_maps = [prep_core(inp, c) for c in range(8)]
    res = run_bass_kernel_spmd(nc, in_maps, core_ids=list(range(8)))
    R = res.results
    f = lambda a: np.asarray(a, dtype=np.float32)
    y_p = np.zeros((4, 8192, 1024), np.float32)
    y_s = np.zeros((16, 16, 1024), np.float32)
    fk_p = np.zeros((1, 4, 8192, 8, 64), np.float32)
    fv_p = np.zeros_like(fk_p)
    lf_p = np.zeros((1, 4, 8192, 8), np.float32)
    sg_p = np.zeros((1, 4, 8, 64, 64), np.float32)
    cv_p = np.zeros((1, 4, 3, 1536), np.float32)
    fk_s = np.zeros((1, 16, 16, 8, 64), np.float32)
    fv_s = np.zeros_like(fk_s)
    lf_s = np.zeros((1, 16, 16, 8), np.float32)
    sg_s = np.zeros((1, 16, 8, 64, 64), np.float32)
    cv_s = np.zeros((1, 16, 3, 1536), np.float32)
    for c in range(8):
        b, half = c // 2, c % 2
        r = R[c]
        sl = slice(half * 4096, (half + 1) * 4096)
        y_p[b, sl] = f(r["y"])[:4096]
        fk_p[0, b, sl] = f(r["fk"])[:4096].reshape(4096, 8, 64)
        fv_p[0, b, sl] = f(r["fv"])[:4096].reshape(4096, 8, 64)
        lf_p[0, b, sl] = f(r["lf"])[:4096]
        if half == 1:
            sg_p[0, b] = f(r["sfin"])[0]
            cv_p[0, b] = f(r["convo"])[0]
        for q in range(2):
            s = 2 * c + q
            rows = slice(4096 + q * 128, 4096 + q * 128 + 16)
            y_s[s] = f(r["y"])[rows]
            fk_s[0, s] = f(r["fk"])[rows].reshape(16, 8, 64)
            fv_s[0, s] = f(r["fv"])[rows].reshape(16, 8, 64)
            lf_s[0, s] = f(r["lf"])[rows]
            sg_s[0, s] = f(r["sfin"])[1 + q]
            cv_s[0, s] = f(r["convo"])[1 + q]
    return (y_p, y_s, fk_p, fv_p, lf_p, sg_p, cv_p, fk_s, fv_s, lf_s, sg_s, cv_s)
```

```python
from contextlib import ExitStack
import numpy as np
import ml_dtypes
from concourse.bass_utils import run_bass_kernel_spmd
import concourse.bass as bass
import concourse.mybir as mybir

F32 = mybir.dt.float32
BF16 = mybir.dt.bfloat16
AF = mybir.ActivationFunctionType
ALU = mybir.AluOpType
AX = mybir.AxisListType

ENGS = ("pe", "act", "dve", "pool", "sp")
EPOCH = 30000
NDMASEM = 30
DMA_ENGS = ("sp", "pool")


class Buf:
    __slots__ = ("name", "w", "r")

    def __init__(self, name):
        self.name = name
        self.w = None
        self.r = {}


class FW:
    def __init__(self, nc, stack):
        self.nc = nc
        self.stack = stack
        self.ops = {e: [] for e in ENGS}
        self.cnt = {e: 0 for e in ENGS}
        self.ccnt = {e: 0 for e in ENGS}
        self.sems = {}
        self.dsem = {}
        self.dcnt = {}
        self.drr = {e: 0 for e in ENGS}
        self.waited = {e: {} for e in ENGS}
        for e in DMA_ENGS:
            for i in range(NDMASEM):
                self.dsem[(e, i)] = stack.enter_context(nc.semaphore(f"d_{e}_{i}"))
                self.dcnt[(e, i)] = 0
        self.nsem_ep = {e: 0 for e in ENGS}
        self.pending = {e: [] for e in ENGS}

    def _sem(self, eng, ep):
        k = (eng, ep)
        if k not in self.sems:
            self.sems[k] = self.stack.enter_context(self.nc.semaphore(f"c_{eng}_{ep}"))
        return self.sems[k]

    def _need(self, eng, tok, waits):
        if tok is None:
            return
        key, val = tok
        if eng == "pe" and key[0] == "c" and key[1] == "pe":
            return
        cur = self.waited[eng].get(key, 0)
        if cur >= val:
            return
        self.waited[eng][key] = val
        waits[key] = max(waits.get(key, 0), val)

    def op(self, eng, fn, reads=(), writes=(), dma=False):
        waits = {}
        if self.pending[eng]:
            for t in self.pending[eng]:
                self._need(eng, t, waits)
            self.pending[eng] = []
        for b in reads:
            self._need(eng, b.w, waits)
        for b in writes:
            self._need(eng, b.w, waits)
            for t in b.r.items():
                self._need(eng, t, waits)
        if dma:
            i = self.drr[eng] % NDMASEM
            self.drr[eng] += 1
            if self.dcnt[(eng, i)] > 0:
                self._need(eng, (("d", eng, i), self.dcnt[(eng, i)]), waits)
            self.dcnt[(eng, i)] += 16
            key = ("d", eng, i)
            tok = (key, self.dcnt[(eng, i)])
            inc = (self.dsem[(eng, i)], 16)
        else:
            n = self.ccnt[eng]
            self.ccnt[eng] += 1
            ep = n // EPOCH
            key = ("c", eng, ep)
            tok = (key, n % EPOCH + 1)
            inc = (self._sem(eng, ep), 1)
        self.cnt[eng] += 1
        for b in writes:
            b.w = tok
            b.r = {}
        for b in reads:
            if b not in writes:
                b.r[tok[0]] = max(b.r.get(tok[0], 0), tok[1])
        self.ops[eng].append((waits, fn, inc))
        return tok

    def I(self, eng, method, reads=(), writes=(), **kw):
        dma = method == "dma_start"
        return self.op(eng, lambda e: getattr(e, method)(**kw), reads=reads, writes=writes, dma=dma)

    def semh(self, key):
        if key[0] == "d":
            return self.dsem[(key[1], key[2])]
        return self._sem(key[1], key[2])

    def barrier(self):
        toks = []
        for e in ENGS:
            n = self.ccnt[e]
            if n > 0:
                ep = (n - 1) // EPOCH
                toks.append((("c", e, ep), (n - 1) % EPOCH + 1))
                for ep2 in range(ep):
                    toks.append((("c", e, ep2), EPOCH))
            for i in range(NDMASEM):
                if e in DMA_ENGS and self.dcnt[(e, i)] > 0:
                    toks.append((("d", e, i), self.dcnt[(e, i)]))
        for e in ENGS:
            if self.ops[e]:
                waits = {}
                for t in toks:
                    self._need(e, t, waits)
                if waits:
                    self.ops[e].append((waits, None, None))
            else:
                self.pending[e] = list(toks)

    def emit(self):
        nc = self.nc
        with nc.Block() as block:
            def mk(eng_name):
                def body(e):
                    for waits, fn, inc in self.ops[eng_name]:
                        for key, val in waits.items():
                            e.wait_ge(self.semh(key), val)
                        if fn is not None:
                            ins = fn(e)
                            ins.then_inc(inc[0], inc[1])
                return body
            regs = {"pe": block.tensor, "act": block.scalar, "dve": block.vector, "pool": block.gpsimd, "sp": block.sync}
            for en in ENGS:
                if self.ops[en]:
                    regs[en](mk(en))
        self.ops = {e: [] for e in ENGS}


NP_ = 4096
NO_ = 4096
NS_ = 256
NT_ = NP_ + NO_ + NS_
NQ_ = NO_ + NS_
EPS = 1e-6
C_FQ, C_FK, C_FV, C_FF, C_GQ, C_GA, C_GB, C_GG, C_A, C_B = 0, 512, 1024, 1536, 1544, 3080, 3088, 3096, 3608, 4632


class Ctx:
    pass


def declare(nc, debug, ext_in=()):
    D = Ctx()
    ei = lambda n, s, dt=F32: nc.dram_tensor(n, s, dt, kind="ExternalInput").ap()
    eo = lambda n, s, dt=F32: nc.dram_tensor(n, s, dt, kind="ExternalOutput").ap()
    sc = lambda n, s, dt=F32: nc.dram_tensor(n, s, dt, kind=("ExternalInput" if n in ext_in else ("ExternalOutput" if debug else "Internal"))).ap()
    D.xall = ei("xall", [NT_, 1024])
    D.kmask = ei("kmask", [128, 32])
    D.vmask = ei("vmask", [128, 1])
    D.w_in = ei("w_in", [1024, 5656])
    D.w_sm = ei("w_sm", [1024, 24])
    D.bias24 = ei("bias24", [1, 24])
    D.sgn24 = ei("sgn24", [1, 24])
    D.a_log = ei("a_log", [1, 8])
    D.convT = ei("convT", [128, 12, 4])
    D.conv_hist = ei("conv_hist", [2, 128, 12, 3])
    D.g_mix_pre = ei("g_mix_pre", [1, 1024])
    D.g_mix_post = ei("g_mix_post", [1, 1024])
    D.g_mlp_pre = ei("g_mlp_pre", [1, 1024])
    D.g_mlp_post = ei("g_mlp_post", [1, 1024])
    D.g_gdn = ei("g_gdn", [1, 64])
    D.w_pa = ei("w_pa", [512, 1024])
    D.w_pb = ei("w_pb", [512, 1024])
    D.w_out = ei("w_out", [1024, 1024])
    D.w_up = ei("w_up", [1024, 4096])
    D.w_down = ei("w_down", [4096, 1024])
    D.ck = ei("ck", [2, 2048, 512])
    D.cv = ei("cv", [2, 2048, 512])
    D.clf = ei("clf", [2, 2048, 8])
    D.sgdn = ei("sgdn", [2, 8, 64, 64])
    D.cmask = ei("cmask", [128, 8, 128], BF16)
    D.cf32 = ei("cf32", [128, 6, 128])
    D.y = eo("y", [NQ_, 1024])
    D.fk = eo("fk", [NQ_, 512])
    D.fv = eo("fv", [NQ_, 512])
    D.lf = eo("lf", [NQ_, 8])
    D.sfin = eo("sfin", [3, 8, 64, 64])
    D.convo = eo("convo", [3, 3, 1536])
    D.qT = sc("qT_s", [512, NQ_], BF16)
    D.kT = sc("kT_s", [512, NT_], BF16)
    D.V = sc("V_s", [NT_, 512], BF16)
    D.lfs = sc("lf_s", [NT_, 8])
    D.g = sc("g_s", [NT_, 8])
    D.beta = sc("beta_s", [NT_, 16])
    D.gqT = sc("gqT_s", [512, NT_], BF16)
    D.gkT = sc("gkT_s", [512, NT_], BF16)
    D.gqkv = sc("gqkv_s", [NT_, 1536], BF16)
    D.gg = sc("gg_s", [NQ_, 512])
    D.sgA = sc("sgA_s", [1024, NQ_], BF16)
    D.sgB = sc("sgB_s", [1024, NQ_], BF16)
    D.foT = sc("foT_s", [512, NQ_], BF16)
    D.go = sc("go_s", [NQ_, 512], BF16)
    D.y1 = sc("y1_s", [NQ_, 1024])
    return D


class Rot:
    def __init__(self, items):
        self.items = items
        self.i = 0

    def next(self):
        it = self.items[self.i % len(self.items)]
        self.i += 1
        return it


def phase_a(nc, fw, D, tiles=None):
    I = fw.I
    with ExitStack() as st:
        def sb(name, shape, dt):
            return st.enter_context(nc.sbuf_tensor("A_" + name, shape, dt)), Buf(name)

        def pst(name, shape, dt):
            return st.enter_context(nc.psum_tensor("A_" + name, shape, dt)), Buf(name)
        Win, bWin = sb("Win", [128, 8, 5656], BF16)
        Wsm, bWsm = sb("Wsm", [128, 8, 24], BF16)
        xts = Rot([sb(f"xt{i}", [128, 4, 1024], F32) for i in range(2)])
        hb, bhb = sb("hb", [128, 4, 1024], BF16)
        hTs = Rot([sb(f"hT{i}", [128, 8, 512], BF16) for i in range(1)])
        junk, bjunk = sb("junk", [128, 1024], BF16)
        ss, bss = sb("ss", [128, 4], F32)
        rs, brs = sb("rs", [128, 4], F32)
        gpre, bgpre = sb("gpre", [128, 1024], F32)
        identb, bidentb = sb("identb", [128, 128], BF16)
        diagw, bdiagw = sb("diagw", [128, 48, 128], BF16)
        cwT, bcwT = sb("cwT", [128, 12, 4], F32)
        xg, bxg = sb("xg", [128, 12, 515], BF16)
        cT, bcT = sb("cT", [128, 12, 512], BF16)
        sbf = Rot([sb(f"sbf{i}", [128, 512], BF16) for i in range(4)])
        sf32 = Rot([sb(f"sf32{i}", [128, 512], F32) for i in range(2)])
        tts = Rot([sb(f"tt{i}", [128, 512], BF16) for i in range(3)])
        sqt, bsqt = sb("sqt", [128, 1024], BF16)
        l2ss, bl2ss = sb("l2ss", [128, 4, 16], F32)
        tball, btball = sb("tball", [128, 4, 1536], BF16)
        sm_t, bsm_t = sb("sm_t", [128, 96], F32)
        sm_e, bsm_e = sb("sm_e", [128, 96], F32)
        sm_l, bsm_l = sb("sm_l", [128, 96], F32)
        sm_o, bsm_o = sb("sm_o", [128, 4, 32], F32)
        b24, bb24 = sb("b24", [128, 24], F32)
        s24, bs24 = sb("s24", [128, 24], F32)
        nea, bnea = sb("nea", [128, 8], F32)
        ptr = Rot([pst(f"ptr{i}", [128, 1024], BF16) for i in range(2)])
        pm = Rot([pst(f"pm{i}", [128, 512], F32) for i in range(4)])
        psm, bpsm = pst("psm", [128, 512], F32)

        I("pool", "dma_start", [], [bWin], out=Win[:], in_=D.w_in.rearrange("(c p) n -> p c n", p=128))
        I("pool", "dma_start", [], [bWsm], out=Wsm[:], in_=D.w_sm.rearrange("(c p) n -> p c n", p=128))
        I("sp", "dma_start", [], [bgpre], out=gpre[:], in_=D.g_mix_pre[0:1, :].broadcast_to([128, 1024]))
        I("sp", "dma_start", [], [bidentb], out=identb[:], in_=D.cmask[:, 0, :])
        I("sp", "dma_start", [], [bcwT], out=cwT[:], in_=D.convT[:, :, :])
        I("sp", "dma_start", [], [bb24], out=b24[:], in_=D.bias24[0:1, :].broadcast_to([128, 24]))
        I("sp", "dma_start", [], [bs24], out=s24[:], in_=D.sgn24[0:1, :].broadcast_to([128, 24]))
        I("sp", "dma_start", [], [bnea], out=nea[:], in_=D.a_log[0:1, :].broadcast_to([128, 8]))
        I("act", "activation", [bnea], [bnea], out=nea[:], in_=nea[:], func=AF.Exp)
        I("dve", "tensor_scalar_mul", [bnea], [bnea], out=nea[:], in0=nea[:], scalar1=-1.0)
        I("dve", "tensor_scalar_mul", [bcwT], [bcwT], out=cwT[:], in0=cwT[:], scalar1=0.5)
        for j in range(12):
            for i in range(4):
                I("dve", "tensor_scalar_mul", [bidentb, bcwT], [bdiagw], out=diagw[:, j * 4 + i, :], in0=identb[:], scalar1=cwT[:, j, i:i + 1])
        I("dve", "memset", [], [bxg], ap=xg[:, :, 0:3], constant=0.0)

        all_tiles = [(i * 512, 4, "p") for i in range(8)] + [(NP_ + i * 512, 4, "o") for i in range(8)] + [(NP_ + NO_, 2, "s")]
        if tiles is not None:
            all_tiles = [all_tiles[i] for i in tiles]
        cpy_rr = [0]

        def evac_copy(out_ap, in_ap, reads, writes, scale=None):
            cpy_rr[0] += 1
            if scale is not None:
                I("act", "activation", reads, writes, out=out_ap, in_=in_ap, func=AF.Copy, scale=scale)
            elif cpy_rr[0] % 2 == 0:
                I("act", "activation", reads, writes, out=out_ap, in_=in_ap, func=AF.Copy)
            else:
                I("dve", "tensor_copy", reads, writes, out=out_ap, in_=in_ap)

        def store(eng, out_ap, in_ap, buf):
            I(eng, "dma_start", [buf], [], out=out_ap, in_=in_ap)

        for (t0, ns, kind) in all_tiles:
            N = ns * 128
            own = kind in ("o", "s")
            tq = t0 - NP_
            xt, bxt = xts.next()
            hT, bhT = hTs.next()
            I("sp", "dma_start", [], [bxt], out=xt[:, 0:ns, :], in_=D.xall[t0:t0 + N, :].rearrange("(s p) m -> p s m", p=128))
            for s in range(ns):
                I("act", "activation", [bxt], [bjunk, bss], out=junk[:], in_=xt[:, s, :], func=AF.Square, accum_out=ss[:, s:s + 1])
            I("act", "activation", [bss], [brs], out=rs[:, 0:ns], in_=ss[:, 0:ns], func=AF.Sqrt, bias=EPS, scale=1.0 / 1024)
            I("dve", "reciprocal", [brs], [brs], out=rs[:, 0:ns], in_=rs[:, 0:ns])
            for s in range(ns):
                I("dve", "scalar_tensor_tensor", [bxt, brs, bgpre], [bhb], out=hb[:, s, :], in0=xt[:, s, :], scalar=rs[:, s:s + 1], in1=gpre[:],
                  op0=ALU.mult, op1=ALU.mult)
            for kcp in range(4):
                pt, bpt = ptr.next()
                for kk in range(2):
                    kc = 2 * kcp + kk
                    for s in range(ns):
                        I("pe", "transpose", [bhb, bidentb], [bpt], out=pt[:, kk * 512 + s * 128: kk * 512 + (s + 1) * 128],
                          in_=hb[:, s, kc * 128:(kc + 1) * 128], identity=identb[:])
                for kk in range(2):
                    evac_copy(hT[:, 2 * kcp + kk, 0:N], pt[:, kk * 512: kk * 512 + N], [bpt], [bhT])

            def fm_mm(c0, w=128):
                p, bp = pm.next()
                for kc in range(8):
                    I("pe", "matmul", [bWin, bhT], [bp], out=p[0:w, 0:N], lhsT=Win[:, kc, c0:c0 + w], rhs=hT[:, kc, 0:N], start=(kc == 0), stop=(kc == 7))
                return p, bp

            def tm_mm(s, c0, w, Wt=None, bW=None, out=None):
                if out is None:
                    p, bp = pm.next()
                    o = p[:, 0:w]
                else:
                    p, bp, o = out
                Wt_ = Win if Wt is None else Wt
                bW_ = bWin if bW is None else bW
                for kc in range(8):
                    I("pe", "matmul", [bW_, bhT], [bp], out=o, lhsT=hT[:, kc, s * 128:(s + 1) * 128], rhs=Wt_[:, kc, c0:c0 + w], start=(kc == 0), stop=(kc == 7))
                return p, bp

            fillers = []

            def do_fq(j):
                p, bp = fm_mm(C_FQ + j * 128)
                s_, bs_ = sbf.next()
                evac_copy(s_[:, 0:N], p[:, 0:N], [bp], [bs_], scale=0.125)
                store("sp", D.qT[j * 128:(j + 1) * 128, tq:tq + N], s_[:, 0:N], bs_)

            def do_fk(j):
                p, bp = fm_mm(C_FK + j * 128)
                s_, bs_ = sbf.next()
                evac_copy(s_[:, 0:N], p[:, 0:N], [bp], [bs_])
                store("sp", D.kT[j * 128:(j + 1) * 128, t0:t0 + N], s_[:, 0:N], bs_)

            def do_gate(j):
                p, bp = fm_mm(C_A + j * 128)
                s_, bs_ = sbf.next()
                tt, btt = tts.next()
                I("act", "activation", [bp], [btt], out=tt[:, 0:N], in_=p[:, 0:N], func=AF.Tanh, scale=0.5)
                I("dve", "tensor_scalar", [btt], [bs_], out=s_[:, 0:N], in0=tt[:, 0:N], scalar1=0.5, scalar2=0.5, op0=ALU.mult, op1=ALU.add)
                dst = D.sgA if j < 8 else D.sgB
                store("sp", dst[(j % 8) * 128:(j % 8 + 1) * 128, tq:tq + N], s_[:, 0:N], bs_)

            def do_tm_k(s):
                r0 = t0 + s * 128
                p, bp = tm_mm(s, C_FK, 512)
                sf, bsf = sf32.next()
                evac_copy(sf[:], p[:], [bp], [bsf])
                store("sp", D.fk[r0 - NP_: r0 - NP_ + 128, :], sf[:], bsf)

            def do_tm_v(s):
                r0 = t0 + s * 128
                p, bp = tm_mm(s, C_FV, 512)
                s_, bs_ = sbf.next()
                if own:
                    sf, bsf = sf32.next()
                    I("dve", "tensor_copy", [bp], [bsf], out=sf[:], in_=p[:])
                    store("sp", D.fv[r0 - NP_: r0 - NP_ + 128, :], sf[:], bsf)
                    I("act", "activation", [bsf], [bs_], out=s_[:], in_=sf[:], func=AF.Copy)
                else:
                    I("dve", "tensor_copy", [bp], [bs_], out=s_[:], in_=p[:])
                store("sp", D.V[r0:r0 + 128, :], s_[:], bs_)

            def do_tm_gg(s):
                r0 = t0 + s * 128
                p, bp = tm_mm(s, C_GG, 512)
                sf, bsf = sf32.next()
                tt, btt = tts.next()
                I("act", "activation", [bp], [btt], out=tt[:], in_=p[:], func=AF.Tanh, scale=0.5)
                I("dve", "scalar_tensor_tensor", [btt, bp], [bsf], out=sf[:], in0=tt[:], scalar=1.0, in1=p[:], op0=ALU.add, op1=ALU.mult)
                store("sp", D.gg[r0 - NP_: r0 - NP_ + 128, :], sf[:], bsf)

            def do_tm_small(s):
                tm_mm(s, 0, 24, Wt=Wsm, bW=bWsm, out=(psm, bpsm, psm[:, s * 24:(s + 1) * 24]))

            if own:
                for j in range(4):
                    fillers.append((do_fq, j))
            for j in range(4):
                fillers.append((do_fk, j))
            for s in range(ns):
                if own:
                    fillers.append((do_tm_k, s))
                fillers.append((do_tm_v, s))
                if own:
                    fillers.append((do_tm_gg, s))
                fillers.append((do_tm_small, s))
            if own:
                for j in range(16):
                    fillers.append((do_gate, j))
            per_step = (len(fillers) + 11) // 12

            def run_fillers(n):
                for _ in range(n):
                    if fillers:
                        f_, a_ = fillers.pop(0)
                        f_(a_)

            for j in range(12):
                p, bp = fm_mm(C_GQ + j * 128)
                evac_copy(xg[:, j, 3:3 + N], p[:, 0:N], [bp], [bxg])
            if kind == "s":
                for q_ in range(2):
                    I("pool", "dma_start", [], [bxg], out=xg[:, :, q_ * 128: q_ * 128 + 3], in_=D.conv_hist[q_])

            for j in range(12):
                p, bp = pm.next()
                for i in range(4):
                    I("pe", "matmul", [bdiagw, bxg], [bp], out=p[:, 0:N], lhsT=diagw[:, j * 4 + i, :], rhs=xg[:, j, i:i + N], start=(i == 0), stop=(i == 3))
                tt, btt = tts.next()
                I("act", "activation", [bp], [btt], out=tt[:, 0:N], in_=p[:, 0:N], func=AF.Tanh)
                I("dve", "scalar_tensor_tensor", [btt, bp], [bcT], out=cT[:, j, 0:N], in0=tt[:, 0:N], scalar=1.0, in1=p[:, 0:N], op0=ALU.add, op1=ALU.mult)
                run_fillers(per_step)
            run_fillers(len(fillers))
            I("dve", "tensor_copy", [bxg], [bxg], out=xg[:, :, 0:3], in_=xg[:, :, N:N + 3])
            for s in range(ns):
                for half in range(2):
                    pt, bpt = ptr.next()
                    nj = 8 if half == 0 else 4
                    for jj in range(nj):
                        j = half * 8 + jj
                        I("pe", "transpose", [bcT, bidentb], [bpt], out=pt[:, jj * 128:(jj + 1) * 128], in_=cT[:, j, s * 128:(s + 1) * 128], identity=identb[:])
                    evac_copy(tball[:, s, half * 1024: half * 1024 + nj * 128], pt[:, 0:nj * 128], [bpt], [btball])
                I("dve", "tensor_tensor", [btball], [bsqt], out=sqt[:], in0=tball[:, s, 0:1024], in1=tball[:, s, 0:1024], op=ALU.mult)
                I("dve", "tensor_reduce", [bsqt], [bl2ss], out=l2ss[:, s, :], in_=sqt[:].rearrange("p (h d) -> p h d", d=64), axis=AX.X, op=ALU.add)
            I("act", "activation", [bl2ss], [bl2ss], out=l2ss[:, 0:ns, :], in_=l2ss[:, 0:ns, :], func=AF.Sqrt, bias=EPS)
            I("dve", "reciprocal", [bl2ss], [bl2ss], out=l2ss[:, 0:ns, :], in_=l2ss[:, 0:ns, :])
            I("dve", "tensor_scalar_mul", [bl2ss], [bl2ss], out=l2ss[:, 0:ns, 0:8], in0=l2ss[:, 0:ns, 0:8], scalar1=0.125)
            for s in range(ns):
                qk = tball[:, s, 0:1024].rearrange("p (h d) -> p h d", d=64)
                I("dve", "tensor_tensor", [btball, bl2ss], [btball], out=qk, in0=qk, in1=l2ss[:, s, :].unsqueeze(2).broadcast_to([128, 16, 64]), op=ALU.mult)
                store("sp", D.gqkv[t0 + s * 128: t0 + (s + 1) * 128, :], tball[:, s, :], btball)
            n24 = ns * 24
            for s in range(ns):
                I("dve", "tensor_tensor", [bpsm, bb24], [bsm_t], out=sm_t[:, s * 24:(s + 1) * 24], in0=psm[:, s * 24:(s + 1) * 24], in1=b24[:], op=ALU.add)
                I("dve", "tensor_tensor", [bsm_t, bs24], [bsm_t], out=sm_t[:, s * 24:(s + 1) * 24], in0=sm_t[:, s * 24:(s + 1) * 24], in1=s24[:], op=ALU.mult)
            I("act", "activation", [bsm_t], [bsm_e], out=sm_e[:, 0:n24], in_=sm_t[:, 0:n24], func=AF.Exp)
            I("act", "activation", [bsm_e], [bsm_l], out=sm_l[:, 0:n24], in_=sm_e[:, 0:n24], func=AF.Ln, bias=1.0)
            for s in range(ns):
                I("dve", "tensor_scalar_mul", [bsm_l], [bsm_o], out=sm_o[:, s, 0:8], in0=sm_l[:, s * 24: s * 24 + 8], scalar1=-1.0)
                I("dve", "tensor_tensor", [bsm_l, bnea], [bsm_o], out=sm_o[:, s, 8:16], in0=sm_l[:, s * 24 + 8: s * 24 + 16], in1=nea[:], op=ALU.mult)
                I("dve", "tensor_scalar_add", [bsm_e], [bsm_o], out=sm_o[:, s, 16:24], in0=sm_e[:, s * 24 + 16: s * 24 + 24], scalar1=1.0)
                I("dve", "reciprocal", [bsm_o], [bsm_o], out=sm_o[:, s, 16:24], in_=sm_o[:, s, 16:24])
                I("dve", "tensor_scalar_mul", [bsm_l], [bsm_o], out=sm_o[:, s, 24:32], in0=sm_l[:, s * 24 + 16: s * 24 + 24], scalar1=-1.0)
            for (dst, c0, cw_) in ((D.lfs, 0, 8), (D.g, 8, 8), (D.beta, 16, 16)):
                I("sp", "dma_start", [bsm_o], [], out=dst[t0:t0 + N, :].rearrange("(s p) m -> p s m", p=128), in_=sm_o[:, 0:ns, c0:c0 + cw_])
            if own:
                I("sp", "dma_start", [bsm_o], [], out=D.lf[tq:tq + N, :].rearrange("(s p) m -> p s m", p=128), in_=sm_o[:, 0:ns, 0:8])
            conv_rows = []
            if kind == "o" and t0 == NP_ + NO_ - 512:
                conv_rows = [(3, 125, 0)]
            if kind == "s":
                conv_rows = [(0, 13, 1), (1, 13, 2)]
            for (s, r, oi) in conv_rows:
                for cg in range(3):
                    p, bp = tm_mm(s, C_GQ + cg * 512, 512)
                    sf, bsf = sf32.next()
                    evac_copy(sf[:], p[:], [bp], [bsf])
                    store("sp", D.convo[oi, :, cg * 512:(cg + 1) * 512], sf[r:r + 3, :], bsf)
        fw.barrier()
        fw.emit()


def phase_b(nc, fw, D, pairs=(0, 1, 2, 3), qtiles=tuple(range(8)), samples=(0, 1)):
    with ExitStack() as st:
        for _ in phase_b_body(nc, fw, D, st, None, pairs, qtiles, samples):
            pass
        fw.barrier()
        fw.emit()


def phase_b_body(nc, fw, D, st, banks, pairs=(0, 1, 2, 3), qtiles=tuple(range(8)), samples=(0, 1)):
    I = fw.I
    if True:
        def sb(name, shape, dt):
            return st.enter_context(nc.sbuf_tensor("B_" + name, shape, dt)), Buf(name)

        def pst(name, shape, dt):
            return st.enter_context(nc.psum_tensor("B_" + name, shape, dt)), Buf(name)
        identb, bidentb = sb("identb", [128, 128], BF16)
        mincl, bmincl = sb("mincl", [128, 128], BF16)
        cf, bcf = sb("cf", [128, 4, 128], F32)
        kmask, bkmask = sb("kmask", [128, 32], F32)
        lf, blf = sb("lf", [128, 64, 8], F32)
        Fin, bFin = sb("Fin", [128, 64, 8], F32)
        tot, btot = sb("tot", [128, 64, 8], F32)
        scA, bscA = sb("scA", [128, 64, 8], F32)
        scB, bscB = sb("scB", [128, 64, 8], F32)
        offs, boffs = sb("offs", [128, 64, 8], F32)
        Fm, bFm = sb("Fm", [128, 64, 8], F32)
        cfac, bcfac = sb("cfac", [128, 64, 8], F32)
        psets = [(sb(f"kTp{i}", [128, 8192], BF16), sb(f"qTp{i}", [128, 4096], BF16), sb(f"Vx{i}", [128, 64, 2, 65], BF16)) for i in range(2)]
        boffs_ = [sb(f"boff{i}", [128, 64], F32) for i in range(2)]
        bdgs_ = [sb(f"bdg{i}", [128, 16], F32) for i in range(2)]
        osbs_ = [sb(f"osb{i}", [65, 512], F32) for i in range(2)]
        PTs = Rot([sb(f"PT{i}", [128, 512], BF16) for i in range(6)])
        rsbs = Rot([sb(f"rsb{i}", [65, 512], F32) for i in range(2)])
        rrecs = Rot([sb(f"rrec{i}", [65, 512], F32) for i in range(2)])
        fos = Rot([sb(f"fo{i}", [64, 512], BF16) for i in range(2)])
        kc_tm, bkc_tm = sb("kc_tm", [128, 16, 512], BF16)
        if banks is None:
            LA = 2
            pS = Rot([pst(f"pS{i}", [128, 512], F32) for i in range(LA + 1)])
            pOos = [pst(f"pOo{i}", [128, 512], F32) for i in range(2)]
            pOds = [pst(f"pOd{i}", [128, 512], F32) for i in range(2)]
            pB_, bpB_ = pst("pB", [128, 512], F32)
            getB = lambda: (pB_, bpB_)
            getT = lambda: (pB_[:].bitcast(BF16), bpB_)
            getJ = lambda: (pB_, bpB_)
        else:
            pS = Rot(list(banks[0:2]))
            pOos = [banks[2], banks[2]]
            pOds = [banks[3], banks[3]]
            LA = 1
            getB = lambda: pS.next()
            getJ = lambda: pS.next()

            def getT():
                t_, b_ = pS.next()
                return t_[:].bitcast(BF16), b_
        jk, bjk = sb("jk", [128, 512], BF16)
        I("dve", "memset", [], [bjk], ap=jk[:], constant=0.0)
        JN = 0
        JB = 12

        I("sp", "dma_start", [], [bidentb], out=identb[:], in_=D.cmask[:, 0, :])
        I("sp", "dma_start", [], [bmincl], out=mincl[:], in_=D.cmask[:, 2, :])
        I("sp", "dma_start", [], [bcf], out=cf[:], in_=D.cf32[:, 0:4, :])
        I("sp", "dma_start", [], [bkmask], out=kmask[:], in_=D.kmask[:, :])
        for (_, _, (Vx_, bVx_)) in psets:
            I("dve", "memset", [], [bVx_], ap=Vx_[:, :, :, 64:65], constant=1.0)

        def flat(t, nb):
            return t[:, 0:nb, :].rearrange("p b h -> p (b h)")

        def build_F(nb, masked):
            n = nb * 8
            pB, bpB = getB()
            I("pe", "matmul", [bcf, blf], [bpB], out=pB[:, 0:n], lhsT=cf[:, 1, :], rhs=flat(lf, nb), start=True, stop=True)
            I("dve", "tensor_copy", [bpB], [bFin], out=flat(Fin, nb), in_=pB[:, 0:n])
            pB, bpB = getB()
            I("pe", "matmul", [bcf, bFin], [bpB], out=pB[:, 0:n], lhsT=cf[:, 3, :], rhs=flat(Fin, nb), start=True, stop=True)
            I("dve", "tensor_copy", [bpB], [btot], out=flat(tot, nb), in_=pB[:, 0:n])
            src, bsrc = tot, btot
            k = 1
            pp = [(scA, bscA), (scB, bscB)]
            ii = 0
            while k < nb:
                dst, bdst = pp[ii % 2]
                ii += 1
                I("dve", "tensor_copy", [bsrc], [bdst], out=dst[:, 0:k, :], in_=src[:, 0:k, :])
                I("dve", "tensor_tensor", [bsrc], [bdst], out=dst[:, k:nb, :], in0=src[:, k:nb, :], in1=src[:, 0:nb - k, :], op=ALU.add)
                src, bsrc = dst, bdst
                k *= 2
            I("dve", "tensor_tensor", [bsrc, btot], [boffs], out=offs[:, 0:nb, :], in0=src[:, 0:nb, :], in1=tot[:, 0:nb, :], op=ALU.subtract)
            I("dve", "tensor_tensor", [bFin, boffs], [bFm], out=Fm[:, 0:nb, :], in0=Fin[:, 0:nb, :], in1=offs[:, 0:nb, :], op=ALU.add)
            if masked:
                for h in range(8):
                    I("dve", "tensor_tensor", [bFm, bkmask], [bFm], out=Fm[:, 0:32, h], in0=Fm[:, 0:32, h], in1=kmask[:, :], op=ALU.subtract)

        def attend(pset, streams):
            (kTp, bkTp), (qTp, bqTp), (Vx, bVx) = pset
            ctx = []
            for si, (hh, h, qcol, W, qb0, nsub, out_col) in enumerate(streams):
                boff, bboff = boffs_[si]
                bdg, bbdg = bdgs_[si]
                rsb, brsb = rsbs.next()
                rrec, brrec = rrecs.next()
                if qb0 > 0:
                    I("dve", "tensor_scalar", [bFm, boffs], [bboff], out=boff[:, 0:qb0], in0=Fm[:, 0:qb0, h], scalar1=offs[:, qb0, h:h + 1], scalar2=-1.0,
                      op0=ALU.subtract, op1=ALU.mult)
                for i in range(nsub):
                    I("dve", "tensor_scalar", [bFm, boffs], [bbdg], out=bdg[:, i * 4: i * 4 + i + 1], in0=Fm[:, qb0: qb0 + i + 1, h], scalar1=offs[:, qb0 + i, h:h + 1],
                      scalar2=-1.0, op0=ALU.subtract, op1=ALU.mult)
                if nsub > 1:
                    I("dve", "tensor_scalar", [boffs], [bcfac], out=cfac[:, qb0:qb0 + nsub, h], in0=offs[:, qb0:qb0 + nsub, h], scalar1=offs[:, qb0, h:h + 1], scalar2=None,
                      op0=ALU.subtract)
                    I("act", "activation", [bcfac], [bcfac], out=cfac[:, qb0:qb0 + nsub, h], in_=cfac[:, qb0:qb0 + nsub, h], func=AF.Exp)
                ctx.append((boff, bboff, bdg, bbdg, rsb, brsb, rrec, brrec))
            for _ in range(JB):
                pJ, bpJ = getJ()
                I("pe", "matmul", [bidentb, bjk], [bpJ], out=pJ[:, 0:512], lhsT=identb[:], rhs=jk[:, 0:512], start=True, stop=True)
            yield
            lists = []
            for si, (hh, h, qcol, W, qb0, nsub, out_col) in enumerate(streams):
                lists.append([(si, "o", kb, 0, 0) for kb in range(qb0)] + [(si, "d", qb0 + j, i, j) for i in range(nsub) for j in range(i + 1)])
            work = []
            for t_ in range(max(len(l_) for l_ in lists)):
                for l_ in lists:
                    if t_ < len(l_):
                        work.append(l_[t_])
            inflight = {}
            for idx in range(len(work) + LA):
                if idx < len(work):
                    si, kind, kb, i, j = work[idx]
                    hh, h, qcol, W, qb0, nsub, out_col = streams[si]
                    ps_ = slice(hh * 64, (hh + 1) * 64)
                    Wd = min(W, 128)
                    p, bp = pS.next()
                    if kind == "o":
                        I("pe", "matmul", [bkTp, bqTp], [bp], out=p[:, 0:W], lhsT=kTp[ps_, kb * 128:(kb + 1) * 128], rhs=qTp[ps_, qcol:qcol + W], start=True, stop=True)
                    else:
                        I("pe", "matmul", [bkTp, bqTp], [bp], out=p[:, 0:Wd], lhsT=kTp[ps_, kb * 128:(kb + 1) * 128], rhs=qTp[ps_, qcol + i * 128: qcol + i * 128 + Wd],
                          start=True, stop=(j != i))
                        if j == i:
                            I("pe", "matmul", [bidentb, bmincl], [bp], out=p[:, 0:Wd], lhsT=identb[:], rhs=mincl[:, 0:Wd], start=False, stop=True)
                    inflight[idx] = (p, bp)
                k2 = idx - LA
                if k2 >= 0:
                    si, kind, kb, i, j = work[k2]
                    hh, h, qcol, W, qb0, nsub, out_col = streams[si]
                    boff, bboff, bdg, bbdg = ctx[si][0:4]
                    pOo, bpOo = pOos[si]
                    pOd, bpOd = pOds[si]
                    Wd = min(W, 128)
                    p, bp = inflight.pop(k2)
                    pt, bpt = PTs.next()
                    if kind == "o":
                        I("act", "activation", [bp, bboff], [bpt], out=pt[:, 0:W], in_=p[:, 0:W], func=AF.Exp, bias=boff[:, kb:kb + 1])
                        I("pe", "matmul", [bVx, bpt], [bpOo], out=pOo[0:65, 0:W], lhsT=Vx[:, kb, hh, :], rhs=pt[:, 0:W], start=(kb == 0), stop=(kb == qb0 - 1))
                    else:
                        I("act", "activation", [bp, bbdg], [bpt], out=pt[:, 0:Wd], in_=p[:, 0:Wd], func=AF.Exp, bias=bdg[:, i * 4 + j: i * 4 + j + 1])
                        I("pe", "matmul", [bVx, bpt], [bpOd], out=pOd[0:65, i * 128: i * 128 + Wd], lhsT=Vx[:, kb, hh, :], rhs=pt[:, 0:Wd], start=(j == 0), stop=(j == i))
                yield
            for si, (hh, h, qcol, W, qb0, nsub, out_col) in enumerate(streams):
                boff, bboff, bdg, bbdg, rsb, brsb, rrec, brrec = ctx[si]
                pOo, bpOo = pOos[si]
                pOd, bpOd = pOds[si]
                osb, bosb = osbs_[si]
                Wd = min(W, 128)
                if qb0 > 0:
                    I("act", "activation", [bpOo], [bosb], out=osb[:, 0:W], in_=pOo[0:65, 0:W], func=AF.Copy)
                    for i in range(nsub):
                        sl = slice(i * 128, i * 128 + Wd)
                        if nsub > 1:
                            I("dve", "scalar_tensor_tensor", [bosb, bcfac, bpOd], [brsb], out=rsb[:, sl], in0=osb[:, sl], scalar=cfac[0:65, qb0 + i, h:h + 1], in1=pOd[0:65, sl],
                              op0=ALU.mult, op1=ALU.add)
                        else:
                            I("dve", "tensor_tensor", [bosb, bpOd], [brsb], out=rsb[:, sl], in0=osb[:, sl], in1=pOd[0:65, sl], op=ALU.add)
                else:
                    I("dve", "tensor_copy", [bpOd], [brsb], out=rsb[:, 0:W], in_=pOd[0:65, 0:W])
                I("dve", "reciprocal", [brsb], [brrec], out=rrec[64:65, 0:W], in_=rsb[64:65, 0:W])
                pB, bpB = getB()
                I("pe", "matmul", [bcf, brrec], [bpB], out=pB[0:64, 0:W], lhsT=cf[64:65, 2, 0:64], rhs=rrec[64:65, 0:W], start=True, stop=True)
                fo, bfo = fos.next()
                I("dve", "tensor_tensor", [brsb, bpB], [bfo], out=fo[:, 0:W], in0=rsb[0:64, 0:W], in1=pB[0:64, 0:W], op=ALU.mult)
                I("pool", "dma_start", [bfo], [], out=D.foT[h * 64:(h + 1) * 64, out_col:out_col + W], in_=fo[:, 0:W])
            yield

        if len(qtiles) > 0:
            I("sp", "dma_start", [], [blf], out=lf[:], in_=D.lfs[0:8192, :].rearrange("(b p) h -> p b h", p=128))
            build_F(64, True)
            def load_pair(pr, pset):
                (kTp, bkTp), (qTp, bqTp), (Vx, bVx) = pset
                I("sp", "dma_start", [], [bkTp], out=kTp[:, :], in_=D.kT[pr * 128:(pr + 1) * 128, 0:8192])
                I("sp", "dma_start", [], [bqTp], out=qTp[:, :], in_=D.qT[pr * 128:(pr + 1) * 128, 0:4096])
                for hh in range(2):
                    I("sp", "dma_start", [], [bVx], out=Vx[:, :, hh, 0:64],
                      in_=D.V[0:8192, (2 * pr + hh) * 64:(2 * pr + hh + 1) * 64].rearrange("(b p) d -> p b d", p=128))
            pl = list(pairs)
            load_pair(pl[0], psets[0])
            for n_, pr in enumerate(pl):
                if n_ + 1 < len(pl):
                    load_pair(pl[n_ + 1], psets[(n_ + 1) % 2])
                for qt in qtiles:
                    yield from attend(psets[n_ % 2], [(hh, 2 * pr + hh, qt * 512, 512, 32 + 4 * qt, 4, qt * 512) for hh in range(2)])
        for q in samples:
            r0 = NP_ + NO_ + q * 128
            I("sp", "dma_start", [], [blf], out=lf[:, 0:16, :], in_=D.clf[q].rearrange("(b p) h -> p b h", p=128))
            I("sp", "dma_start", [], [blf], out=lf[:, 16, :], in_=D.lfs[r0:r0 + 128, :])
            build_F(17, False)
            I("pool", "dma_start", [], [bkc_tm], out=kc_tm[:], in_=D.ck[q].rearrange("(b p) c -> p b c", p=128))
            for n_, pr in enumerate(pairs):
                pset = psets[n_ % 2]
                (kTp, bkTp), (qTp, bqTp), (Vx, bVx) = pset
                for b4 in range(2):
                    pT, bpT = getT()
                    for bb in range(8):
                        blk = b4 * 8 + bb
                        I("pe", "transpose", [bkc_tm, bidentb], [bpT], out=pT[:, bb * 128:(bb + 1) * 128], in_=kc_tm[:, blk, pr * 128:(pr + 1) * 128], identity=identb[:])
                    I("dve", "tensor_copy", [bpT], [bkTp], out=kTp[:, b4 * 1024:(b4 + 1) * 1024], in_=pT[:, :])
                I("sp", "dma_start", [], [bkTp], out=kTp[:, 2048:2176], in_=D.kT[pr * 128:(pr + 1) * 128, r0:r0 + 128])
                I("sp", "dma_start", [], [bqTp], out=qTp[:, 0:128], in_=D.qT[pr * 128:(pr + 1) * 128, NO_ + q * 128: NO_ + (q + 1) * 128])
                for hh in range(2):
                    h = 2 * pr + hh
                    I("pool", "dma_start", [], [bVx], out=Vx[:, 0:16, hh, 0:64], in_=D.cv[q][:, h * 64:(h + 1) * 64].rearrange("(b p) d -> p b d", p=128))
                    I("sp", "dma_start", [], [bVx], out=Vx[:, 16, hh, 0:64], in_=D.V[r0:r0 + 128, h * 64:(h + 1) * 64])
                yield from attend(pset, [(hh, 2 * pr + hh, 0, 128, 16, 1, NO_ + q * 128) for hh in range(2)])


def phase_c(nc, fw, D, blocks=None, samples=(0, 1), o_all=False):
    with ExitStack() as st:
        for _ in phase_c_body(nc, fw, D, st, None, blocks, samples, o_all):
            pass
        fw.barrier()
        fw.emit()


def phase_c_body(nc, fw, D, st, banks, blocks=None, samples=(0, 1), o_all=False):
    I = fw.I
    if True:
        def sb(name, shape, dt):
            return st.enter_context(nc.sbuf_tensor("C_" + name, shape, dt)), Buf(name)

        def pst(name, shape, dt):
            return st.enter_context(nc.psum_tensor("C_" + name, shape, dt)), Buf(name)
        identb, bidentb = sb("identb", [128, 128], BF16)
        identf, bidentf = sb("identf", [128, 128], F32)
        mincl, bmincl = sb("mincl", [128, 128], BF16)
        mstr, bmstr = sb("mstr", [128, 128], BF16)
        cf, bcf = sb("cf", [128, 6, 128], F32)
        ggdn, bggdn = sb("ggdn", [128, 8, 64], F32)
        vmask, bvmask = sb("vmask", [128, 1], F32)
        insets = [(sb(f"qkv{i}", [128, 3, 8, 64], BF16), sb(f"kT{i}", [64, 8, 128], BF16), sb(f"qT{i}", [64, 8, 128], BF16),
                   sb(f"g{i}", [128, 8], F32), sb(f"be{i}", [128, 16], F32)) for i in range(2)]
        gc, bgc = sb("gc", [128, 8], F32)
        ngc, bngc = sb("ngc", [128, 8], F32)
        vec2, bvec2 = sb("vec2", [128, 8], F32)
        eg, beg = sb("eg", [128, 8], F32)
        bee, bbee = sb("bee", [128, 8], F32)
        glb, bglb = sb("glb", [128, 8], F32)
        gl12, bgl12 = sb("gl12", [128, 16], F32)
        egl, begl = sb("egl", [128, 2, 8], F32)
        ekl, bekl = sb("ekl", [128, 8], F32)
        dg1, bdg1 = sb("dg1", [128, 8, 128], F32)
        dg2, bdg2 = sb("dg2", [128, 8, 128], F32)
        Dincl, bDincl = sb("Dincl", [128, 8, 128], BF16)
        AbT, bAbT = sb("AbT", [128, 8, 128], BF16)
        Ns = [sb(f"N{i}", [128, 8, 128], BF16) for i in range(2)]
        Ls = [sb(f"L{i}", [128, 8, 128], BF16) for i in range(2)]
        Ws = [sb(f"W{i}", [128, 8, 128], BF16) for i in range(2)]
        AqkT, bAqkT = sb("AqkT", [128, 8, 128], BF16)
        rhs2, brhs2 = sb("rhs2", [128, 8, 128], BF16)
        khat, bkhat = sb("khat", [128, 8, 64], BF16)
        qtl, bqtl = sb("qtl", [128, 8, 64], BF16)
        qtT, bqtT = sb("qtT", [64, 8, 128], BF16)
        U, bU = sb("U", [128, 8, 64], F32)
        WkT, bWkT = sb("WkT", [64, 8, 128], BF16)
        vnew, bvnew = sb("vnew", [128, 8, 64], BF16)
        S, bS = sb("S", [64, 8, 64], F32)
        Sb, bSb = sb("Sb", [64, 8, 64], BF16)
        Sb2, bSb2 = sb("Sb2", [64, 8, 64], BF16)
        osb, bosb = sb("osb", [128, 8, 64], F32)
        osq, bosq = sb("osq", [128, 8, 64], F32)
        oss, boss = sb("oss", [128, 8], F32)
        ggt, bggt = sb("ggt", [128, 8, 64], F32)
        ob, bob = sb("ob", [128, 8, 64], BF16)
        if banks is None:
            pA = [pst(f"pA{i}", [128, 512], F32) for i in range(2)]
            pL = [pst(f"pL{i}", [128, 512], F32) for i in range(2)]
            pW = [pst(f"pW{i}", [128, 512], F32) for i in range(2)]
            pX, bpX = pst("pX", [128, 512], F32)
            pTt, bpTt = pst("pTt", [128, 1024], BF16)
        else:
            pA = [banks[0], banks[0]]
            pL = [banks[1], banks[1]]
            pW = [banks[2], banks[2]]
            pX, bpX = banks[3]
            pTt, bpTt = banks[0][0][:].bitcast(BF16), banks[0][1]

        I("sp", "dma_start", [], [bidentb], out=identb[:], in_=D.cmask[:, 0, :])
        I("sp", "dma_start", [], [bmincl], out=mincl[:], in_=D.cmask[:, 6, :])
        I("sp", "dma_start", [], [bmstr], out=mstr[:], in_=D.cmask[:, 7, :])
        I("sp", "dma_start", [], [bcf], out=cf[:], in_=D.cf32[:, :, :])
        I("sp", "dma_start", [], [bidentf], out=identf[:], in_=D.cf32[:, 0, :])
        for h in range(8):
            I("sp", "dma_start", [], [bggdn], out=ggdn[:, h, :], in_=D.g_gdn[0:1, :].broadcast_to([128, 64]))
        I("sp", "dma_start", [], [bvmask], out=vmask[:], in_=D.vmask[:, :])
        jk, bjk = sb("jk", [128, 512], BF16)
        I("dve", "memset", [], [bjk], ap=jk[:], constant=0.0)
        JC = 0
        I("dve", "tensor_scalar_mul", [bggdn], [bggdn], out=ggdn[:].rearrange("p h d -> p (h d)"), in0=ggdn[:].rearrange("p h d -> p (h d)"), scalar1=0.5)

        def load(r0, si):
            (qkv, bqkv), (kT, bkT), (qT, bqT), (g_, bg_), (be, bbe) = insets[si]
            I("sp", "dma_start", [], [bqkv], out=qkv[:].rearrange("p a h d -> p (a h d)"), in_=D.gqkv[r0:r0 + 128, :])
            I("sp", "dma_start", [], [bg_], out=g_[:], in_=D.g[r0:r0 + 128, :])
            I("sp", "dma_start", [], [bbe], out=be[:], in_=D.beta[r0:r0 + 128, :])

        def process(si, qrow, want_o, sample):
            (qkv, bqkv), (kT, bkT), (qT, bqT), (g_, bg_), (be, bbe) = insets[si]
            if sample:
                I("dve", "tensor_scalar_mul", [bg_, bvmask], [bg_], out=g_[:], in0=g_[:], scalar1=vmask[:, 0:1])
                I("dve", "tensor_scalar_mul", [bqkv, bvmask], [bqkv], out=qkv[:].rearrange("p a h d -> p (a h d)"), in0=qkv[:].rearrange("p a h d -> p (a h d)"),
                  scalar1=vmask[:, 0:1])
            for h in range(8):
                I("pe", "transpose", [bqkv, bidentb], [bpTt], out=pTt[0:64, h * 128:(h + 1) * 128], in_=qkv[:, 1, h, :], identity=identb[:])
            I("act", "activation", [bpTt], [bkT], out=kT[:].rearrange("p h i -> p (h i)"), in_=pTt[0:64, :], func=AF.Copy)
            if want_o:
                for h in range(8):
                    I("pe", "transpose", [bqkv, bidentb], [bpTt], out=pTt[0:64, h * 128:(h + 1) * 128], in_=qkv[:, 0, h, :], identity=identb[:])
                I("dve", "tensor_copy", [bpTt], [bqT], out=qT[:].rearrange("p h i -> p (h i)"), in_=pTt[0:64, :])
            yield
            I("pe", "matmul", [bcf, bg_], [bpX], out=pX[:, 0:8], lhsT=cf[:, 4, :], rhs=g_[:], start=True, stop=True)
            I("dve", "tensor_copy", [bpX], [bgc], out=gc[:], in_=pX[:, 0:8])
            I("dve", "tensor_scalar_mul", [bgc], [bngc], out=ngc[:], in0=gc[:], scalar1=-1.0)
            I("pe", "matmul", [bcf, bgc], [bpX], out=pX[:, 8:16], lhsT=cf[:, 5, :], rhs=gc[:], start=True, stop=True)
            I("pe", "matmul", [bcf, bgc], [bpX], out=pX[:, 16:24], lhsT=cf[:, 3, :], rhs=gc[:], start=True, stop=True)
            I("dve", "tensor_copy", [bpX], [bgl12], out=gl12[:], in_=pX[:, 8:24])
            I("dve", "tensor_copy", [bgl12], [bglb], out=glb[0:64, :], in_=gl12[0:64, 0:8])
            I("dve", "tensor_copy", [bgl12], [bglb], out=glb[64:128, :], in_=gl12[64:128, 8:16])
            I("dve", "tensor_tensor", [bbe, bgc], [bvec2], out=vec2[:], in0=be[:, 8:16], in1=gc[:], op=ALU.add)
            I("act", "activation", [bgc], [beg], out=eg[:], in_=gc[:], func=AF.Exp)
            I("act", "activation", [bvec2], [bbee], out=bee[:], in_=vec2[:], func=AF.Exp)
            I("act", "activation", [bgl12], [begl], out=egl[:].rearrange("p a h -> p (a h)"), in_=gl12[:], func=AF.Exp)
            I("dve", "tensor_tensor", [bglb, bgc], [bekl], out=ekl[:], in0=glb[:], in1=gc[:], op=ALU.subtract)
            I("act", "activation", [bekl], [bekl], out=ekl[:], in_=ekl[:], func=AF.Exp)
            if sample:
                I("dve", "tensor_scalar_mul", [bbe, bvmask], [bbe], out=be[:, 0:8], in0=be[:, 0:8], scalar1=vmask[:, 0:1])
                I("dve", "tensor_scalar_mul", [bbee, bvmask], [bbee], out=bee[:], in0=bee[:], scalar1=vmask[:, 0:1])
            yield
            for h in range(8):
                if want_o:
                    I("dve", "tensor_scalar_mul", [bidentf, bgc], [bdg1], out=dg1[:, h, :], in0=identf[:], scalar1=gc[:, h:h + 1])
                I("dve", "tensor_scalar_mul", [bidentf, bvec2], [bdg2], out=dg2[:, h, :], in0=identf[:], scalar1=vec2[:, h:h + 1])
            N0, bN0 = Ns[0]
            L0, bL0 = Ls[0]
            yield
            for _ in range(JC):
                I("pe", "matmul", [bidentb, bjk], [bpX], out=pX[:, 0:512], lhsT=identb[:], rhs=jk[:, 0:512], start=True, stop=True)
            for hb in range(2):
                yield
                pK, bpK = pA[hb]
                pQ, bpQ = pL[hb]
                p1, bp1 = pW[hb]
                for hq in range(4):
                    h = hb * 4 + hq
                    sl = slice(hq * 128, (hq + 1) * 128)
                    I("pe", "matmul", [bkT], [bpK], out=pK[:, sl], lhsT=kT[:, h, :], rhs=kT[:, h, :], start=True, stop=True)
                    if want_o:
                        I("pe", "matmul", [bkT, bqT], [bpQ], out=pQ[:, sl], lhsT=kT[:, h, :], rhs=qT[:, h, :], start=True, stop=True)
                        I("pe", "matmul", [bcf, bdg1], [bp1], out=p1[:, sl], lhsT=cf[:, 2, :], rhs=dg1[:, h, :], start=True, stop=False)
                        I("pe", "matmul", [bidentb, bmincl], [bp1], out=p1[:, sl], lhsT=identb[:], rhs=mincl[:], start=False, stop=True)
                        I("act", "activation", [bp1, bngc], [bDincl], out=Dincl[:, h, :], in_=p1[:, sl], func=AF.Exp, bias=ngc[:, h:h + 1])
                for hq in range(4):
                    h = hb * 4 + hq
                    sl = slice(hq * 128, (hq + 1) * 128)
                    I("pe", "matmul", [bcf, bdg2], [bpX], out=pX[:, sl], lhsT=cf[:, 2, :], rhs=dg2[:, h, :], start=True, stop=False)
                    I("pe", "matmul", [bidentb, bmstr], [bpX], out=pX[:, sl], lhsT=identb[:], rhs=mstr[:], start=False, stop=True)
                    I("act", "activation", [bpX, bngc], [bAbT], out=AbT[:, h, :], in_=pX[:, sl], func=AF.Exp, bias=ngc[:, h:h + 1])
                hs = slice(hb * 4, hb * 4 + 4)
                fl = lambda t: t[:, hs, :].rearrange("p h i -> p (h i)")
                I("dve", "scalar_tensor_tensor", [bpK, bAbT], [bN0], out=fl(N0), in0=pK[:, :], scalar=-1.0, in1=fl(AbT), op0=ALU.mult, op1=ALU.mult)
                if want_o:
                    I("dve", "tensor_tensor", [bpQ, bDincl], [bAqkT], out=fl(AqkT), in0=pQ[:, :], in1=fl(Dincl), op=ALU.mult)
            yield
            for h in range(8):
                I("pe", "transpose", [bN0, bidentb], [bpTt], out=pTt[:, h * 128:(h + 1) * 128], in_=N0[:, h, :], identity=identb[:])
            I("act", "activation", [bpTt], [bL0], out=L0[:].rearrange("p h i -> p (h i)"), in_=pTt[:, :], func=AF.Copy)
            W0, bW0 = Ws[0]
            I("dve", "tensor_tensor", [bN0, bidentb], [bW0], out=W0[:], in0=N0[:], in1=identb[:].unsqueeze(1).broadcast_to([128, 8, 128]), op=ALU.add)
            cur = 0
            for m in range(1, 6):
                yield
                Nc, bNc = Ns[cur]
                Lc, bLc = Ls[cur]
                Wc, bWc = Ws[cur]
                Nn, bNn = Ns[1 - cur]
                Ln, bLn = Ls[1 - cur]
                Wn, bWn = Ws[1 - cur]
                def fl(t, hb):
                    return t[:, hb * 4:hb * 4 + 4, :].rearrange("p h i -> p (h i)")
                for hb in range(2):
                    pl_, bpl_ = pL[hb]
                    for hq in range(4):
                        h = hb * 4 + hq
                        I("pe", "matmul", [bNc, bLc], [bpl_], out=pl_[:, hq * 128:(hq + 1) * 128], lhsT=Nc[:, h, :], rhs=Lc[:, h, :], start=True, stop=True)
                    I("act", "activation", [bpl_], [bLn], out=fl(Ln, hb), in_=pl_[:, :], func=AF.Copy)
                if m < 5:
                    for hb in range(2):
                        pa_, bpa_ = pA[hb]
                        for hq in range(4):
                            h = hb * 4 + hq
                            I("pe", "matmul", [bNc, bLc], [bpa_], out=pa_[:, hq * 128:(hq + 1) * 128], lhsT=Lc[:, h, :], rhs=Nc[:, h, :], start=True, stop=True)
                        I("dve", "tensor_copy", [bpa_], [bNn], out=fl(Nn, hb), in_=pa_[:, :])
                for hb in range(2):
                    pw_, bpw_ = pW[hb]
                    for hq in range(4):
                        h = hb * 4 + hq
                        I("pe", "matmul", [bLn, bWc], [bpw_], out=pw_[:, hq * 128:(hq + 1) * 128], lhsT=Ln[:, h, :], rhs=Wc[:, h, :], start=True, stop=True)
                    I("dve", "tensor_tensor", [bpw_, bWc], [bWn], out=fl(Wn, hb), in0=pw_[:, :], in1=fl(Wc, hb), op=ALU.add)
                cur = 1 - cur
            Wf, bWf = Ws[cur]
            yield
            bc = lambda t: t[:, :].unsqueeze(2).broadcast_to([128, 8, 64])
            I("dve", "tensor_tensor", [bqkv, bbe], [brhs2], out=rhs2[:, :, 0:64], in0=qkv[:, 2, :, :], in1=be[:, 0:8].unsqueeze(2).broadcast_to([128, 8, 64]), op=ALU.mult)
            I("dve", "tensor_tensor", [bqkv, bbee], [brhs2], out=rhs2[:, :, 64:128], in0=qkv[:, 1, :, :], in1=bc(bee), op=ALU.mult)
            I("pool", "tensor_tensor", [bqkv, bekl], [bkhat], out=khat[:], in0=qkv[:, 1, :, :], in1=bc(ekl), op=ALU.mult)
            if want_o:
                I("pool", "tensor_tensor", [bqkv, beg], [bqtl], out=qtl[:], in0=qkv[:, 0, :, :], in1=bc(eg), op=ALU.mult)
            for hb in range(2):
                yield
                pu, bpu = pA[hb]
                for hq in range(4):
                    h = hb * 4 + hq
                    I("pe", "matmul", [bWf, brhs2], [bpu], out=pu[:, hq * 128:(hq + 1) * 128], lhsT=Wf[:, h, :], rhs=rhs2[:, h, :], start=True, stop=True)
                I("dve", "tensor_copy", [bpu], [bU], out=U[:, hb * 4:hb * 4 + 4, :], in_=pu[:, :].rearrange("p (h c) -> p h c", c=128)[:, :, 0:64])
                pk_, bpk_ = pL[hb]
                for hq in range(4):
                    h = hb * 4 + hq
                    I("pe", "matmul", [brhs2, bWf], [bpk_], out=pk_[0:64, hq * 128:(hq + 1) * 128], lhsT=rhs2[:, h, 64:128], rhs=Wf[:, h, :], start=True, stop=True)
                I("act", "activation", [bpk_], [bWkT], out=WkT[:, hb * 4:hb * 4 + 4, :].rearrange("p h i -> p (h i)"), in_=pk_[0:64, :], func=AF.Copy)
            if want_o:
                for h in range(8):
                    I("pe", "transpose", [bqtl, bidentb], [bpTt], out=pTt[0:64, h * 128:(h + 1) * 128], in_=qtl[:, h, :], identity=identb[:])
                I("act", "activation", [bpTt], [bqtT], out=qtT[:].rearrange("p h i -> p (h i)"), in_=pTt[0:64, :], func=AF.Copy)
            fo_ = lambda t: t[:].rearrange("p h d -> p (h d)")
            halves = [(slice(0, 64), Sb, bSb, 0), (slice(64, 128), Sb2, bSb2, 1)]
            for (rs_, Sc, bSc, ci) in halves:
                yield
                for h in range(8):
                    I("pe", "matmul", [bWkT, bSc], [bpX], out=pX[:, h * 64:(h + 1) * 64], lhsT=WkT[:, h, :], rhs=Sc[:, h, :], start=True, stop=True)
                I("dve", "tensor_tensor", [bU, bpX], [bvnew], out=fo_(vnew)[rs_, :], in0=fo_(U)[rs_, :], in1=pX[rs_, :], op=ALU.subtract)
                ps_, bps_ = pW[1]
                for h in range(8):
                    I("pe", "matmul", [bkhat, bvnew], [bps_], out=ps_[0:64, h * 64:(h + 1) * 64], lhsT=khat[rs_, h, :], rhs=vnew[rs_, h, :], start=True, stop=True)
                I("dve", "tensor_tensor", [bS, begl], [bS], out=S[:], in0=S[:], in1=egl[0:64, ci, :].unsqueeze(2).broadcast_to([64, 8, 64]), op=ALU.mult)
                I("dve", "tensor_tensor", [bS, bps_], [bS], out=fo_(S), in0=fo_(S), in1=ps_[0:64, :], op=ALU.add)
                Sn, bSn = (Sb2, bSb2) if ci == 0 else (Sb, bSb)
                if want_o or ci == 0:
                    pass
                I("act", "activation", [bS], [bSn], out=fo_(Sn), in_=fo_(S), func=AF.Copy)
                if want_o:
                    po, bpo = pW[0]
                    for h in range(8):
                        I("pe", "matmul", [bqtT, bSc], [bpo], out=po[:, h * 64:(h + 1) * 64], lhsT=qtT[:, h, :], rhs=Sc[:, h, :], start=True, stop=False)
                        I("pe", "matmul", [bAqkT, bvnew], [bpo], out=po[:, h * 64:(h + 1) * 64], lhsT=AqkT[:, h, :], rhs=vnew[:, h, :], start=False, stop=True)
                    I("act", "activation", [bpo], [bosb], out=fo_(osb)[rs_, :], in_=po[rs_, :], func=AF.Copy)
            yield
            if want_o:
                I("pool", "tensor_tensor", [bosb], [bosq], out=fo_(osq), in0=fo_(osb), in1=fo_(osb), op=ALU.mult)
                I("dve", "tensor_reduce", [bosq], [boss], out=oss[:], in_=osq[:], axis=AX.X, op=ALU.add)
                I("act", "activation", [boss], [boss], out=oss[:], in_=oss[:], func=AF.Sqrt, bias=EPS, scale=1.0 / 64)
                I("dve", "reciprocal", [boss], [boss], out=oss[:], in_=oss[:])
                I("sp", "dma_start", [], [bggt], out=fo_(ggt), in_=D.gg[qrow:qrow + 128, :])
                I("dve", "tensor_tensor", [bosb, boss], [bosb], out=osb[:], in0=osb[:], in1=oss[:, :].unsqueeze(2).broadcast_to([128, 8, 64]), op=ALU.mult)
                I("pool", "tensor_tensor", [bggt, bggdn], [bggt], out=fo_(ggt), in0=fo_(ggt), in1=fo_(ggdn), op=ALU.mult)
                I("dve", "tensor_tensor", [bosb, bggt], [bob], out=fo_(ob), in0=fo_(osb), in1=fo_(ggt), op=ALU.mult)
                I("pool", "dma_start", [bob], [], out=D.go[qrow:qrow + 128, :], in_=fo_(ob))

        blks = list(range(64)) if blocks is None else list(blocks)
        if blks:
            I("dve", "memset", [], [bvnew], ap=vnew[:].rearrange("p h d -> p (h d)"), constant=0.0)
            I("dve", "memset", [], [bS], ap=S[:].rearrange("p h d -> p (h d)"), constant=0.0)
            I("dve", "memset", [], [bSb], ap=Sb[:].rearrange("p h d -> p (h d)"), constant=0.0)
            load(blks[0] * 128, 0)
            for n_, b in enumerate(blks):
                if n_ + 1 < len(blks):
                    load(blks[n_ + 1] * 128, (n_ + 1) % 2)
                want = o_all or b >= 32
                yield from process(n_ % 2, max(b * 128 - NP_, 0), want, False)
            I("sp", "dma_start", [bS], [], out=D.sfin[0].rearrange("h k v -> k h v"), in_=S[:])
        for q in samples:
            load(NP_ + NO_ + q * 128, q % 2)
            I("sp", "dma_start", [], [bS], out=S[:], in_=D.sgdn[q].rearrange("h k v -> k h v"))
            I("act", "activation", [bS], [bSb], out=Sb[:].rearrange("p h d -> p (h d)"), in_=S[:].rearrange("p h d -> p (h d)"), func=AF.Copy)
            yield from process(q % 2, NO_ + q * 128, True, True)
            I("sp", "dma_start", [bS], [], out=D.sfin[1 + q].rearrange("h k v -> k h v"), in_=S[:])


QTILES = [(i * 512, 4) for i in range(8)] + [(NO_, 2)]


def phase_d1(nc, fw, D, tiles=None):
    I = fw.I
    with ExitStack() as st:
        def sb(name, shape, dt):
            return st.enter_context(nc.sbuf_tensor("D1_" + name, shape, dt)), Buf(name)

        def pst(name, shape, dt):
            return st.enter_context(nc.psum_tensor("D1_" + name, shape, dt)), Buf(name)
        identb, bidentb = sb("identb", [128, 128], BF16)
        wpa, bwpa = sb("wpa", [128, 4, 1024], BF16)
        wpb, bwpb = sb("wpb", [128, 4, 1024], BF16)
        wout, bwout = sb("wout", [128, 8, 1024], BF16)
        gpost, bgpost = sb("gpost", [128, 1024], F32)
        foT, bfoT = sb("foT", [128, 4, 512], BF16)
        gob, bgob = sb("gob", [128, 4, 512], BF16)
        goT, bgoT = sb("goT", [128, 4, 512], BF16)
        sgA, bsgA = sb("sgA", [128, 8, 512], BF16)
        sgB, bsgB = sb("sgB", [128, 8, 512], BF16)
        t1s = Rot([sb(f"t1{i}", [128, 512], F32) for i in range(2)])
        t2s = Rot([sb(f"t2{i}", [128, 512], F32) for i in range(2)])
        mT, bmT = sb("mT", [128, 8, 512], BF16)
        xt, bxt = sb("xt", [128, 4, 1024], F32)
        mixes = Rot([sb(f"mix{i}", [128, 1024], F32) for i in range(2)])
        junk, bjunk = sb("junk", [128, 1024], BF16)
        sss = [sb(f"ss{i}", [128, 1], F32) for i in range(4)]
        y1, by1 = sb("y1", [128, 4, 1024], F32)
        pa = Rot([pst(f"pa{i}", [128, 512], F32) for i in range(2)])
        pb = Rot([pst(f"pb{i}", [128, 512], F32) for i in range(2)])
        pm = Rot([pst(f"pm{i}", [128, 512], F32) for i in range(2)])
        ptr, bptr = pst("ptr", [128, 1024], BF16)

        I("sp", "dma_start", [], [bidentb], out=identb[:], in_=D.cmask[:, 0, :])
        I("pool", "dma_start", [], [bwpa], out=wpa[:], in_=D.w_pa.rearrange("(c p) n -> p c n", p=128))
        I("pool", "dma_start", [], [bwpb], out=wpb[:], in_=D.w_pb.rearrange("(c p) n -> p c n", p=128))
        I("pool", "dma_start", [], [bwout], out=wout[:], in_=D.w_out.rearrange("(c p) n -> p c n", p=128))
        I("sp", "dma_start", [], [bgpost], out=gpost[:], in_=D.g_mix_post[0:1, :].broadcast_to([128, 1024]))
        tl = QTILES if tiles is None else [QTILES[i] for i in tiles]
        for (tq, ns) in tl:
            N = ns * 128
            I("sp", "dma_start", [], [bfoT], out=foT[:, :, 0:N], in_=D.foT[:, tq:tq + N].rearrange("(c p) n -> p c n", p=128))
            I("sp", "dma_start", [], [bgob], out=gob[:, 0:ns, :], in_=D.go[tq:tq + N, :].rearrange("(s p) n -> p s n", p=128))
            I("sp", "dma_start", [], [bsgA], out=sgA[:, :, 0:N], in_=D.sgA[:, tq:tq + N].rearrange("(c p) n -> p c n", p=128))
            I("sp", "dma_start", [], [bsgB], out=sgB[:, :, 0:N], in_=D.sgB[:, tq:tq + N].rearrange("(c p) n -> p c n", p=128))
            I("sp", "dma_start", [], [bxt], out=xt[:, 0:ns, :], in_=D.xall[NP_ + tq: NP_ + tq + N, :].rearrange("(s p) m -> p s m", p=128))
            for half in range(2):
                for cc in range(2):
                    c = half * 2 + cc
                    for s in range(ns):
                        I("pe", "transpose", [bgob, bidentb], [bptr], out=ptr[:, cc * 512 + s * 128: cc * 512 + (s + 1) * 128], in_=gob[:, s, c * 128:(c + 1) * 128],
                          identity=identb[:])
                for cc in range(2):
                    I("act", "activation", [bptr], [bgoT], out=goT[:, half * 2 + cc, 0:N], in_=ptr[:, cc * 512: cc * 512 + N], func=AF.Copy)
            for oc in range(8):
                p1, bp1 = pa.next()
                p2, bp2 = pb.next()
                for c in range(4):
                    I("pe", "matmul", [bwpa, bfoT], [bp1], out=p1[:, 0:N], lhsT=wpa[:, c, oc * 128:(oc + 1) * 128], rhs=foT[:, c, 0:N], start=(c == 0), stop=(c == 3))
                for c in range(4):
                    I("pe", "matmul", [bwpb, bgoT], [bp2], out=p2[:, 0:N], lhsT=wpb[:, c, oc * 128:(oc + 1) * 128], rhs=goT[:, c, 0:N], start=(c == 0), stop=(c == 3))
                t1, bt1 = t1s.next()
                t2, bt2 = t2s.next()
                I("dve", "tensor_tensor", [bp1, bsgA], [bt1], out=t1[:, 0:N], in0=p1[:, 0:N], in1=sgA[:, oc, 0:N], op=ALU.mult)
                I("dve", "tensor_tensor", [bp2, bsgB], [bt2], out=t2[:, 0:N], in0=p2[:, 0:N], in1=sgB[:, oc, 0:N], op=ALU.mult)
                I("pool", "tensor_tensor", [bt1, bt2], [bmT], out=mT[:, oc, 0:N], in0=t1[:, 0:N], in1=t2[:, 0:N], op=ALU.add)
            for s in range(ns):
                mix, bmix = mixes.next()
                ss, bss = sss[s]
                for cg in range(2):
                    p, bp = pm.next()
                    for kc in range(8):
                        I("pe", "matmul", [bmT, bwout], [bp], out=p[:, :], lhsT=mT[:, kc, s * 128:(s + 1) * 128], rhs=wout[:, kc, cg * 512:(cg + 1) * 512], start=(kc == 0), stop=(kc == 7))
                    if cg == 0:
                        I("act", "activation", [bp], [bmix], out=mix[:, 0:512], in_=p[:, :], func=AF.Copy)
                    else:
                        I("dve", "tensor_copy", [bp], [bmix], out=mix[:, 512:1024], in_=p[:, :])
                I("act", "activation", [bmix], [bjunk, bss], out=junk[:], in_=mix[:], func=AF.Square, accum_out=ss[:, 0:1])
                I("act", "activation", [bss], [bss], out=ss[:], in_=ss[:], func=AF.Sqrt, bias=EPS, scale=1.0 / 1024)
                I("dve", "reciprocal", [bss], [bss], out=ss[:], in_=ss[:])
                I("dve", "scalar_tensor_tensor", [bmix, bss, bgpost], [bmix], out=mix[:], in0=mix[:], scalar=ss[:, 0:1], in1=gpost[:], op0=ALU.mult, op1=ALU.mult)
                I("dve", "tensor_tensor", [bmix, bxt], [by1], out=y1[:, s, :], in0=mix[:], in1=xt[:, s, :], op=ALU.add)
            I("sp", "dma_start", [by1], [], out=D.y1[tq:tq + N, :].rearrange("(s p) m -> p s m", p=128), in_=y1[:, 0:ns, :])
        fw.barrier()
        fw.emit()


def phase_d2(nc, fw, D, tiles=None):
    I = fw.I
    with ExitStack() as st:
        def sb(name, shape, dt):
            return st.enter_context(nc.sbuf_tensor("D2_" + name, shape, dt)), Buf(name)

        def pst(name, shape, dt):
            return st.enter_context(nc.psum_tensor("D2_" + name, shape, dt)), Buf(name)
        identb, bidentb = sb("identb", [128, 128], BF16)
        wup, bwup = sb("wup", [128, 8, 4096], BF16)
        wdn, bwdn = sb("wdn", [128, 32, 1024], BF16)
        gpre, bgpre = sb("gpre", [128, 1024], F32)
        gpost, bgpost = sb("gpost", [128, 1024], F32)
        y1, by1 = sb("y1", [128, 4, 1024], F32)
        hbs = Rot([sb(f"hb{i}", [128, 1024], BF16) for i in range(2)])
        hT, bhT = sb("hT", [128, 8, 512], BF16)
        uT, buT = sb("uT", [128, 32, 512], BF16)
        rl = Rot([sb(f"rl{i}", [128, 512], F32) for i in range(2)])
        dsb, bdsb = sb("dsb", [128, 1024], F32)
        junk, bjunk = sb("junk", [128, 1024], BF16)
        ss, bss = sb("ss", [128, 4], F32)
        pu = Rot([pst(f"pu{i}", [128, 512], F32) for i in range(4)])
        pd = Rot([pst(f"pd{i}", [128, 512], F32) for i in range(2)])
        ptr = Rot([pst(f"ptr{i}", [128, 1024], BF16) for i in range(2)])

        I("sp", "dma_start", [], [bidentb], out=identb[:], in_=D.cmask[:, 0, :])
        for kc in range(8):
            I("pool", "dma_start", [], [bwup], out=wup[:, kc, :], in_=D.w_up[kc * 128:(kc + 1) * 128, :])
        for f4 in range(8):
            I("pool", "dma_start", [], [bwdn], out=wdn[:, f4 * 4:(f4 + 1) * 4, :], in_=D.w_down[f4 * 512:(f4 + 1) * 512, :].rearrange("(c p) n -> p c n", p=128))
        I("sp", "dma_start", [], [bgpre], out=gpre[:], in_=D.g_mlp_pre[0:1, :].broadcast_to([128, 1024]))
        I("sp", "dma_start", [], [bgpost], out=gpost[:], in_=D.g_mlp_post[0:1, :].broadcast_to([128, 1024]))
        tl = QTILES if tiles is None else [QTILES[i] for i in tiles]
        for (tq, ns) in tl:
            N = ns * 128
            I("sp", "dma_start", [], [by1], out=y1[:, 0:ns, :], in_=D.y1[tq:tq + N, :].rearrange("(s p) m -> p s m", p=128))
            for s in range(ns):
                I("act", "activation", [by1], [bjunk, bss], out=junk[:], in_=y1[:, s, :], func=AF.Square, accum_out=ss[:, s:s + 1])
            I("act", "activation", [bss], [bss], out=ss[:, 0:ns], in_=ss[:, 0:ns], func=AF.Sqrt, bias=EPS, scale=1.0 / 1024)
            I("dve", "reciprocal", [bss], [bss], out=ss[:, 0:ns], in_=ss[:, 0:ns])
            for s in range(ns):
                hb, bhb = hbs.next()
                I("dve", "scalar_tensor_tensor", [by1, bss, bgpre], [bhb], out=hb[:], in0=y1[:, s, :], scalar=ss[:, s:s + 1], in1=gpre[:], op0=ALU.mult, op1=ALU.mult)
                pt, bpt = ptr.next()
                for kc in range(8):
                    I("pe", "transpose", [bhb, bidentb], [bpt], out=pt[:, kc * 128:(kc + 1) * 128], in_=hb[:, kc * 128:(kc + 1) * 128], identity=identb[:])
                if s % 2 == 0:
                    I("act", "activation", [bpt], [bhT], out=hT[:, :, s * 128:(s + 1) * 128], in_=pt[:, :].rearrange("p (k t) -> p k t", t=128), func=AF.Copy)
                else:
                    I("dve", "tensor_copy", [bpt], [bhT], out=hT[:, :, s * 128:(s + 1) * 128], in_=pt[:, :].rearrange("p (k t) -> p k t", t=128))
            for fc in range(32):
                p, bp = pu.next()
                for kc in range(8):
                    I("pe", "matmul", [bwup, bhT], [bp], out=p[:, 0:N], lhsT=wup[:, kc, fc * 128:(fc + 1) * 128], rhs=hT[:, kc, 0:N], start=(kc == 0), stop=(kc == 7))
                r, br = rl.next()
                I("act", "activation", [bp], [br], out=r[:, 0:N], in_=p[:, 0:N], func=AF.Relu)
                eng = "pool" if fc % 2 == 0 else "dve"
                I(eng, "tensor_tensor", [br], [buT], out=uT[:, fc, 0:N], in0=r[:, 0:N], in1=r[:, 0:N], op=ALU.mult)
            for s in range(ns):
                for cg in range(2):
                    p, bp = pd.next()
                    for fc in range(32):
                        I("pe", "matmul", [buT, bwdn], [bp], out=p[:, :], lhsT=uT[:, fc, s * 128:(s + 1) * 128], rhs=wdn[:, fc, cg * 512:(cg + 1) * 512], start=(fc == 0), stop=(fc == 31))
                    if cg == 0:
                        I("act", "activation", [bp], [bdsb], out=dsb[:, 0:512], in_=p[:, :], func=AF.Copy)
                    else:
                        I("dve", "tensor_copy", [bp], [bdsb], out=dsb[:, 512:1024], in_=p[:, :])
                I("act", "activation", [bdsb], [bjunk, bss], out=junk[:], in_=dsb[:], func=AF.Square, accum_out=ss[:, s:s + 1])
                I("act", "activation", [bss], [bss], out=ss[:, s:s + 1], in_=ss[:, s:s + 1], func=AF.Sqrt, bias=EPS, scale=1.0 / 1024)
                I("dve", "reciprocal", [bss], [bss], out=ss[:, s:s + 1], in_=ss[:, s:s + 1])
                I("dve", "scalar_tensor_tensor", [bdsb, bss, bgpost], [bdsb], out=dsb[:], in0=dsb[:], scalar=ss[:, s:s + 1], in1=gpost[:], op0=ALU.mult, op1=ALU.mult)
                I("pool", "tensor_tensor", [bdsb, by1], [by1], out=y1[:, s, :], in0=dsb[:], in1=y1[:, s, :], op=ALU.add)
            I("sp", "dma_start", [by1], [], out=D.y[tq:tq + N, :].rearrange("(s p) m -> p s m", p=128), in_=y1[:, 0:ns, :])
        fw.barrier()
        fw.emit()


def phase_bc(nc, fw, D, ratio=2.7):
    with ExitStack() as st:
        banks = []
        for i in range(8):
            t = st.enter_context(nc.psum_tensor(f"BC_ps{i}", [128, 512], F32))
            banks.append((t, Buf(f"BC_ps{i}")))
        gb = phase_b_body(nc, fw, D, st, banks[0:4])
        gc = phase_c_body(nc, fw, D, st, banks[4:8])
        b_alive = c_alive = True
        acc = 0.0
        while b_alive or c_alive:
            if c_alive:
                try:
                    next(gc)
                except StopIteration:
                    c_alive = False
            acc += ratio if c_alive else 1.0
            while b_alive and acc >= 1.0:
                acc -= 1.0
                try:
                    next(gb)
                except StopIteration:
                    b_alive = False
        fw.barrier()
        fw.emit()


BF = ml_dtypes.bfloat16


def const_masks():
    p = np.arange(128)
    cm = np.zeros((128, 8, 128), np.float32)
    cm[:, 0, :] = np.eye(128)
    cm[:, 1, :] = (p[:, None] // 64 == p[None, :] // 64)
    NEG = -30000.0
    cm[:, 2, :] = np.where(p[None, :] >= p[:, None], 0.0, NEG)
    cm[:, 3, :] = np.where(p[None, :] > p[:, None], 0.0, NEG)
    cm[:, 4, :] = np.where(p[:, None] > p[None, :], 0.0, NEG)
    cm[:, 5, :] = 1.0
    same = (p[:, None] // 64 == p[None, :] // 64)
    cm[:, 6, :] = np.where(same & (p[None, :] >= p[:, None]), 0.0, NEG)
    cm[:, 7, :] = np.where(same & (p[None, :] > p[:, None]), 0.0, NEG)
    cf = np.zeros((128, 6, 128), np.float32)
    cf[:, 0, :] = np.eye(128)
    cf[:, 1, :] = (p[:, None] <= p[None, :])
    cf[:, 2, :] = 1.0
    cf[127, 3, :] = 1.0
    cf[:, 4, :] = (p[:, None] <= p[None, :]) & (p[:, None] // 64 == p[None, :] // 64)
    cf[63, 5, :] = 1.0
    return cm.astype(BF), cf


def prep_core(inp, c):
    b, half = c // 2, c % 2
    xp = inp["x_prompt"][b]
    xall = np.zeros((4096 + 4096 + 256, 1024), np.float32)
    kmask = np.zeros((128, 32), np.float32)
    if half == 1:
        xall[:8192] = xp
    else:
        xall[4096:8192] = xp[:4096]
        kmask[:] = -30000.0
    for q in range(2):
        xall[8192 + q * 128: 8192 + q * 128 + 16] = inp["x_sample"][2 * c + q]
    w_in = inp["w_in"][0]
    w_sm = np.concatenate([w_in[:, 1536:1544], w_in[:, 3080:3096]], axis=1)
    bias24 = np.concatenate([inp["fox_forget_bias"][0], inp["gdn_dt_bias"][0], np.zeros(8, np.float32)])[None, :]
    sgn24 = np.concatenate([-np.ones(8), np.ones(8), -np.ones(8)]).astype(np.float32)[None, :]
    convT = np.ascontiguousarray(inp["gdn_conv_w"][0].T.reshape(12, 128, 4).transpose(1, 0, 2))
    ch = inp["state_gdn_conv"][0, 2 * c: 2 * c + 2]
    conv_hist = np.ascontiguousarray(ch.transpose(0, 2, 1).reshape(2, 12, 128, 3).transpose(0, 2, 1, 3))
    cm, cf = const_masks()
    d = {
        "xall": xall, "kmask": kmask, "vmask": (np.arange(128) < 16).astype(np.float32)[:, None], "w_in": w_in, "w_sm": np.ascontiguousarray(w_sm), "bias24": bias24.astype(np.float32),
        "sgn24": sgn24, "a_log": inp["gdn_a_log"], "convT": convT, "conv_hist": conv_hist,
        "g_mix_pre": inp["norm_mix_pre"], "g_mix_post": inp["norm_mix_post"], "g_mlp_pre": inp["norm_mlp_pre"], "g_mlp_post": inp["norm_mlp_post"],
        "g_gdn": inp["gdn_norm_g"], "w_pa": inp["w_proj_fox"][0], "w_pb": inp["w_proj_gdn"][0], "w_out": inp["w_out"][0],
        "w_up": inp["w_up"][0], "w_down": inp["w_down"][0],
        "ck": inp["cache_fox_k"][0, 2 * c:2 * c + 2].reshape(2, 2048, 512), "cv": inp["cache_fox_v"][0, 2 * c:2 * c + 2].reshape(2, 2048, 512),
        "clf": inp["cache_fox_logf"][0, 2 * c:2 * c + 2], "sgdn": inp["state_gdn"][0, 2 * c:2 * c + 2],
        "cmask": cm, "cf32": cf,
    }
    return {k: np.ascontiguousarray(v) for k, v in d.items()}


def build_program():
    nc = bass.Bass("TRN2", target_bir_lowering=False)
    with ExitStack() as st:
        D = declare(nc, False)
        fw = FW(nc, st)
        phase_a(nc, fw, D)
        phase_b(nc, fw, D)
        phase_c(nc, fw, D)
        phase_d1(nc, fw, D)
        phase_d2(nc, fw, D)
    return nc


def kernel(**inp):
    inp = {k: np.asarray(v) for k, v in inp.items()}
    nc = build_program()
    in_maps = [prep_core(inp, c) for c in range(8)]
    res = run_bass_kernel_spmd(nc, in_maps, core_ids=list(range(8)))
    R = res.results
    f = lambda a: np.asarray(a, dtype=np.float32)
    y_p = np.zeros((4, 8192, 1024), np.float32)
    y_s = np.zeros((16, 16, 1024), np.float32)
    fk_p = np.zeros((1, 4, 8192, 8, 64), np.float32)
    fv_p = np.zeros_like(fk_p)
    lf_p = np.zeros((1, 4, 8192, 8), np.float32)
    sg_p = np.zeros((1, 4, 8, 64, 64), np.float32)
    cv_p = np.zeros((1, 4, 3, 1536), np.float32)
    fk_s = np.zeros((1, 16, 16, 8, 64), np.float32)
    fv_s = np.zeros_like(fk_s)
    lf_s = np.zeros((1, 16, 16, 8), np.float32)
    sg_s = np.zeros((1, 16, 8, 64, 64), np.float32)
    cv_s = np.zeros((1, 16, 3, 1536), np.float32)
    for c in range(8):
        b, half = c // 2, c % 2
        r = R[c]
        sl = slice(half * 4096, (half + 1) * 4096)
        y_p[b, sl] = f(r["y"])[:4096]
        fk_p[0, b, sl] = f(r["fk"])[:4096].reshape(4096, 8, 64)
        fv_p[0, b, sl] = f(r["fv"])[:4096].reshape(4096, 8, 64)
        lf_p[0, b, sl] = f(r["lf"])[:4096]
        if half == 1:
            sg_p[0, b] = f(r["sfin"])[0]
            cv_p[0, b] = f(r["convo"])[0]
        for q in range(2):
            s = 2 * c + q
            rows = slice(4096 + q * 128, 4096 + q * 128 + 16)
            y_s[s] = f(r["y"])[rows]
            fk_s[0, s] = f(r["fk"])[rows].reshape(16, 8, 64)
            fv_s[0, s] = f(r["fv"])[rows].reshape(16, 8, 64)
            lf_s[0, s] = f(r["lf"])[rows]
            sg_s[0, s] = f(r["sfin"])[1 + q]
            cv_s[0, s] = f(r["convo"])[1 + q]
    return (y_p, y_s, fk_p, fv_p, lf_p, sg_p, cv_p, fk_s, fv_s, lf_s, sg_s, cv_s)
```

```python
from contextlib import ExitStack
import numpy as np
import ml_dtypes
from concourse.bass_utils import run_bass_kernel_spmd
import concourse.bass as bass
import concourse.mybir as mybir

F32 = mybir.dt.float32
BF16 = mybir.dt.bfloat16
AF = mybir.ActivationFunctionType
ALU = mybir.AluOpType
AX = mybir.AxisListType

ENGS = ("pe", "act", "dve", "pool", "sp")
EPOCH = 30000
NDMASEM = 30
DMA_ENGS = ("sp", "pool")


class Buf:
    __slots__ = ("name", "w", "r")

    def __init__(self, name):
        self.name = name
        self.w = None
        self.r = {}


class FW:
    def __init__(self, nc, stack):
        self.nc = nc
        self.stack = stack
        self.ops = {e: [] for e in ENGS}
        self.cnt = {e: 0 for e in ENGS}
        self.ccnt = {e: 0 for e in ENGS}
        self.sems = {}
        self.dsem = {}
        self.dcnt = {}
        self.drr = {e: 0 for e in ENGS}
        self.waited = {e: {} for e in ENGS}
        for e in DMA_ENGS:
            for i in range(NDMASEM):
                self.dsem[(e, i)] = stack.enter_context(nc.semaphore(f"d_{e}_{i}"))
                self.dcnt[(e, i)] = 0
        self.nsem_ep = {e: 0 for e in ENGS}
        self.pending = {e: [] for e in ENGS}

    def _sem(self, eng, ep):
        k = (eng, ep)
        if k not in self.sems:
            self.sems[k] = self.stack.enter_context(self.nc.semaphore(f"c_{eng}_{ep}"))
        return self.sems[k]

    def _need(self, eng, tok, waits):
        if tok is None:
            return
        key, val = tok
        if eng == "pe" and key[0] == "c" and key[1] == "pe":
            return
        cur = self.waited[eng].get(key, 0)
        if cur >= val:
            return
        self.waited[eng][key] = val
        waits[key] = max(waits.get(key, 0), val)

    def op(self, eng, fn, reads=(), writes=(), dma=False):
        waits = {}
        if self.pending[eng]:
            for t in self.pending[eng]:
                self._need(eng, t, waits)
            self.pending[eng] = []
        for b in reads:
            self._need(eng, b.w, waits)
        for b in writes:
            self._need(eng, b.w, waits)
            for t in b.r.items():
                self._need(eng, t, waits)
        if dma:
            i = self.drr[eng] % NDMASEM
            self.drr[eng] += 1
            if self.dcnt[(eng, i)] > 0:
                self._need(eng, (("d", eng, i), self.dcnt[(eng, i)]), waits)
            self.dcnt[(eng, i)] += 16
            key = ("d", eng, i)
            tok = (key, self.dcnt[(eng, i)])
            inc = (self.dsem[(eng, i)], 16)
        else:
            n = self.ccnt[eng]
            self.ccnt[eng] += 1
            ep = n // EPOCH
            key = ("c", eng, ep)
            tok = (key, n % EPOCH + 1)
            inc = (self._sem(eng, ep), 1)
        self.cnt[eng] += 1
        for b in writes:
            b.w = tok
            b.r = {}
        for b in reads:
            if b not in writes:
                b.r[tok[0]] = max(b.r.get(tok[0], 0), tok[1])
        self.ops[eng].append((waits, fn, inc))
        return tok

    def I(self, eng, method, reads=(), writes=(), **kw):
        dma = method == "dma_start"
        return self.op(eng, lambda e: getattr(e, method)(**kw), reads=reads, writes=writes, dma=dma)

    def semh(self, key):
        if key[0] == "d":
            return self.dsem[(key[1], key[2])]
        return self._sem(key[1], key[2])

    def barrier(self):
        toks = []
        for e in ENGS:
            n = self.ccnt[e]
            if n > 0:
                ep = (n - 1) // EPOCH
                toks.append((("c", e, ep), (n - 1) % EPOCH + 1))
                for ep2 in range(ep):
                    toks.append((("c", e, ep2), EPOCH))
            for i in range(NDMASEM):
                if e in DMA_ENGS and self.dcnt[(e, i)] > 0:
                    toks.append((("d", e, i), self.dcnt[(e, i)]))
        for e in ENGS:
            if self.ops[e]:
                waits = {}
                for t in toks:
                    self._need(e, t, waits)
                if waits:
                    self.ops[e].append((waits, None, None))
            else:
                self.pending[e] = list(toks)

    def emit(self):
        nc = self.nc
        with nc.Block() as block:
            def mk(eng_name):
                def body(e):
                    for waits, fn, inc in self.ops[eng_name]:
                        for key, val in waits.items():
                            e.wait_ge(self.semh(key), val)
                        if fn is not None:
                            ins = fn(e)
                            ins.then_inc(inc[0], inc[1])
                return body
            regs = {"pe": block.tensor, "act": block.scalar, "dve": block.vector, "pool": block.gpsimd, "sp": block.sync}
            for en in ENGS:
                if self.ops[en]:
                    regs[en](mk(en))
        self.ops = {e: [] for e in ENGS}


NP_ = 4096
NO_ = 4096
NS_ = 256
NT_ = NP_ + NO_ + NS_
NQ_ = NO_ + NS_
EPS = 1e-6
C_FQ, C_FK, C_FV, C_FF, C_GQ, C_GA, C_GB, C_GG, C_A, C_B = 0, 512, 1024, 1536, 1544, 3080, 3088, 3096, 3608, 4632


class Ctx:
    pass


def declare(nc, debug, ext_in=()):
    D = Ctx()
    ei = lambda n, s, dt=F32: nc.dram_tensor(n, s, dt, kind="ExternalInput").ap()
    eo = lambda n, s, dt=F32: nc.dram_tensor(n, s, dt, kind="ExternalOutput").ap()
    sc = lambda n, s, dt=F32: nc.dram_tensor(n, s, dt, kind=("ExternalInput" if n in ext_in else ("ExternalOutput" if debug else "Internal"))).ap()
    D.xall = ei("xall", [NT_, 1024])
    D.kmask = ei("kmask", [128, 32])
    D.vmask = ei("vmask", [128, 1])
    D.w_in = ei("w_in", [1024, 5656])
    D.w_sm = ei("w_sm", [1024, 24])
    D.bias24 = ei("bias24", [1, 24])
    D.sgn24 = ei("sgn24", [1, 24])
    D.a_log = ei("a_log", [1, 8])
    D.convT = ei("convT", [128, 12, 4])
    D.conv_hist = ei("conv_hist", [2, 128, 12, 3])
    D.g_mix_pre = ei("g_mix_pre", [1, 1024])
    D.g_mix_post = ei("g_mix_post", [1, 1024])
    D.g_mlp_pre = ei("g_mlp_pre", [1, 1024])
    D.g_mlp_post = ei("g_mlp_post", [1, 1024])
    D.g_gdn = ei("g_gdn", [1, 64])
    D.w_pa = ei("w_pa", [512, 1024])
    D.w_pb = ei("w_pb", [512, 1024])
    D.w_out = ei("w_out", [1024, 1024])
    D.w_up = ei("w_up", [1024, 4096])
    D.w_down = ei("w_down", [4096, 1024])
    D.ck = ei("ck", [2, 2048, 512])
    D.cv = ei("cv", [2, 2048, 512])
    D.clf = ei("clf", [2, 2048, 8])
    D.sgdn = ei("sgdn", [2, 8, 64, 64])
    D.cmask = ei("cmask", [128, 8, 128], BF16)
    D.cf32 = ei("cf32", [128, 6, 128])
    D.y = eo("y", [NQ_, 1024])
    D.fk = eo("fk", [NQ_, 512])
    D.fv = eo("fv", [NQ_, 512])
    D.lf = eo("lf", [NQ_, 8])
    D.sfin = eo("sfin", [3, 8, 64, 64])
    D.convo = eo("convo", [3, 3, 1536])
    D.qT = sc("qT_s", [512, NQ_], BF16)
    D.kT = sc("kT_s", [512, NT_], BF16)
    D.V = sc("V_s", [NT_, 512], BF16)
    D.lfs = sc("lf_s", [NT_, 8])
    D.g = sc("g_s", [NT_, 8])
    D.beta = sc("beta_s", [NT_, 16])
    D.gqT = sc("gqT_s", [512, NT_], BF16)
    D.gkT = sc("gkT_s", [512, NT_], BF16)
    D.gqkv = sc("gqkv_s", [NT_, 1536], BF16)
    D.gg = sc("gg_s", [NQ_, 512])
    D.sgA = sc("sgA_s", [1024, NQ_], BF16)
    D.sgB = sc("sgB_s", [1024, NQ_], BF16)
    D.foT = sc("foT_s", [512, NQ_], BF16)
    D.go = sc("go_s", [NQ_, 512], BF16)
    D.y1 = sc("y1_s", [NQ_, 1024])
    return D


class Rot:
    def __init__(self, items):
        self.items = items
        self.i = 0

    def next(self):
        it = self.items[self.i % len(self.items)]
        self.i += 1
        return it


def phase_a(nc, fw, D, tiles=None):
    I = fw.I
    with ExitStack() as st:
        def sb(name, shape, dt):
            return st.enter_context(nc.sbuf_tensor("A_" + name, shape, dt)), Buf(name)

        def pst(name, shape, dt):
            return st.enter_context(nc.psum_tensor("A_" + name, shape, dt)), Buf(name)
        Win, bWin = sb("Win", [128, 8, 5656], BF16)
        Wsm, bWsm = sb("Wsm", [128, 8, 24], BF16)
        xts = Rot([sb(f"xt{i}", [128, 4, 1024], F32) for i in range(2)])
        hb, bhb = sb("hb", [128, 4, 1024], BF16)
        hTs = Rot([sb(f"hT{i}", [128, 8, 512], BF16) for i in range(1)])
        junk, bjunk = sb("junk", [128, 1024], BF16)
        ss, bss = sb("ss", [128, 4], F32)
        rs, brs = sb("rs", [128, 4], F32)
        gpre, bgpre = sb("gpre", [128, 1024], F32)
        identb, bidentb = sb("identb", [128, 128], BF16)
        diagw, bdiagw = sb("diagw", [128, 48, 128], BF16)
        cwT, bcwT = sb("cwT", [128, 12, 4], F32)
        xg, bxg = sb("xg", [128, 12, 515], BF16)
        cT, bcT = sb("cT", [128, 12, 512], BF16)
        sbf = Rot([sb(f"sbf{i}", [128, 512], BF16) for i in range(4)])
        sf32 = Rot([sb(f"sf32{i}", [128, 512], F32) for i in range(2)])
        tts = Rot([sb(f"tt{i}", [128, 512], BF16) for i in range(3)])
        sqt, bsqt = sb("sqt", [128, 1024], BF16)
        l2ss, bl2ss = sb("l2ss", [128, 4, 16], F32)
        tball, btball = sb("tball", [128, 4, 1536], BF16)
        sm_t, bsm_t = sb("sm_t", [128, 96], F32)
        sm_e, bsm_e = sb("sm_e", [128, 96], F32)
        sm_l, bsm_l = sb("sm_l", [128, 96], F32)
        sm_o, bsm_o = sb("sm_o", [128, 4, 32], F32)
        b24, bb24 = sb("b24", [128, 24], F32)
        s24, bs24 = sb("s24", [128, 24], F32)
        nea, bnea = sb("nea", [128, 8], F32)
        ptr = Rot([pst(f"ptr{i}", [128, 1024], BF16) for i in range(2)])
        pm = Rot([pst(f"pm{i}", [128, 512], F32) for i in range(4)])
        psm, bpsm = pst("psm", [128, 512], F32)

        I("pool", "dma_start", [], [bWin], out=Win[:], in_=D.w_in.rearrange("(c p) n -> p c n", p=128))
        I("pool", "dma_start", [], [bWsm], out=Wsm[:], in_=D.w_sm.rearrange("(c p) n -> p c n", p=128))
        I("sp", "dma_start", [], [bgpre], out=gpre[:], in_=D.g_mix_pre[0:1, :].broadcast_to([128, 1024]))
        I("sp", "dma_start", [], [bidentb], out=identb[:], in_=D.cmask[:, 0, :])
        I("sp", "dma_start", [], [bcwT], out=cwT[:], in_=D.convT[:, :, :])
        I("sp", "dma_start", [], [bb24], out=b24[:], in_=D.bias24[0:1, :].broadcast_to([128, 24]))
        I("sp", "dma_start", [], [bs24], out=s24[:], in_=D.sgn24[0:1, :].broadcast_to([128, 24]))
        I("sp", "dma_start", [], [bnea], out=nea[:], in_=D.a_log[0:1, :].broadcast_to([128, 8]))
        I("act", "activation", [bnea], [bnea], out=nea[:], in_=nea[:], func=AF.Exp)
        I("dve", "tensor_scalar_mul", [bnea], [bnea], out=nea[:], in0=nea[:], scalar1=-1.0)
        I("dve", "tensor_scalar_mul", [bcwT], [bcwT], out=cwT[:], in0=cwT[:], scalar1=0.5)
        for j in range(12):
            for i in range(4):
                I("dve", "tensor_scalar_mul", [bidentb, bcwT], [bdiagw], out=diagw[:, j * 4 + i, :], in0=identb[:], scalar1=cwT[:, j, i:i + 1])
        I("dve", "memset", [], [bxg], ap=xg[:, :, 0:3], constant=0.0)

        all_tiles = [(i * 512, 4, "p") for i in range(8)] + [(NP_ + i * 512, 4, "o") for i in range(8)] + [(NP_ + NO_, 2, "s")]
        if tiles is not None:
            all_tiles = [all_tiles[i] for i in tiles]
        cpy_rr = [0]

        def evac_copy(out_ap, in_ap, reads, writes, scale=None):
            cpy_rr[0] += 1
            if scale is not None:
                I("act", "activation", reads, writes, out=out_ap, in_=in_ap, func=AF.Copy, scale=scale)
            elif cpy_rr[0] % 2 == 0:
                I("act", "activation", reads, writes, out=out_ap, in_=in_ap, func=AF.Copy)
            else:
                I("dve", "tensor_copy", reads, writes, out=out_ap, in_=in_ap)

        def store(eng, out_ap, in_ap, buf):
            I(eng, "dma_start", [buf], [], out=out_ap, in_=in_ap)

        for (t0, ns, kind) in all_tiles:
            N = ns * 128
            own = kind in ("o", "s")
            tq = t0 - NP_
            xt, bxt = xts.next()
            hT, bhT = hTs.next()
            I("sp", "dma_start", [], [bxt], out=xt[:, 0:ns, :], in_=D.xall[t0:t0 + N, :].rearrange("(s p) m -> p s m", p=128))
            for s in range(ns):
                I("act", "activation", [bxt], [bjunk, bss], out=junk[:], in_=xt[:, s, :], func=AF.Square, accum_out=ss[:, s:s + 1])
            I("act", "activation", [bss], [brs], out=rs[:, 0:ns], in_=ss[:, 0:ns], func=AF.Sqrt, bias=EPS, scale=1.0 / 1024)
            I("dve", "reciprocal", [brs], [brs], out=rs[:, 0:ns], in_=rs[:, 0:ns])
            for s in range(ns):
                I("dve", "scalar_tensor_tensor", [bxt, brs, bgpre], [bhb], out=hb[:, s, :], in0=xt[:, s, :], scalar=rs[:, s:s + 1], in1=gpre[:],
                  op0=ALU.mult, op1=ALU.mult)
            for kcp in range(4):
                pt, bpt = ptr.next()
                for kk in range(2):
                    kc = 2 * kcp + kk
                    for s in range(ns):
                        I("pe", "transpose", [bhb, bidentb], [bpt], out=pt[:, kk * 512 + s * 128: kk * 512 + (s + 1) * 128],
                          in_=hb[:, s, kc * 128:(kc + 1) * 128], identity=identb[:])
                for kk in range(2):
                    evac_copy(hT[:, 2 * kcp + kk, 0:N], pt[:, kk * 512: kk * 512 + N], [bpt], [bhT])

            def fm_mm(c0, w=128):
                p, bp = pm.next()
                for kc in range(8):
                    I("pe", "matmul", [bWin, bhT], [bp], out=p[0:w, 0:N], lhsT=Win[:, kc, c0:c0 + w], rhs=hT[:, kc, 0:N], start=(kc == 0), stop=(kc == 7))
                return p, bp

            def tm_mm(s, c0, w, Wt=None, bW=None, out=None):
                if out is None:
                    p, bp = pm.next()
                    o = p[:, 0:w]
                else:
                    p, bp, o = out
                Wt_ = Win if Wt is None else Wt
                bW_ = bWin if bW is None else bW
                for kc in range(8):
                    I("pe", "matmul", [bW_, bhT], [bp], out=o, lhsT=hT[:, kc, s * 128:(s + 1) * 128], rhs=Wt_[:, kc, c0:c0 + w], start=(kc == 0), stop=(kc == 7))
                return p, bp

            fillers = []

            def do_fq(j):
                p, bp = fm_mm(C_FQ + j * 128)
                s_, bs_ = sbf.next()
                evac_copy(s_[:, 0:N], p[:, 0:N], [bp], [bs_], scale=0.125)
                store("sp", D.qT[j * 128:(j + 1) * 128, tq:tq + N], s_[:, 0:N], bs_)

            def do_fk(j):
                p, bp = fm_mm(C_FK + j * 128)
                s_, bs_ = sbf.next()
                evac_copy(s_[:, 0:N], p[:, 0:N], [bp], [bs_])
                store("sp", D.kT[j * 128:(j + 1) * 128, t0:t0 + N], s_[:, 0:N], bs_)

            def do_gate(j):
                p, bp = fm_mm(C_A + j * 128)
                s_, bs_ = sbf.next()
                tt, btt = tts.next()
                I("act", "activation", [bp], [btt], out=tt[:, 0:N], in_=p[:, 0:N], func=AF.Tanh, scale=0.5)
                I("dve", "tensor_scalar", [btt], [bs_], out=s_[:, 0:N], in0=tt[:, 0:N], scalar1=0.5, scalar2=0.5, op0=ALU.mult, op1=ALU.add)
                dst = D.sgA if j < 8 else D.sgB
                store("sp", dst[(j % 8) * 128:(j % 8 + 1) * 128, tq:tq + N], s_[:, 0:N], bs_)

            def do_tm_k(s):
                r0 = t0 + s * 128
                p, bp = tm_mm(s, C_FK, 512)
                sf, bsf = sf32.next()
                evac_copy(sf[:], p[:], [bp], [bsf])
                store("sp", D.fk[r0 - NP_: r0 - NP_ + 128, :], sf[:], bsf)

            def do_tm_v(s):
                r0 = t0 + s * 128
                p, bp = tm_mm(s, C_FV, 512)
                s_, bs_ = sbf.next()
                if own:
                    sf, bsf = sf32.next()
                    I("dve", "tensor_copy", [bp], [bsf], out=sf[:], in_=p[:])
                    store("sp", D.fv[r0 - NP_: r0 - NP_ + 128, :], sf[:], bsf)
                    I("act", "activation", [bsf], [bs_], out=s_[:], in_=sf[:], func=AF.Copy)
                else:
                    I("dve", "tensor_copy", [bp], [bs_], out=s_[:], in_=p[:])
                store("sp", D.V[r0:r0 + 128, :], s_[:], bs_)

            def do_tm_gg(s):
                r0 = t0 + s * 128
                p, bp = tm_mm(s, C_GG, 512)
                sf, bsf = sf32.next()
                tt, btt = tts.next()
                I("act", "activation", [bp], [btt], out=tt[:], in_=p[:], func=AF.Tanh, scale=0.5)
                I("dve", "scalar_tensor_tensor", [btt, bp], [bsf], out=sf[:], in0=tt[:], scalar=1.0, in1=p[:], op0=ALU.add, op1=ALU.mult)
                store("sp", D.gg[r0 - NP_: r0 - NP_ + 128, :], sf[:], bsf)

            def do_tm_small(s):
                tm_mm(s, 0, 24, Wt=Wsm, bW=bWsm, out=(psm, bpsm, psm[:, s * 24:(s + 1) * 24]))

            if own:
                for j in range(4):
                    fillers.append((do_fq, j))
            for j in range(4):
                fillers.append((do_fk, j))
            for s in range(ns):
                if own:
                    fillers.append((do_tm_k, s))
                fillers.append((do_tm_v, s))
                if own:
                    fillers.append((do_tm_gg, s))
                fillers.append((do_tm_small, s))
            if own:
                for j in range(16):
                    fillers.append((do_gate, j))
            per_step = (len(fillers) + 11) // 12

            def run_fillers(n):
                for _ in range(n):
                    if fillers:
                        f_, a_ = fillers.pop(0)
                        f_(a_)

            for j in range(12):
                p, bp = fm_mm(C_GQ + j * 128)
                evac_copy(xg[:, j, 3:3 + N], p[:, 0:N], [bp], [bxg])
            if kind == "s":
                for q_ in range(2):
                    I("pool", "dma_start", [], [bxg], out=xg[:, :, q_ * 128: q_ * 128 + 3], in_=D.conv_hist[q_])

            for j in range(12):
                p, bp = pm.next()
                for i in range(4):
                    I("pe", "matmul", [bdiagw, bxg], [bp], out=p[:, 0:N], lhsT=diagw[:, j * 4 + i, :], rhs=xg[:, j, i:i + N], start=(i == 0), stop=(i == 3))
                tt, btt = tts.next()
                I("act", "activation", [bp], [btt], out=tt[:, 0:N], in_=p[:, 0:N], func=AF.Tanh)
                I("dve", "scalar_tensor_tensor", [btt, bp], [bcT], out=cT[:, j, 0:N], in0=tt[:, 0:N], scalar=1.0, in1=p[:, 0:N], op0=ALU.add, op1=ALU.mult)
                run_fillers(per_step)
            run_fillers(len(fillers))
            I("dve", "tensor_copy", [bxg], [bxg], out=xg[:, :, 0:3], in_=xg[:, :, N:N + 3])
            for s in range(ns):
                for half in range(2):
                    pt, bpt = ptr.next()
                    nj = 8 if half == 0 else 4
                    for jj in range(nj):
                        j = half * 8 + jj
                        I("pe", "transpose", [bcT, bidentb], [bpt], out=pt[:, jj * 128:(jj + 1) * 128], in_=cT[:, j, s * 128:(s + 1) * 128], identity=identb[:])
                    evac_copy(tball[:, s, half * 1024: half * 1024 + nj * 128], pt[:, 0:nj * 128], [bpt], [btball])
                I("dve", "tensor_tensor", [btball], [bsqt], out=sqt[:], in0=tball[:, s, 0:1024], in1=tball[:, s, 0:1024], op=ALU.mult)
                I("dve", "tensor_reduce", [bsqt], [bl2ss], out=l2ss[:, s, :], in_=sqt[:].rearrange("p (h d) -> p h d", d=64), axis=AX.X, op=ALU.add)
            I("act", "activation", [bl2ss], [bl2ss], out=l2ss[:, 0:ns, :], in_=l2ss[:, 0:ns, :], func=AF.Sqrt, bias=EPS)
            I("dve", "reciprocal", [bl2ss], [bl2ss], out=l2ss[:, 0:ns, :], in_=l2ss[:, 0:ns, :])
            I("dve", "tensor_scalar_mul", [bl2ss], [bl2ss], out=l2ss[:, 0:ns, 0:8], in0=l2ss[:, 0:ns, 0:8], scalar1=0.125)
            for s in range(ns):
                qk = tball[:, s, 0:1024].rearrange("p (h d) -> p h d", d=64)
                I("dve", "tensor_tensor", [btball, bl2ss], [btball], out=qk, in0=qk, in1=l2ss[:, s, :].unsqueeze(2).broadcast_to([128, 16, 64]), op=ALU.mult)
                store("sp", D.gqkv[t0 + s * 128: t0 + (s + 1) * 128, :], tball[:, s, :], btball)
            n24 = ns * 24
            for s in range(ns):
                I("dve", "tensor_tensor", [bpsm, bb24], [bsm_t], out=sm_t[:, s * 24:(s + 1) * 24], in0=psm[:, s * 24:(s + 1) * 24], in1=b24[:], op=ALU.add)
                I("dve", "tensor_tensor", [bsm_t, bs24], [bsm_t], out=sm_t[:, s * 24:(s + 1) * 24], in0=sm_t[:, s * 24:(s + 1) * 24], in1=s24[:], op=ALU.mult)
            I("act", "activation", [bsm_t], [bsm_e], out=sm_e[:, 0:n24], in_=sm_t[:, 0:n24], func=AF.Exp)
            I("act", "activation", [bsm_e], [bsm_l], out=sm_l[:, 0:n24], in_=sm_e[:, 0:n24], func=AF.Ln, bias=1.0)
            for s in range(ns):
                I("dve", "tensor_scalar_mul", [bsm_l], [bsm_o], out=sm_o[:, s, 0:8], in0=sm_l[:, s * 24: s * 24 + 8], scalar1=-1.0)
                I("dve", "tensor_tensor", [bsm_l, bnea], [bsm_o], out=sm_o[:, s, 8:16], in0=sm_l[:, s * 24 + 8: s * 24 + 16], in1=nea[:], op=ALU.mult)
                I("dve", "tensor_scalar_add", [bsm_e], [bsm_o], out=sm_o[:, s, 16:24], in0=sm_e[:, s * 24 + 16: s * 24 + 24], scalar1=1.0)
                I("dve", "reciprocal", [bsm_o], [bsm_o], out=sm_o[:, s, 16:24], in_=sm_o[:, s, 16:24])
                I("dve", "tensor_scalar_mul", [bsm_l], [bsm_o], out=sm_o[:, s, 24:32], in0=sm_l[:, s * 24 + 16: s * 24 + 24], scalar1=-1.0)
            for (dst, c0, cw_) in ((D.lfs, 0, 8), (D.g, 8, 8), (D.beta, 16, 16)):
                I("sp", "dma_start", [bsm_o], [], out=dst[t0:t0 + N, :].rearrange("(s p) m -> p s m", p=128), in_=sm_o[:, 0:ns, c0:c0 + cw_])
            if own:
                I("sp", "dma_start", [bsm_o], [], out=D.lf[tq:tq + N, :].rearrange("(s p) m -> p s m", p=128), in_=sm_o[:, 0:ns, 0:8])
            conv_rows = []
            if kind == "o" and t0 == NP_ + NO_ - 512:
                conv_rows = [(3, 125, 0)]
            if kind == "s":
                conv_rows = [(0, 13, 1), (1, 13, 2)]
            for (s, r, oi) in conv_rows:
                for cg in range(3):
                    p, bp = tm_mm(s, C_GQ + cg * 512, 512)
                    sf, bsf = sf32.next()
                    evac_copy(sf[:], p[:], [bp], [bsf])
                    store("sp", D.convo[oi, :, cg * 512:(cg + 1) * 512], sf[r:r + 3, :], bsf)
        fw.barrier()
        fw.emit()


def phase_b(nc, fw, D, pairs=(0, 1, 2, 3), qtiles=tuple(range(8)), samples=(0, 1)):
    with ExitStack() as st:
        for _ in phase_b_body(nc, fw, D, st, None, pairs, qtiles, samples):
            pass
        fw.barrier()
        fw.emit()


def phase_b_body(nc, fw, D, st, banks, pairs=(0, 1, 2, 3), qtiles=tuple(range(8)), samples=(0, 1)):
    I = fw.I
    if True:
        def sb(name, shape, dt):
            return st.enter_context(nc.sbuf_tensor("B_" + name, shape, dt)), Buf(name)

        def pst(name, shape, dt):
            return st.enter_context(nc.psum_tensor("B_" + name, shape, dt)), Buf(name)
        identb, bidentb = sb("identb", [128, 128], BF16)
        mincl, bmincl = sb("mincl", [128, 128], BF16)
        cf, bcf = sb("cf", [128, 4, 128], F32)
        kmask, bkmask = sb("kmask", [128, 32], F32)
        lf, blf = sb("lf", [128, 64, 8], F32)
        Fin, bFin = sb("Fin", [128, 64, 8], F32)
        tot, btot = sb("tot", [128, 64, 8], F32)
        scA, bscA = sb("scA", [128, 64, 8], F32)
        scB, bscB = sb("scB", [128, 64, 8], F32)
        offs, boffs = sb("offs", [128, 64, 8], F32)
        Fm, bFm = sb("Fm", [128, 64, 8], F32)
        cfac, bcfac = sb("cfac", [128, 64, 8], F32)
        psets = [(sb(f"kTp{i}", [128, 8192], BF16), sb(f"qTp{i}", [128, 4096], BF16), sb(f"Vx{i}", [128, 64, 2, 65], BF16)) for i in range(2)]
        boffs_ = [sb(f"boff{i}", [128, 64], F32) for i in range(2)]
        bdgs_ = [sb(f"bdg{i}", [128, 16], F32) for i in range(2)]
        osbs_ = [sb(f"osb{i}", [65, 512], F32) for i in range(2)]
        PTs = Rot([sb(f"PT{i}", [128, 512], BF16) for i in range(6)])
        rsbs = Rot([sb(f"rsb{i}", [65, 512], F32) for i in range(2)])
        rrecs = Rot([sb(f"rrec{i}", [65, 512], F32) for i in range(2)])
        fos = Rot([sb(f"fo{i}", [64, 512], BF16) for i in range(2)])
        kc_tm, bkc_tm = sb("kc_tm", [128, 16, 512], BF16)
        if banks is None:
            LA = 2
            pS = Rot([pst(f"pS{i}", [128, 512], F32) for i in range(LA + 1)])
            pOos = [pst(f"pOo{i}", [128, 512], F32) for i in range(2)]
            pOds = [pst(f"pOd{i}", [128, 512], F32) for i in range(2)]
            pB_, bpB_ = pst("pB", [128, 512], F32)
            getB = lambda: (pB_, bpB_)
            getT = lambda: (pB_[:].bitcast(BF16), bpB_)
            getJ = lambda: (pB_, bpB_)
        else:
            pS = Rot(list(banks[0:2]))
            pOos = [banks[2], banks[2]]
            pOds = [banks[3], banks[3]]
            LA = 1
            getB = lambda: pS.next()
            getJ = lambda: pS.next()

            def getT():
                t_, b_ = pS.next()
                return t_[:].bitcast(BF16), b_
        jk, bjk = sb("jk", [128, 512], BF16)
        I("dve", "memset", [], [bjk], ap=jk[:], constant=0.0)
        JN = 0
        JB = 12

        I("sp", "dma_start", [], [bidentb], out=identb[:], in_=D.cmask[:, 0, :])
        I("sp", "dma_start", [], [bmincl], out=mincl[:], in_=D.cmask[:, 2, :])
        I("sp", "dma_start", [], [bcf], out=cf[:], in_=D.cf32[:, 0:4, :])
        I("sp", "dma_start", [], [bkmask], out=kmask[:], in_=D.kmask[:, :])
        for (_, _, (Vx_, bVx_)) in psets:
            I("dve", "memset", [], [bVx_], ap=Vx_[:, :, :, 64:65], constant=1.0)

        def flat(t, nb):
            return t[:, 0:nb, :].rearrange("p b h -> p (b h)")

        def build_F(nb, masked):
            n = nb * 8
            pB, bpB = getB()
            I("pe", "matmul", [bcf, blf], [bpB], out=pB[:, 0:n], lhsT=cf[:, 1, :], rhs=flat(lf, nb), start=True, stop=True)
            I("dve", "tensor_copy", [bpB], [bFin], out=flat(Fin, nb), in_=pB[:, 0:n])
            pB, bpB = getB()
            I("pe", "matmul", [bcf, bFin], [bpB], out=pB[:, 0:n], lhsT=cf[:, 3, :], rhs=flat(Fin, nb), start=True, stop=True)
            I("dve", "tensor_copy", [bpB], [btot], out=flat(tot, nb), in_=pB[:, 0:n])
            src, bsrc = tot, btot
            k = 1
            pp = [(scA, bscA), (scB, bscB)]
            ii = 0
            while k < nb:
                dst, bdst = pp[ii % 2]
                ii += 1
                I("dve", "tensor_copy", [bsrc], [bdst], out=dst[:, 0:k, :], in_=src[:, 0:k, :])
                I("dve", "tensor_tensor", [bsrc], [bdst], out=dst[:, k:nb, :], in0=src[:, k:nb, :], in1=src[:, 0:nb - k, :], op=ALU.add)
                src, bsrc = dst, bdst
                k *= 2
            I("dve", "tensor_tensor", [bsrc, btot], [boffs], out=offs[:, 0:nb, :], in0=src[:, 0:nb, :], in1=tot[:, 0:nb, :], op=ALU.subtract)
            I("dve", "tensor_tensor", [bFin, boffs], [bFm], out=Fm[:, 0:nb, :], in0=Fin[:, 0:nb, :], in1=offs[:, 0:nb, :], op=ALU.add)
            if masked:
                for h in range(8):
                    I("dve", "tensor_tensor", [bFm, bkmask], [bFm], out=Fm[:, 0:32, h], in0=Fm[:, 0:32, h], in1=kmask[:, :], op=ALU.subtract)

        def attend(pset, streams):
            (kTp, bkTp), (qTp, bqTp), (Vx, bVx) = pset
            ctx = []
            for si, (hh, h, qcol, W, qb0, nsub, out_col) in enumerate(streams):
                boff, bboff = boffs_[si]
                bdg, bbdg = bdgs_[si]
                rsb, brsb = rsbs.next()
                rrec, brrec = rrecs.next()
                if qb0 > 0:
                    I("dve", "tensor_scalar", [bFm, boffs], [bboff], out=boff[:, 0:qb0], in0=Fm[:, 0:qb0, h], scalar1=offs[:, qb0, h:h + 1], scalar2=-1.0,
                      op0=ALU.subtract, op1=ALU.mult)
                for i in range(nsub):
                    I("dve", "tensor_scalar", [bFm, boffs], [bbdg], out=bdg[:, i * 4: i * 4 + i + 1], in0=Fm[:, qb0: qb0 + i + 1, h], scalar1=offs[:, qb0 + i, h:h + 1],
                      scalar2=-1.0, op0=ALU.subtract, op1=ALU.mult)
                if nsub > 1:
                    I("dve", "tensor_scalar", [boffs], [bcfac], out=cfac[:, qb0:qb0 + nsub, h], in0=offs[:, qb0:qb0 + nsub, h], scalar1=offs[:, qb0, h:h + 1], scalar2=None,
                      op0=ALU.subtract)
                    I("act", "activation", [bcfac], [bcfac], out=cfac[:, qb0:qb0 + nsub, h], in_=cfac[:, qb0:qb0 + nsub, h], func=AF.Exp)
                ctx.append((boff, bboff, bdg, bbdg, rsb, brsb, rrec, brrec))
            for _ in range(JB):
                pJ, bpJ = getJ()
                I("pe", "matmul", [bidentb, bjk], [bpJ], out=pJ[:, 0:512], lhsT=identb[:], rhs=jk[:, 0:512], start=True, stop=True)
            yield
            lists = []
            for si, (hh, h, qcol, W, qb0, nsub, out_col) in enumerate(streams):
                lists.append([(si, "o", kb, 0, 0) for kb in range(qb0)] + [(si, "d", qb0 + j, i, j) for i in range(nsub) for j in range(i + 1)])
            work = []
            for t_ in range(max(len(l_) for l_ in lists)):
                for l_ in lists:
                    if t_ < len(l_):
                        work.append(l_[t_])
            inflight = {}
            for idx in range(len(work) + LA):
                if idx < len(work):
                    si, kind, kb, i, j = work[idx]
                    hh, h, qcol, W, qb0, nsub, out_col = streams[si]
                    ps_ = slice(hh * 64, (hh + 1) * 64)
                    Wd = min(W, 128)
                    p, bp = pS.next()
                    if kind == "o":
                        I("pe", "matmul", [bkTp, bqTp], [bp], out=p[:, 0:W], lhsT=kTp[ps_, kb * 128:(kb + 1) * 128], rhs=qTp[ps_, qcol:qcol + W], start=True, stop=True)
                    else:
                        I("pe", "matmul", [bkTp, bqTp], [bp], out=p[:, 0:Wd], lhsT=kTp[ps_, kb * 128:(kb + 1) * 128], rhs=qTp[ps_, qcol + i * 128: qcol + i * 128 + Wd],
                          start=True, stop=(j != i))
                        if j == i:
                            I("pe", "matmul", [bidentb, bmincl], [bp], out=p[:, 0:Wd], lhsT=identb[:], rhs=mincl[:, 0:Wd], start=False, stop=True)
                    inflight[idx] = (p, bp)
                k2 = idx - LA
                if k2 >= 0:
                    si, kind, kb, i, j = work[k2]
                    hh, h, qcol, W, qb0, nsub, out_col = streams[si]
                    boff, bboff, bdg, bbdg = ctx[si][0:4]
                    pOo, bpOo = pOos[si]
                    pOd, bpOd = pOds[si]
                    Wd = min(W, 128)
                    p, bp = inflight.pop(k2)
                    pt, bpt = PTs.next()
                    if kind == "o":
                        I("act", "activation", [bp, bboff], [bpt], out=pt[:, 0:W], in_=p[:, 0:W], func=AF.Exp, bias=boff[:, kb:kb + 1])
                        I("pe", "matmul", [bVx, bpt], [bpOo], out=pOo[0:65, 0:W], lhsT=Vx[:, kb, hh, :], rhs=pt[:, 0:W], start=(kb == 0), stop=(kb == qb0 - 1))
                    else:
                        I("act", "activation", [bp, bbdg], [bpt], out=pt[:, 0:Wd], in_=p[:, 0:Wd], func=AF.Exp, bias=bdg[:, i * 4 + j: i * 4 + j + 1])
                        I("pe", "matmul", [bVx, bpt], [bpOd], out=pOd[0:65, i * 128: i * 128 + Wd], lhsT=Vx[:, kb, hh, :], rhs=pt[:, 0:Wd], start=(j == 0), stop=(j == i))
                yield
            for si, (hh, h, qcol, W, qb0, nsub, out_col) in enumerate(streams):
                boff, bboff, bdg, bbdg, rsb, brsb, rrec, brrec = ctx[si]
                pOo, bpOo = pOos[si]
                pOd, bpOd = pOds[si]
                osb, bosb = osbs_[si]
                Wd = min(W, 128)
                if qb0 > 0:
                    I("act", "activation", [bpOo], [bosb], out=osb[:, 0:W], in_=pOo[0:65, 0:W], func=AF.Copy)
                    for i in range(nsub):
                        sl = slice(i * 128, i * 128 + Wd)
                        if nsub > 1:
                            I("dve", "scalar_tensor_tensor", [bosb, bcfac, bpOd], [brsb], out=rsb[:, sl], in0=osb[:, sl], scalar=cfac[0:65, qb0 + i, h:h + 1], in1=pOd[0:65, sl],
                              op0=ALU.mult, op1=ALU.add)
                        else:
                            I("dve", "tensor_tensor", [bosb, bpOd], [brsb], out=rsb[:, sl], in0=osb[:, sl], in1=pOd[0:65, sl], op=ALU.add)
                else:
                    I("dve", "tensor_copy", [bpOd], [brsb], out=rsb[:, 0:W], in_=pOd[0:65, 0:W])
                I("dve", "reciprocal", [brsb], [brrec], out=rrec[64:65, 0:W], in_=rsb[64:65, 0:W])
                pB, bpB = getB()
                I("pe", "matmul", [bcf, brrec], [bpB], out=pB[0:64, 0:W], lhsT=cf[64:65, 2, 0:64], rhs=rrec[64:65, 0:W], start=True, stop=True)
                fo, bfo = fos.next()
                I("dve", "tensor_tensor", [brsb, bpB], [bfo], out=fo[:, 0:W], in0=rsb[0:64, 0:W], in1=pB[0:64, 0:W], op=ALU.mult)
                I("pool", "dma_start", [bfo], [], out=D.foT[h * 64:(h + 1) * 64, out_col:out_col + W], in_=fo[:, 0:W])
            yield

        if len(qtiles) > 0:
            I("sp", "dma_start", [], [blf], out=lf[:], in_=D.lfs[0:8192, :].rearrange("(b p) h -> p b h", p=128))
            build_F(64, True)
            def load_pair(pr, pset):
                (kTp, bkTp), (qTp, bqTp), (Vx, bVx) = pset
                I("sp", "dma_start", [], [bkTp], out=kTp[:, :], in_=D.kT[pr * 128:(pr + 1) * 128, 0:8192])
                I("sp", "dma_start", [], [bqTp], out=qTp[:, :], in_=D.qT[pr * 128:(pr + 1) * 128, 0:4096])
                for hh in range(2):
                    I("sp", "dma_start", [], [bVx], out=Vx[:, :, hh, 0:64],
                      in_=D.V[0:8192, (2 * pr + hh) * 64:(2 * pr + hh + 1) * 64].rearrange("(b p) d -> p b d", p=128))
            pl = list(pairs)
            load_pair(pl[0], psets[0])
            for n_, pr in enumerate(pl):
                if n_ + 1 < len(pl):
                    load_pair(pl[n_ + 1], psets[(n_ + 1) % 2])
                for qt in qtiles:
                    yield from attend(psets[n_ % 2], [(hh, 2 * pr + hh, qt * 512, 512, 32 + 4 * qt, 4, qt * 512) for hh in range(2)])
        for q in samples:
            r0 = NP_ + NO_ + q * 128
            I("sp", "dma_start", [], [blf], out=lf[:, 0:16, :], in_=D.clf[q].rearrange("(b p) h -> p b h", p=128))
            I("sp", "dma_start", [], [blf], out=lf[:, 16, :], in_=D.lfs[r0:r0 + 128, :])
            build_F(17, False)
            I("pool", "dma_start", [], [bkc_tm], out=kc_tm[:], in_=D.ck[q].rearrange("(b p) c -> p b c", p=128))
            for n_, pr in enumerate(pairs):
                pset = psets[n_ % 2]
                (kTp, bkTp), (qTp, bqTp), (Vx, bVx) = pset
                for b4 in range(2):
                    pT, bpT = getT()
                    for bb in range(8):
                        blk = b4 * 8 + bb
                        I("pe", "transpose", [bkc_tm, bidentb], [bpT], out=pT[:, bb * 128:(bb + 1) * 128], in_=kc_tm[:, blk, pr * 128:(pr + 1) * 128], identity=identb[:])
                    I("dve", "tensor_copy", [bpT], [bkTp], out=kTp[:, b4 * 1024:(b4 + 1) * 1024], in_=pT[:, :])
                I("sp", "dma_start", [], [bkTp], out=kTp[:, 2048:2176], in_=D.kT[pr * 128:(pr + 1) * 128, r0:r0 + 128])
                I("sp", "dma_start", [], [bqTp], out=qTp[:, 0:128], in_=D.qT[pr * 128:(pr + 1) * 128, NO_ + q * 128: NO_ + (q + 1) * 128])
                for hh in range(2):
                    h = 2 * pr + hh
                    I("pool", "dma_start", [], [bVx], out=Vx[:, 0:16, hh, 0:64], in_=D.cv[q][:, h * 64:(h + 1) * 64].rearrange("(b p) d -> p b d", p=128))
                    I("sp", "dma_start", [], [bVx], out=Vx[:, 16, hh, 0:64], in_=D.V[r0:r0 + 128, h * 64:(h + 1) * 64])
                yield from attend(pset, [(hh, 2 * pr + hh, 0, 128, 16, 1, NO_ + q * 128) for hh in range(2)])


def phase_c(nc, fw, D, blocks=None, samples=(0, 1), o_all=False):
    with ExitStack() as st:
        for _ in phase_c_body(nc, fw, D, st, None, blocks, samples, o_all):
            pass
        fw.barrier()
        fw.emit()


def phase_c_body(nc, fw, D, st, banks, blocks=None, samples=(0, 1), o_all=False):
    I = fw.I
    if True:
        def sb(name, shape, dt):
            return st.enter_context(nc.sbuf_tensor("C_" + name, shape, dt)), Buf(name)

        def pst(name, shape, dt):
            return st.enter_context(nc.psum_tensor("C_" + name, shape, dt)), Buf(name)
        identb, bidentb = sb("identb", [128, 128], BF16)
        identf, bidentf = sb("identf", [128, 128], F32)
        mincl, bmincl = sb("mincl", [128, 128], BF16)
        mstr, bmstr = sb("mstr", [128, 128], BF16)
        cf, bcf = sb("cf", [128, 6, 128], F32)
        ggdn, bggdn = sb("ggdn", [128, 8, 64], F32)
        vmask, bvmask = sb("vmask", [128, 1], F32)
        insets = [(sb(f"qkv{i}", [128, 3, 8, 64], BF16), sb(f"kT{i}", [64, 8, 128], BF16), sb(f"qT{i}", [64, 8, 128], BF16),
                   sb(f"g{i}", [128, 8], F32), sb(f"be{i}", [128, 16], F32)) for i in range(2)]
        gc, bgc = sb("gc", [128, 8], F32)
        ngc, bngc = sb("ngc", [128, 8], F32)
        vec2, bvec2 = sb("vec2", [128, 8], F32)
        eg, beg = sb("eg", [128, 8], F32)
        bee, bbee = sb("bee", [128, 8], F32)
        glb, bglb = sb("glb", [128, 8], F32)
        gl12, bgl12 = sb("gl12", [128, 16], F32)
        egl, begl = sb("egl", [128, 2, 8], F32)
        ekl, bekl = sb("ekl", [128, 8], F32)
        dg1, bdg1 = sb("dg1", [128, 8, 128], F32)
        dg2, bdg2 = sb("dg2", [128, 8, 128], F32)
        Dincl, bDincl = sb("Dincl", [128, 8, 128], BF16)
        AbT, bAbT = sb("AbT", [128, 8, 128], BF16)
        Ns = [sb(f"N{i}", [128, 8, 128], BF16) for i in range(2)]
        Ls = [sb(f"L{i}", [128, 8, 128], BF16) for i in range(2)]
        Ws = [sb(f"W{i}", [128, 8, 128], BF16) for i in range(2)]
        AqkT, bAqkT = sb("AqkT", [128, 8, 128], BF16)
        rhs2, brhs2 = sb("rhs2", [128, 8, 128], BF16)
        khat, bkhat = sb("khat", [128, 8, 64], BF16)
        qtl, bqtl = sb("qtl", [128, 8, 64], BF16)
        qtT, bqtT = sb("qtT", [64, 8, 128], BF16)
        U, bU = sb("U", [128, 8, 64], F32)
        WkT, bWkT = sb("WkT", [64, 8, 128], BF16)
        vnew, bvnew = sb("vnew", [128, 8, 64], BF16)
        S, bS = sb("S", [64, 8, 64], F32)
        Sb, bSb = sb("Sb", [64, 8, 64], BF16)
        Sb2, bSb2 = sb("Sb2", [64, 8, 64], BF16)
        osb, bosb = sb("osb", [128, 8, 64], F32)
        osq, bosq = sb("osq", [128, 8, 64], F32)
        oss, boss = sb("oss", [128, 8], F32)
        ggt, bggt = sb("ggt", [128, 8, 64], F32)
        ob, bob = sb("ob", [128, 8, 64], BF16)
        if banks is None:
            pA = [pst(f"pA{i}", [128, 512], F32) for i in range(2)]
            pL = [pst(f"pL{i}", [128, 512], F32) for i in range(2)]
            pW = [pst(f"pW{i}", [128, 512], F32) for i in range(2)]
            pX, bpX = pst("pX", [128, 512], F32)
            pTt, bpTt = pst("pTt", [128, 1024], BF16)
        else:
            pA = [banks[0], banks[0]]
            pL = [banks[1], banks[1]]
            pW = [banks[2], banks[2]]
            pX, bpX = banks[3]
            pTt, bpTt = banks[0][0][:].bitcast(BF16), banks[0][1]

        I("sp", "dma_start", [], [bidentb], out=identb[:], in_=D.cmask[:, 0, :])
        I("sp", "dma_start", [], [bmincl], out=mincl[:], in_=D.cmask[:, 6, :])
        I("sp", "dma_start", [], [bmstr], out=mstr[:], in_=D.cmask[:, 7, :])
        I("sp", "dma_start", [], [bcf], out=cf[:], in_=D.cf32[:, :, :])
        I("sp", "dma_start", [], [bidentf], out=identf[:], in_=D.cf32[:, 0, :])
        for h in range(8):
            I("sp", "dma_start", [], [bggdn], out=ggdn[:, h, :], in_=D.g_gdn[0:1, :].broadcast_to([128, 64]))
        I("sp", "dma_start", [], [bvmask], out=vmask[:], in_=D.vmask[:, :])
        jk, bjk = sb("jk", [128, 512], BF16)
        I("dve", "memset", [], [bjk], ap=jk[:], constant=0.0)
        JC = 0
        I("dve", "tensor_scalar_mul", [bggdn], [bggdn], out=ggdn[:].rearrange("p h d -> p (h d)"), in0=ggdn[:].rearrange("p h d -> p (h d)"), scalar1=0.5)

        def load(r0, si):
            (qkv, bqkv), (kT, bkT), (qT, bqT), (g_, bg_), (be, bbe) = insets[si]
            I("sp", "dma_start", [], [bqkv], out=qkv[:].rearrange("p a h d -> p (a h d)"), in_=D.gqkv[r0:r0 + 128, :])
            I("sp", "dma_start", [], [bg_], out=g_[:], in_=D.g[r0:r0 + 128, :])
            I("sp", "dma_start", [], [bbe], out=be[:], in_=D.beta[r0:r0 + 128, :])

        def process(si, qrow, want_o, sample):
            (qkv, bqkv), (kT, bkT), (qT, bqT), (g_, bg_), (be, bbe) = insets[si]
            if sample:
                I("dve", "tensor_scalar_mul", [bg_, bvmask], [bg_], out=g_[:], in0=g_[:], scalar1=vmask[:, 0:1])
                I("dve", "tensor_scalar_mul", [bqkv, bvmask], [bqkv], out=qkv[:].rearrange("p a h d -> p (a h d)"), in0=qkv[:].rearrange("p a h d -> p (a h d)"),
                  scalar1=vmask[:, 0:1])
            for h in range(8):
                I("pe", "transpose", [bqkv, bidentb], [bpTt], out=pTt[0:64, h * 128:(h + 1) * 128], in_=qkv[:, 1, h, :], identity=identb[:])
            I("act", "activation", [bpTt], [bkT], out=kT[:].rearrange("p h i -> p (h i)"), in_=pTt[0:64, :], func=AF.Copy)
            if want_o:
                for h in range(8):
                    I("pe", "transpose", [bqkv, bidentb], [bpTt], out=pTt[0:64, h * 128:(h + 1) * 128], in_=qkv[:, 0, h, :], identity=identb[:])
                I("dve", "tensor_copy", [bpTt], [bqT], out=qT[:].rearrange("p h i -> p (h i)"), in_=pTt[0:64, :])
            yield
            I("pe", "matmul", [bcf, bg_], [bpX], out=pX[:, 0:8], lhsT=cf[:, 4, :], rhs=g_[:], start=True, stop=True)
            I("dve", "tensor_copy", [bpX], [bgc], out=gc[:], in_=pX[:, 0:8])
            I("dve", "tensor_scalar_mul", [bgc], [bngc], out=ngc[:], in0=gc[:], scalar1=-1.0)
            I("pe", "matmul", [bcf, bgc], [bpX], out=pX[:, 8:16], lhsT=cf[:, 5, :], rhs=gc[:], start=True, stop=True)
            I("pe", "matmul", [bcf, bgc], [bpX], out=pX[:, 16:24], lhsT=cf[:, 3, :], rhs=gc[:], start=True, stop=True)
            I("dve", "tensor_copy", [bpX], [bgl12], out=gl12[:], in_=pX[:, 8:24])
            I("dve", "tensor_copy", [bgl12], [bglb], out=glb[0:64, :], in_=gl12[0:64, 0:8])
            I("dve", "tensor_copy", [bgl12], [bglb], out=glb[64:128, :], in_=gl12[64:128, 8:16])
            I("dve", "tensor_tensor", [bbe, bgc], [bvec2], out=vec2[:], in0=be[:, 8:16], in1=gc[:], op=ALU.add)
            I("act", "activation", [bgc], [beg], out=eg[:], in_=gc[:], func=AF.Exp)
            I("act", "activation", [bvec2], [bbee], out=bee[:], in_=vec2[:], func=AF.Exp)
            I("act", "activation", [bgl12], [begl], out=egl[:].rearrange("p a h -> p (a h)"), in_=gl12[:], func=AF.Exp)
            I("dve", "tensor_tensor", [bglb, bgc], [bekl], out=ekl[:], in0=glb[:], in1=gc[:], op=ALU.subtract)
            I("act", "activation", [bekl], [bekl], out=ekl[:], in_=ekl[:], func=AF.Exp)
            if sample:
                I("dve", "tensor_scalar_mul", [bbe, bvmask], [bbe], out=be[:, 0:8], in0=be[:, 0:8], scalar1=vmask[:, 0:1])
                I("dve", "tensor_scalar_mul", [bbee, bvmask], [bbee], out=bee[:], in0=bee[:], scalar1=vmask[:, 0:1])
            yield
            for h in range(8):
                if want_o:
                    I("dve", "tensor_scalar_mul", [bidentf, bgc], [bdg1], out=dg1[:, h, :], in0=identf[:], scalar1=gc[:, h:h + 1])
                I("dve", "tensor_scalar_mul", [bidentf, bvec2], [bdg2], out=dg2[:, h, :], in0=identf[:], scalar1=vec2[:, h:h + 1])
            N0, bN0 = Ns[0]
            L0, bL0 = Ls[0]
            yield
            for _ in range(JC):
                I("pe", "matmul", [bidentb, bjk], [bpX], out=pX[:, 0:512], lhsT=identb[:], rhs=jk[:, 0:512], start=True, stop=True)
            for hb in range(2):
                yield
                pK, bpK = pA[hb]
                pQ, bpQ = pL[hb]
                p1, bp1 = pW[hb]
                for hq in range(4):
                    h = hb * 4 + hq
                    sl = slice(hq * 128, (hq + 1) * 128)
                    I("pe", "matmul", [bkT], [bpK], out=pK[:, sl], lhsT=kT[:, h, :], rhs=kT[:, h, :], start=True, stop=True)
                    if want_o:
                        I("pe", "matmul", [bkT, bqT], [bpQ], out=pQ[:, sl], lhsT=kT[:, h, :], rhs=qT[:, h, :], start=True, stop=True)
                        I("pe", "matmul", [bcf, bdg1], [bp1], out=p1[:, sl], lhsT=cf[:, 2, :], rhs=dg1[:, h, :], start=True, stop=False)
                        I("pe", "matmul", [bidentb, bmincl], [bp1], out=p1[:, sl], lhsT=identb[:], rhs=mincl[:], start=False, stop=True)
                        I("act", "activation", [bp1, bngc], [bDincl], out=Dincl[:, h, :], in_=p1[:, sl], func=AF.Exp, bias=ngc[:, h:h + 1])
                for hq in range(4):
                    h = hb * 4 + hq
                    sl = slice(hq * 128, (hq + 1) * 128)
                    I("pe", "matmul", [bcf, bdg2], [bpX], out=pX[:, sl], lhsT=cf[:, 2, :], rhs=dg2[:, h, :], start=True, stop=False)
                    I("pe", "matmul", [bidentb, bmstr], [bpX], out=pX[:, sl], lhsT=identb[:], rhs=mstr[:], start=False, stop=True)
                    I("act", "activation", [bpX, bngc], [bAbT], out=AbT[:, h, :], in_=pX[:, sl], func=AF.Exp, bias=ngc[:, h:h + 1])
                hs = slice(hb * 4, hb * 4 + 4)
                fl = lambda t: t[:, hs, :].rearrange("p h i -> p (h i)")
                I("dve", "scalar_tensor_tensor", [bpK, bAbT], [bN0], out=fl(N0), in0=pK[:, :], scalar=-1.0, in1=fl(AbT), op0=ALU.mult, op1=ALU.mult)
                if want_o:
                    I("dve", "tensor_tensor", [bpQ, bDincl], [bAqkT], out=fl(AqkT), in0=pQ[:, :], in1=fl(Dincl), op=ALU.mult)
            yield
            for h in range(8):
                I("pe", "transpose", [bN0, bidentb], [bpTt], out=pTt[:, h * 128:(h + 1) * 128], in_=N0[:, h, :], identity=identb[:])
            I("act", "activation", [bpTt], [bL0], out=L0[:].rearrange("p h i -> p (h i)"), in_=pTt[:, :], func=AF.Copy)
            W0, bW0 = Ws[0]
            I("dve", "tensor_tensor", [bN0, bidentb], [bW0], out=W0[:], in0=N0[:], in1=identb[:].unsqueeze(1).broadcast_to([128, 8, 128]), op=ALU.add)
            cur = 0
            for m in range(1, 6):
                yield
                Nc, bNc = Ns[cur]
                Lc, bLc = Ls[cur]
                Wc, bWc = Ws[cur]
                Nn, bNn = Ns[1 - cur]
                Ln, bLn = Ls[1 - cur]
                Wn, bWn = Ws[1 - cur]
                def fl(t, hb):
                    return t[:, hb * 4:hb * 4 + 4, :].rearrange("p h i -> p (h i)")
                for hb in range(2):
                    pl_, bpl_ = pL[hb]
                    for hq in range(4):
                        h = hb * 4 + hq
                        I("pe", "matmul", [bNc, bLc], [bpl_], out=pl_[:, hq * 128:(hq + 1) * 128], lhsT=Nc[:, h, :], rhs=Lc[:, h, :], start=True, stop=True)
                    I("act", "activation", [bpl_], [bLn], out=fl(Ln, hb), in_=pl_[:, :], func=AF.Copy)
                if m < 5:
                    for hb in range(2):
                        pa_, bpa_ = pA[hb]
                        for hq in range(4):
                            h = hb * 4 + hq
                            I("pe", "matmul", [bNc, bLc], [bpa_], out=pa_[:, hq * 128:(hq + 1) * 128], lhsT=Lc[:, h, :], rhs=Nc[:, h, :], start=True, stop=True)
                        I("dve", "tensor_copy", [bpa_], [bNn], out=fl(Nn, hb), in_=pa_[:, :])
                for hb in range(2):
                    pw_, bpw_ = pW[hb]
                    for hq in range(4):
                        h = hb * 4 + hq
                        I("pe", "matmul", [bLn, bWc], [bpw_], out=pw_[:, hq * 128:(hq + 1) * 128], lhsT=Ln[:, h, :], rhs=Wc[:, h, :], start=True, stop=True)
                    I("dve", "tensor_tensor", [bpw_, bWc], [bWn], out=fl(Wn, hb), in0=pw_[:, :], in1=fl(Wc, hb), op=ALU.add)
                cur = 1 - cur
            Wf, bWf = Ws[cur]
            yield
            bc = lambda t: t[:, :].unsqueeze(2).broadcast_to([128, 8, 64])
            I("dve", "tensor_tensor", [bqkv, bbe], [brhs2], out=rhs2[:, :, 0:64], in0=qkv[:, 2, :, :], in1=be[:, 0:8].unsqueeze(2).broadcast_to([128, 8, 64]), op=ALU.mult)
            I("dve", "tensor_tensor", [bqkv, bbee], [brhs2], out=rhs2[:, :, 64:128], in0=qkv[:, 1, :, :], in1=bc(bee), op=ALU.mult)
            I("pool", "tensor_tensor", [bqkv, bekl], [bkhat], out=khat[:], in0=qkv[:, 1, :, :], in1=bc(ekl), op=ALU.mult)
            if want_o:
                I("pool", "tensor_tensor", [bqkv, beg], [bqtl], out=qtl[:], in0=qkv[:, 0, :, :], in1=bc(eg), op=ALU.mult)
            for hb in range(2):
                yield
                pu, bpu = pA[hb]
                for hq in range(4):
                    h = hb * 4 + hq
                    I("pe", "matmul", [bWf, brhs2], [bpu], out=pu[:, hq * 128:(hq + 1) * 128], lhsT=Wf[:, h, :], rhs=rhs2[:, h, :], start=True, stop=True)
                I("dve", "tensor_copy", [bpu], [bU], out=U[:, hb * 4:hb * 4 + 4, :], in_=pu[:, :].rearrange("p (h c) -> p h c", c=128)[:, :, 0:64])
                pk_, bpk_ = pL[hb]
                for hq in range(4):
                    h = hb * 4 + hq
                    I("pe", "matmul", [brhs2, bWf], [bpk_], out=pk_[0:64, hq * 128:(hq + 1) * 128], lhsT=rhs2[:, h, 64:128], rhs=Wf[:, h, :], start=True, stop=True)
                I("act", "activation", [bpk_], [bWkT], out=WkT[:, hb * 4:hb * 4 + 4, :].rearrange("p h i -> p (h i)"), in_=pk_[0:64, :], func=AF.Copy)
            if want_o:
                for h in range(8):
                    I("pe", "transpose", [bqtl, bidentb], [bpTt], out=pTt[0:64, h * 128:(h + 1) * 128], in_=qtl[:, h, :], identity=identb[:])
                I("act", "activation", [bpTt], [bqtT], out=qtT[:].rearrange("p h i -> p (h i)"), in_=pTt[0:64, :], func=AF.Copy)
            fo_ = lambda t: t[:].rearrange("p h d -> p (h d)")
            halves = [(slice(0, 64), Sb, bSb, 0), (slice(64, 128), Sb2, bSb2, 1)]
            for (rs_, Sc, bSc, ci) in halves:
                yield
                for h in range(8):
                    I("pe", "matmul", [bWkT, bSc], [bpX], out=pX[:, h * 64:(h + 1) * 64], lhsT=WkT[:, h, :], rhs=Sc[:, h, :], start=True, stop=True)
                I("dve", "tensor_tensor", [bU, bpX], [bvnew], out=fo_(vnew)[rs_, :], in0=fo_(U)[rs_, :], in1=pX[rs_, :], op=ALU.subtract)
                ps_, bps_ = pW[1]
                for h in range(8):
                    I("pe", "matmul", [bkhat, bvnew], [bps_], out=ps_[0:64, h * 64:(h + 1) * 64], lhsT=khat[rs_, h, :], rhs=vnew[rs_, h, :], start=True, stop=True)
                I("dve", "tensor_tensor", [bS, begl], [bS], out=S[:], in0=S[:], in1=egl[0:64, ci, :].unsqueeze(2).broadcast_to([64, 8, 64]), op=ALU.mult)
                I("dve", "tensor_tensor", [bS, bps_], [bS], out=fo_(S), in0=fo_(S), in1=ps_[0:64, :], op=ALU.add)
                Sn, bSn = (Sb2, bSb2) if ci == 0 else (Sb, bSb)
                if want_o or ci == 0:
                    pass
                I("act", "activation", [bS], [bSn], out=fo_(Sn), in_=fo_(S), func=AF.Copy)
                if want_o:
                    po, bpo = pW[0]
                    for h in range(8):
                        I("pe", "matmul", [bqtT, bSc], [bpo], out=po[:, h * 64:(h + 1) * 64], lhsT=qtT[:, h, :], rhs=Sc[:, h, :], start=True, stop=False)
                        I("pe", "matmul", [bAqkT, bvnew], [bpo], out=po[:, h * 64:(h + 1) * 64], lhsT=AqkT[:, h, :], rhs=vnew[:, h, :], start=False, stop=True)
                    I("act", "activation", [bpo], [bosb], out=fo_(osb)[rs_, :], in_=po[rs_, :], func=AF.Copy)
            yield
            if want_o:
                I("pool", "tensor_tensor", [bosb], [bosq], out=fo_(osq), in0=fo_(osb), in1=fo_(osb), op=ALU.mult)
                I("dve", "tensor_reduce", [bosq], [boss], out=oss[:], in_=osq[:], axis=AX.X, op=ALU.add)
                I("act", "activation", [boss], [boss], out=oss[:], in_=oss[:], func=AF.Sqrt, bias=EPS, scale=1.0 / 64)
                I("dve", "reciprocal", [boss], [boss], out=oss[:], in_=oss[:])
                I("sp", "dma_start", [], [bggt], out=fo_(ggt), in_=D.gg[qrow:qrow + 128, :])
                I("dve", "tensor_tensor", [bosb, boss], [bosb], out=osb[:], in0=osb[:], in1=oss[:, :].unsqueeze(2).broadcast_to([128, 8, 64]), op=ALU.mult)
                I("pool", "tensor_tensor", [bggt, bggdn], [bggt], out=fo_(ggt), in0=fo_(ggt), in1=fo_(ggdn), op=ALU.mult)
                I("dve", "tensor_tensor", [bosb, bggt], [bob], out=fo_(ob), in0=fo_(osb), in1=fo_(ggt), op=ALU.mult)
                I("pool", "dma_start", [bob], [], out=D.go[qrow:qrow + 128, :], in_=fo_(ob))

        blks = list(range(64)) if blocks is None else list(blocks)
        if blks:
            I("dve", "memset", [], [bvnew], ap=vnew[:].rearrange("p h d -> p (h d)"), constant=0.0)
            I("dve", "memset", [], [bS], ap=S[:].rearrange("p h d -> p (h d)"), constant=0.0)
            I("dve", "memset", [], [bSb], ap=Sb[:].rearrange("p h d -> p (h d)"), constant=0.0)
            load(blks[0] * 128, 0)
            for n_, b in enumerate(blks):
                if n_ + 1 < len(blks):
                    load(blks[n_ + 1] * 128, (n_ + 1) % 2)
                want = o_all or b >= 32
                yield from process(n_ % 2, max(b * 128 - NP_, 0), want, False)
            I("sp", "dma_start", [bS], [], out=D.sfin[0].rearrange("h k v -> k h v"), in_=S[:])
        for q in samples:
            load(NP_ + NO_ + q * 128, q % 2)
            I("sp", "dma_start", [], [bS], out=S[:], in_=D.sgdn[q].rearrange("h k v -> k h v"))
            I("act", "activation", [bS], [bSb], out=Sb[:].rearrange("p h d -> p (h d)"), in_=S[:].rearrange("p h d -> p (h d)"), func=AF.Copy)
            yield from process(q % 2, NO_ + q * 128, True, True)
            I("sp", "dma_start", [bS], [], out=D.sfin[1 + q].rearrange("h k v -> k h v"), in_=S[:])


QTILES = [(i * 512, 4) for i in range(8)] + [(NO_, 2)]


def phase_d1(nc, fw, D, tiles=None):
    I = fw.I
    with ExitStack() as st:
        def sb(name, shape, dt):
            return st.enter_context(nc.sbuf_tensor("D1_" + name, shape, dt)), Buf(name)

        def pst(name, shape, dt):
            return st.enter_context(nc.psum_tensor("D1_" + name, shape, dt)), Buf(name)
        identb, bidentb = sb("identb", [128, 128], BF16)
        wpa, bwpa = sb("wpa", [128, 4, 1024], BF16)
        wpb, bwpb = sb("wpb", [128, 4, 1024], BF16)
        wout, bwout = sb("wout", [128, 8, 1024], BF16)
        gpost, bgpost = sb("gpost", [128, 1024], F32)
        insets = [(sb(f"foT{i}", [128, 4, 512], BF16), sb(f"gob{i}", [128, 4, 512], BF16), sb(f"sgA{i}", [128, 8, 512], BF16),
                   sb(f"sgB{i}", [128, 8, 512], BF16), sb(f"xt{i}", [128, 4, 1024], F32)) for i in range(2)]
        goT, bgoT = sb("goT", [128, 4, 512], BF16)
        t1s = Rot([sb(f"t1{i}", [128, 512], F32) for i in range(2)])
        t2s = Rot([sb(f"t2{i}", [128, 512], F32) for i in range(2)])
        mT, bmT = sb("mT", [128, 8, 512], BF16)
        mixes = Rot([sb(f"mix{i}", [128, 1024], F32) for i in range(2)])
        junk, bjunk = sb("junk", [128, 1024], BF16)
        sss = [sb(f"ss{i}", [128, 1], F32) for i in range(4)]
        y1, by1 = sb("y1", [128, 4, 1024], F32)
        pa = Rot([pst(f"pa{i}", [128, 512], F32) for i in range(2)])
        pb = Rot([pst(f"pb{i}", [128, 512], F32) for i in range(2)])
        pm = Rot([pst(f"pm{i}", [128, 512], F32) for i in range(2)])
        ptr, bptr = pst("ptr", [128, 1024], BF16)

        I("sp", "dma_start", [], [bidentb], out=identb[:], in_=D.cmask[:, 0, :])
        I("pool", "dma_start", [], [bwpa], out=wpa[:], in_=D.w_pa.rearrange("(c p) n -> p c n", p=128))
        I("pool", "dma_start", [], [bwpb], out=wpb[:], in_=D.w_pb.rearrange("(c p) n -> p c n", p=128))
        I("pool", "dma_start", [], [bwout], out=wout[:], in_=D.w_out.rearrange("(c p) n -> p c n", p=128))
        I("sp", "dma_start", [], [bgpost], out=gpost[:], in_=D.g_mix_post[0:1, :].broadcast_to([128, 1024]))
        tl = QTILES if tiles is None else [QTILES[i] for i in tiles]
        def load_tile(tq, ns, si):
            N = ns * 128
            (foT, bfoT), (gob, bgob), (sgA, bsgA), (sgB, bsgB), (xt, bxt) = insets[si]
            I("sp", "dma_start", [], [bfoT], out=foT[:, :, 0:N], in_=D.foT[:, tq:tq + N].rearrange("(c p) n -> p c n", p=128))
            I("sp", "dma_start", [], [bgob], out=gob[:, 0:ns, :], in_=D.go[tq:tq + N, :].rearrange("(s p) n -> p s n", p=128))
            I("sp", "dma_start", [], [bsgA], out=sgA[:, :, 0:N], in_=D.sgA[:, tq:tq + N].rearrange("(c p) n -> p c n", p=128))
            I("sp", "dma_start", [], [bsgB], out=sgB[:, :, 0:N], in_=D.sgB[:, tq:tq + N].rearrange("(c p) n -> p c n", p=128))
            I("sp", "dma_start", [], [bxt], out=xt[:, 0:ns, :], in_=D.xall[NP_ + tq: NP_ + tq + N, :].rearrange("(s p) m -> p s m", p=128))
        if tl:
            load_tile(tl[0][0], tl[0][1], 0)
        for n_, (tq, ns) in enumerate(tl):
            N = ns * 128
            if n_ + 1 < len(tl):
                load_tile(tl[n_ + 1][0], tl[n_ + 1][1], (n_ + 1) % 2)
            (foT, bfoT), (gob, bgob), (sgA, bsgA), (sgB, bsgB), (xt, bxt) = insets[n_ % 2]
            for half in range(2):
                for cc in range(2):
                    c = half * 2 + cc
                    for s in range(ns):
                        I("pe", "transpose", [bgob, bidentb], [bptr], out=ptr[:, cc * 512 + s * 128: cc * 512 + (s + 1) * 128], in_=gob[:, s, c * 128:(c + 1) * 128],
                          identity=identb[:])
                for cc in range(2):
                    I("act", "activation", [bptr], [bgoT], out=goT[:, half * 2 + cc, 0:N], in_=ptr[:, cc * 512: cc * 512 + N], func=AF.Copy)
            for oc in range(8):
                p1, bp1 = pa.next()
                p2, bp2 = pb.next()
                for c in range(4):
                    I("pe", "matmul", [bwpa, bfoT], [bp1], out=p1[:, 0:N], lhsT=wpa[:, c, oc * 128:(oc + 1) * 128], rhs=foT[:, c, 0:N], start=(c == 0), stop=(c == 3))
                for c in range(4):
                    I("pe", "matmul", [bwpb, bgoT], [bp2], out=p2[:, 0:N], lhsT=wpb[:, c, oc * 128:(oc + 1) * 128], rhs=goT[:, c, 0:N], start=(c == 0), stop=(c == 3))
                t1, bt1 = t1s.next()
                t2, bt2 = t2s.next()
                I("dve", "tensor_tensor", [bp1, bsgA], [bt1], out=t1[:, 0:N], in0=p1[:, 0:N], in1=sgA[:, oc, 0:N], op=ALU.mult)
                I("dve", "tensor_tensor", [bp2, bsgB], [bt2], out=t2[:, 0:N], in0=p2[:, 0:N], in1=sgB[:, oc, 0:N], op=ALU.mult)
                I("pool", "tensor_tensor", [bt1, bt2], [bmT], out=mT[:, oc, 0:N], in0=t1[:, 0:N], in1=t2[:, 0:N], op=ALU.add)
            for s in range(ns):
                mix, bmix = mixes.next()
                ss, bss = sss[s]
                for cg in range(2):
                    p, bp = pm.next()
                    for kc in range(8):
                        I("pe", "matmul", [bmT, bwout], [bp], out=p[:, :], lhsT=mT[:, kc, s * 128:(s + 1) * 128], rhs=wout[:, kc, cg * 512:(cg + 1) * 512], start=(kc == 0), stop=(kc == 7))
                    if cg == 0:
                        I("act", "activation", [bp], [bmix], out=mix[:, 0:512], in_=p[:, :], func=AF.Copy)
                    else:
                        I("dve", "tensor_copy", [bp], [bmix], out=mix[:, 512:1024], in_=p[:, :])
                I("act", "activation", [bmix], [bjunk, bss], out=junk[:], in_=mix[:], func=AF.Square, accum_out=ss[:, 0:1])
                I("act", "activation", [bss], [bss], out=ss[:], in_=ss[:], func=AF.Sqrt, bias=EPS, scale=1.0 / 1024)
                I("dve", "reciprocal", [bss], [bss], out=ss[:], in_=ss[:])
                I("dve", "scalar_tensor_tensor", [bmix, bss, bgpost], [bmix], out=mix[:], in0=mix[:], scalar=ss[:, 0:1], in1=gpost[:], op0=ALU.mult, op1=ALU.mult)
                I("dve", "tensor_tensor", [bmix, bxt], [by1], out=y1[:, s, :], in0=mix[:], in1=xt[:, s, :], op=ALU.add)
            I("sp", "dma_start", [by1], [], out=D.y1[tq:tq + N, :].rearrange("(s p) m -> p s m", p=128), in_=y1[:, 0:ns, :])
        fw.barrier()
        fw.emit()


def phase_d2(nc, fw, D, tiles=None):
    I = fw.I
    with ExitStack() as st:
        def sb(name, shape, dt):
            return st.enter_context(nc.sbuf_tensor("D2_" + name, shape, dt)), Buf(name)

        def pst(name, shape, dt):
            return st.enter_context(nc.psum_tensor("D2_" + name, shape, dt)), Buf(name)
        identb, bidentb = sb("identb", [128, 128], BF16)
        wup, bwup = sb("wup", [128, 8, 4096], BF16)
        wdn, bwdn = sb("wdn", [128, 32, 1024], BF16)
        gpre, bgpre = sb("gpre", [128, 1024], F32)
        gpost, bgpost = sb("gpost", [128, 1024], F32)
        y1, by1 = sb("y1", [128, 4, 1024], F32)
        hbs = Rot([sb(f"hb{i}", [128, 1024], BF16) for i in range(2)])
        hT, bhT = sb("hT", [128, 8, 512], BF16)
        uT, buT = sb("uT", [128, 32, 512], BF16)
        rl = Rot([sb(f"rl{i}", [128, 512], F32) for i in range(2)])
        dsb, bdsb = sb("dsb", [128, 1024], F32)
        junk, bjunk = sb("junk", [128, 1024], BF16)
        ss, bss = sb("ss", [128, 4], F32)
        pu = Rot([pst(f"pu{i}", [128, 512], F32) for i in range(4)])
        pd = Rot([pst(f"pd{i}", [128, 512], F32) for i in range(2)])
        ptr = Rot([pst(f"ptr{i}", [128, 1024], BF16) for i in range(2)])

        I("sp", "dma_start", [], [bidentb], out=identb[:], in_=D.cmask[:, 0, :])
        for kc in range(8):
            I("pool", "dma_start", [], [bwup], out=wup[:, kc, :], in_=D.w_up[kc * 128:(kc + 1) * 128, :])
        for f4 in range(8):
            I("pool", "dma_start", [], [bwdn], out=wdn[:, f4 * 4:(f4 + 1) * 4, :], in_=D.w_down[f4 * 512:(f4 + 1) * 512, :].rearrange("(c p) n -> p c n", p=128))
        I("sp", "dma_start", [], [bgpre], out=gpre[:], in_=D.g_mlp_pre[0:1, :].broadcast_to([128, 1024]))
        I("sp", "dma_start", [], [bgpost], out=gpost[:], in_=D.g_mlp_post[0:1, :].broadcast_to([128, 1024]))
        tl = QTILES if tiles is None else [QTILES[i] for i in tiles]
        for (tq, ns) in tl:
            N = ns * 128
            I("sp", "dma_start", [], [by1], out=y1[:, 0:ns, :], in_=D.y1[tq:tq + N, :].rearrange("(s p) m -> p s m", p=128))
            for s in range(ns):
                I("act", "activation", [by1], [bjunk, bss], out=junk[:], in_=y1[:, s, :], func=AF.Square, accum_out=ss[:, s:s + 1])
            I("act", "activation", [bss], [bss], out=ss[:, 0:ns], in_=ss[:, 0:ns], func=AF.Sqrt, bias=EPS, scale=1.0 / 1024)
            I("dve", "reciprocal", [bss], [bss], out=ss[:, 0:ns], in_=ss[:, 0:ns])
            for s in range(ns):
                hb, bhb = hbs.next()
                I("dve", "scalar_tensor_tensor", [by1, bss, bgpre], [bhb], out=hb[:], in0=y1[:, s, :], scalar=ss[:, s:s + 1], in1=gpre[:], op0=ALU.mult, op1=ALU.mult)
                pt, bpt = ptr.next()
                for kc in range(8):
                    I("pe", "transpose", [bhb, bidentb], [bpt], out=pt[:, kc * 128:(kc + 1) * 128], in_=hb[:, kc * 128:(kc + 1) * 128], identity=identb[:])
                if s % 2 == 0:
                    I("act", "activation", [bpt], [bhT], out=hT[:, :, s * 128:(s + 1) * 128], in_=pt[:, :].rearrange("p (k t) -> p k t", t=128), func=AF.Copy)
                else:
                    I("dve", "tensor_copy", [bpt], [bhT], out=hT[:, :, s * 128:(s + 1) * 128], in_=pt[:, :].rearrange("p (k t) -> p k t", t=128))
            for fc in range(32):
                p, bp = pu.next()
                for kc in range(8):
                    I("pe", "matmul", [bwup, bhT], [bp], out=p[:, 0:N], lhsT=wup[:, kc, fc * 128:(fc + 1) * 128], rhs=hT[:, kc, 0:N], start=(kc == 0), stop=(kc == 7))
                r, br = rl.next()
                I("act", "activation", [bp], [br], out=r[:, 0:N], in_=p[:, 0:N], func=AF.Relu)
                eng = "pool" if fc % 2 == 0 else "dve"
                I(eng, "tensor_tensor", [br], [buT], out=uT[:, fc, 0:N], in0=r[:, 0:N], in1=r[:, 0:N], op=ALU.mult)
            for s in range(ns):
                for cg in range(2):
                    p, bp = pd.next()
                    for fc in range(32):
                        I("pe", "matmul", [buT, bwdn], [bp], out=p[:, :], lhsT=uT[:, fc, s * 128:(s + 1) * 128], rhs=wdn[:, fc, cg * 512:(cg + 1) * 512], start=(fc == 0), stop=(fc == 31))
                    if cg == 0:
                        I("act", "activation", [bp], [bdsb], out=dsb[:, 0:512], in_=p[:, :], func=AF.Copy)
                    else:
                        I("dve", "tensor_copy", [bp], [bdsb], out=dsb[:, 512:1024], in_=p[:, :])
                I("act", "activation", [bdsb], [bjunk, bss], out=junk[:], in_=dsb[:], func=AF.Square, accum_out=ss[:, s:s + 1])
                I("act", "activation", [bss], [bss], out=ss[:, s:s + 1], in_=ss[:, s:s + 1], func=AF.Sqrt, bias=EPS, scale=1.0 / 1024)
                I("dve", "reciprocal", [bss], [bss], out=ss[:, s:s + 1], in_=ss[:, s:s + 1])
                I("dve", "scalar_tensor_tensor", [bdsb, bss, bgpost], [bdsb], out=dsb[:], in0=dsb[:], scalar=ss[:, s:s + 1], in1=gpost[:], op0=ALU.mult, op1=ALU.mult)
                I("pool", "tensor_tensor", [bdsb, by1], [by1], out=y1[:, s, :], in0=dsb[:], in1=y1[:, s, :], op=ALU.add)
            I("sp", "dma_start", [by1], [], out=D.y[tq:tq + N, :].rearrange("(s p) m -> p s m", p=128), in_=y1[:, 0:ns, :])
        fw.barrier()
        fw.emit()


def phase_bc(nc, fw, D, ratio=2.7):
    with ExitStack() as st:
        banks = []
        for i in range(8):
            t = st.enter_context(nc.psum_tensor(f"BC_ps{i}", [128, 512], F32))
            banks.append((t, Buf(f"BC_ps{i}")))
        gb = phase_b_body(nc, fw, D, st, banks[0:4])
        gc = phase_c_body(nc, fw, D, st, banks[4:8])
        b_alive = c_alive = True
        acc = 0.0
        while b_alive or c_alive:
            if c_alive:
                try:
                    next(gc)
                except StopIteration:
                    c_alive = False
            acc += ratio if c_alive else 1.0
            while b_alive and acc >= 1.0:
                acc -= 1.0
                try:
                    next(gb)
                except StopIteration:
                    b_alive = False
        fw.barrier()
        fw.emit()


BF = ml_dtypes.bfloat16


def const_masks():
    p = np.arange(128)
    cm = np.zeros((128, 8, 128), np.float32)
    cm[:, 0, :] = np.eye(128)
    cm[:, 1, :] = (p[:, None] // 64 == p[None, :] // 64)
    NEG = -30000.0
    cm[:, 2, :] = np.where(p[None, :] >= p[:, None], 0.0, NEG)
    cm[:, 3, :] = np.where(p[None, :] > p[:, None], 0.0, NEG)
    cm[:, 4, :] = np.where(p[:, None] > p[None, :], 0.0, NEG)
    cm[:, 5, :] = 1.0
    same = (p[:, None] // 64 == p[None, :] // 64)
    cm[:, 6, :] = np.where(same & (p[None, :] >= p[:, None]), 0.0, NEG)
    cm[:, 7, :] = np.where(same & (p[None, :] > p[:, None]), 0.0, NEG)
    cf = np.zeros((128, 6, 128), np.float32)
    cf[:, 0, :] = np.eye(128)
    cf[:, 1, :] = (p[:, None] <= p[None, :])
    cf[:, 2, :] = 1.0
    cf[127, 3, :] = 1.0
    cf[:, 4, :] = (p[:, None] <= p[None, :]) & (p[:, None] // 64 == p[None, :] // 64)
    cf[63, 5, :] = 1.0
    return cm.astype(BF), cf


def prep_core(inp, c):
    b, half = c // 2, c % 2
    xp = inp["x_prompt"][b]
    xall = np.zeros((4096 + 4096 + 256, 1024), np.float32)
    kmask = np.zeros((128, 32), np.float32)
    if half == 1:
        xall[:8192] = xp
    else:
        xall[4096:8192] = xp[:4096]
        kmask[:] = -30000.0
    for q in range(2):
        xall[8192 + q * 128: 8192 + q * 128 + 16] = inp["x_sample"][2 * c + q]
    w_in = inp["w_in"][0]
    w_sm = np.concatenate([w_in[:, 1536:1544], w_in[:, 3080:3096]], axis=1)
    bias24 = np.concatenate([inp["fox_forget_bias"][0], inp["gdn_dt_bias"][0], np.zeros(8, np.float32)])[None, :]
    sgn24 = np.concatenate([-np.ones(8), np.ones(8), -np.ones(8)]).astype(np.float32)[None, :]
    convT = np.ascontiguousarray(inp["gdn_conv_w"][0].T.reshape(12, 128, 4).transpose(1, 0, 2))
    ch = inp["state_gdn_conv"][0, 2 * c: 2 * c + 2]
    conv_hist = np.ascontiguousarray(ch.transpose(0, 2, 1).reshape(2, 12, 128, 3).transpose(0, 2, 1, 3))
    cm, cf = const_masks()
    d = {
        "xall": xall, "kmask": kmask, "vmask": (np.arange(128) < 16).astype(np.float32)[:, None], "w_in": w_in, "w_sm": np.ascontiguousarray(w_sm), "bias24": bias24.astype(np.float32),
        "sgn24": sgn24, "a_log": inp["gdn_a_log"], "convT": convT, "conv_hist": conv_hist,
        "g_mix_pre": inp["norm_mix_pre"], "g_mix_post": inp["norm_mix_post"], "g_mlp_pre": inp["norm_mlp_pre"], "g_mlp_post": inp["norm_mlp_post"],
        "g_gdn": inp["gdn_norm_g"], "w_pa": inp["w_proj_fox"][0], "w_pb": inp["w_proj_gdn"][0], "w_out": inp["w_out"][0],
        "w_up": inp["w_up"][0], "w_down": inp["w_down"][0],
        "ck": inp["cache_fox_k"][0, 2 * c:2 * c + 2].reshape(2, 2048, 512), "cv": inp["cache_fox_v"][0, 2 * c:2 * c + 2].reshape(2, 2048, 512),
        "clf": inp["cache_fox_logf"][0, 2 * c:2 * c + 2], "sgdn": inp["state_gdn"][0, 2 * c:2 * c + 2],
        "cmask": cm, "cf32": cf,
    }
    return {k: np.ascontiguousarray(v) for k, v in d.items()}


def build_program():
    nc = bass.Bass("TRN2", target_bir_lowering=False)
    with ExitStack() as st:
        D = declare(nc, False)
        fw = FW(nc, st)
        phase_a(nc, fw, D)
        phase_b(nc, fw, D)
        phase_c(nc, fw, D)
        phase_d1(nc, fw, D)
        phase_d2(nc, fw, D)
    return nc


def kernel(**inp):
    inp = {k: np.asarray(v) for k, v in inp.items()}
    nc = build_program()
    in_maps = [prep_core(inp, c) for c in range(8)]
    res = run_bass_kernel_spmd(nc, in_maps, core_ids=list(range(8)))
    R = res.results
    f = lambda a: np.asarray(a, dtype=np.float32)
    y_p = np.zeros((4, 8192, 1024), np.float32)
    y_s = np.zeros((16, 16, 1024), np.float32)
    fk_p = np.zeros((1, 4, 8192, 8, 64), np.float32)
    fv_p = np.zeros_like(fk_p)
    lf_p = np.zeros((1, 4, 8192, 8), np.float32)
    sg_p = np.zeros((1, 4, 8, 64, 64), np.float32)
    cv_p = np.zeros((1, 4, 3, 1536), np.float32)
    fk_s = np.zeros((1, 16, 16, 8, 64), np.float32)
    fv_s = np.zeros_like(fk_s)
    lf_s = np.zeros((1, 16, 16, 8), np.float32)
    sg_s = np.zeros((1, 16, 8, 64, 64), np.float32)
    cv_s = np.zeros((1, 16, 3, 1536), np.float32)
    for c in range(8):
        b, half = c // 2, c % 2
        r = R[c]
        sl = slice(half * 4096, (half + 1) * 4096)
        y_p[b, sl] = f(r["y"])[:4096]
        fk_p[0, b, sl] = f(r["fk"])[:4096].reshape(4096, 8, 64)
        fv_p[0, b, sl] = f(r["fv"])[:4096].reshape(4096, 8, 64)
        lf_p[0, b, sl] = f(r["lf"])[:4096]
        if half == 1:
            sg_p[0, b] = f(r["sfin"])[0]
            cv_p[0, b] = f(r["convo"])[0]
        for q in range(2):
            s = 2 * c + q
            rows = slice(4096 + q * 128, 4096 + q * 128 + 16)
            y_s[s] = f(r["y"])[rows]
            fk_s[0, s] = f(r["fk"])[rows].reshape(16, 8, 64)
            fv_s[0, s] = f(r["fv"])[rows].reshape(16, 8, 64)
            lf_s[0, s] = f(r["lf"])[rows]
            sg_s[0, s] = f(r["sfin"])[1 + q]
            cv_s[0, s] = f(r["convo"])[1 + q]
    return (y_p, y_s, fk_p, fv_p, lf_p, sg_p, cv_p, fk_s, fv_s, lf_s, sg_s, cv_s)
```

```python
from contextlib import ExitStack
import numpy as np
import ml_dtypes
from concourse.bass_utils import run_bass_kernel_spmd
import concourse.bass as bass
import concourse.mybir as mybir

F32 = mybir.dt.float32
BF16 = mybir.dt.bfloat16
AF = mybir.ActivationFunctionType
ALU = mybir.AluOpType
AX = mybir.AxisListType

ENGS = ("pe", "act", "dve", "pool", "sp")
EPOCH = 30000
NDMASEM = 30
DMA_ENGS = ("sp", "pool")


class Buf:
    __slots__ = ("name", "w", "r")

    def __init__(self, name):
        self.name = name
        self.w = None
        self.r = {}


class FW:
    def __init__(self, nc, stack):
        self.nc = nc
        self.stack = stack
        self.ops = {e: [] for e in ENGS}
        self.cnt = {e: 0 for e in ENGS}
        self.ccnt = {e: 0 for e in ENGS}
        self.sems = {}
        self.dsem = {}
        self.dcnt = {}
        self.drr = {e: 0 for e in ENGS}
        self.waited = {e: {} for e in ENGS}
        for e in DMA_ENGS:
            for i in range(NDMASEM):
                self.dsem[(e, i)] = stack.enter_context(nc.semaphore(f"d_{e}_{i}"))
                self.dcnt[(e, i)] = 0
        self.nsem_ep = {e: 0 for e in ENGS}
        self.pending = {e: [] for e in ENGS}

    def _sem(self, eng, ep):
        k = (eng, ep)
        if k not in self.sems:
            self.sems[k] = self.stack.enter_context(self.nc.semaphore(f"c_{eng}_{ep}"))
        return self.sems[k]

    def _need(self, eng, tok, waits):
        if tok is None:
            return
        key, val = tok
        if eng == "pe" and key[0] == "c" and key[1] == "pe":
            return
        cur = self.waited[eng].get(key, 0)
        if cur >= val:
            return
        self.waited[eng][key] = val
        waits[key] = max(waits.get(key, 0), val)

    def op(self, eng, fn, reads=(), writes=(), dma=False):
        waits = {}
        if self.pending[eng]:
            for t in self.pending[eng]:
                self._need(eng, t, waits)
            self.pending[eng] = []
        for b in reads:
            self._need(eng, b.w, waits)
        for b in writes:
            self._need(eng, b.w, waits)
            for t in b.r.items():
                self._need(eng, t, waits)
        if dma:
            i = self.drr[eng] % NDMASEM
            self.drr[eng] += 1
            if self.dcnt[(eng, i)] > 0:
                self._need(eng, (("d", eng, i), self.dcnt[(eng, i)]), waits)
            self.dcnt[(eng, i)] += 16
            key = ("d", eng, i)
            tok = (key, self.dcnt[(eng, i)])
            inc = (self.dsem[(eng, i)], 16)
        else:
            n = self.ccnt[eng]
            self.ccnt[eng] += 1
            ep = n // EPOCH
            key = ("c", eng, ep)
            tok = (key, n % EPOCH + 1)
            inc = (self._sem(eng, ep), 1)
        self.cnt[eng] += 1
        for b in writes:
            b.w = tok
            b.r = {}
        for b in reads:
            if b not in writes:
                b.r[tok[0]] = max(b.r.get(tok[0], 0), tok[1])
        self.ops[eng].append((waits, fn, inc))
        return tok

    def I(self, eng, method, reads=(), writes=(), **kw):
        dma = method == "dma_start"
        return self.op(eng, lambda e: getattr(e, method)(**kw), reads=reads, writes=writes, dma=dma)

    def semh(self, key):
        if key[0] == "d":
            return self.dsem[(key[1], key[2])]
        return self._sem(key[1], key[2])

    def barrier(self):
        toks = []
        for e in ENGS:
            n = self.ccnt[e]
            if n > 0:
                ep = (n - 1) // EPOCH
                toks.append((("c", e, ep), (n - 1) % EPOCH + 1))
                for ep2 in range(ep):
                    toks.append((("c", e, ep2), EPOCH))
            for i in range(NDMASEM):
                if e in DMA_ENGS and self.dcnt[(e, i)] > 0:
                    toks.append((("d", e, i), self.dcnt[(e, i)]))
        for e in ENGS:
            if self.ops[e]:
                waits = {}
                for t in toks:
                    self._need(e, t, waits)
                if waits:
                    self.ops[e].append((waits, None, None))
            else:
                self.pending[e] = list(toks)

    def emit(self):
        nc = self.nc
        with nc.Block() as block:
            def mk(eng_name):
                def body(e):
                    for waits, fn, inc in self.ops[eng_name]:
                        for key, val in waits.items():
                            e.wait_ge(self.semh(key), val)
                        if fn is not None:
                            ins = fn(e)
                            ins.then_inc(inc[0], inc[1])
                return body
            regs = {"pe": block.tensor, "act": block.scalar, "dve": block.vector, "pool": block.gpsimd, "sp": block.sync}
            for en in ENGS:
                if self.ops[en]:
                    regs[en](mk(en))
        self.ops = {e: [] for e in ENGS}


NP_ = 4096
NO_ = 4096
NS_ = 256
NT_ = NP_ + NO_ + NS_
NQ_ = NO_ + NS_
EPS = 1e-6
C_FQ, C_FK, C_FV, C_FF, C_GQ, C_GA, C_GB, C_GG, C_A, C_B = 0, 512, 1024, 1536, 1544, 3080, 3088, 3096, 3608, 4632


class Ctx:
    pass


def declare(nc, debug, ext_in=()):
    D = Ctx()
    ei = lambda n, s, dt=F32: nc.dram_tensor(n, s, dt, kind="ExternalInput").ap()
    eo = lambda n, s, dt=F32: nc.dram_tensor(n, s, dt, kind="ExternalOutput").ap()
    sc = lambda n, s, dt=F32: nc.dram_tensor(n, s, dt, kind=("ExternalInput" if n in ext_in else ("ExternalOutput" if debug else "Internal"))).ap()
    D.xall = ei("xall", [NT_, 1024])
    D.kmask = ei("kmask", [128, 32])
    D.vmask = ei("vmask", [128, 1])
    D.w_in = ei("w_in", [1024, 5656])
    D.w_sm = ei("w_sm", [1024, 24])
    D.bias24 = ei("bias24", [1, 24])
    D.sgn24 = ei("sgn24", [1, 24])
    D.a_log = ei("a_log", [1, 8])
    D.convT = ei("convT", [128, 12, 4])
    D.conv_hist = ei("conv_hist", [2, 128, 12, 3])
    D.g_mix_pre = ei("g_mix_pre", [1, 1024])
    D.g_mix_post = ei("g_mix_post", [1, 1024])
    D.g_mlp_pre = ei("g_mlp_pre", [1, 1024])
    D.g_mlp_post = ei("g_mlp_post", [1, 1024])
    D.g_gdn = ei("g_gdn", [1, 64])
    D.w_pa = ei("w_pa", [512, 1024])
    D.w_pb = ei("w_pb", [512, 1024])
    D.w_out = ei("w_out", [1024, 1024])
    D.w_up = ei("w_up", [1024, 4096])
    D.w_down = ei("w_down", [4096, 1024])
    D.ck = ei("ck", [2, 2048, 512])
    D.cv = ei("cv", [2, 2048, 512])
    D.clf = ei("clf", [2, 2048, 8])
    D.sgdn = ei("sgdn", [2, 8, 64, 64])
    D.cmask = ei("cmask", [128, 8, 128], BF16)
    D.cf32 = ei("cf32", [128, 6, 128])
    D.y = eo("y", [NQ_, 1024])
    D.fk = eo("fk", [NQ_, 512])
    D.fv = eo("fv", [NQ_, 512])
    D.lf = eo("lf", [NQ_, 8])
    D.sfin = eo("sfin", [3, 8, 64, 64])
    D.convo = eo("convo", [3, 3, 1536])
    D.qT = sc("qT_s", [512, NQ_], BF16)
    D.kT = sc("kT_s", [512, NT_], BF16)
    D.V = sc("V_s", [NT_, 512], BF16)
    D.lfs = sc("lf_s", [NT_, 8])
    D.g = sc("g_s", [NT_, 8])
    D.beta = sc("beta_s", [NT_, 16])
    D.gqT = sc("gqT_s", [512, NT_], BF16)
    D.gkT = sc("gkT_s", [512, NT_], BF16)
    D.gqkv = sc("gqkv_s", [NT_, 1536], BF16)
    D.gg = sc("gg_s", [NQ_, 512])
    D.sgA = sc("sgA_s", [1024, NQ_], BF16)
    D.sgB = sc("sgB_s", [1024, NQ_], BF16)
    D.foT = sc("foT_s", [512, NQ_], BF16)
    D.go = sc("go_s", [NQ_, 512], BF16)
    D.y1 = sc("y1_s", [NQ_, 1024])
    return D


class Rot:
    def __init__(self, items):
        self.items = items
        self.i = 0

    def next(self):
        it = self.items[self.i % len(self.items)]
        self.i += 1
        return it


def phase_a(nc, fw, D, tiles=None):
    I = fw.I
    with ExitStack() as st:
        def sb(name, shape, dt):
            return st.enter_context(nc.sbuf_tensor("A_" + name, shape, dt)), Buf(name)

        def pst(name, shape, dt):
            return st.enter_context(nc.psum_tensor("A_" + name, shape, dt)), Buf(name)
        Win, bWin = sb("Win", [128, 8, 5656], BF16)
        Wsm, bWsm = sb("Wsm", [128, 8, 24], BF16)
        xts = Rot([sb(f"xt{i}", [128, 4, 1024], F32) for i in range(2)])
        hb, bhb = sb("hb", [128, 4, 1024], BF16)
        hTs = Rot([sb(f"hT{i}", [128, 8, 512], BF16) for i in range(1)])
        junk, bjunk = sb("junk", [128, 1024], BF16)
        ss, bss = sb("ss", [128, 4], F32)
        rs, brs = sb("rs", [128, 4], F32)
        gpre, bgpre = sb("gpre", [128, 1024], F32)
        identb, bidentb = sb("identb", [128, 128], BF16)
        diagw, bdiagw = sb("diagw", [128, 48, 128], BF16)
        cwT, bcwT = sb("cwT", [128, 12, 4], F32)
        xg, bxg = sb("xg", [128, 12, 515], BF16)
        cT, bcT = sb("cT", [128, 12, 512], BF16)
        sbf = Rot([sb(f"sbf{i}", [128, 512], BF16) for i in range(4)])
        sf32 = Rot([sb(f"sf32{i}", [128, 512], F32) for i in range(2)])
        tts = Rot([sb(f"tt{i}", [128, 512], BF16) for i in range(3)])
        sqt, bsqt = sb("sqt", [128, 1024], BF16)
        l2ss, bl2ss = sb("l2ss", [128, 4, 16], F32)
        tball, btball = sb("tball", [128, 4, 1536], BF16)
        sm_t, bsm_t = sb("sm_t", [128, 96], F32)
        sm_e, bsm_e = sb("sm_e", [128, 96], F32)
        sm_l, bsm_l = sb("sm_l", [128, 96], F32)
        sm_o, bsm_o = sb("sm_o", [128, 4, 32], F32)
        b24, bb24 = sb("b24", [128, 24], F32)
        s24, bs24 = sb("s24", [128, 24], F32)
        nea, bnea = sb("nea", [128, 8], F32)
        ptr = Rot([pst(f"ptr{i}", [128, 1024], BF16) for i in range(2)])
        pm = Rot([pst(f"pm{i}", [128, 512], F32) for i in range(4)])
        psm, bpsm = pst("psm", [128, 512], F32)

        I("pool", "dma_start", [], [bWin], out=Win[:], in_=D.w_in.rearrange("(c p) n -> p c n", p=128))
        I("pool", "dma_start", [], [bWsm], out=Wsm[:], in_=D.w_sm.rearrange("(c p) n -> p c n", p=128))
        I("sp", "dma_start", [], [bgpre], out=gpre[:], in_=D.g_mix_pre[0:1, :].broadcast_to([128, 1024]))
        I("sp", "dma_start", [], [bidentb], out=identb[:], in_=D.cmask[:, 0, :])
        I("sp", "dma_start", [], [bcwT], out=cwT[:], in_=D.convT[:, :, :])
        I("sp", "dma_start", [], [bb24], out=b24[:], in_=D.bias24[0:1, :].broadcast_to([128, 24]))
        I("sp", "dma_start", [], [bs24], out=s24[:], in_=D.sgn24[0:1, :].broadcast_to([128, 24]))
        I("sp", "dma_start", [], [bnea], out=nea[:], in_=D.a_log[0:1, :].broadcast_to([128, 8]))
        I("act", "activation", [bnea], [bnea], out=nea[:], in_=nea[:], func=AF.Exp)
        I("dve", "tensor_scalar_mul", [bnea], [bnea], out=nea[:], in0=nea[:], scalar1=-1.0)
        I("dve", "tensor_scalar_mul", [bcwT], [bcwT], out=cwT[:], in0=cwT[:], scalar1=0.5)
        for j in range(12):
            for i in range(4):
                I("dve", "tensor_scalar_mul", [bidentb, bcwT], [bdiagw], out=diagw[:, j * 4 + i, :], in0=identb[:], scalar1=cwT[:, j, i:i + 1])
        I("dve", "memset", [], [bxg], ap=xg[:, :, 0:3], constant=0.0)

        all_tiles = [(i * 512, 4, "p") for i in range(8)] + [(NP_ + i * 512, 4, "o") for i in range(8)] + [(NP_ + NO_, 2, "s")]
        if tiles is not None:
            all_tiles = [all_tiles[i] for i in tiles]
        cpy_rr = [0]

        def evac_copy(out_ap, in_ap, reads, writes, scale=None):
            cpy_rr[0] += 1
            if scale is not None:
                I("act", "activation", reads, writes, out=out_ap, in_=in_ap, func=AF.Copy, scale=scale)
            elif cpy_rr[0] % 2 == 0:
                I("act", "activation", reads, writes, out=out_ap, in_=in_ap, func=AF.Copy)
            else:
                I("dve", "tensor_copy", reads, writes, out=out_ap, in_=in_ap)

        def store(eng, out_ap, in_ap, buf):
            I(eng, "dma_start", [buf], [], out=out_ap, in_=in_ap)

        def load_x(n_):
            t0_, ns_, _k = all_tiles[n_]
            xt_, bxt_ = xts.items[n_ % 2]
            I("sp", "dma_start", [], [bxt_], out=xt_[:, 0:ns_, :], in_=D.xall[t0_:t0_ + ns_ * 128, :].rearrange("(s p) m -> p s m", p=128))
        if all_tiles:
            load_x(0)
        for n_, (t0, ns, kind) in enumerate(all_tiles):
            N = ns * 128
            own = kind in ("o", "s")
            tq = t0 - NP_
            xt, bxt = xts.items[n_ % 2]
            hT, bhT = hTs.next()
            if n_ + 1 < len(all_tiles):
                load_x(n_ + 1)
            for s in range(ns):
                I("act", "activation", [bxt], [bjunk, bss], out=junk[:], in_=xt[:, s, :], func=AF.Square, accum_out=ss[:, s:s + 1])
            I("act", "activation", [bss], [brs], out=rs[:, 0:ns], in_=ss[:, 0:ns], func=AF.Sqrt, bias=EPS, scale=1.0 / 1024)
            I("dve", "reciprocal", [brs], [brs], out=rs[:, 0:ns], in_=rs[:, 0:ns])
            for s in range(ns):
                I("dve", "scalar_tensor_tensor", [bxt, brs, bgpre], [bhb], out=hb[:, s, :], in0=xt[:, s, :], scalar=rs[:, s:s + 1], in1=gpre[:],
                  op0=ALU.mult, op1=ALU.mult)
            for kcp in range(4):
                pt, bpt = ptr.next()
                for kk in range(2):
                    kc = 2 * kcp + kk
                    for s in range(ns):
                        I("pe", "transpose", [bhb, bidentb], [bpt], out=pt[:, kk * 512 + s * 128: kk * 512 + (s + 1) * 128],
                          in_=hb[:, s, kc * 128:(kc + 1) * 128], identity=identb[:])
                for kk in range(2):
                    evac_copy(hT[:, 2 * kcp + kk, 0:N], pt[:, kk * 512: kk * 512 + N], [bpt], [bhT])

            def fm_mm(c0, w=128):
                p, bp = pm.next()
                for kc in range(8):
                    I("pe", "matmul", [bWin, bhT], [bp], out=p[0:w, 0:N], lhsT=Win[:, kc, c0:c0 + w], rhs=hT[:, kc, 0:N], start=(kc == 0), stop=(kc == 7))
                return p, bp

            def tm_mm(s, c0, w, Wt=None, bW=None, out=None):
                if out is None:
                    p, bp = pm.next()
                    o = p[:, 0:w]
                else:
                    p, bp, o = out
                Wt_ = Win if Wt is None else Wt
                bW_ = bWin if bW is None else bW
                for kc in range(8):
                    I("pe", "matmul", [bW_, bhT], [bp], out=o, lhsT=hT[:, kc, s * 128:(s + 1) * 128], rhs=Wt_[:, kc, c0:c0 + w], start=(kc == 0), stop=(kc == 7))
                return p, bp

            fillers = []

            def do_fq(j):
                p, bp = fm_mm(C_FQ + j * 128)
                s_, bs_ = sbf.next()
                evac_copy(s_[:, 0:N], p[:, 0:N], [bp], [bs_], scale=0.125)
                store("sp", D.qT[j * 128:(j + 1) * 128, tq:tq + N], s_[:, 0:N], bs_)

            def do_fk(j):
                p, bp = fm_mm(C_FK + j * 128)
                s_, bs_ = sbf.next()
                evac_copy(s_[:, 0:N], p[:, 0:N], [bp], [bs_])
                store("sp", D.kT[j * 128:(j + 1) * 128, t0:t0 + N], s_[:, 0:N], bs_)

            def do_gate(j):
                p, bp = fm_mm(C_A + j * 128)
                s_, bs_ = sbf.next()
                tt, btt = tts.next()
                I("act", "activation", [bp], [btt], out=tt[:, 0:N], in_=p[:, 0:N], func=AF.Tanh, scale=0.5)
                I("dve", "tensor_scalar", [btt], [bs_], out=s_[:, 0:N], in0=tt[:, 0:N], scalar1=0.5, scalar2=0.5, op0=ALU.mult, op1=ALU.add)
                dst = D.sgA if j < 8 else D.sgB
                store("sp", dst[(j % 8) * 128:(j % 8 + 1) * 128, tq:tq + N], s_[:, 0:N], bs_)

            def do_tm_k(s):
                r0 = t0 + s * 128
                p, bp = tm_mm(s, C_FK, 512)
                sf, bsf = sf32.next()
                evac_copy(sf[:], p[:], [bp], [bsf])
                store("sp", D.fk[r0 - NP_: r0 - NP_ + 128, :], sf[:], bsf)

            def do_tm_v(s):
                r0 = t0 + s * 128
                p, bp = tm_mm(s, C_FV, 512)
                s_, bs_ = sbf.next()
                if own:
                    sf, bsf = sf32.next()
                    I("dve", "tensor_copy", [bp], [bsf], out=sf[:], in_=p[:])
                    store("sp", D.fv[r0 - NP_: r0 - NP_ + 128, :], sf[:], bsf)
                    I("act", "activation", [bsf], [bs_], out=s_[:], in_=sf[:], func=AF.Copy)
                else:
                    I("dve", "tensor_copy", [bp], [bs_], out=s_[:], in_=p[:])
                store("sp", D.V[r0:r0 + 128, :], s_[:], bs_)

            def do_tm_gg(s):
                r0 = t0 + s * 128
                p, bp = tm_mm(s, C_GG, 512)
                sf, bsf = sf32.next()
                tt, btt = tts.next()
                I("act", "activation", [bp], [btt], out=tt[:], in_=p[:], func=AF.Tanh, scale=0.5)
                I("dve", "scalar_tensor_tensor", [btt, bp], [bsf], out=sf[:], in0=tt[:], scalar=1.0, in1=p[:], op0=ALU.add, op1=ALU.mult)
                store("sp", D.gg[r0 - NP_: r0 - NP_ + 128, :], sf[:], bsf)

            def do_tm_small(s):
                tm_mm(s, 0, 24, Wt=Wsm, bW=bWsm, out=(psm, bpsm, psm[:, s * 24:(s + 1) * 24]))

            if own:
                for j in range(4):
                    fillers.append((do_fq, j))
            for j in range(4):
                fillers.append((do_fk, j))
            for s in range(ns):
                if own:
                    fillers.append((do_tm_k, s))
                fillers.append((do_tm_v, s))
                if own:
                    fillers.append((do_tm_gg, s))
                fillers.append((do_tm_small, s))
            if own:
                for j in range(16):
                    fillers.append((do_gate, j))
            per_step = (len(fillers) + 11) // 12

            def run_fillers(n):
                for _ in range(n):
                    if fillers:
                        f_, a_ = fillers.pop(0)
                        f_(a_)

            for j in range(12):
                p, bp = fm_mm(C_GQ + j * 128)
                evac_copy(xg[:, j, 3:3 + N], p[:, 0:N], [bp], [bxg])
            if kind == "s":
                for q_ in range(2):
                    I("pool", "dma_start", [], [bxg], out=xg[:, :, q_ * 128: q_ * 128 + 3], in_=D.conv_hist[q_])

            for j in range(12):
                p, bp = pm.next()
                for i in range(4):
                    I("pe", "matmul", [bdiagw, bxg], [bp], out=p[:, 0:N], lhsT=diagw[:, j * 4 + i, :], rhs=xg[:, j, i:i + N], start=(i == 0), stop=(i == 3))
                tt, btt = tts.next()
                I("act", "activation", [bp], [btt], out=tt[:, 0:N], in_=p[:, 0:N], func=AF.Tanh)
                I("dve", "scalar_tensor_tensor", [btt, bp], [bcT], out=cT[:, j, 0:N], in0=tt[:, 0:N], scalar=1.0, in1=p[:, 0:N], op0=ALU.add, op1=ALU.mult)
                run_fillers(per_step)
            run_fillers(len(fillers))
            I("dve", "tensor_copy", [bxg], [bxg], out=xg[:, :, 0:3], in_=xg[:, :, N:N + 3])
            for s in range(ns):
                for half in range(2):
                    pt, bpt = ptr.next()
                    nj = 8 if half == 0 else 4
                    for jj in range(nj):
                        j = half * 8 + jj
                        I("pe", "transpose", [bcT, bidentb], [bpt], out=pt[:, jj * 128:(jj + 1) * 128], in_=cT[:, j, s * 128:(s + 1) * 128], identity=identb[:])
                    evac_copy(tball[:, s, half * 1024: half * 1024 + nj * 128], pt[:, 0:nj * 128], [bpt], [btball])
                I("dve", "tensor_tensor", [btball], [bsqt], out=sqt[:], in0=tball[:, s, 0:1024], in1=tball[:, s, 0:1024], op=ALU.mult)
                I("dve", "tensor_reduce", [bsqt], [bl2ss], out=l2ss[:, s, :], in_=sqt[:].rearrange("p (h d) -> p h d", d=64), axis=AX.X, op=ALU.add)
            I("act", "activation", [bl2ss], [bl2ss], out=l2ss[:, 0:ns, :], in_=l2ss[:, 0:ns, :], func=AF.Sqrt, bias=EPS)
            I("dve", "reciprocal", [bl2ss], [bl2ss], out=l2ss[:, 0:ns, :], in_=l2ss[:, 0:ns, :])
            I("dve", "tensor_scalar_mul", [bl2ss], [bl2ss], out=l2ss[:, 0:ns, 0:8], in0=l2ss[:, 0:ns, 0:8], scalar1=0.125)
            for s in range(ns):
                qk = tball[:, s, 0:1024].rearrange("p (h d) -> p h d", d=64)
                I("dve", "tensor_tensor", [btball, bl2ss], [btball], out=qk, in0=qk, in1=l2ss[:, s, :].unsqueeze(2).broadcast_to([128, 16, 64]), op=ALU.mult)
                store("sp", D.gqkv[t0 + s * 128: t0 + (s + 1) * 128, :], tball[:, s, :], btball)
            n24 = ns * 24
            for s in range(ns):
                I("dve", "tensor_tensor", [bpsm, bb24], [bsm_t], out=sm_t[:, s * 24:(s + 1) * 24], in0=psm[:, s * 24:(s + 1) * 24], in1=b24[:], op=ALU.add)
                I("dve", "tensor_tensor", [bsm_t, bs24], [bsm_t], out=sm_t[:, s * 24:(s + 1) * 24], in0=sm_t[:, s * 24:(s + 1) * 24], in1=s24[:], op=ALU.mult)
            I("act", "activation", [bsm_t], [bsm_e], out=sm_e[:, 0:n24], in_=sm_t[:, 0:n24], func=AF.Exp)
            I("act", "activation", [bsm_e], [bsm_l], out=sm_l[:, 0:n24], in_=sm_e[:, 0:n24], func=AF.Ln, bias=1.0)
            for s in range(ns):
                I("dve", "tensor_scalar_mul", [bsm_l], [bsm_o], out=sm_o[:, s, 0:8], in0=sm_l[:, s * 24: s * 24 + 8], scalar1=-1.0)
                I("dve", "tensor_tensor", [bsm_l, bnea], [bsm_o], out=sm_o[:, s, 8:16], in0=sm_l[:, s * 24 + 8: s * 24 + 16], in1=nea[:], op=ALU.mult)
                I("dve", "tensor_scalar_add", [bsm_e], [bsm_o], out=sm_o[:, s, 16:24], in0=sm_e[:, s * 24 + 16: s * 24 + 24], scalar1=1.0)
                I("dve", "reciprocal", [bsm_o], [bsm_o], out=sm_o[:, s, 16:24], in_=sm_o[:, s, 16:24])
                I("dve", "tensor_scalar_mul", [bsm_l], [bsm_o], out=sm_o[:, s, 24:32], in0=sm_l[:, s * 24 + 16: s * 24 + 24], scalar1=-1.0)
            for (dst, c0, cw_) in ((D.lfs, 0, 8), (D.g, 8, 8), (D.beta, 16, 16)):
                I("sp", "dma_start", [bsm_o], [], out=dst[t0:t0 + N, :].rearrange("(s p) m -> p s m", p=128), in_=sm_o[:, 0:ns, c0:c0 + cw_])
            if own:
                I("sp", "dma_start", [bsm_o], [], out=D.lf[tq:tq + N, :].rearrange("(s p) m -> p s m", p=128), in_=sm_o[:, 0:ns, 0:8])
            conv_rows = []
            if kind == "o" and t0 == NP_ + NO_ - 512:
                conv_rows = [(3, 125, 0)]
            if kind == "s":
                conv_rows = [(0, 13, 1), (1, 13, 2)]
            for (s, r, oi) in conv_rows:
                for cg in range(3):
                    p, bp = tm_mm(s, C_GQ + cg * 512, 512)
                    sf, bsf = sf32.next()
                    evac_copy(sf[:], p[:], [bp], [bsf])
                    store("sp", D.convo[oi, :, cg * 512:(cg + 1) * 512], sf[r:r + 3, :], bsf)
        fw.barrier()
        fw.emit()


def phase_b(nc, fw, D, pairs=(0, 1, 2, 3), qtiles=tuple(range(8)), samples=(0, 1)):
    with ExitStack() as st:
        for _ in phase_b_body(nc, fw, D, st, None, pairs, qtiles, samples):
            pass
        fw.barrier()
        fw.emit()


def phase_b_body(nc, fw, D, st, banks, pairs=(0, 1, 2, 3), qtiles=tuple(range(8)), samples=(0, 1)):
    I = fw.I
    if True:
        def sb(name, shape, dt):
            return st.enter_context(nc.sbuf_tensor("B_" + name, shape, dt)), Buf(name)

        def pst(name, shape, dt):
            return st.enter_context(nc.psum_tensor("B_" + name, shape, dt)), Buf(name)
        identb, bidentb = sb("identb", [128, 128], BF16)
        mincl, bmincl = sb("mincl", [128, 128], BF16)
        cf, bcf = sb("cf", [128, 4, 128], F32)
        kmask, bkmask = sb("kmask", [128, 32], F32)
        lf, blf = sb("lf", [128, 64, 8], F32)
        Fin, bFin = sb("Fin", [128, 64, 8], F32)
        tot, btot = sb("tot", [128, 64, 8], F32)
        scA, bscA = sb("scA", [128, 64, 8], F32)
        scB, bscB = sb("scB", [128, 64, 8], F32)
        offs, boffs = sb("offs", [128, 64, 8], F32)
        Fm, bFm = sb("Fm", [128, 64, 8], F32)
        cfac, bcfac = sb("cfac", [128, 64, 8], F32)
        psets = [(sb(f"kTp{i}", [128, 8192], BF16), sb(f"qTp{i}", [128, 4096], BF16), sb(f"Vx{i}", [128, 64, 2, 65], BF16)) for i in range(2)]
        boffs_ = [sb(f"boff{i}", [128, 64], F32) for i in range(2)]
        bdgs_ = [sb(f"bdg{i}", [128, 16], F32) for i in range(2)]
        osbs_ = [sb(f"osb{i}", [65, 512], F32) for i in range(2)]
        PTs = Rot([sb(f"PT{i}", [128, 512], BF16) for i in range(6)])
        rsbs = Rot([sb(f"rsb{i}", [65, 512], F32) for i in range(2)])
        rrecs = Rot([sb(f"rrec{i}", [65, 512], F32) for i in range(2)])
        fos = Rot([sb(f"fo{i}", [64, 512], BF16) for i in range(2)])
        kc_tm, bkc_tm = sb("kc_tm", [128, 16, 512], BF16)
        if banks is None:
            LA = 2
            pS = Rot([pst(f"pS{i}", [128, 512], F32) for i in range(LA + 1)])
            pOos = [pst(f"pOo{i}", [128, 512], F32) for i in range(2)]
            pOds = [pst(f"pOd{i}", [128, 512], F32) for i in range(2)]
            pB_, bpB_ = pst("pB", [128, 512], F32)
            getB = lambda: (pB_, bpB_)
            getT = lambda: (pB_[:].bitcast(BF16), bpB_)
            getJ = lambda: (pB_, bpB_)
        else:
            pS = Rot(list(banks[0:2]))
            pOos = [banks[2], banks[2]]
            pOds = [banks[3], banks[3]]
            LA = 1
            getB = lambda: pS.next()
            getJ = lambda: pS.next()

            def getT():
                t_, b_ = pS.next()
                return t_[:].bitcast(BF16), b_
        jk, bjk = sb("jk", [128, 512], BF16)
        I("dve", "memset", [], [bjk], ap=jk[:], constant=0.0)
        JN = 0
        JB = 12

        I("sp", "dma_start", [], [bidentb], out=identb[:], in_=D.cmask[:, 0, :])
        I("sp", "dma_start", [], [bmincl], out=mincl[:], in_=D.cmask[:, 2, :])
        I("sp", "dma_start", [], [bcf], out=cf[:], in_=D.cf32[:, 0:4, :])
        I("sp", "dma_start", [], [bkmask], out=kmask[:], in_=D.kmask[:, :])
        for (_, _, (Vx_, bVx_)) in psets:
            I("dve", "memset", [], [bVx_], ap=Vx_[:, :, :, 64:65], constant=1.0)

        def flat(t, nb):
            return t[:, 0:nb, :].rearrange("p b h -> p (b h)")

        def build_F(nb, masked):
            n = nb * 8
            pB, bpB = getB()
            I("pe", "matmul", [bcf, blf], [bpB], out=pB[:, 0:n], lhsT=cf[:, 1, :], rhs=flat(lf, nb), start=True, stop=True)
            I("dve", "tensor_copy", [bpB], [bFin], out=flat(Fin, nb), in_=pB[:, 0:n])
            pB, bpB = getB()
            I("pe", "matmul", [bcf, bFin], [bpB], out=pB[:, 0:n], lhsT=cf[:, 3, :], rhs=flat(Fin, nb), start=True, stop=True)
            I("dve", "tensor_copy", [bpB], [btot], out=flat(tot, nb), in_=pB[:, 0:n])
            src, bsrc = tot, btot
            k = 1
            pp = [(scA, bscA), (scB, bscB)]
            ii = 0
            while k < nb:
                dst, bdst = pp[ii % 2]
                ii += 1
                I("dve", "tensor_copy", [bsrc], [bdst], out=dst[:, 0:k, :], in_=src[:, 0:k, :])
                I("dve", "tensor_tensor", [bsrc], [bdst], out=dst[:, k:nb, :], in0=src[:, k:nb, :], in1=src[:, 0:nb - k, :], op=ALU.add)
                src, bsrc = dst, bdst
                k *= 2
            I("dve", "tensor_tensor", [bsrc, btot], [boffs], out=offs[:, 0:nb, :], in0=src[:, 0:nb, :], in1=tot[:, 0:nb, :], op=ALU.subtract)
            I("dve", "tensor_tensor", [bFin, boffs], [bFm], out=Fm[:, 0:nb, :], in0=Fin[:, 0:nb, :], in1=offs[:, 0:nb, :], op=ALU.add)
            if masked:
                for h in range(8):
                    I("dve", "tensor_tensor", [bFm, bkmask], [bFm], out=Fm[:, 0:32, h], in0=Fm[:, 0:32, h], in1=kmask[:, :], op=ALU.subtract)

        def attend(pset, streams):
            (kTp, bkTp), (qTp, bqTp), (Vx, bVx) = pset
            ctx = []
            for si, (hh, h, qcol, W, qb0, nsub, out_col) in enumerate(streams):
                boff, bboff = boffs_[si]
                bdg, bbdg = bdgs_[si]
                rsb, brsb = rsbs.next()
                rrec, brrec = rrecs.next()
                if qb0 > 0:
                    I("dve", "tensor_scalar", [bFm, boffs], [bboff], out=boff[:, 0:qb0], in0=Fm[:, 0:qb0, h], scalar1=offs[:, qb0, h:h + 1], scalar2=-1.0,
                      op0=ALU.subtract, op1=ALU.mult)
                for i in range(nsub):
                    I("dve", "tensor_scalar", [bFm, boffs], [bbdg], out=bdg[:, i * 4: i * 4 + i + 1], in0=Fm[:, qb0: qb0 + i + 1, h], scalar1=offs[:, qb0 + i, h:h + 1],
                      scalar2=-1.0, op0=ALU.subtract, op1=ALU.mult)
                if nsub > 1:
                    I("dve", "tensor_scalar", [boffs], [bcfac], out=cfac[:, qb0:qb0 + nsub, h], in0=offs[:, qb0:qb0 + nsub, h], scalar1=offs[:, qb0, h:h + 1], scalar2=None,
                      op0=ALU.subtract)
                    I("act", "activation", [bcfac], [bcfac], out=cfac[:, qb0:qb0 + nsub, h], in_=cfac[:, qb0:qb0 + nsub, h], func=AF.Exp)
                ctx.append((boff, bboff, bdg, bbdg, rsb, brsb, rrec, brrec))
            for _ in range(JB):
                pJ, bpJ = getJ()
                I("pe", "matmul", [bidentb, bjk], [bpJ], out=pJ[:, 0:512], lhsT=identb[:], rhs=jk[:, 0:512], start=True, stop=True)
            yield
            lists = []
            for si, (hh, h, qcol, W, qb0, nsub, out_col) in enumerate(streams):
                lists.append([(si, "o", kb, 0, 0) for kb in range(qb0)] + [(si, "d", qb0 + j, i, j) for i in range(nsub) for j in range(i + 1)])
            work = []
            for t_ in range(max(len(l_) for l_ in lists)):
                for l_ in lists:
                    if t_ < len(l_):
                        work.append(l_[t_])
            inflight = {}
            for idx in range(len(work) + LA):
                if idx < len(work):
                    si, kind, kb, i, j = work[idx]
                    hh, h, qcol, W, qb0, nsub, out_col = streams[si]
                    ps_ = slice(hh * 64, (hh + 1) * 64)
                    Wd = min(W, 128)
                    p, bp = pS.next()
                    if kind == "o":
                        I("pe", "matmul", [bkTp, bqTp], [bp], out=p[:, 0:W], lhsT=kTp[ps_, kb * 128:(kb + 1) * 128], rhs=qTp[ps_, qcol:qcol + W], start=True, stop=True)
                    else:
                        I("pe", "matmul", [bkTp, bqTp], [bp], out=p[:, 0:Wd], lhsT=kTp[ps_, kb * 128:(kb + 1) * 128], rhs=qTp[ps_, qcol + i * 128: qcol + i * 128 + Wd],
                          start=True, stop=(j != i))
                        if j == i:
                            I("pe", "matmul", [bidentb, bmincl], [bp], out=p[:, 0:Wd], lhsT=identb[:], rhs=mincl[:, 0:Wd], start=False, stop=True)
                    inflight[idx] = (p, bp)
                k2 = idx - LA
                if k2 >= 0:
                    si, kind, kb, i, j = work[k2]
                    hh, h, qcol, W, qb0, nsub, out_col = streams[si]
                    boff, bboff, bdg, bbdg = ctx[si][0:4]
                    pOo, bpOo = pOos[si]
                    pOd, bpOd = pOds[si]
                    Wd = min(W, 128)
                    p, bp = inflight.pop(k2)
                    pt, bpt = PTs.next()
                    if kind == "o":
                        I("act", "activation", [bp, bboff], [bpt], out=pt[:, 0:W], in_=p[:, 0:W], func=AF.Exp, bias=boff[:, kb:kb + 1])
                        I("pe", "matmul", [bVx, bpt], [bpOo], out=pOo[0:65, 0:W], lhsT=Vx[:, kb, hh, :], rhs=pt[:, 0:W], start=(kb == 0), stop=(kb == qb0 - 1))
                    else:
                        I("act", "activation", [bp, bbdg], [bpt], out=pt[:, 0:Wd], in_=p[:, 0:Wd], func=AF.Exp, bias=bdg[:, i * 4 + j: i * 4 + j + 1])
                        I("pe", "matmul", [bVx, bpt], [bpOd], out=pOd[0:65, i * 128: i * 128 + Wd], lhsT=Vx[:, kb, hh, :], rhs=pt[:, 0:Wd], start=(j == 0), stop=(j == i))
                yield
            for si, (hh, h, qcol, W, qb0, nsub, out_col) in enumerate(streams):
                boff, bboff, bdg, bbdg, rsb, brsb, rrec, brrec = ctx[si]
                pOo, bpOo = pOos[si]
                pOd, bpOd = pOds[si]
                osb, bosb = osbs_[si]
                Wd = min(W, 128)
                if qb0 > 0:
                    I("act", "activation", [bpOo], [bosb], out=osb[:, 0:W], in_=pOo[0:65, 0:W], func=AF.Copy)
                    for i in range(nsub):
                        sl = slice(i * 128, i * 128 + Wd)
                        if nsub > 1:
                            I("dve", "scalar_tensor_tensor", [bosb, bcfac, bpOd], [brsb], out=rsb[:, sl], in0=osb[:, sl], scalar=cfac[0:65, qb0 + i, h:h + 1], in1=pOd[0:65, sl],
                              op0=ALU.mult, op1=ALU.add)
                        else:
                            I("dve", "tensor_tensor", [bosb, bpOd], [brsb], out=rsb[:, sl], in0=osb[:, sl], in1=pOd[0:65, sl], op=ALU.add)
                else:
                    I("dve", "tensor_copy", [bpOd], [brsb], out=rsb[:, 0:W], in_=pOd[0:65, 0:W])
                I("dve", "reciprocal", [brsb], [brrec], out=rrec[64:65, 0:W], in_=rsb[64:65, 0:W])
                pB, bpB = getB()
                I("pe", "matmul", [bcf, brrec], [bpB], out=pB[0:64, 0:W], lhsT=cf[64:65, 2, 0:64], rhs=rrec[64:65, 0:W], start=True, stop=True)
                fo, bfo = fos.next()
                I("dve", "tensor_tensor", [brsb, bpB], [bfo], out=fo[:, 0:W], in0=rsb[0:64, 0:W], in1=pB[0:64, 0:W], op=ALU.mult)
                I("pool", "dma_start", [bfo], [], out=D.foT[h * 64:(h + 1) * 64, out_col:out_col + W], in_=fo[:, 0:W])
            yield

        if len(qtiles) > 0:
            I("sp", "dma_start", [], [blf], out=lf[:], in_=D.lfs[0:8192, :].rearrange("(b p) h -> p b h", p=128))
            build_F(64, True)
            def load_pair(pr, pset):
                (kTp, bkTp), (qTp, bqTp), (Vx, bVx) = pset
                I("sp", "dma_start", [], [bkTp], out=kTp[:, :], in_=D.kT[pr * 128:(pr + 1) * 128, 0:8192])
                I("sp", "dma_start", [], [bqTp], out=qTp[:, :], in_=D.qT[pr * 128:(pr + 1) * 128, 0:4096])
                for hh in range(2):
                    I("sp", "dma_start", [], [bVx], out=Vx[:, :, hh, 0:64],
                      in_=D.V[0:8192, (2 * pr + hh) * 64:(2 * pr + hh + 1) * 64].rearrange("(b p) d -> p b d", p=128))
            pl = list(pairs)
            load_pair(pl[0], psets[0])
            for n_, pr in enumerate(pl):
                if n_ + 1 < len(pl):
                    load_pair(pl[n_ + 1], psets[(n_ + 1) % 2])
                for qt in qtiles:
                    yield from attend(psets[n_ % 2], [(hh, 2 * pr + hh, qt * 512, 512, 32 + 4 * qt, 4, qt * 512) for hh in range(2)])
        for q in samples:
            r0 = NP_ + NO_ + q * 128
            I("sp", "dma_start", [], [blf], out=lf[:, 0:16, :], in_=D.clf[q].rearrange("(b p) h -> p b h", p=128))
            I("sp", "dma_start", [], [blf], out=lf[:, 16, :], in_=D.lfs[r0:r0 + 128, :])
            build_F(17, False)
            I("pool", "dma_start", [], [bkc_tm], out=kc_tm[:], in_=D.ck[q].rearrange("(b p) c -> p b c", p=128))
            for n_, pr in enumerate(pairs):
                pset = psets[n_ % 2]
                (kTp, bkTp), (qTp, bqTp), (Vx, bVx) = pset
                for b4 in range(2):
                    pT, bpT = getT()
                    for bb in range(8):
                        blk = b4 * 8 + bb
                        I("pe", "transpose", [bkc_tm, bidentb], [bpT], out=pT[:, bb * 128:(bb + 1) * 128], in_=kc_tm[:, blk, pr * 128:(pr + 1) * 128], identity=identb[:])
                    I("dve", "tensor_copy", [bpT], [bkTp], out=kTp[:, b4 * 1024:(b4 + 1) * 1024], in_=pT[:, :])
                I("sp", "dma_start", [], [bkTp], out=kTp[:, 2048:2176], in_=D.kT[pr * 128:(pr + 1) * 128, r0:r0 + 128])
                I("sp", "dma_start", [], [bqTp], out=qTp[:, 0:128], in_=D.qT[pr * 128:(pr + 1) * 128, NO_ + q * 128: NO_ + (q + 1) * 128])
                for hh in range(2):
                    h = 2 * pr + hh
                    I("pool", "dma_start", [], [bVx], out=Vx[:, 0:16, hh, 0:64], in_=D.cv[q][:, h * 64:(h + 1) * 64].rearrange("(b p) d -> p b d", p=128))
                    I("sp", "dma_start", [], [bVx], out=Vx[:, 16, hh, 0:64], in_=D.V[r0:r0 + 128, h * 64:(h + 1) * 64])
                yield from attend(pset, [(hh, 2 * pr + hh, 0, 128, 16, 1, NO_ + q * 128) for hh in range(2)])


def phase_c(nc, fw, D, blocks=None, samples=(0, 1), o_all=False):
    with ExitStack() as st:
        for _ in phase_c_body(nc, fw, D, st, None, blocks, samples, o_all):
            pass
        fw.barrier()
        fw.emit()


def phase_c_body(nc, fw, D, st, banks, blocks=None, samples=(0, 1), o_all=False):
    I = fw.I
    if True:
        def sb(name, shape, dt):
            return st.enter_context(nc.sbuf_tensor("C_" + name, shape, dt)), Buf(name)

        def pst(name, shape, dt):
            return st.enter_context(nc.psum_tensor("C_" + name, shape, dt)), Buf(name)
        identb, bidentb = sb("identb", [128, 128], BF16)
        identf, bidentf = sb("identf", [128, 128], F32)
        mincl, bmincl = sb("mincl", [128, 128], BF16)
        mstr, bmstr = sb("mstr", [128, 128], BF16)
        cf, bcf = sb("cf", [128, 6, 128], F32)
        ggdn, bggdn = sb("ggdn", [128, 8, 64], F32)
        vmask, bvmask = sb("vmask", [128, 1], F32)
        insets = [(sb(f"qkv{i}", [128, 3, 8, 64], BF16), sb(f"kT{i}", [64, 8, 128], BF16), sb(f"qT{i}", [64, 8, 128], BF16),
                   sb(f"g{i}", [128, 8], F32), sb(f"be{i}", [128, 16], F32)) for i in range(2)]
        gc, bgc = sb("gc", [128, 8], F32)
        ngc, bngc = sb("ngc", [128, 8], F32)
        vec2, bvec2 = sb("vec2", [128, 8], F32)
        eg, beg = sb("eg", [128, 8], F32)
        bee, bbee = sb("bee", [128, 8], F32)
        glb, bglb = sb("glb", [128, 8], F32)
        gl12, bgl12 = sb("gl12", [128, 16], F32)
        egl, begl = sb("egl", [128, 2, 8], F32)
        ekl, bekl = sb("ekl", [128, 8], F32)
        dg1, bdg1 = sb("dg1", [128, 8, 128], F32)
        dg2, bdg2 = sb("dg2", [128, 8, 128], F32)
        Dincl, bDincl = sb("Dincl", [128, 8, 128], BF16)
        AbT, bAbT = sb("AbT", [128, 8, 128], BF16)
        Ns = [sb(f"N{i}", [128, 8, 128], BF16) for i in range(2)]
        Ls = [sb(f"L{i}", [128, 8, 128], BF16) for i in range(2)]
        Ws = [sb(f"W{i}", [128, 8, 128], BF16) for i in range(2)]
        AqkT, bAqkT = sb("AqkT", [128, 8, 128], BF16)
        rhs2, brhs2 = sb("rhs2", [128, 8, 128], BF16)
        khat, bkhat = sb("khat", [128, 8, 64], BF16)
        qtl, bqtl = sb("qtl", [128, 8, 64], BF16)
        qtT, bqtT = sb("qtT", [64, 8, 128], BF16)
        U, bU = sb("U", [128, 8, 64], F32)
        WkT, bWkT = sb("WkT", [64, 8, 128], BF16)
        vnew, bvnew = sb("vnew", [128, 8, 64], BF16)
        S, bS = sb("S", [64, 8, 64], F32)
        Sb, bSb = sb("Sb", [64, 8, 64], BF16)
        Sb2, bSb2 = sb("Sb2", [64, 8, 64], BF16)
        osb, bosb = sb("osb", [128, 8, 64], F32)
        osq, bosq = sb("osq", [128, 8, 64], F32)
        oss, boss = sb("oss", [128, 8], F32)
        ggt, bggt = sb("ggt", [128, 8, 64], F32)
        ob, bob = sb("ob", [128, 8, 64], BF16)
        if banks is None:
            pA = [pst(f"pA{i}", [128, 512], F32) for i in range(2)]
            pL = [pst(f"pL{i}", [128, 512], F32) for i in range(2)]
            pW = [pst(f"pW{i}", [128, 512], F32) for i in range(2)]
            pX, bpX = pst("pX", [128, 512], F32)
            pTt, bpTt = pst("pTt", [128, 1024], BF16)
        else:
            pA = [banks[0], banks[0]]
            pL = [banks[1], banks[1]]
            pW = [banks[2], banks[2]]
            pX, bpX = banks[3]
            pTt, bpTt = banks[0][0][:].bitcast(BF16), banks[0][1]

        I("sp", "dma_start", [], [bidentb], out=identb[:], in_=D.cmask[:, 0, :])
        I("sp", "dma_start", [], [bmincl], out=mincl[:], in_=D.cmask[:, 6, :])
        I("sp", "dma_start", [], [bmstr], out=mstr[:], in_=D.cmask[:, 7, :])
        I("sp", "dma_start", [], [bcf], out=cf[:], in_=D.cf32[:, :, :])
        I("sp", "dma_start", [], [bidentf], out=identf[:], in_=D.cf32[:, 0, :])
        for h in range(8):
            I("sp", "dma_start", [], [bggdn], out=ggdn[:, h, :], in_=D.g_gdn[0:1, :].broadcast_to([128, 64]))
        I("sp", "dma_start", [], [bvmask], out=vmask[:], in_=D.vmask[:, :])
        jk, bjk = sb("jk", [128, 512], BF16)
        I("dve", "memset", [], [bjk], ap=jk[:], constant=0.0)
        JC = 0
        I("dve", "tensor_scalar_mul", [bggdn], [bggdn], out=ggdn[:].rearrange("p h d -> p (h d)"), in0=ggdn[:].rearrange("p h d -> p (h d)"), scalar1=0.5)

        def load(r0, si):
            (qkv, bqkv), (kT, bkT), (qT, bqT), (g_, bg_), (be, bbe) = insets[si]
            I("sp", "dma_start", [], [bqkv], out=qkv[:].rearrange("p a h d -> p (a h d)"), in_=D.gqkv[r0:r0 + 128, :])
            I("sp", "dma_start", [], [bg_], out=g_[:], in_=D.g[r0:r0 + 128, :])
            I("sp", "dma_start", [], [bbe], out=be[:], in_=D.beta[r0:r0 + 128, :])

        def process(si, qrow, want_o, sample):
            (qkv, bqkv), (kT, bkT), (qT, bqT), (g_, bg_), (be, bbe) = insets[si]
            if sample:
                I("dve", "tensor_scalar_mul", [bg_, bvmask], [bg_], out=g_[:], in0=g_[:], scalar1=vmask[:, 0:1])
                I("dve", "tensor_scalar_mul", [bqkv, bvmask], [bqkv], out=qkv[:].rearrange("p a h d -> p (a h d)"), in0=qkv[:].rearrange("p a h d -> p (a h d)"),
                  scalar1=vmask[:, 0:1])
            for h in range(8):
                I("pe", "transpose", [bqkv, bidentb], [bpTt], out=pTt[0:64, h * 128:(h + 1) * 128], in_=qkv[:, 1, h, :], identity=identb[:])
            I("act", "activation", [bpTt], [bkT], out=kT[:].rearrange("p h i -> p (h i)"), in_=pTt[0:64, :], func=AF.Copy)
            if want_o:
                for h in range(8):
                    I("pe", "transpose", [bqkv, bidentb], [bpTt], out=pTt[0:64, h * 128:(h + 1) * 128], in_=qkv[:, 0, h, :], identity=identb[:])
                I("dve", "tensor_copy", [bpTt], [bqT], out=qT[:].rearrange("p h i -> p (h i)"), in_=pTt[0:64, :])
            yield
            I("pe", "matmul", [bcf, bg_], [bpX], out=pX[:, 0:8], lhsT=cf[:, 4, :], rhs=g_[:], start=True, stop=True)
            I("dve", "tensor_copy", [bpX], [bgc], out=gc[:], in_=pX[:, 0:8])
            I("dve", "tensor_scalar_mul", [bgc], [bngc], out=ngc[:], in0=gc[:], scalar1=-1.0)
            I("pe", "matmul", [bcf, bgc], [bpX], out=pX[:, 8:16], lhsT=cf[:, 5, :], rhs=gc[:], start=True, stop=True)
            I("pe", "matmul", [bcf, bgc], [bpX], out=pX[:, 16:24], lhsT=cf[:, 3, :], rhs=gc[:], start=True, stop=True)
            I("dve", "tensor_copy", [bpX], [bgl12], out=gl12[:], in_=pX[:, 8:24])
            I("dve", "tensor_copy", [bgl12], [bglb], out=glb[0:64, :], in_=gl12[0:64, 0:8])
            I("dve", "tensor_copy", [bgl12], [bglb], out=glb[64:128, :], in_=gl12[64:128, 8:16])
            I("dve", "tensor_tensor", [bbe, bgc], [bvec2], out=vec2[:], in0=be[:, 8:16], in1=gc[:], op=ALU.add)
            I("act", "activation", [bgc], [beg], out=eg[:], in_=gc[:], func=AF.Exp)
            I("act", "activation", [bvec2], [bbee], out=bee[:], in_=vec2[:], func=AF.Exp)
            I("act", "activation", [bgl12], [begl], out=egl[:].rearrange("p a h -> p (a h)"), in_=gl12[:], func=AF.Exp)
            I("dve", "tensor_tensor", [bglb, bgc], [bekl], out=ekl[:], in0=glb[:], in1=gc[:], op=ALU.subtract)
            I("act", "activation", [bekl], [bekl], out=ekl[:], in_=ekl[:], func=AF.Exp)
            if sample:
                I("dve", "tensor_scalar_mul", [bbe, bvmask], [bbe], out=be[:, 0:8], in0=be[:, 0:8], scalar1=vmask[:, 0:1])
                I("dve", "tensor_scalar_mul", [bbee, bvmask], [bbee], out=bee[:], in0=bee[:], scalar1=vmask[:, 0:1])
            yield
            for h in range(8):
                if want_o:
                    I("dve", "tensor_scalar_mul", [bidentf, bgc], [bdg1], out=dg1[:, h, :], in0=identf[:], scalar1=gc[:, h:h + 1])
                I("dve", "tensor_scalar_mul", [bidentf, bvec2], [bdg2], out=dg2[:, h, :], in0=identf[:], scalar1=vec2[:, h:h + 1])
            N0, bN0 = Ns[0]
            L0, bL0 = Ls[0]
            yield
            for _ in range(JC):
                I("pe", "matmul", [bidentb, bjk], [bpX], out=pX[:, 0:512], lhsT=identb[:], rhs=jk[:, 0:512], start=True, stop=True)
            for hb in range(2):
                yield
                pK, bpK = pA[hb]
                pQ, bpQ = pL[hb]
                p1, bp1 = pW[hb]
                for hq in range(4):
                    h = hb * 4 + hq
                    sl = slice(hq * 128, (hq + 1) * 128)
                    I("pe", "matmul", [bkT], [bpK], out=pK[:, sl], lhsT=kT[:, h, :], rhs=kT[:, h, :], start=True, stop=True)
                    if want_o:
                        I("pe", "matmul", [bkT, bqT], [bpQ], out=pQ[:, sl], lhsT=kT[:, h, :], rhs=qT[:, h, :], start=True, stop=True)
                        I("pe", "matmul", [bcf, bdg1], [bp1], out=p1[:, sl], lhsT=cf[:, 2, :], rhs=dg1[:, h, :], start=True, stop=False)
                        I("pe", "matmul", [bidentb, bmincl], [bp1], out=p1[:, sl], lhsT=identb[:], rhs=mincl[:], start=False, stop=True)
                        I("act", "activation", [bp1, bngc], [bDincl], out=Dincl[:, h, :], in_=p1[:, sl], func=AF.Exp, bias=ngc[:, h:h + 1])
                for hq in range(4):
                    h = hb * 4 + hq
                    sl = slice(hq * 128, (hq + 1) * 128)
                    I("pe", "matmul", [bcf, bdg2], [bpX], out=pX[:, sl], lhsT=cf[:, 2, :], rhs=dg2[:, h, :], start=True, stop=False)
                    I("pe", "matmul", [bidentb, bmstr], [bpX], out=pX[:, sl], lhsT=identb[:], rhs=mstr[:], start=False, stop=True)
                    I("act", "activation", [bpX, bngc], [bAbT], out=AbT[:, h, :], in_=pX[:, sl], func=AF.Exp, bias=ngc[:, h:h + 1])
                hs = slice(hb * 4, hb * 4 + 4)
                fl = lambda t: t[:, hs, :].rearrange("p h i -> p (h i)")
                I("dve", "scalar_tensor_tensor", [bpK, bAbT], [bN0], out=fl(N0), in0=pK[:, :], scalar=-1.0, in1=fl(AbT), op0=ALU.mult, op1=ALU.mult)
                if want_o:
                    I("dve", "tensor_tensor", [bpQ, bDincl], [bAqkT], out=fl(AqkT), in0=pQ[:, :], in1=fl(Dincl), op=ALU.mult)
            yield
            for h in range(8):
                I("pe", "transpose", [bN0, bidentb], [bpTt], out=pTt[:, h * 128:(h + 1) * 128], in_=N0[:, h, :], identity=identb[:])
            I("act", "activation", [bpTt], [bL0], out=L0[:].rearrange("p h i -> p (h i)"), in_=pTt[:, :], func=AF.Copy)
            W0, bW0 = Ws[0]
            I("dve", "tensor_tensor", [bN0, bidentb], [bW0], out=W0[:], in0=N0[:], in1=identb[:].unsqueeze(1).broadcast_to([128, 8, 128]), op=ALU.add)
            cur = 0
            for m in range(1, 6):
                yield
                Nc, bNc = Ns[cur]
                Lc, bLc = Ls[cur]
                Wc, bWc = Ws[cur]
                Nn, bNn = Ns[1 - cur]
                Ln, bLn = Ls[1 - cur]
                Wn, bWn = Ws[1 - cur]
                def fl(t, hb):
                    return t[:, hb * 4:hb * 4 + 4, :].rearrange("p h i -> p (h i)")
                for hb in range(2):
                    pl_, bpl_ = pL[hb]
                    for hq in range(4):
                        h = hb * 4 + hq
                        I("pe", "matmul", [bNc, bLc], [bpl_], out=pl_[:, hq * 128:(hq + 1) * 128], lhsT=Nc[:, h, :], rhs=Lc[:, h, :], start=True, stop=True)
                    I("act", "activation", [bpl_], [bLn], out=fl(Ln, hb), in_=pl_[:, :], func=AF.Copy)
                if m < 5:
                    for hb in range(2):
                        pa_, bpa_ = pA[hb]
                        for hq in range(4):
                            h = hb * 4 + hq
                            I("pe", "matmul", [bNc, bLc], [bpa_], out=pa_[:, hq * 128:(hq + 1) * 128], lhsT=Lc[:, h, :], rhs=Nc[:, h, :], start=True, stop=True)
                        I("dve", "tensor_copy", [bpa_], [bNn], out=fl(Nn, hb), in_=pa_[:, :])
                for hb in range(2):
                    pw_, bpw_ = pW[hb]
                    for hq in range(4):
                        h = hb * 4 + hq
                        I("pe", "matmul", [bLn, bWc], [bpw_], out=pw_[:, hq * 128:(hq + 1) * 128], lhsT=Ln[:, h, :], rhs=Wc[:, h, :], start=True, stop=True)
                    I("dve", "tensor_tensor", [bpw_, bWc], [bWn], out=fl(Wn, hb), in0=pw_[:, :], in1=fl(Wc, hb), op=ALU.add)
                cur = 1 - cur
            Wf, bWf = Ws[cur]
            yield
            bc = lambda t: t[:, :].unsqueeze(2).broadcast_to([128, 8, 64])
            I("dve", "tensor_tensor", [bqkv, bbe], [brhs2], out=rhs2[:, :, 0:64], in0=qkv[:, 2, :, :], in1=be[:, 0:8].unsqueeze(2).broadcast_to([128, 8, 64]), op=ALU.mult)
            I("dve", "tensor_tensor", [bqkv, bbee], [brhs2], out=rhs2[:, :, 64:128], in0=qkv[:, 1, :, :], in1=bc(bee), op=ALU.mult)
            I("pool", "tensor_tensor", [bqkv, bekl], [bkhat], out=khat[:], in0=qkv[:, 1, :, :], in1=bc(ekl), op=ALU.mult)
            if want_o:
                I("pool", "tensor_tensor", [bqkv, beg], [bqtl], out=qtl[:], in0=qkv[:, 0, :, :], in1=bc(eg), op=ALU.mult)
            for hb in range(2):
                yield
                pu, bpu = pA[hb]
                for hq in range(4):
                    h = hb * 4 + hq
                    I("pe", "matmul", [bWf, brhs2], [bpu], out=pu[:, hq * 128:(hq + 1) * 128], lhsT=Wf[:, h, :], rhs=rhs2[:, h, :], start=True, stop=True)
                I("dve", "tensor_copy", [bpu], [bU], out=U[:, hb * 4:hb * 4 + 4, :], in_=pu[:, :].rearrange("p (h c) -> p h c", c=128)[:, :, 0:64])
                pk_, bpk_ = pL[hb]
                for hq in range(4):
                    h = hb * 4 + hq
                    I("pe", "matmul", [brhs2, bWf], [bpk_], out=pk_[0:64, hq * 128:(hq + 1) * 128], lhsT=rhs2[:, h, 64:128], rhs=Wf[:, h, :], start=True, stop=True)
                I("act", "activation", [bpk_], [bWkT], out=WkT[:, hb * 4:hb * 4 + 4, :].rearrange("p h i -> p (h i)"), in_=pk_[0:64, :], func=AF.Copy)
            if want_o:
                for h in range(8):
                    I("pe", "transpose", [bqtl, bidentb], [bpTt], out=pTt[0:64, h * 128:(h + 1) * 128], in_=qtl[:, h, :], identity=identb[:])
                I("act", "activation", [bpTt], [bqtT], out=qtT[:].rearrange("p h i -> p (h i)"), in_=pTt[0:64, :], func=AF.Copy)
            fo_ = lambda t: t[:].rearrange("p h d -> p (h d)")
            halves = [(slice(0, 64), Sb, bSb, 0), (slice(64, 128), Sb2, bSb2, 1)]
            for (rs_, Sc, bSc, ci) in halves:
                yield
                for h in range(8):
                    I("pe", "matmul", [bWkT, bSc], [bpX], out=pX[:, h * 64:(h + 1) * 64], lhsT=WkT[:, h, :], rhs=Sc[:, h, :], start=True, stop=True)
                I("dve", "tensor_tensor", [bU, bpX], [bvnew], out=fo_(vnew)[rs_, :], in0=fo_(U)[rs_, :], in1=pX[rs_, :], op=ALU.subtract)
                ps_, bps_ = pW[1]
                for h in range(8):
                    I("pe", "matmul", [bkhat, bvnew], [bps_], out=ps_[0:64, h * 64:(h + 1) * 64], lhsT=khat[rs_, h, :], rhs=vnew[rs_, h, :], start=True, stop=True)
                I("dve", "tensor_tensor", [bS, begl], [bS], out=S[:], in0=S[:], in1=egl[0:64, ci, :].unsqueeze(2).broadcast_to([64, 8, 64]), op=ALU.mult)
                I("dve", "tensor_tensor", [bS, bps_], [bS], out=fo_(S), in0=fo_(S), in1=ps_[0:64, :], op=ALU.add)
                Sn, bSn = (Sb2, bSb2) if ci == 0 else (Sb, bSb)
                if want_o or ci == 0:
                    pass
                I("act", "activation", [bS], [bSn], out=fo_(Sn), in_=fo_(S), func=AF.Copy)
                if want_o:
                    po, bpo = pW[0]
                    for h in range(8):
                        I("pe", "matmul", [bqtT, bSc], [bpo], out=po[:, h * 64:(h + 1) * 64], lhsT=qtT[:, h, :], rhs=Sc[:, h, :], start=True, stop=False)
                        I("pe", "matmul", [bAqkT, bvnew], [bpo], out=po[:, h * 64:(h + 1) * 64], lhsT=AqkT[:, h, :], rhs=vnew[:, h, :], start=False, stop=True)
                    I("act", "activation", [bpo], [bosb], out=fo_(osb)[rs_, :], in_=po[rs_, :], func=AF.Copy)
            yield
            if want_o:
                I("pool", "tensor_tensor", [bosb], [bosq], out=fo_(osq), in0=fo_(osb), in1=fo_(osb), op=ALU.mult)
                I("dve", "tensor_reduce", [bosq], [boss], out=oss[:], in_=osq[:], axis=AX.X, op=ALU.add)
                I("act", "activation", [boss], [boss], out=oss[:], in_=oss[:], func=AF.Sqrt, bias=EPS, scale=1.0 / 64)
                I("dve", "reciprocal", [boss], [boss], out=oss[:], in_=oss[:])
                I("sp", "dma_start", [], [bggt], out=fo_(ggt), in_=D.gg[qrow:qrow + 128, :])
                I("dve", "tensor_tensor", [bosb, boss], [bosb], out=osb[:], in0=osb[:], in1=oss[:, :].unsqueeze(2).broadcast_to([128, 8, 64]), op=ALU.mult)
                I("pool", "tensor_tensor", [bggt, bggdn], [bggt], out=fo_(ggt), in0=fo_(ggt), in1=fo_(ggdn), op=ALU.mult)
                I("dve", "tensor_tensor", [bosb, bggt], [bob], out=fo_(ob), in0=fo_(osb), in1=fo_(ggt), op=ALU.mult)
                I("pool", "dma_start", [bob], [], out=D.go[qrow:qrow + 128, :], in_=fo_(ob))

        blks = list(range(64)) if blocks is None else list(blocks)
        if blks:
            I("dve", "memset", [], [bvnew], ap=vnew[:].rearrange("p h d -> p (h d)"), constant=0.0)
            I("dve", "memset", [], [bS], ap=S[:].rearrange("p h d -> p (h d)"), constant=0.0)
            I("dve", "memset", [], [bSb], ap=Sb[:].rearrange("p h d -> p (h d)"), constant=0.0)
            load(blks[0] * 128, 0)
            for n_, b in enumerate(blks):
                if n_ + 1 < len(blks):
                    load(blks[n_ + 1] * 128, (n_ + 1) % 2)
                want = o_all or b >= 32
                yield from process(n_ % 2, max(b * 128 - NP_, 0), want, False)
            I("sp", "dma_start", [bS], [], out=D.sfin[0].rearrange("h k v -> k h v"), in_=S[:])
        for q in samples:
            load(NP_ + NO_ + q * 128, q % 2)
            I("sp", "dma_start", [], [bS], out=S[:], in_=D.sgdn[q].rearrange("h k v -> k h v"))
            I("act", "activation", [bS], [bSb], out=Sb[:].rearrange("p h d -> p (h d)"), in_=S[:].rearrange("p h d -> p (h d)"), func=AF.Copy)
            yield from process(q % 2, NO_ + q * 128, True, True)
            I("sp", "dma_start", [bS], [], out=D.sfin[1 + q].rearrange("h k v -> k h v"), in_=S[:])


QTILES = [(i * 512, 4) for i in range(8)] + [(NO_, 2)]


def phase_d1(nc, fw, D, tiles=None):
    I = fw.I
    with ExitStack() as st:
        def sb(name, shape, dt):
            return st.enter_context(nc.sbuf_tensor("D1_" + name, shape, dt)), Buf(name)

        def pst(name, shape, dt):
            return st.enter_context(nc.psum_tensor("D1_" + name, shape, dt)), Buf(name)
        identb, bidentb = sb("identb", [128, 128], BF16)
        wpa, bwpa = sb("wpa", [128, 4, 1024], BF16)
        wpb, bwpb = sb("wpb", [128, 4, 1024], BF16)
        wout, bwout = sb("wout", [128, 8, 1024], BF16)
        gpost, bgpost = sb("gpost", [128, 1024], F32)
        insets = [(sb(f"foT{i}", [128, 4, 512], BF16), sb(f"gob{i}", [128, 4, 512], BF16), sb(f"sgA{i}", [128, 8, 512], BF16),
                   sb(f"sgB{i}", [128, 8, 512], BF16), sb(f"xt{i}", [128, 4, 1024], F32)) for i in range(2)]
        goT, bgoT = sb("goT", [128, 4, 512], BF16)
        t1s = Rot([sb(f"t1{i}", [128, 512], F32) for i in range(2)])
        t2s = Rot([sb(f"t2{i}", [128, 512], F32) for i in range(2)])
        mT, bmT = sb("mT", [128, 8, 512], BF16)
        mixes = Rot([sb(f"mix{i}", [128, 1024], F32) for i in range(2)])
        junk, bjunk = sb("junk", [128, 1024], BF16)
        sss = [sb(f"ss{i}", [128, 1], F32) for i in range(4)]
        y1, by1 = sb("y1", [128, 4, 1024], F32)
        pa = Rot([pst(f"pa{i}", [128, 512], F32) for i in range(2)])
        pb = Rot([pst(f"pb{i}", [128, 512], F32) for i in range(2)])
        pm = Rot([pst(f"pm{i}", [128, 512], F32) for i in range(2)])
        ptr, bptr = pst("ptr", [128, 1024], BF16)

        I("sp", "dma_start", [], [bidentb], out=identb[:], in_=D.cmask[:, 0, :])
        I("pool", "dma_start", [], [bwpa], out=wpa[:], in_=D.w_pa.rearrange("(c p) n -> p c n", p=128))
        I("pool", "dma_start", [], [bwpb], out=wpb[:], in_=D.w_pb.rearrange("(c p) n -> p c n", p=128))
        I("pool", "dma_start", [], [bwout], out=wout[:], in_=D.w_out.rearrange("(c p) n -> p c n", p=128))
        I("sp", "dma_start", [], [bgpost], out=gpost[:], in_=D.g_mix_post[0:1, :].broadcast_to([128, 1024]))
        tl = QTILES if tiles is None else [QTILES[i] for i in tiles]
        def load_tile(tq, ns, si):
            N = ns * 128
            (foT, bfoT), (gob, bgob), (sgA, bsgA), (sgB, bsgB), (xt, bxt) = insets[si]
            I("sp", "dma_start", [], [bfoT], out=foT[:, :, 0:N], in_=D.foT[:, tq:tq + N].rearrange("(c p) n -> p c n", p=128))
            I("sp", "dma_start", [], [bgob], out=gob[:, 0:ns, :], in_=D.go[tq:tq + N, :].rearrange("(s p) n -> p s n", p=128))
            I("sp", "dma_start", [], [bsgA], out=sgA[:, :, 0:N], in_=D.sgA[:, tq:tq + N].rearrange("(c p) n -> p c n", p=128))
            I("sp", "dma_start", [], [bsgB], out=sgB[:, :, 0:N], in_=D.sgB[:, tq:tq + N].rearrange("(c p) n -> p c n", p=128))
            I("sp", "dma_start", [], [bxt], out=xt[:, 0:ns, :], in_=D.xall[NP_ + tq: NP_ + tq + N, :].rearrange("(s p) m -> p s m", p=128))
        if tl:
            load_tile(tl[0][0], tl[0][1], 0)
        for n_, (tq, ns) in enumerate(tl):
            N = ns * 128
            if n_ + 1 < len(tl):
                load_tile(tl[n_ + 1][0], tl[n_ + 1][1], (n_ + 1) % 2)
            (foT, bfoT), (gob, bgob), (sgA, bsgA), (sgB, bsgB), (xt, bxt) = insets[n_ % 2]
            for half in range(2):
                for cc in range(2):
                    c = half * 2 + cc
                    for s in range(ns):
                        I("pe", "transpose", [bgob, bidentb], [bptr], out=ptr[:, cc * 512 + s * 128: cc * 512 + (s + 1) * 128], in_=gob[:, s, c * 128:(c + 1) * 128],
                          identity=identb[:])
                for cc in range(2):
                    I("act", "activation", [bptr], [bgoT], out=goT[:, half * 2 + cc, 0:N], in_=ptr[:, cc * 512: cc * 512 + N], func=AF.Copy)
            for oc in range(8):
                p1, bp1 = pa.next()
                p2, bp2 = pb.next()
                for c in range(4):
                    I("pe", "matmul", [bwpa, bfoT], [bp1], out=p1[:, 0:N], lhsT=wpa[:, c, oc * 128:(oc + 1) * 128], rhs=foT[:, c, 0:N], start=(c == 0), stop=(c == 3))
                for c in range(4):
                    I("pe", "matmul", [bwpb, bgoT], [bp2], out=p2[:, 0:N], lhsT=wpb[:, c, oc * 128:(oc + 1) * 128], rhs=goT[:, c, 0:N], start=(c == 0), stop=(c == 3))
                t1, bt1 = t1s.next()
                t2, bt2 = t2s.next()
                I("dve", "tensor_tensor", [bp1, bsgA], [bt1], out=t1[:, 0:N], in0=p1[:, 0:N], in1=sgA[:, oc, 0:N], op=ALU.mult)
                I("dve", "tensor_tensor", [bp2, bsgB], [bt2], out=t2[:, 0:N], in0=p2[:, 0:N], in1=sgB[:, oc, 0:N], op=ALU.mult)
                I("pool", "tensor_tensor", [bt1, bt2], [bmT], out=mT[:, oc, 0:N], in0=t1[:, 0:N], in1=t2[:, 0:N], op=ALU.add)
            for s in range(ns):
                mix, bmix = mixes.next()
                ss, bss = sss[s]
                for cg in range(2):
                    p, bp = pm.next()
                    for kc in range(8):
                        I("pe", "matmul", [bmT, bwout], [bp], out=p[:, :], lhsT=mT[:, kc, s * 128:(s + 1) * 128], rhs=wout[:, kc, cg * 512:(cg + 1) * 512], start=(kc == 0), stop=(kc == 7))
                    if cg == 0:
                        I("act", "activation", [bp], [bmix], out=mix[:, 0:512], in_=p[:, :], func=AF.Copy)
                    else:
                        I("dve", "tensor_copy", [bp], [bmix], out=mix[:, 512:1024], in_=p[:, :])
                I("act", "activation", [bmix], [bjunk, bss], out=junk[:], in_=mix[:], func=AF.Square, accum_out=ss[:, 0:1])
                I("act", "activation", [bss], [bss], out=ss[:], in_=ss[:], func=AF.Sqrt, bias=EPS, scale=1.0 / 1024)
                I("dve", "reciprocal", [bss], [bss], out=ss[:], in_=ss[:])
                I("dve", "scalar_tensor_tensor", [bmix, bss, bgpost], [bmix], out=mix[:], in0=mix[:], scalar=ss[:, 0:1], in1=gpost[:], op0=ALU.mult, op1=ALU.mult)
                I("dve", "tensor_tensor", [bmix, bxt], [by1], out=y1[:, s, :], in0=mix[:], in1=xt[:, s, :], op=ALU.add)
            I("sp", "dma_start", [by1], [], out=D.y1[tq:tq + N, :].rearrange("(s p) m -> p s m", p=128), in_=y1[:, 0:ns, :])
        fw.barrier()
        fw.emit()


def phase_d2(nc, fw, D, tiles=None):
    I = fw.I
    with ExitStack() as st:
        def sb(name, shape, dt):
            return st.enter_context(nc.sbuf_tensor("D2_" + name, shape, dt)), Buf(name)

        def pst(name, shape, dt):
            return st.enter_context(nc.psum_tensor("D2_" + name, shape, dt)), Buf(name)
        identb, bidentb = sb("identb", [128, 128], BF16)
        wup, bwup = sb("wup", [128, 8, 4096], BF16)
        wdn, bwdn = sb("wdn", [128, 32, 1024], BF16)
        gpre, bgpre = sb("gpre", [128, 1024], F32)
        gpost, bgpost = sb("gpost", [128, 1024], F32)
        y1, by1 = sb("y1", [128, 4, 1024], F32)
        hbs = Rot([sb(f"hb{i}", [128, 1024], BF16) for i in range(2)])
        hT, bhT = sb("hT", [128, 8, 512], BF16)
        uT, buT = sb("uT", [128, 32, 512], BF16)
        rl = Rot([sb(f"rl{i}", [128, 512], F32) for i in range(2)])
        dsb, bdsb = sb("dsb", [128, 1024], F32)
        junk, bjunk = sb("junk", [128, 1024], BF16)
        ss, bss = sb("ss", [128, 4], F32)
        pu = Rot([pst(f"pu{i}", [128, 512], F32) for i in range(4)])
        pd = Rot([pst(f"pd{i}", [128, 512], F32) for i in range(2)])
        ptr = Rot([pst(f"ptr{i}", [128, 1024], BF16) for i in range(2)])

        I("sp", "dma_start", [], [bidentb], out=identb[:], in_=D.cmask[:, 0, :])
        for kc in range(8):
            I("pool", "dma_start", [], [bwup], out=wup[:, kc, :], in_=D.w_up[kc * 128:(kc + 1) * 128, :])
        for f4 in range(8):
            I("pool", "dma_start", [], [bwdn], out=wdn[:, f4 * 4:(f4 + 1) * 4, :], in_=D.w_down[f4 * 512:(f4 + 1) * 512, :].rearrange("(c p) n -> p c n", p=128))
        I("sp", "dma_start", [], [bgpre], out=gpre[:], in_=D.g_mlp_pre[0:1, :].broadcast_to([128, 1024]))
        I("sp", "dma_start", [], [bgpost], out=gpost[:], in_=D.g_mlp_post[0:1, :].broadcast_to([128, 1024]))
        tl = QTILES if tiles is None else [QTILES[i] for i in tiles]
        for (tq, ns) in tl:
            N = ns * 128
            I("sp", "dma_start", [], [by1], out=y1[:, 0:ns, :], in_=D.y1[tq:tq + N, :].rearrange("(s p) m -> p s m", p=128))
            for s in range(ns):
                I("act", "activation", [by1], [bjunk, bss], out=junk[:], in_=y1[:, s, :], func=AF.Square, accum_out=ss[:, s:s + 1])
            I("act", "activation", [bss], [bss], out=ss[:, 0:ns], in_=ss[:, 0:ns], func=AF.Sqrt, bias=EPS, scale=1.0 / 1024)
            I("dve", "reciprocal", [bss], [bss], out=ss[:, 0:ns], in_=ss[:, 0:ns])
            for s in range(ns):
                hb, bhb = hbs.next()
                I("dve", "scalar_tensor_tensor", [by1, bss, bgpre], [bhb], out=hb[:], in0=y1[:, s, :], scalar=ss[:, s:s + 1], in1=gpre[:], op0=ALU.mult, op1=ALU.mult)
                pt, bpt = ptr.next()
                for kc in range(8):
                    I("pe", "transpose", [bhb, bidentb], [bpt], out=pt[:, kc * 128:(kc + 1) * 128], in_=hb[:, kc * 128:(kc + 1) * 128], identity=identb[:])
                if s % 2 == 0:
                    I("act", "activation", [bpt], [bhT], out=hT[:, :, s * 128:(s + 1) * 128], in_=pt[:, :].rearrange("p (k t) -> p k t", t=128), func=AF.Copy)
                else:
                    I("dve", "tensor_copy", [bpt], [bhT], out=hT[:, :, s * 128:(s + 1) * 128], in_=pt[:, :].rearrange("p (k t) -> p k t", t=128))
            for fc in range(32):
                p, bp = pu.next()
                for kc in range(8):
                    I("pe", "matmul", [bwup, bhT], [bp], out=p[:, 0:N], lhsT=wup[:, kc, fc * 128:(fc + 1) * 128], rhs=hT[:, kc, 0:N], start=(kc == 0), stop=(kc == 7))
                r, br = rl.next()
                I("act", "activation", [bp], [br], out=r[:, 0:N], in_=p[:, 0:N], func=AF.Relu)
                eng = "pool" if fc % 2 == 0 else "dve"
                I(eng, "tensor_tensor", [br], [buT], out=uT[:, fc, 0:N], in0=r[:, 0:N], in1=r[:, 0:N], op=ALU.mult)
            for s in range(ns):
                for cg in range(2):
                    p, bp = pd.next()
                    for fc in range(32):
                        I("pe", "matmul", [buT, bwdn], [bp], out=p[:, :], lhsT=uT[:, fc, s * 128:(s + 1) * 128], rhs=wdn[:, fc, cg * 512:(cg + 1) * 512], start=(fc == 0), stop=(fc == 31))
                    if cg == 0:
                        I("act", "activation", [bp], [bdsb], out=dsb[:, 0:512], in_=p[:, :], func=AF.Copy)
                    else:
                        I("dve", "tensor_copy", [bp], [bdsb], out=dsb[:, 512:1024], in_=p[:, :])
                I("act", "activation", [bdsb], [bjunk, bss], out=junk[:], in_=dsb[:], func=AF.Square, accum_out=ss[:, s:s + 1])
                I("act", "activation", [bss], [bss], out=ss[:, s:s + 1], in_=ss[:, s:s + 1], func=AF.Sqrt, bias=EPS, scale=1.0 / 1024)
                I("dve", "reciprocal", [bss], [bss], out=ss[:, s:s + 1], in_=ss[:, s:s + 1])
                I("dve", "scalar_tensor_tensor", [bdsb, bss, bgpost], [bdsb], out=dsb[:], in0=dsb[:], scalar=ss[:, s:s + 1], in1=gpost[:], op0=ALU.mult, op1=ALU.mult)
                I("pool", "tensor_tensor", [bdsb, by1], [by1], out=y1[:, s, :], in0=dsb[:], in1=y1[:, s, :], op=ALU.add)
            I("sp", "dma_start", [by1], [], out=D.y[tq:tq + N, :].rearrange("(s p) m -> p s m", p=128), in_=y1[:, 0:ns, :])
        fw.barrier()
        fw.emit()


def phase_bc(nc, fw, D, ratio=2.7):
    with ExitStack() as st:
        banks = []
        for i in range(8):
            t = st.enter_context(nc.psum_tensor(f"BC_ps{i}", [128, 512], F32))
            banks.append((t, Buf(f"BC_ps{i}")))
        gb = phase_b_body(nc, fw, D, st, banks[0:4])
        gc = phase_c_body(nc, fw, D, st, banks[4:8])
        b_alive = c_alive = True
        acc = 0.0
        while b_alive or c_alive:
            if c_alive:
                try:
                    next(gc)
                except StopIteration:
                    c_alive = False
            acc += ratio if c_alive else 1.0
            while b_alive and acc >= 1.0:
                acc -= 1.0
                try:
                    next(gb)
                except StopIteration:
                    b_alive = False
        fw.barrier()
        fw.emit()


BF = ml_dtypes.bfloat16


def const_masks():
    p = np.arange(128)
    cm = np.zeros((128, 8, 128), np.float32)
    cm[:, 0, :] = np.eye(128)
    cm[:, 1, :] = (p[:, None] // 64 == p[None, :] // 64)
    NEG = -30000.0
    cm[:, 2, :] = np.where(p[None, :] >= p[:, None], 0.0, NEG)
    cm[:, 3, :] = np.where(p[None, :] > p[:, None], 0.0, NEG)
    cm[:, 4, :] = np.where(p[:, None] > p[None, :], 0.0, NEG)
    cm[:, 5, :] = 1.0
    same = (p[:, None] // 64 == p[None, :] // 64)
    cm[:, 6, :] = np.where(same & (p[None, :] >= p[:, None]), 0.0, NEG)
    cm[:, 7, :] = np.where(same & (p[None, :] > p[:, None]), 0.0, NEG)
    cf = np.zeros((128, 6, 128), np.float32)
    cf[:, 0, :] = np.eye(128)
    cf[:, 1, :] = (p[:, None] <= p[None, :])
    cf[:, 2, :] = 1.0
    cf[127, 3, :] = 1.0
    cf[:, 4, :] = (p[:, None] <= p[None, :]) & (p[:, None] // 64 == p[None, :] // 64)
    cf[63, 5, :] = 1.0
    return cm.astype(BF), cf


def prep_core(inp, c):
    b, half = c // 2, c % 2
    xp = inp["x_prompt"][b]
    xall = np.zeros((4096 + 4096 + 256, 1024), np.float32)
    kmask = np.zeros((128, 32), np.float32)
    if half == 1:
        xall[:8192] = xp
    else:
        xall[4096:8192] = xp[:4096]
        kmask[:] = -30000.0
    for q in range(2):
        xall[8192 + q * 128: 8192 + q * 128 + 16] = inp["x_sample"][2 * c + q]
    w_in = inp["w_in"][0]
    w_sm = np.concatenate([w_in[:, 1536:1544], w_in[:, 3080:3096]], axis=1)
    bias24 = np.concatenate([inp["fox_forget_bias"][0], inp["gdn_dt_bias"][0], np.zeros(8, np.float32)])[None, :]
    sgn24 = np.concatenate([-np.ones(8), np.ones(8), -np.ones(8)]).astype(np.float32)[None, :]
    convT = np.ascontiguousarray(inp["gdn_conv_w"][0].T.reshape(12, 128, 4).transpose(1, 0, 2))
    ch = inp["state_gdn_conv"][0, 2 * c: 2 * c + 2]
    conv_hist = np.ascontiguousarray(ch.transpose(0, 2, 1).reshape(2, 12, 128, 3).transpose(0, 2, 1, 3))
    cm, cf = const_masks()
    d = {
        "xall": xall, "kmask": kmask, "vmask": (np.arange(128) < 16).astype(np.float32)[:, None], "w_in": w_in, "w_sm": np.ascontiguousarray(w_sm), "bias24": bias24.astype(np.float32),
        "sgn24": sgn24, "a_log": inp["gdn_a_log"], "convT": convT, "conv_hist": conv_hist,
        "g_mix_pre": inp["norm_mix_pre"], "g_mix_post": inp["norm_mix_post"], "g_mlp_pre": inp["norm_mlp_pre"], "g_mlp_post": inp["norm_mlp_post"],
        "g_gdn": inp["gdn_norm_g"], "w_pa": inp["w_proj_fox"][0], "w_pb": inp["w_proj_gdn"][0], "w_out": inp["w_out"][0],
        "w_up": inp["w_up"][0], "w_down": inp["w_down"][0],
        "ck": inp["cache_fox_k"][0, 2 * c:2 * c + 2].reshape(2, 2048, 512), "cv": inp["cache_fox_v"][0, 2 * c:2 * c + 2].reshape(2, 2048, 512),
        "clf": inp["cache_fox_logf"][0, 2 * c:2 * c + 2], "sgdn": inp["state_gdn"][0, 2 * c:2 * c + 2],
        "cmask": cm, "cf32": cf,
    }
    return {k: np.ascontiguousarray(v) for k, v in d.items()}


def build_program():
    nc = bass.Bass("TRN2", target_bir_lowering=False)
    with ExitStack() as st:
        D = declare(nc, False)
        fw = FW(nc, st)
        phase_a(nc, fw, D)
        phase_b(nc, fw, D)
        phase_c(nc, fw, D)
        phase_d1(nc, fw, D)
        phase_d2(nc, fw, D)
    return nc


def kernel(**inp):
    inp = {k: np.asarray(v) for k, v in inp.items()}
    nc = build_program()
    in_maps = [prep_core(inp, c) for c in range(8)]
    res = run_bass_kernel_spmd(nc, in_maps, core_ids=list(range(8)))
    R = res.results
    f = lambda a: np.asarray(a, dtype=np.float32)
    y_p = np.zeros((4, 8192, 1024), np.float32)
    y_s = np.zeros((16, 16, 1024), np.float32)
    fk_p = np.zeros((1, 4, 8192, 8, 64), np.float32)
    fv_p = np.zeros_like(fk_p)
    lf_p = np.zeros((1, 4, 8192, 8), np.float32)
    sg_p = np.zeros((1, 4, 8, 64, 64), np.float32)
    cv_p = np.zeros((1, 4, 3, 1536), np.float32)
    fk_s = np.zeros((1, 16, 16, 8, 64), np.float32)
    fv_s = np.zeros_like(fk_s)
    lf_s = np.zeros((1, 16, 16, 8), np.float32)
    sg_s = np.zeros((1, 16, 8, 64, 64), np.float32)
    cv_s = np.zeros((1, 16, 3, 1536), np.float32)
    for c in range(8):
        b, half = c // 2, c % 2
        r = R[c]
        sl = slice(half * 4096, (half + 1) * 4096)
        y_p[b, sl] = f(r["y"])[:4096]
        fk_p[0, b, sl] = f(r["fk"])[:4096].reshape(4096, 8, 64)
        fv_p[0, b, sl] = f(r["fv"])[:4096].reshape(4096, 8, 64)
        lf_p[0, b, sl] = f(r["lf"])[:4096]
        if half == 1:
            sg_p[0, b] = f(r["sfin"])[0]
            cv_p[0, b] = f(r["convo"])[0]
        for q in range(2):
            s = 2 * c + q
            rows = slice(4096 + q * 128, 4096 + q * 128 + 16)
            y_s[s] = f(r["y"])[rows]
            fk_s[0, s] = f(r["fk"])[rows].reshape(16, 8, 64)
            fv_s[0, s] = f(r["fv"])[rows].reshape(16, 8, 64)
            lf_s[0, s] = f(r["lf"])[rows]
            sg_s[0, s] = f(r["sfin"])[1 + q]
            cv_s[0, s] = f(r["convo"])[1 + q]
    return (y_p, y_s, fk_p, fv_p, lf_p, sg_p, cv_p, fk_s, fv_s, lf_s, sg_s, cv_s)
```
